# Optimizing a Trainium2 kernel written in Bass

```python
import math
import jax, jax.numpy as jnp
from jax import lax
import numpy as np


D_MODEL = 2048
BATCH = 4
SEQ = 4096
DEPTH = 4

N_MIXERS = 3
EPS = 1e-6

GLA_HEADS = 4
GLA_DK = D_MODEL // 2
GLA_DV = D_MODEL
GLA_HK = GLA_DK // GLA_HEADS
GLA_HV = GLA_DV // GLA_HEADS
GLA_RANK = 16
GLA_TAU = 16.0
GLA_CHUNK = 64
GLA_IN = 2 * GLA_DK + 2 * GLA_DV + GLA_RANK

POOL_WINDOWS = (2, 4, 8, 16)
POOL_GROUPS = 4
POOL_GW = D_MODEL // POOL_GROUPS

DIFF_HEADS = 8
DIFF_HD = D_MODEL // DIFF_HEADS // 2
DIFF_VD = 2 * DIFF_HD
Q_BLOCK = 128
REL_BUCKETS = 32
REL_MAX_DIST = 128

FFN_HIDDEN = ((8 * D_MODEL + 3 * 256 - 1) // (3 * 256)) * 256

N_GLA = (DEPTH + 2) // N_MIXERS
N_POOL = (DEPTH + 1) // N_MIXERS
N_DIFF = DEPTH // N_MIXERS

kernel_name = "hybrid_gla_pool_diffattn_trunk"


def rms_norm(x, g):
    xf = x.astype(jnp.float32)
    y = xf * lax.rsqrt(jnp.mean(xf * xf, axis=-1, keepdims=True) + EPS)
    return (y * g.astype(jnp.float32)).astype(x.dtype)


def gla_mixer(h, w_in, w_a2, b_a, g_norm, w_out):
    B, S, _ = h.shape
    f32 = jnp.float32
    C = GLA_CHUNK
    nc = S // C
    proj = h @ w_in
    q, k, v, r, a_lr = jnp.split(
        proj, [GLA_DK, 2 * GLA_DK, 2 * GLA_DK + GLA_DV, 2 * GLA_DK + 2 * GLA_DV], axis=-1)
    log_a = jax.nn.log_sigmoid((a_lr @ w_a2 + b_a).astype(f32)) / GLA_TAU

    def to_chunks(t, hd):
        return t.astype(f32).reshape(B, nc, C, GLA_HEADS, hd).transpose(1, 0, 3, 2, 4)

    qc = to_chunks(q, GLA_HK) * (GLA_HK ** -0.5)
    kc = to_chunks(k, GLA_HK)
    vc = to_chunks(v, GLA_HV)
    bc = jnp.cumsum(to_chunks(log_a, GLA_HK), axis=3)
    causal = jnp.tril(jnp.ones((C, C), dtype=bool))[:, :, None]

    def step(state, inp):
        qb, kb, vb, bb = inp
        o_inter = jnp.einsum('bhtd,bhdv->bhtv', qb * jnp.exp(bb), state)
        diff = bb[:, :, :, None, :] - bb[:, :, None, :, :]
        decay = jnp.exp(jnp.where(causal, diff, -jnp.inf))
        scores = jnp.einsum('bhtd,bhsd,bhtsd->bhts', qb, kb, decay)
        o_intra = jnp.einsum('bhts,bhsv->bhtv', scores, vb)
        b_last = bb[:, :, -1:, :]
        k_dec = kb * jnp.exp(b_last - bb)
        state = jnp.exp(b_last[:, :, 0, :, None]) * state + jnp.einsum('bhsd,bhsv->bhdv', k_dec, vb)
        return state, o_inter + o_intra

    s0 = jnp.zeros((B, GLA_HEADS, GLA_HK, GLA_HV), f32)
    _, o = lax.scan(step, s0, (qc, kc, vc, bc))
    o = o.transpose(1, 0, 3, 2, 4).reshape(B, S, GLA_HEADS, GLA_HV)
    o = rms_norm(o, g_norm).reshape(B, S, GLA_DV) * jax.nn.silu(r.astype(f32))
    return o.astype(h.dtype) @ w_out


def pool_mixer(h, w_pool, scale):
    B, S, D = h.shape
    f32 = jnp.float32
    hf = h.astype(f32)
    cs = jnp.pad(jnp.cumsum(hf, axis=1), ((0, 0), (1, 0), (0, 0)))
    t = jnp.arange(S)
    outs = []
    for g, w in enumerate(POOL_WINDOWS):
        c = cs[:, :, g * POOL_GW:(g + 1) * POOL_GW]
        start = jnp.maximum(t + 1 - w, 0)
        count = (t + 1 - start).astype(f32)
        pooled = (c[:, 1:] - c[:, start]) / count[None, :, None]
        outs.append(pooled - hf[:, :, g * POOL_GW:(g + 1) * POOL_GW])
    y = jnp.stack(outs, axis=2).astype(h.dtype)
    y = jnp.einsum('bsgc,gcd->bsgd', y, w_pool).reshape(B, S, D)
    return y * scale


def rel_bucket(rel):
    n = jnp.maximum(rel, 0)
    max_exact = REL_BUCKETS // 2
    nf = jnp.maximum(n, 1).astype(jnp.float32)
    large = max_exact + (jnp.log(nf / max_exact) / math.log(REL_MAX_DIST / max_exact)
                         * (REL_BUCKETS - max_exact)).astype(jnp.int32)
    large = jnp.minimum(large, REL_BUCKETS - 1)
    return jnp.where(n < max_exact, n, large)


def diff_attn_mixer(h, w_in, q_gain, k_gain, lam_params, sub_gain, w_out, rel_table, layer_idx):
    B, S, D = h.shape
    f32 = jnp.float32
    H2 = 2 * DIFF_HEADS
    proj = h @ w_in
    q, k, v = jnp.split(proj, [D, 2 * D], axis=-1)
    q = rms_norm(q.reshape(B, S, H2, DIFF_HD), q_gain).transpose(0, 2, 1, 3)
    k = rms_norm(k.reshape(B, S, H2, DIFF_HD), k_gain).transpose(0, 2, 1, 3)
    v = v.reshape(B, S, DIFF_HEADS, DIFF_VD).transpose(0, 2, 1, 3)
    lam_init = 0.8 - 0.6 * math.exp(-0.3 * layer_idx)
    lp = lam_params.astype(f32)
    lam = jnp.exp(jnp.sum(lp[0] * lp[1])) - jnp.exp(jnp.sum(lp[2] * lp[3])) + lam_init
    nb = S // Q_BLOCK
    qb = q.reshape(B, H2, nb, Q_BLOCK, DIFF_HD).transpose(2, 0, 1, 3, 4)
    kpos = jnp.arange(S)
    scale = DIFF_HD ** -0.5

    def block(args):
        qblk, bi = args
        qpos = bi * Q_BLOCK + jnp.arange(Q_BLOCK)
        rel = qpos[:, None] - kpos[None, :]
        bias = rel_table[rel_bucket(rel)].astype(f32).transpose(2, 0, 1)
        logits = jnp.einsum('bhqd,bhkd->bhqk', qblk, k).astype(f32) * scale + bias
        logits = jnp.where(rel >= 0, logits, -jnp.inf)
        p = jax.nn.softmax(logits, axis=-1).reshape(B, DIFF_HEADS, 2, Q_BLOCK, S)
        attn = p[:, :, 0] - lam * p[:, :, 1]
        return jnp.einsum('bhqk,bhkd->bhqd', attn.astype(v.dtype), v)

    o = lax.map(block, (qb, jnp.arange(nb)))
    o = o.transpose(1, 0, 3, 2, 4).reshape(B, S, DIFF_HEADS, DIFF_VD)
    o = rms_norm(o, sub_gain) * (1.0 - lam_init)
    return o.reshape(B, S, D).astype(h.dtype) @ w_out


def swiglu(h, w_gu, w_down):
    g, u = jnp.split(h @ w_gu, 2, axis=-1)
    return (jax.nn.silu(g) * u) @ w_down


def setup_inputs(seed: int = 0) -> dict:
    key = jax.random.key(seed)
    ks = jax.random.split(key, 20)
    D = D_MODEL
    nrm = jax.random.normal
    out_scale = (2 * DEPTH) ** -0.5
    return {
        "x": nrm(ks[0], (BATCH, SEQ, D), jnp.float32),
        "norm_g": 1.0 + 0.02 * nrm(ks[1], (DEPTH, 2, D), jnp.float32),
        "gla_w_in": nrm(ks[2], (N_GLA, D, GLA_IN), jnp.float32) * D ** -0.5,
        "gla_w_a2": nrm(ks[3], (N_GLA, GLA_RANK, GLA_DK), jnp.float32) * GLA_RANK ** -0.5,
        "gla_b_a": 0.1 * nrm(ks[4], (N_GLA, GLA_DK), jnp.float32),
        "gla_g_norm": 1.0 + 0.02 * nrm(ks[5], (N_GLA, GLA_HV), jnp.float32),
        "gla_w_out": nrm(ks[6], (N_GLA, GLA_DV, D), jnp.float32) * GLA_DV ** -0.5 * out_scale,
        "pool_w": nrm(ks[7], (N_POOL, POOL_GROUPS, POOL_GW, POOL_GW), jnp.float32) * POOL_GW ** -0.5,
        "pool_scale": 1.0 + 0.1 * nrm(ks[8], (N_POOL, D), jnp.float32),
        "diff_w_in": nrm(ks[9], (N_DIFF, D, 3 * D), jnp.float32) * D ** -0.5,
        "diff_q_gain": 1.0 + 0.02 * nrm(ks[10], (N_DIFF, DIFF_HD), jnp.float32),
        "diff_k_gain": 1.0 + 0.02 * nrm(ks[11], (N_DIFF, DIFF_HD), jnp.float32),
        "diff_lambda": 0.1 * nrm(ks[12], (N_DIFF, 4, DIFF_HD), jnp.float32),
        "diff_sub_gain": 1.0 + 0.02 * nrm(ks[13], (N_DIFF, DIFF_VD), jnp.float32),
        "diff_w_out": nrm(ks[14], (N_DIFF, D, D), jnp.float32) * D ** -0.5 * out_scale,
        "rel_bias": 0.5 * nrm(ks[15], (REL_BUCKETS, 2 * DIFF_HEADS), jnp.float32),
        "ffn_w_gu": nrm(ks[16], (DEPTH, D, 2 * FFN_HIDDEN), jnp.float32) * D ** -0.5,
        "ffn_w_down": nrm(ks[17], (DEPTH, FFN_HIDDEN, D), jnp.float32) * FFN_HIDDEN ** -0.5 * out_scale,
    }


def reference(x, norm_g, gla_w_in, gla_w_a2, gla_b_a, gla_g_norm, gla_w_out,
              pool_w, pool_scale, diff_w_in, diff_q_gain, diff_k_gain, diff_lambda,
              diff_sub_gain, diff_w_out, rel_bias, ffn_w_gu, ffn_w_down):
    for i in range(DEPTH):
        kind = i % N_MIXERS
        slot = i // N_MIXERS
        h = rms_norm(x, norm_g[i, 0])
        if kind == 0:
            y = gla_mixer(h, gla_w_in[slot], gla_w_a2[slot], gla_b_a[slot],
                          gla_g_norm[slot], gla_w_out[slot])
        elif kind == 1:
            y = pool_mixer(h, pool_w[slot], pool_scale[slot])
        else:
            y = diff_attn_mixer(h, diff_w_in[slot], diff_q_gain[slot], diff_k_gain[slot],
                                diff_lambda[slot], diff_sub_gain[slot], diff_w_out[slot],
                                rel_bias, i)
        x = x + y.astype(x.dtype)
        x = x + swiglu(rms_norm(x, norm_g[i, 1]), ffn_w_gu[i], ffn_w_down[i]).astype(x.dtype)
    return x
```

```python
from contextlib import ExitStack
import math
import numpy as np
import concourse.bass as bass
import concourse.mybir as mybir
from concourse.bass_utils import run_bass_kernel_spmd

F32 = mybir.dt.float32
BF16 = mybir.dt.bfloat16
AF = mybir.ActivationFunctionType
ALU = mybir.AluOpType
AX = mybir.AxisListType

D = 2048
DC = D // 128
TC = 2048
FH = 5632
HC = FH // 128
EPS = 1e-6
NCORES = 8

ENGINES = ("pe", "act", "dve", "pool", "sp")


class Op:
    __slots__ = ("eng", "fn", "reads", "writes", "dma", "waits", "sig", "idx", "cc", "barrier")

    def __init__(self, eng, fn, reads, writes, dma):
        self.cc = False
        self.barrier = False
        self.eng = eng
        self.fn = fn
        self.reads = reads
        self.writes = writes
        self.dma = dma
        self.waits = []
        self.sig = None
        self.idx = -1


class Prog:
    NDMA = 32

    def __init__(self, nc):
        self.nc = nc
        self.ops = []

    LOOPVARS = frozenset("tt tsl j jsl h hs dc di dl blk c fc fi fb d dd ob vb rb i ei s sl mi grp b g w at atk kd kdk pq pk_ pv py_ ptr pkv cur oth sh a Wq Wk Wv Wr Wo wb db dp hf step q qk qb kb ki qi hp".split())

    def op(self, eng, fn, reads=(), writes=(), dma=False):
        bad = self.LOOPVARS.intersection(fn.__code__.co_freevars)
        if bad:
            raise RuntimeError("late-bound loop variable(s) %s in lambda at line %d" % (sorted(bad), fn.__code__.co_firstlineno))
        o = Op(eng, fn, tuple(reads), tuple(writes), dma)
        o.idx = len(self.ops)
        self.ops.append(o)
        return o

    def pe(self, fn, reads=(), writes=()):
        return self.op("pe", fn, reads, writes)

    def act(self, fn, reads=(), writes=()):
        return self.op("act", fn, reads, writes)

    def dve(self, fn, reads=(), writes=()):
        return self.op("dve", fn, reads, writes)

    def pool(self, fn, reads=(), writes=()):
        return self.op("pool", fn, reads, writes)

    def dma(self, eng, fn, reads=(), writes=()):
        return self.op(eng, fn, reads, writes, dma=True)

    def cc(self, fn, reads=(), writes=()):
        o = self.op("pool", fn, reads, writes, dma=True)
        o.cc = True
        return o

    def barrier(self, fn):
        o = self.op("dve", fn, (), ())
        o.barrier = True
        return o

    def finalize(self, final_wait_keys=()):
        ops = self.ops
        deps = [set() for _ in ops]
        last_writer = {}
        readers = {}
        since = []
        last_barrier = None
        for o in ops:
            if o.barrier:
                deps[o.idx] |= set(since)
                if last_barrier is not None:
                    deps[o.idx].add(last_barrier)
                since = []
                last_barrier = o.idx
                last_writer_final = dict(last_writer)
                last_writer = {}
                readers = {}
                continue
            since.append(o.idx)
            if last_barrier is not None:
                deps[o.idx].add(last_barrier)
            for k in o.reads:
                w = last_writer.get(k)
                if w is not None:
                    deps[o.idx].add(w.idx)
            for k in o.writes:
                w = last_writer.get(k)
                if w is not None:
                    deps[o.idx].add(w.idx)
                for r in readers.get(k, ()):
                    if r.idx != o.idx:
                        deps[o.idx].add(r.idx)
            for k in o.reads:
                readers.setdefault(k, []).append(o)
            for k in o.writes:
                last_writer[k] = o
                readers[k] = []
        final_ops = [last_writer[k].idx for k in final_wait_keys if k in last_writer]
        if last_barrier is not None:
            final_ops.append(last_barrier)
        needed = [set() for _ in ops]
        for o in ops:
            best = {}
            for d in deps[o.idx]:
                p = ops[d]
                if p.dma:
                    needed[o.idx].add(d)
                    continue
                if p.eng == "pe" and o.eng == "pe" and not o.dma:
                    continue
                if best.get(p.eng, -1) < d:
                    best[p.eng] = d
            needed[o.idx] |= set(best.values())
        signaled = set(final_ops)
        for o in ops:
            signaled |= needed[o.idx]
        eng_count = {e: 0 for e in ENGINES}
        NDMA = self.NDMA
        dma_count = [0] * NDMA
        dma_last = [None] * NDMA
        rr = 0
        cc_count = 0
        cc_last = None
        for o in ops:
            if o.dma and not o.cc:
                signaled.add(o.idx)
            if o.cc:
                signaled.add(o.idx)
                if cc_last is not None:
                    needed[o.idx].add(cc_last)
                cc_count += 1
                cc_last = o.idx
                o.sig = (("cc", 0), None, cc_count)
                continue
            if o.idx not in signaled:
                continue
            if o.dma:
                s = rr % NDMA
                rr += 1
                if dma_last[s] is not None:
                    needed[o.idx].add(dma_last[s])
                dma_count[s] += 1
                dma_last[s] = o.idx
                o.sig = (("dma", s), 16, dma_count[s] * 16)
            else:
                eng_count[o.eng] += 1
                o.sig = (("eng", o.eng), 1, eng_count[o.eng])
        seen = {e: {} for e in ENGINES}
        for o in ops:
            ws = {}
            for d in needed[o.idx]:
                semkey, _, val = ops[d].sig
                if ws.get(semkey, 0) < val:
                    ws[semkey] = val
            for semkey, val in ws.items():
                if seen[o.eng].get(semkey, 0) >= val:
                    continue
                seen[o.eng][semkey] = val
                o.waits.append((semkey, val))
        fw = {}
        for d in final_ops:
            semkey, _, val = ops[d].sig
            fw[semkey] = max(fw.get(semkey, 0), val)
        self.final_waits = fw
        self.eng_count = eng_count

    def emit(self, block, sems):
        per_eng = {e: [] for e in ENGINES}
        for o in self.ops:
            per_eng[o.eng].append(o)
        final_waits = self.final_waits

        def run(engobj, lst, is_last):
            for o in lst:
                for semkey, val in o.waits:
                    engobj.wait_ge(sems[semkey], val)
                ins = o.fn(engobj)
                if o.sig is not None:
                    if o.sig[1] is None:
                        ins.then_inc(sems[o.sig[0]])
                    else:
                        ins.then_inc(sems[o.sig[0]], o.sig[1])
            if is_last:
                for semkey, val in final_waits.items():
                    engobj.wait_ge(sems[semkey], val)

        @block.tensor
        def _(e):
            run(e, per_eng["pe"], False)

        @block.scalar
        def _(e):
            run(e, per_eng["act"], False)

        @block.vector
        def _(e):
            run(e, per_eng["dve"], False)

        @block.gpsimd
        def _(e):
            run(e, per_eng["pool"], False)

        @block.sync
        def _(e):
            run(e, per_eng["sp"], True)


class Phase:
    def __init__(self, master=None, prefix="", bind=None):
        self.master = master
        self.prefix = prefix
        self.bind = bind or {}
        if master is None:
            self.nc = bass.Bass("TRN2", target_bir_lowering=False)
            self.P = Prog(self.nc)
            self.out_keys = []
        else:
            self.nc = master.nc
            self.P = master.P
            self.out_keys = master.out_keys
        self.es = ExitStack()

    def din(self, name, shape, dtype=F32):
        if name in self.bind:
            return self.bind[name]
        return self.nc.dram_tensor(self.prefix + name, list(shape), dtype, kind="ExternalInput").ap()

    def dout(self, name, shape, dtype=F32):
        if name in self.bind:
            return self.bind[name]
        return self.nc.dram_tensor(self.prefix + name, list(shape), dtype, kind="ExternalOutput").ap()

    def dint(self, name, shape, dtype=F32):
        return self.nc.dram_tensor(self.prefix + name, list(shape), dtype).ap()

    def sb(self, name, shape, dtype=F32):
        return self.es.enter_context(self.nc.sbuf_tensor(self.prefix + name, list(shape), dtype))

    def ps(self, name, shape, dtype=F32):
        return self.es.enter_context(self.nc.psum_tensor(self.prefix + name, list(shape), dtype))

    def finish(self):
        if self.master is not None:
            self.es.close()
            bt = self.master.btile
            self.P.barrier(lambda e: e.memset(bt[:], 0.0))
            return None
        P = self.P
        P.finalize(final_wait_keys=self.out_keys)
        sems = {}
        for e in ENGINES:
            sems[("eng", e)] = self.es.enter_context(self.nc.semaphore("s_" + e))
        for i in range(P.NDMA):
            sems[("dma", i)] = self.es.enter_context(self.nc.semaphore("d%d" % i))
        sems[("cc", 0)] = self.es.enter_context(self.nc.semaphore("s_cc"))
        block = self.es.enter_context(self.nc.Block())
        P.emit(block, sems)
        self.es.close()
        return self.nc


def emit_rmsnorm(ph, xT, hT, gcol, ones_bf, sq, pss, rstd, ntok, xkey, hkey, tag):
    P = ph.P
    nsub = ntok // 512
    for s in range(nsub):
        sl = slice(s * 512, (s + 1) * 512)
        pb = pss[s % 2]
        pk = "pss%d" % (s % 2)
        for c in range(DC):
            q = sq[c % 2]
            qk = "sq%d" % (c % 2)
            P.act(lambda e, q=q, c=c, sl=sl: e.activation(out=q[:], in_=xT[:, c, sl], func=AF.Square),
                  reads=[(xkey, c)], writes=[qk])
            P.pe(lambda e, q=q, c=c, pb=pb: e.matmul(pb[:], ones_bf[:], q[:], start=(c == 0), stop=(c == DC - 1)),
                 reads=[qk, "ones"], writes=[pk])
        rk = ("rstd", tag, s)
        P.dve(lambda e, pb=pb, sl=sl: e.tensor_scalar(out=rstd[:, sl], in0=pb[:], scalar1=1.0 / D, scalar2=EPS,
                                                      op0=ALU.mult, op1=ALU.add),
              reads=[pk], writes=[rk])
        P.act(lambda e, sl=sl: e.activation(out=rstd[:, sl], in_=rstd[:, sl], func=AF.Sqrt),
              reads=[rk], writes=[rk])
        P.dve(lambda e, sl=sl: e.reciprocal(out=rstd[:, sl], in_=rstd[:, sl]),
              reads=[rk], writes=[rk])
        for c in range(DC):
            P.dve(lambda e, c=c, sl=sl: e.scalar_tensor_tensor(out=hT[:, c, sl], in0=xT[:, c, sl],
                                                               scalar=gcol[:, c:c + 1], in1=rstd[:, sl],
                                                               op0=ALU.mult, op1=ALU.mult),
                  reads=[(xkey, c), rk, "gcol"], writes=[(hkey, c, s)])


def build_ffn_phase(ph=None):
    ph = ph or Phase()
    P = ph.P
    nc = ph.nc
    xin = ph.din("xT", [D, TC])
    g_in = ph.din("g", [128, DC])
    wgu = ph.din("wgu", [D, 2 * FH])
    wdn = ph.din("wdn", [FH, D])
    xout = ph.dout("xTo", [D, TC])
    NT = 1024
    NS = NT // 512
    HG = 11
    NG = HC // HG
    xT = ph.sb("xTs", [128, DC, NT], F32)
    hT = ph.sb("hTs", [128, DC, NT], BF16)
    aT = ph.sb("aTs", [128, HG, NT], BF16)
    wg = [ph.sb("wg%d" % i, [128, DC, 256], BF16) for i in range(3)]
    wd = [ph.sb("wd%d" % i, [128, HG, 256], BF16) for i in range(2)]
    sq = [ph.sb("sq%d" % i, [128, 512], BF16) for i in range(2)]
    sg = [ph.sb("sg%d" % i, [128, 512], F32) for i in range(2)]
    rstd = ph.sb("rstd", [128, NT], F32)
    gcol = ph.sb("gcol", [128, DC], F32)
    ones = ph.sb("ones", [128, 128], BF16)
    pss = [ph.ps("pss%d" % i, [128, 512]) for i in range(2)]
    pg = [ph.ps("pg%d" % i, [128, 512]) for i in range(2)]
    pu = [ph.ps("pu%d" % i, [128, 512]) for i in range(2)]
    py = [ph.ps("py%d" % i, [128, 512]) for i in range(2)]

    P.dma("sp", lambda e: e.dma_start(out=gcol[:], in_=g_in[:, :]), writes=["gcol"])
    P.dve(lambda e: e.memset(ones[:], 1.0), writes=["ones"])
    xin_v = xin.rearrange("(c p) t -> p c t", p=128)
    xout_v = xout.rearrange("(c p) t -> p c t", p=128)
    wgu_v = wgu.rearrange("(c p) n -> p c n", p=128)
    wdn_v = wdn.rearrange("(m p) n -> p m n", p=128)
    wgi = 0
    wdi = 0
    cnt = 0
    for tt in range(TC // NT):
        tsl = slice(tt * NT, (tt + 1) * NT)
        for c in range(DC):
            P.dma("sp",
                  lambda e, c=c, tsl=tsl: e.dma_start(out=xT[:, c, :], in_=xin_v[:, c, tsl]),
                  writes=[("x", c)])
        emit_rmsnorm(ph, xT, hT, gcol, ones, sq, pss, rstd, NT, "x", "h", tt)
        hkeys = [("h", c, s) for c in range(DC) for s in range(NS)]
        for grp in range(NG):
            for mi in range(HG):
                m = grp * HG + mi
                wb = wg[wgi % 3]
                wk = "wg%d" % (wgi % 3)
                wgi += 1
                P.dma("pool", lambda e, wb=wb, m=m: e.dma_start(out=wb[:, :, 0:128], in_=wgu_v[:, :, m * 128:(m + 1) * 128]),
                      writes=[wk])
                P.dma("pool", lambda e, wb=wb, m=m: e.dma_start(out=wb[:, :, 128:256],
                                                                 in_=wgu_v[:, :, FH + m * 128:FH + (m + 1) * 128]),
                      writes=[wk])
                for s in range(NS):
                    sl = slice(s * 512, (s + 1) * 512)
                    b = cnt % 2
                    cnt += 1
                    for c in range(DC):
                        P.pe(lambda e, wb=wb, c=c, sl=sl, b=b: e.matmul(pg[b][:], wb[:, c, 0:128], hT[:, c, sl],
                                                                        start=(c == 0), stop=(c == DC - 1)),
                             reads=[wk, ("h", c, s)], writes=["pg%d" % b])
                    for c in range(DC):
                        P.pe(lambda e, wb=wb, c=c, sl=sl, b=b: e.matmul(pu[b][:], wb[:, c, 128:256], hT[:, c, sl],
                                                                        start=(c == 0), stop=(c == DC - 1)),
                             reads=[wk, ("h", c, s)], writes=["pu%d" % b])
                    P.act(lambda e, b=b: e.activation(out=sg[b][:], in_=pg[b][:], func=AF.Silu),
                          reads=["pg%d" % b], writes=["sg%d" % b])
                    P.dve(lambda e, b=b, mi=mi, sl=sl: e.tensor_tensor(out=aT[:, mi, sl], in0=sg[b][:], in1=pu[b][:],
                                                                       op=ALU.mult),
                          reads=["sg%d" % b, "pu%d" % b], writes=[("a", mi, s)])
            for dp in range(DC // 2):
                db = wd[wdi % 2]
                dk = "wd%d" % (wdi % 2)
                wdi += 1
                P.dma("pool", lambda e, db=db, dp=dp, grp=grp: e.dma_start(
                    out=db[:], in_=wdn_v[:, grp * HG:(grp + 1) * HG, dp * 256:(dp + 1) * 256]), writes=[dk])
                for dd in range(2):
                    d = dp * 2 + dd
                    for s in range(NS):
                        sl = slice(s * 512, (s + 1) * 512)
                        b = cnt % 2
                        cnt += 1
                        for mi in range(HG):
                            P.pe(lambda e, db=db, mi=mi, dd=dd, sl=sl, b=b: e.matmul(
                                py[b][:], db[:, mi, dd * 128:(dd + 1) * 128], aT[:, mi, sl],
                                start=(mi == 0), stop=(mi == HG - 1)),
                                reads=[dk, ("a", mi, s)], writes=["py%d" % b])
                        P.dve(lambda e, d=d, sl=sl, b=b: e.tensor_tensor(out=xT[:, d, sl], in0=xT[:, d, sl], in1=py[b][:],
                                                                         op=ALU.add),
                              reads=["py%d" % b, ("x", d)], writes=[("x", d)])
        for c in range(DC):
            P.dma("sp",
                  lambda e, c=c, tsl=tsl: e.dma_start(out=xout_v[:, c, tsl], in_=xT[:, c, :]),
                  reads=[("x", c)], writes=[("xo", tt, c)])
            ph.out_keys.append(("xo", tt, c))
    return ph.finish()


def emit_rmsnorm_cols(ph, xT, xoff, hT, hoff, ncols, gcol, ones_bf, sq, pb, pk, rstd, xkeys, hkeys, tag):
    P = ph.P
    xsl_ = slice(xoff, xoff + ncols)
    hsl_ = slice(hoff, hoff + ncols)
    for c in range(DC):
        q = sq[c % 2]
        qk = "sq%d" % (c % 2)
        P.act(lambda e, q=q, c=c: e.activation(out=q[:, 0:ncols], in_=xT[:, c, xsl_], func=AF.Square),
              reads=[xkeys(c)], writes=[qk])
        P.pe(lambda e, q=q, c=c: e.matmul(pb[:, 0:ncols], ones_bf[:], q[:, 0:ncols], start=(c == 0), stop=(c == DC - 1)),
             reads=[qk, "ones"], writes=[pk])
    rk = ("rstd", tag)
    P.dve(lambda e: e.tensor_scalar(out=rstd[:, 0:ncols], in0=pb[:, 0:ncols], scalar1=1.0 / D, scalar2=EPS,
                                    op0=ALU.mult, op1=ALU.add), reads=[pk], writes=[rk])
    P.act(lambda e: e.activation(out=rstd[:, 0:ncols], in_=rstd[:, 0:ncols], func=AF.Sqrt), reads=[rk], writes=[rk])
    P.dve(lambda e: e.reciprocal(out=rstd[:, 0:ncols], in_=rstd[:, 0:ncols]), reads=[rk], writes=[rk])
    for c in range(DC):
        P.dve(lambda e, c=c: e.scalar_tensor_tensor(out=hT[:, c, hsl_], in0=xT[:, c, xsl_], scalar=gcol[:, c:c + 1],
                                                    in1=rstd[:, 0:ncols], op0=ALU.mult, op1=ALU.mult),
              reads=[xkeys(c), rk, "gcol"], writes=[hkeys(c)])


POOL_W = (2, 4, 8, 16)


def build_pool_phase(ph=None):
    ph = ph or Phase()
    P = ph.P
    fused = ph.master is not None
    xin = None if fused else ph.din("xTe", [D, 16 + TC])
    g_in = ph.din("g", [128, DC])
    wp_in = ph.din("wp", [4, 512, 512])
    sc_in = ph.din("psc", [128, DC])
    ic_in = ph.din("invc", [128, 4, 16])
    xout = ph.dout("xTo", [D, TC])
    NT = 512
    xT = ph.sb("xTs", [128, DC, NT], F32)
    xh = ph.sb("xh", [128, DC, 16], F32)
    hx = ph.sb("hx", [128, DC, 16 + NT], F32)
    sA = [ph.sb("sA%d" % i, [128, 16 + NT], F32) for i in range(2)]
    sB = [ph.sb("sB%d" % i, [128, 16 + NT], F32) for i in range(2)]
    yT = ph.sb("yT", [128, DC, NT], BF16)
    wp = ph.sb("wps", [128, 4, 4, 512], BF16)
    sq = [ph.sb("sq%d" % i, [128, 512], BF16) for i in range(2)]
    rstd = ph.sb("rstd", [128, 512], F32)
    gcol = ph.sb("gcol", [128, DC], F32)
    psc = ph.sb("pscs", [128, DC], F32)
    invc = ph.sb("invcs", [128, 4, 16], F32)
    ones = ph.sb("ones", [128, 128], BF16)
    pss = ph.ps("pss", [128, 512])
    pz = [ph.ps("pz%d" % i, [128, 512]) for i in range(2)]

    P.dma("sp", lambda e: e.dma_start(out=gcol[:], in_=g_in[:, :]), writes=["gcol"])
    P.dma("sp", lambda e: e.dma_start(out=psc[:], in_=sc_in[:, :]), writes=["psc"])
    P.dma("sp", lambda e: e.dma_start(out=invc[:], in_=ic_in[:, :, :]), writes=["invc"])
    P.dve(lambda e: e.memset(ones[:], 1.0), writes=["ones"])
    wp_v = wp_in.rearrange("g (ci p) n -> p g ci n", p=128)
    for g in range(4):
        P.dma("pool", lambda e, g=g: e.dma_start(out=wp[:, g, :, :], in_=wp_v[:, g, :, :]), writes=["wp"])
    if fused:
        xmain_v = ph.bind["xT"].rearrange("(c p) t -> p c t", p=128)
        halo_v = ph.bind["halo"].rearrange("(c p) t -> p c t", p=128)
        isb = ph.sb("isb", [128, 16], F32)
        P.dma("sp", lambda e: e.dma_start(out=isb[:], in_=ph.bind["isb"][:, :]), writes=["isb"])
        P.dma("sp", lambda e: e.dma_start(out=xh[:], in_=halo_v[:, :, :]), writes=["xh"])
        P.dve(lambda e: e.tensor_scalar(out=xh[:], in0=xh[:], scalar1=isb[:, 0:1], scalar2=None, op0=ALU.mult),
              reads=["xh", "isb"], writes=["xh"])
        OFF = 0
    else:
        xin_v = xin.rearrange("(c p) t -> p c t", p=128)
        xmain_v = xin_v
        OFF = 16
        P.dma("sp", lambda e: e.dma_start(out=xh[:], in_=xin_v[:, :, 0:16]), writes=["xh"])
    xout_v = xout.rearrange("(c p) t -> p c t", p=128)
    emit_rmsnorm_cols(ph, xh, 0, hx, 0, 16, gcol, ones, sq, pss, "pss", rstd,
                      lambda c: "xh", lambda c: ("hx", c), "halo")
    cnt = 0
    for tt in range(TC // NT):
        for c in range(DC):
            P.dma("sp", lambda e, c=c, tt=tt: e.dma_start(out=xT[:, c, :], in_=xmain_v[:, c, OFF + tt * NT:OFF + (tt + 1) * NT]),
                  writes=[("x", c)])
        emit_rmsnorm_cols(ph, xT, 0, hx, 16, NT, gcol, ones, sq, pss, "pss", rstd,
                          lambda c: ("x", c), lambda c: ("hx", c), ("t", tt))
        W = 16 + NT
        for c in range(DC):
            g = c // 4
            eng = P.dve if c % 2 == 0 else P.pool
            a = sA[c % 2]
            b = sB[c % 2]
            ak = "sA%d" % (c % 2)
            bk = "sB%d" % (c % 2)
            eng(lambda e, a=a, c=c: e.tensor_tensor(out=a[:, 1:W], in0=hx[:, c, 1:W], in1=hx[:, c, 0:W - 1], op=ALU.add),
                reads=[("hx", c)], writes=[ak])
            cur, curk, oth, othk = a, ak, b, bk
            sh = 2
            for step in range(g):
                eng(lambda e, cur=cur, oth=oth, sh=sh: e.tensor_tensor(out=oth[:, 1 + sh:W], in0=cur[:, 1 + sh:W],
                                                                      in1=cur[:, 1:W - sh], op=ALU.add),
                    reads=[curk], writes=[othk])
                cur, curk, oth, othk = oth, othk, cur, curk
                sh *= 2
            w = POOL_W[g]
            P.dve(lambda e, cur=cur, c=c, w=w: e.scalar_tensor_tensor(out=yT[:, c, :], in0=cur[:, 16:W], scalar=1.0 / w,
                                                                    in1=hx[:, c, 16:W], op0=ALU.mult, op1=ALU.subtract),
                reads=[curk, ("hx", c)], writes=[("y", c)])
            if tt == 0:
                eng(lambda e, cur=cur, g=g: e.tensor_tensor(out=cur[:, 16:32], in0=cur[:, 16:32], in1=invc[:, g, :], op=ALU.mult),
                    reads=[curk, "invc", ("y", c)], writes=[curk])
                eng(lambda e, cur=cur, c=c: e.tensor_tensor(out=yT[:, c, 0:16], in0=cur[:, 16:32], in1=hx[:, c, 16:32],
                                                            op=ALU.subtract),
                    reads=[curk, ("hx", c)], writes=[("y", c)])
            if tt + 1 < TC // NT:
                eng(lambda e, c=c: e.tensor_copy(out=hx[:, c, 0:16], in_=hx[:, c, NT:NT + 16]),
                    reads=[("y", c), curk, ak, bk], writes=[("hx", c)])
        for d in range(DC):
            g = d // 4
            b = cnt % 2
            cnt += 1
            for ci in range(4):
                P.pe(lambda e, g=g, ci=ci, d=d, b=b: e.matmul(pz[b][:], wp[:, g, ci, (d % 4) * 128:(d % 4 + 1) * 128],
                                                             yT[:, 4 * g + ci, :], start=(ci == 0), stop=(ci == 3)),
                     reads=["wp", ("y", 4 * g + ci)], writes=["pz%d" % b])
            P.dve(lambda e, d=d, b=b: e.scalar_tensor_tensor(out=xT[:, d, :], in0=pz[b][:], scalar=psc[:, d:d + 1],
                                                            in1=xT[:, d, :], op0=ALU.mult, op1=ALU.add),
                  reads=["pz%d" % b, "psc", ("x", d)], writes=[("x", d)])
        for c in range(DC):
            P.dma("sp", lambda e, c=c, tt=tt: e.dma_start(out=xout_v[:, c, tt * NT:(tt + 1) * NT], in_=xT[:, c, :]),
                  reads=[("x", c)], writes=[("xo", tt, c)])
            ph.out_keys.append(("xo", tt, c))
    return ph.finish()


def col16(v):
    return np.ascontiguousarray(np.asarray(v, np.float32).reshape(DC, 128).T)


def pool_inputs(x_seq, half, g, wp, psc):
    xe = np.zeros((D, 16 + TC), np.float32)
    t0 = half * TC
    xe[:, 16:] = x_seq[t0:t0 + TC].T
    if half == 1:
        xe[:, :16] = x_seq[t0 - 16:t0].T
    invc = np.zeros((128, 4, 16), np.float32)
    for gi, w in enumerate(POOL_W):
        for t in range(16):
            cnt = min(t + 1, w) if half == 0 else w
            invc[:, gi, t] = 1.0 / cnt
    return {"xTe": xe, "g": col16(g), "wp": np.ascontiguousarray(wp, dtype=np.float32), "psc": col16(psc), "invc": invc}


GLA_DKT = 1024
GLA_NCOL = 6160


def build_gla_phase(ph=None, state_only=False):
    ph = ph or Phase()
    P = ph.P
    fused = ph.master is not None
    xin = ph.din("xT", [D, TC])
    g_in = ph.din("g", [128, DC])
    win = ph.din("win", [D, GLA_NCOL])
    wa2b_in = ph.din("wa2b", [17, GLA_DKT])
    gn_in = ph.din("gnb", [128, 2048])
    wout = ph.din("wout", [D, D])
    st_in = ph.bind.get("st_in") if fused else ph.din("st_in", [128, 8, 512])
    tri_in = ph.din("tri", [128, 128])
    id_in = ph.din("ident", [128, 128])
    xout = None if state_only else ph.dout("xTo", [D, TC])
    st_out = ph.bind.get("st_out") if fused else ph.dout("st_out", [128, 8, 512])
    NT = 512
    NJ = 4
    xs = [ph.sb("xs%d" % i, [128, 512], F32) for i in range(3)]
    hT = ph.sb("hTs", [128, DC, NT], BF16)
    Wb = [ph.sb("Wb%d" % i, [128, DC, 512], BF16) for i in range(2)]
    Wa = ph.sb("Wa", [128, DC, 16], BF16)
    qT = ph.sb("qT", [128, 8, NT], BF16)
    kT = ph.sb("kT", [128, 8, NT], BF16)
    kdec = ph.sb("kdec", [128, NJ, 1024], BF16)
    kd_s = [ph.sb("kds%d" % i, [128, 512], BF16) for i in range(2)]
    vt = ph.sb("vt", [128, NJ, 2048], BF16)
    sr = ph.sb("sr", [128, NJ, 2048], BF16)
    gated = ph.sb("gated", [128, 2048], BF16)
    gT = ph.sb("gT", [128, DC, NT], BF16)
    S = ph.sb("S", [128, 8, 512], F32)
    Sb = ph.sb("Sb", [128, 8, 512], BF16)
    lt = ph.sb("lt", [128, NJ, 1024], F32)
    e1 = ph.sb("e1", [128, 1024], F32)
    Eq = [ph.sb("Eq%d" % i, [128, 512], F32) for i in range(1)]
    Ek = [ph.sb("Ek%d" % i, [128, 512], F32) for i in range(1)]
    Elast = ph.sb("Elast", [128, 8, NJ], F32)
    alr1 = ph.sb("alr1", [32, NT], F32)
    wa2b = ph.sb("wa2bs", [32, GLA_DKT], F32)
    gnb = ph.sb("gnbs", [128, 2048], BF16)
    tri = ph.sb("tris", [128, 128], F32)
    ident = ph.sb("idents", [128, 128], BF16)
    AT = [ph.sb("AT%d" % i, [128, 128], BF16) for i in range(2)]
    osq = ph.sb("osq", [128, 2048], BF16)
    ssq = ph.sb("ssq", [128, 4], F32)
    sq = [ph.sb("sq%d" % i, [128, 512], BF16) for i in range(2)]
    rstd = ph.sb("rstd", [128, 512], F32)
    gcol = ph.sb("gcol", [128, DC], F32)
    ones = ph.sb("ones", [128, 128], BF16)
    pb = [ph.ps("pb%d" % i, [128, 512]) for i in range(8)]
    pbk = ["pb%d" % i for i in range(8)]

    P.dma("sp", lambda e: e.dma_start(out=gcol[:], in_=g_in[:, :]), writes=["gcol"])
    P.dma("sp", lambda e: e.dma_start(out=tri[:], in_=tri_in[:, :]), writes=["tri"])
    P.dma("sp", lambda e: e.dma_start(out=wa2b[0:17, :], in_=wa2b_in[:, :]), writes=["wa2b"])
    if st_in is None:
        P.dve(lambda e: e.memset(S[:], 0.0), writes=[("S", dc) for dc in range(8)])
    else:
        P.dma("sp", lambda e: e.dma_start(out=S[:], in_=st_in[:, :, :]), writes=[("S", dc) for dc in range(8)])
        if fused:
            isb = ph.sb("isb", [128, 16], F32)
            P.dma("sp", lambda e: e.dma_start(out=isb[:], in_=ph.bind["isb"][:, :]), writes=["isb"])
            P.dve(lambda e: e.tensor_scalar(out=S[:], in0=S[:], scalar1=isb[:, 0:1], scalar2=None, op0=ALU.mult),
                  reads=[("S", dc) for dc in range(8)] + ["isb"], writes=[("S", dc) for dc in range(8)])
    P.dma("pool", lambda e: e.dma_start(out=ident[:], in_=id_in[:, :]), writes=["ident"])
    P.dma("pool", lambda e: e.dma_start(out=gnb[:], in_=gn_in[:, :]), writes=["gnb"])
    P.dve(lambda e: e.memset(ones[:], 1.0), writes=["ones"])
    P.dve(lambda e: e.memset(alr1[:], 1.0), writes=["alr1"])
    P.act(lambda e: e.copy(out=Sb[:], in_=S[:]), reads=[("S", dc) for dc in range(8)], writes=[("Sb", dc) for dc in range(8)])
    win_v = win.rearrange("(c p) n -> p c n", p=128)
    wout_v = wout.rearrange("(c p) n -> p c n", p=128)
    xin_v = xin.rearrange("(c p) t -> p c t", p=128)
    xout_v = None if state_only else xout.rearrange("(c p) t -> p c t", p=128)
    P.dma("pool", lambda e: e.dma_start(out=Wa[:], in_=win_v[:, :, 6144:6160]), writes=["Wa"])

    wctr = [0]

    def load_w(src_v, col0):
        i = wctr[0] % 2
        wctr[0] += 1
        P.dma("pool", lambda e, i=i: e.dma_start(out=Wb[i][:], in_=src_v[:, :, col0:col0 + 512]), writes=["Wb%d" % i])
        return Wb[i], "Wb%d" % i

    pctr = [0]

    def next_pb(lo=4, n=4):
        i = lo + pctr[0] % n
        pctr[0] += 1
        return pb[i], pbk[i]

    xctr = [0]
    for tt in range(TC // NT):
        tsl = slice(tt * NT, (tt + 1) * NT)
        p_ss, p_ssk = pb[0], pbk[0]
        for c in range(DC):
            i = xctr[0] % 3
            xctr[0] += 1
            P.dma("sp", lambda e, i=i, c=c, tsl=tsl: e.dma_start(out=xs[i][:], in_=xin_v[:, c, tsl]), writes=["xs%d" % i])
            q = sq[c % 2]
            qk = "sq%d" % (c % 2)
            P.act(lambda e, q=q, i=i: e.activation(out=q[:], in_=xs[i][:], func=AF.Square), reads=["xs%d" % i], writes=[qk])
            P.pe(lambda e, q=q, c=c: e.matmul(p_ss[:], ones[:], q[:], start=(c == 0), stop=(c == DC - 1)),
                 reads=[qk, "ones"], writes=[p_ssk])
        P.dve(lambda e: e.tensor_scalar(out=rstd[:], in0=p_ss[:], scalar1=1.0 / D, scalar2=EPS, op0=ALU.mult, op1=ALU.add),
              reads=[p_ssk], writes=["rstd"])
        P.act(lambda e: e.activation(out=rstd[:], in_=rstd[:], func=AF.Sqrt), reads=["rstd"], writes=["rstd"])
        P.dve(lambda e: e.reciprocal(out=rstd[:], in_=rstd[:]), reads=["rstd"], writes=["rstd"])
        for c in range(DC):
            i = xctr[0] % 3
            xctr[0] += 1
            P.dma("sp", lambda e, i=i, c=c, tsl=tsl: e.dma_start(out=xs[i][:], in_=xin_v[:, c, tsl]), writes=["xs%d" % i])
            P.dve(lambda e, i=i, c=c: e.scalar_tensor_tensor(out=hT[:, c, :], in0=xs[i][:], scalar=gcol[:, c:c + 1],
                                                             in1=rstd[:], op0=ALU.mult, op1=ALU.mult),
                  reads=["xs%d" % i, "rstd", "gcol"], writes=[("h", c)])
        hk = [("h", c) for c in range(DC)]
        pa, pak = pb[1], pbk[1]
        for c in range(DC):
            P.pe(lambda e, c=c: e.matmul(pa[0:16, :], Wa[:, c, :], hT[:, c, :], start=(c == 0), stop=(c == DC - 1)),
                 reads=["Wa", ("h", c)], writes=[pak])
        P.act(lambda e: e.copy(out=alr1[0:16, :], in_=pa[0:16, :]), reads=[pak], writes=["alr1"])
        for j in range(NJ):
            jsl = slice(j * 128, (j + 1) * 128)
            for hf in range(2):
                P.pe(lambda e, jsl=jsl, hf=hf: e.matmul(pb[2 + hf][:], alr1[0:17, jsl], wa2b[0:17, hf * 512:(hf + 1) * 512],
                                                        start=True, stop=True),
                     reads=["alr1", "wa2b"], writes=[pbk[2 + hf]])
                P.act(lambda e, hf=hf: e.activation(out=e1[:, hf * 512:(hf + 1) * 512], in_=pb[2 + hf][:], func=AF.Exp, scale=-1.0),
                      reads=[pbk[2 + hf]], writes=[("e1", hf)])
                P.act(lambda e, hf=hf, j=j: e.activation(out=lt[:, j, hf * 512:(hf + 1) * 512], in_=e1[:, hf * 512:(hf + 1) * 512],
                                                         func=AF.Ln, bias=1.0),
                      reads=[("e1", hf)], writes=[("lt", j)])
        for blk in range(2):
            if not state_only:
                Wq, Wqk = load_w(win_v, blk * 512)
            Wk, Wkk = load_w(win_v, 1024 + blk * 512)
            for dl in range(4):
                dc = blk * 4 + dl
                pbt, pbtk = pb[0], pbk[0]
                for j in range(NJ):
                    jsl = slice(j * 128, (j + 1) * 128)
                    P.pe(lambda e, j=j, jsl=jsl, dc=dc: e.matmul(pbt[:, jsl], lt[:, j, dc * 128:(dc + 1) * 128], tri[:],
                                                                 start=True, stop=True),
                         reads=[("lt", j), "tri"], writes=[pbtk])
                ei = 0
                P.act(lambda e, ei=ei: e.activation(out=Eq[ei][:], in_=pbt[:], func=AF.Exp, scale=-1.0 / 16.0),
                      reads=[pbtk], writes=["Eq%d" % ei])
                P.act(lambda e, ei=ei: e.activation(out=Ek[ei][:], in_=pbt[:], func=AF.Exp, scale=1.0 / 16.0),
                      reads=[pbtk], writes=["Ek%d" % ei])
                P.dve(lambda e, ei=ei, dc=dc: e.tensor_copy(out=Elast[:, dc, :], in_=Eq[ei][:, 127::128]),
                      reads=["Eq%d" % ei], writes=[("Elast", dc)])
                if not state_only:
                    pq, pqk = next_pb()
                    for c in range(DC):
                        P.pe(lambda e, c=c, dl=dl, pq=pq, Wq=Wq: e.matmul(pq[:], Wq[:, c, dl * 128:(dl + 1) * 128], hT[:, c, :],
                                                                   start=(c == 0), stop=(c == DC - 1)),
                             reads=[Wqk, ("h", c)], writes=[pqk])
                    P.dve(lambda e, pq=pq, ei=ei, dc=dc: e.scalar_tensor_tensor(out=qT[:, dc, :], in0=pq[:], scalar=1.0 / 16.0,
                                                                                in1=Eq[ei][:], op0=ALU.mult, op1=ALU.mult),
                          reads=[pqk, "Eq%d" % ei], writes=[("qT", dc)])
                pk_, pkk = next_pb()
                for c in range(DC):
                    P.pe(lambda e, c=c, dl=dl, pk_=pk_, Wk=Wk: e.matmul(pk_[:], Wk[:, c, dl * 128:(dl + 1) * 128], hT[:, c, :],
                                                                 start=(c == 0), stop=(c == DC - 1)),
                         reads=[Wkk, ("h", c)], writes=[pkk])
                P.dve(lambda e, pk_=pk_, ei=ei, dc=dc: e.tensor_tensor(out=kT[:, dc, :], in0=pk_[:], in1=Ek[ei][:], op=ALU.mult),
                      reads=[pkk, "Ek%d" % ei], writes=[("kT", dc)])
                kd = kd_s[dc % 2]
                kdk = "kds%d" % (dc % 2)
                for j in range(NJ):
                    jsl = slice(j * 128, (j + 1) * 128)
                    P.dve(lambda e, kd=kd, dc=dc, j=j, jsl=jsl: e.tensor_scalar(out=kd[:, jsl], in0=kT[:, dc, jsl],
                                                                                scalar1=Elast[:, dc, j:j + 1], scalar2=None,
                                                                                op0=ALU.mult),
                          reads=[("kT", dc), ("Elast", dc)], writes=[kdk])
                ptr, ptrk = next_pb()
                for j in range(NJ):
                    jsl = slice(j * 128, (j + 1) * 128)
                    P.pe(lambda e, kd=kd, jsl=jsl, ptr=ptr, j=j: e.transpose(pbf(ptr)[:, jsl], kd[:, jsl], ident[:]),
                         reads=[kdk, "ident"], writes=[ptrk])
                P.act(lambda e, ptr=ptr, dc=dc: e.copy(out=kdec[:, :, dc * 128:(dc + 1) * 128],
                                                       in_=pbf(ptr)[:, 0:512].rearrange("p (j d) -> p j d", j=NJ)),
                      reads=[ptrk], writes=[("kdec", dc)])
        for vb in range(4):
            Wv, Wvk = load_w(win_v, 2048 + vb * 512)
            for j in range(NJ):
                jsl = slice(j * 128, (j + 1) * 128)
                pv, pvk = next_pb()
                for c in range(DC):
                    P.pe(lambda e, c=c, jsl=jsl, pv=pv, Wv=Wv: e.matmul(pv[:], hT[:, c, jsl], Wv[:, c, :],
                                                                        start=(c == 0), stop=(c == DC - 1)),
                         reads=[Wvk, ("h", c)], writes=[pvk])
                P.act(lambda e, pv=pv, j=j, vb=vb: e.copy(out=vt[:, j, vb * 512:(vb + 1) * 512], in_=pv[:]),
                      reads=[pvk], writes=[("vt", j, vb)])
        for rb in (range(4) if not state_only else ()):
            Wr, Wrk = load_w(win_v, 4096 + rb * 512)
            for j in range(NJ):
                jsl = slice(j * 128, (j + 1) * 128)
                pv, pvk = next_pb()
                for c in range(DC):
                    P.pe(lambda e, c=c, jsl=jsl, pv=pv, Wr=Wr: e.matmul(pv[:], hT[:, c, jsl], Wr[:, c, :],
                                                                        start=(c == 0), stop=(c == DC - 1)),
                         reads=[Wrk, ("h", c)], writes=[pvk])
                P.act(lambda e, pv=pv, j=j, rb=rb: e.activation(out=sr[:, j, rb * 512:(rb + 1) * 512], in_=pv[:], func=AF.Silu),
                      reads=[pvk], writes=[("sr", j)])
        for j in range(NJ):
            jsl = slice(j * 128, (j + 1) * 128)
            if not state_only:
                P.pool(lambda e, j=j: e.tensor_tensor(out=sr[:, j, :], in0=sr[:, j, :], in1=gnb[:], op=ALU.mult),
                       reads=[("sr", j), "gnb"], writes=[("sr", j)])
            for h in range(4):
                vkeys = [("vt", j, h)]
                if not state_only:
                    pst, pstk = pb[4], pbk[4]
                    hs = slice(h * 128, (h + 1) * 128)
                    for di in range(2):
                        dc = 2 * h + di
                        P.pe(lambda e, dc=dc, di=di, hs=hs, jsl=jsl: e.matmul(pst[:, hs], kT[:, dc, jsl], qT[:, dc, jsl],
                                                                              start=(di == 0), stop=(di == 1)),
                             reads=[("kT", dc), ("qT", dc)], writes=[pstk])
                    at = AT[h % 2]
                    atk = "AT%d" % (h % 2)
                    P.dve(lambda e, at=at, hs=hs: e.tensor_tensor(out=at[:], in0=pst[:, hs], in1=tri[:], op=ALU.mult),
                          reads=[pstk, "tri"], writes=[atk])
                    for di in range(2):
                        dc = 2 * h + di
                        P.pe(lambda e, dc=dc, di=di, h=h, jsl=jsl: e.matmul(pb[h][:], qT[:, dc, jsl], Sb[:, dc, :],
                                                                            start=(di == 0), stop=False),
                             reads=[("qT", dc), ("Sb", dc)], writes=[pbk[h]])
                    P.pe(lambda e, at=at, h=h, j=j: e.matmul(pb[h][:], at[:], vt[:, j, h * 512:(h + 1) * 512], start=False, stop=True),
                         reads=[atk] + vkeys, writes=[pbk[h]])
                for di in range(2):
                    dc = 2 * h + di
                    pkv, pkvk = pb[5 + di], pbk[5 + di]
                    P.pe(lambda e, dc=dc, j=j, h=h, pkv=pkv: e.matmul(pkv[:], kdec[:, j, dc * 128:(dc + 1) * 128],
                                                                      vt[:, j, h * 512:(h + 1) * 512], start=True, stop=True),
                         reads=[("kdec", dc)] + vkeys, writes=[pkvk])
                    P.dve(lambda e, dc=dc, j=j, pkv=pkv: e.scalar_tensor_tensor(out=S[:, dc, :], in0=S[:, dc, :],
                                                                                scalar=Elast[:, dc, j:j + 1], in1=pkv[:],
                                                                                op0=ALU.mult, op1=ALU.add),
                          reads=[pkvk, ("Elast", dc), ("S", dc)], writes=[("S", dc)])
                    if not state_only:
                        P.act(lambda e, dc=dc: e.copy(out=Sb[:, dc, :], in_=S[:, dc, :]), reads=[("S", dc)], writes=[("Sb", dc)])
            if state_only:
                continue
            for h in range(4):
                P.act(lambda e, h=h: e.activation(out=osq[:, h * 512:(h + 1) * 512], in_=pb[h][:], func=AF.Square),
                      reads=[pbk[h]], writes=[("osq", h)])
            P.dve(lambda e: e.reduce_sum(out=ssq[:], in_=osq[:].rearrange("p (h v) -> p h v", h=4), axis=AX.X),
                  reads=[("osq", h) for h in range(4)], writes=["ssq"])
            P.dve(lambda e: e.tensor_scalar(out=ssq[:], in0=ssq[:], scalar1=1.0 / 512.0, scalar2=EPS, op0=ALU.mult, op1=ALU.add),
                  reads=["ssq"], writes=["ssq"])
            P.act(lambda e: e.activation(out=ssq[:], in_=ssq[:], func=AF.Sqrt), reads=["ssq"], writes=["ssq"])
            P.dve(lambda e: e.reciprocal(out=ssq[:], in_=ssq[:]), reads=["ssq"], writes=["ssq"])
            for h in range(4):
                P.dve(lambda e, h=h, j=j: e.scalar_tensor_tensor(out=gated[:, h * 512:(h + 1) * 512], in0=pb[h][:],
                                                            scalar=ssq[:, h:h + 1], in1=sr[:, j, h * 512:(h + 1) * 512],
                                                            op0=ALU.mult, op1=ALU.mult),
                      reads=[pbk[h], "ssq", ("sr", j)], writes=[("gated", h)])
            for fb in range(4):
                ptr, ptrk = pb[7], pbk[7]
                for fi in range(4):
                    fc = fb * 4 + fi
                    P.pe(lambda e, fc=fc, fi=fi, ptr=ptr: e.transpose(pbf(ptr)[:, fi * 128:(fi + 1) * 128], gated[:, fc * 128:(fc + 1) * 128],
                                                            ident[:]),
                         reads=[("gated", fb), "ident"], writes=[ptrk])
                P.act(lambda e, fb=fb, jsl=jsl, ptr=ptr: e.copy(out=gT[:, fb * 4:(fb + 1) * 4, jsl],
                                                       in_=pbf(ptr)[:, 0:512].rearrange("p (f t) -> p f t", f=4)),
                      reads=[ptrk], writes=[("gT", fb, j)])
        for ob in (range(4) if not state_only else ()):
            Wo, Wok = load_w(wout_v, ob * 512)
            for dd in range(4):
                d = ob * 4 + dd
                py_, pyk = next_pb()
                for fc in range(DC):
                    P.pe(lambda e, fc=fc, dd=dd, py_=py_, Wo=Wo: e.matmul(py_[:], Wo[:, fc, dd * 128:(dd + 1) * 128], gT[:, fc, :],
                                                                          start=(fc == 0), stop=(fc == DC - 1)),
                         reads=[Wok] + [("gT", fc // 4, j) for j in range(NJ)], writes=[pyk])
                i = xctr[0] % 3
                xctr[0] += 1
                P.dma("sp", lambda e, i=i, d=d, tsl=tsl: e.dma_start(out=xs[i][:], in_=xin_v[:, d, tsl]), writes=["xs%d" % i])
                P.dve(lambda e, i=i, py_=py_: e.tensor_tensor(out=xs[i][:], in0=xs[i][:], in1=py_[:], op=ALU.add),
                      reads=[pyk, "xs%d" % i], writes=["xs%d" % i])
                P.dma("sp", lambda e, i=i, d=d, tsl=tsl: e.dma_start(out=xout_v[:, d, tsl], in_=xs[i][:]),
                      reads=["xs%d" % i], writes=[("xo", tt, d)])
                ph.out_keys.append(("xo", tt, d))
    if st_out is not None:
        P.dma("sp", lambda e: e.dma_start(out=st_out[:, :, :], in_=S[:]), reads=[("S", dc) for dc in range(8)],
              writes=["st_out"])
        ph.out_keys.append("st_out")
    return ph.finish()


def pbf(ptile):
    return ptile[:].bitcast(BF16) if hasattr(ptile[:], "bitcast") else ptile


def gla_consts():
    tri = np.triu(np.ones((128, 128), np.float32))
    ident = np.eye(128, dtype=np.float32)
    return tri, ident


def gla_inputs(xT_core, g, w_in, w_a2, b_a, g_norm, w_out, state):
    tri, ident = gla_consts()
    st = np.ascontiguousarray(np.asarray(state, np.float32).reshape(4, 2, 128, 512).transpose(2, 0, 1, 3).reshape(128, 8, 512))
    gnb = np.ascontiguousarray(np.broadcast_to(np.tile(np.asarray(g_norm, np.float32), 4)[None, :], (128, 2048)))
    wa2b = np.ascontiguousarray(np.concatenate([np.asarray(w_a2, np.float32), np.asarray(b_a, np.float32)[None, :]], axis=0))
    return {"xT": np.ascontiguousarray(xT_core, dtype=np.float32), "g": col16(g), "win": np.ascontiguousarray(w_in, dtype=np.float32),
            "wa2b": wa2b, "gnb": gnb, "wout": np.ascontiguousarray(w_out, dtype=np.float32), "st_in": st, "tri": tri, "ident": ident}


def gla_state_from_out(st_out):
    return np.ascontiguousarray(st_out.reshape(128, 4, 2, 512).transpose(1, 2, 0, 3).reshape(4, 256, 512))


def build_diff1_phase(do_qk=True, do_v=True, do_norm=True, ph=None):
    ph = ph or Phase()
    P = ph.P
    xin = ph.din("xT", [D, TC])
    g_in = ph.din("g", [128, DC])
    win = ph.din("win", [D, 3 * D])
    qkg_in = ph.din("qkg", [128, 16])
    qko = ph.dout("qkT", [2 * D, TC])
    vo = ph.dout("v", [TC, D])
    NT = 512
    xs = [ph.sb("xs%d" % i, [128, 512], F32) for i in range(3)]
    hT = ph.sb("hTs", [128, DC, NT], BF16)
    Wb = [ph.sb("Wb%d" % i, [128, DC, 512], BF16) for i in range(2)]
    qraw = [ph.sb("qraw%d" % i, [128, 512], F32) for i in range(2)]
    qn = [ph.sb("qn%d" % i, [128, 512], F32) for i in range(3)]
    vs = [ph.sb("vs%d" % i, [128, 512], F32) for i in range(3)]
    sq = [ph.sb("sq%d" % i, [128, 512], BF16) for i in range(2)]
    rstd = ph.sb("rstd", [128, 512], F32)
    rs2 = [ph.sb("rs2%d" % i, [128, 512], F32) for i in range(2)]
    gcol = ph.sb("gcol", [128, DC], F32)
    qkg = ph.sb("qkgs", [128, 16], F32)
    ones = ph.sb("ones", [128, 128], BF16)
    pb = [ph.ps("pb%d" % i, [128, 512]) for i in range(8)]
    pbk = ["pb%d" % i for i in range(8)]
    P.dma("sp", lambda e: e.dma_start(out=gcol[:], in_=g_in[:, :]), writes=["gcol"])
    P.dma("sp", lambda e: e.dma_start(out=qkg[:], in_=qkg_in[:, :]), writes=["qg", "kg"])
    P.dve(lambda e: e.memset(ones[:], 1.0), writes=["ones"])
    win_v = win.rearrange("(c p) n -> p c n", p=128)
    xin_v = xin.rearrange("(c p) t -> p c t", p=128)
    k_ds = ph.bind.get("k_ds")
    v_ds = ph.bind.get("v_ds")
    if k_ds is not None:
        q_v = ph.bind["q_d"].rearrange("(c p) t -> p c t", p=128)
        k_vs = [kd_.rearrange("(c p) t -> p c t", p=128) for kd_ in k_ds]
        v_vs = [vd_.rearrange("(j p) n -> p j n", p=128) for vd_ in v_ds]
    else:
        qko_v = qko.rearrange("(c p) t -> p c t", p=128)
    vo_v = vo.rearrange("(j p) n -> p j n", p=128)
    xctr = [0]
    wctr = [0]
    pctr = [0]
    nctr = [0]

    def load_w(col0):
        i = wctr[0] % 2
        wctr[0] += 1
        P.dma("pool", lambda e, i=i, col0=col0: e.dma_start(out=Wb[i][:], in_=win_v[:, :, col0:col0 + 512]), writes=["Wb%d" % i])
        return Wb[i], "Wb%d" % i

    def next_pb():
        i = 2 + pctr[0] % 4
        pctr[0] += 1
        return pb[i], pbk[i]

    for tt in range(TC // NT):
        tsl = slice(tt * NT, (tt + 1) * NT)
        p_ss, p_ssk = pb[0], pbk[0]
        for c in range(DC):
            i = xctr[0] % 3
            xctr[0] += 1
            P.dma("sp", lambda e, i=i, c=c, tsl=tsl: e.dma_start(out=xs[i][:], in_=xin_v[:, c, tsl]), writes=["xs%d" % i])
            q = sq[c % 2]
            qk = "sq%d" % (c % 2)
            P.act(lambda e, q=q, i=i: e.activation(out=q[:], in_=xs[i][:], func=AF.Square), reads=["xs%d" % i], writes=[qk])
            P.pe(lambda e, q=q, c=c: e.matmul(p_ss[:], ones[:], q[:], start=(c == 0), stop=(c == DC - 1)),
                 reads=[qk, "ones"], writes=[p_ssk])
        P.dve(lambda e: e.tensor_scalar(out=rstd[:], in0=p_ss[:], scalar1=1.0 / D, scalar2=EPS, op0=ALU.mult, op1=ALU.add),
              reads=[p_ssk], writes=["rstd"])
        P.act(lambda e: e.activation(out=rstd[:], in_=rstd[:], func=AF.Sqrt), reads=["rstd"], writes=["rstd"])
        P.dve(lambda e: e.reciprocal(out=rstd[:], in_=rstd[:]), reads=["rstd"], writes=["rstd"])
        for c in range(DC):
            i = xctr[0] % 3
            xctr[0] += 1
            P.dma("sp", lambda e, i=i, c=c, tsl=tsl: e.dma_start(out=xs[i][:], in_=xin_v[:, c, tsl]), writes=["xs%d" % i])
            P.dve(lambda e, i=i, c=c: e.scalar_tensor_tensor(out=hT[:, c, :], in0=xs[i][:], scalar=gcol[:, c:c + 1],
                                                             in1=rstd[:], op0=ALU.mult, op1=ALU.mult),
                  reads=["xs%d" % i, "rstd", "gcol"], writes=[("h", c)])
        for which in (range(2) if do_qk else ()):
            gk = "qg" if which == 0 else "kg"
            for blk in range(4):
                Wq, Wqk = load_w(which * D + blk * 512)
                for dl in range(4):
                    hd = blk * 4 + dl
                    pq, pqk = next_pb()
                    for c in range(DC):
                        P.pe(lambda e, c=c, dl=dl, pq=pq, Wq=Wq: e.matmul(pq[:], Wq[:, c, dl * 128:(dl + 1) * 128], hT[:, c, :],
                                                                          start=(c == 0), stop=(c == DC - 1)),
                             reads=[Wqk, ("h", c)], writes=[pqk])
                    if do_norm:
                        qr = qraw[hd % 2]
                        qrk = "qraw%d" % (hd % 2)
                        sqb = sq[hd % 2]
                        sqk = "sq%d" % (hd % 2)
                        P.act(lambda e, pq=pq, sqb=sqb: e.activation(out=sqb[:], in_=pq[:], func=AF.Square), reads=[pqk], writes=[sqk])
                        P.act(lambda e, pq=pq, qr=qr: e.copy(out=qr[:], in_=pq[:]), reads=[pqk], writes=[qrk])
                        p2, p2k = pb[6 + hd % 2], pbk[6 + hd % 2]
                        P.pe(lambda e, p2=p2, sqb=sqb: e.matmul(p2[:], ones[:], sqb[:], start=True, stop=True),
                             reads=[sqk, "ones"], writes=[p2k])
                        r2 = rs2[hd % 2]
                        r2k = "rs2%d" % (hd % 2)
                        P.dve(lambda e, p2=p2, r2=r2: e.tensor_scalar(out=r2[:], in0=p2[:], scalar1=1.0 / 128.0, scalar2=EPS,
                                                                      op0=ALU.mult, op1=ALU.add), reads=[p2k], writes=[r2k])
                        P.act(lambda e, r2=r2: e.activation(out=r2[:], in_=r2[:], func=AF.Sqrt), reads=[r2k], writes=[r2k])
                        P.dve(lambda e, r2=r2: e.reciprocal(out=r2[:], in_=r2[:]), reads=[r2k], writes=[r2k])
                        ni = nctr[0] % 3
                        nctr[0] += 1
                        P.dve(lambda e, ni=ni, qr=qr, r2=r2, which=which: e.scalar_tensor_tensor(out=qn[ni][:], in0=qr[:], scalar=qkg[:, which:which + 1],
                                                                                              in1=r2[:], op0=ALU.mult, op1=ALU.mult),
                              reads=[qrk, r2k, gk], writes=["qn%d" % ni])
                    else:
                        ni = nctr[0] % 3
                        nctr[0] += 1
                        P.act(lambda e, pq=pq, ni=ni: e.copy(out=qn[ni][:], in_=pq[:]), reads=[pqk], writes=["qn%d" % ni])
                    if k_ds is not None:
                        dst_ap = q_v[:, hd, tsl] if which == 0 else k_vs[hd // 2][:, hd % 2, tsl]
                    else:
                        dst_ap = qko_v[:, which * 16 + hd, tsl]
                    P.dma("sp", lambda e, ni=ni, dst_ap=dst_ap: e.dma_start(out=dst_ap, in_=qn[ni][:]),
                          reads=["qn%d" % ni], writes=[("qo", which, tt, hd)])
                    ph.out_keys.append(("qo", which, tt, hd))
        for vb in (range(4) if do_v else ()):
            Wv, Wvk = load_w(2 * D + vb * 512)
            for j in range(4):
                jsl = slice(j * 128, (j + 1) * 128)
                pv, pvk = next_pb()
                for c in range(DC):
                    P.pe(lambda e, c=c, jsl=jsl, pv=pv, Wv=Wv: e.matmul(pv[:], hT[:, c, jsl], Wv[:, c, :],
                                                                        start=(c == 0), stop=(c == DC - 1)),
                         reads=[Wvk, ("h", c)], writes=[pvk])
                vi = nctr[0] % 3
                nctr[0] += 1
                P.act(lambda e, pv=pv, vi=vi: e.copy(out=vs[vi][:], in_=pv[:]), reads=[pvk], writes=["vs%d" % vi])
                if v_ds is not None:
                    for hh in range(2):
                        P.dma("sp", lambda e, vi=vi, tt=tt, j=j, vb=vb, hh=hh: e.dma_start(
                            out=v_vs[2 * vb + hh][:, tt * 4 + j, :], in_=vs[vi][:, hh * 256:(hh + 1) * 256]),
                            reads=["vs%d" % vi], writes=[("vo", tt, j, vb, hh)])
                else:
                    P.dma("sp", lambda e, vi=vi, tt=tt, j=j, vb=vb: e.dma_start(out=vo_v[:, tt * 4 + j, vb * 512:(vb + 1) * 512], in_=vs[vi][:]),
                          reads=["vs%d" % vi], writes=[("vo", tt, j, vb)])
                    ph.out_keys.append(("vo", tt, j, vb))
    return ph.finish()


LAM_INIT2 = 0.8 - 0.6 * math.exp(-0.3 * 2)
NEG = -30000.0


def build_diff2_phase(ph=None):
    ph = ph or Phase()
    P = ph.P
    fused = ph.master is not None
    qin = ph.din("qT", [D, TC])
    kin = None if fused else ph.din("kT", [D, 2 * TC])
    vin = None if fused else ph.din("va", [2 * TC, 8, 257])
    xin = ph.din("xT", [D, TC])
    wout = ph.din("wout", [D, D])
    b0_in = ph.din("B0", [128, 16, 128])
    b1_in = ph.din("B1", [128, 16, 128])
    b1p_in = ph.din("B1p", [128, 16, 128])
    cf_in = ph.din("cfar", [128, 16])
    cp_in = ph.din("cpre", [128, 16])
    lp_in = ph.din("lpb", [128, 4, 128])
    sg_in = ph.din("sgb", [128, 256])
    id_in = ph.din("ident", [128, 128])
    xout = ph.dout("xTo", [D, TC])
    SCALE = 128 ** -0.5
    Kt = ph.sb("Kt", [128, 2, 2 * TC], BF16)
    Va = ph.sb("Va", [128, 32, 257], BF16)
    Qt = ph.sb("Qt", [128, 2, TC], BF16)
    ao = ph.sb("ao", [128, 16, 2048], BF16)
    M0 = ph.sb("M0", [128, 16, 128], BF16)
    M1 = ph.sb("M1", [128, 16, 128], BF16)
    M1p = ph.sb("M1p", [128, 16, 128], BF16)
    btmp = ph.sb("btmp", [128, 16, 128], F32)
    cfar = ph.sb("cfars", [128, 16], F32)
    cpre = ph.sb("cpres", [128, 16], F32)
    negc = ph.sb("negc", [128, 16], F32)
    negcp = ph.sb("negcp", [128, 16], F32)
    lpb = ph.sb("lpbs", [128, 4, 128], F32)
    lt1 = ph.sb("lt1", [128, 128], F32)
    lsum = ph.sb("lsum", [128, 2], F32)
    neglam = ph.sb("neglam", [128, 1], F32)
    sgs = ph.sb("sgs", [128, 256], F32)
    ident = ph.sb("idents", [128, 128], BF16)
    PT = [ph.sb("PT%d" % i, [128, 2, 256], BF16) for i in range(3)]
    rc = ph.sb("rc", [128, 4], F32)
    uu = ph.sb("uu", [128, 256], F32)
    att = ph.sb("att", [128, 256], F32)
    asq = ph.sb("asq", [128, 256], F32)
    ssn = ph.sb("ssn", [128, 1], F32)
    aoT = ph.sb("aoT", [128, DC, 512], BF16)
    Wb = [ph.sb("Wb%d" % i, [128, DC, 512], BF16) for i in range(2)]
    xs = [ph.sb("xs%d" % i, [128, 512], F32) for i in range(3)]
    pb = [ph.ps("pb%d" % i, [128, 512]) for i in range(8)]
    pbk = ["pb%d" % i for i in range(8)]

    for (dst, src, k) in ((cfar, cf_in, "cfar"), (cpre, cp_in, "cpre"), (sgs, sg_in, "sgs")):
        P.dma("sp", lambda e, dst=dst, src=src: e.dma_start(out=dst[:], in_=src[:, :]), writes=[k])
    P.dma("sp", lambda e: e.dma_start(out=lpb[:], in_=lp_in[:, :, :]), writes=["lpb"])
    P.dma("pool", lambda e: e.dma_start(out=ident[:], in_=id_in[:, :]), writes=["ident"])
    P.dve(lambda e: e.tensor_scalar(out=negc[:], in0=cfar[:], scalar1=-1.0, scalar2=None, op0=ALU.mult), reads=["cfar"], writes=["negc"])
    P.dve(lambda e: e.tensor_scalar(out=negcp[:], in0=cpre[:], scalar1=-1.0, scalar2=None, op0=ALU.mult), reads=["cpre"], writes=["negcp"])
    for (Mt, src, nb, k) in ((M0, b0_in, negc, "M0"), (M1, b1_in, negc, "M1"), (M1p, b1p_in, negc, "M1p")):
        P.dma("sp", lambda e, src=src: e.dma_start(out=btmp[:], in_=src[:, :, :]), writes=["btmp"])
        for h in range(16):
            P.act(lambda e, Mt=Mt, nb=nb, h=h: e.activation(out=Mt[:, h, :], in_=btmp[:, h, :], func=AF.Exp, bias=nb[:, h:h + 1]),
                  reads=["btmp", "negc", "negcp"], writes=[k])
    for pi in range(2):
        P.dve(lambda e, pi=pi: e.tensor_tensor(out=lt1[:], in0=lpb[:, 2 * pi, :], in1=lpb[:, 2 * pi + 1, :], op=ALU.mult),
              reads=["lpb"], writes=["lt1"])
        P.dve(lambda e, pi=pi: e.reduce_sum(out=lsum[:, pi:pi + 1], in_=lt1[:], axis=AX.X), reads=["lt1"], writes=["lsum"])
    P.act(lambda e: e.activation(out=lsum[:], in_=lsum[:], func=AF.Exp), reads=["lsum"], writes=["lsum"])
    P.dve(lambda e: e.tensor_tensor(out=neglam[:], in0=lsum[:, 1:2], in1=lsum[:, 0:1], op=ALU.subtract), reads=["lsum"], writes=["neglam"])
    P.dve(lambda e: e.tensor_scalar(out=neglam[:], in0=neglam[:], scalar1=-LAM_INIT2, scalar2=None, op0=ALU.add),
          reads=["neglam"], writes=["neglam"])
    P.dve(lambda e: e.tensor_scalar(out=sgs[:], in0=sgs[:], scalar1=1.0 - LAM_INIT2, scalar2=None, op0=ALU.mult),
          reads=["sgs"], writes=["sgs"])

    qin_v = qin.rearrange("(c p) t -> p c t", p=128)
    if fused:
        kpre_vs = [a_.rearrange("(c p) t -> p c t", p=128) for a_ in ph.bind["kpres"]]
        kown_vs = [a_.rearrange("(c p) t -> p c t", p=128) for a_ in ph.bind["kowns"]]
        vpre_vs = [a_.rearrange("(kb p) n -> p kb n", p=128) for a_ in ph.bind["vpres"]]
        vown_vs = [a_.rearrange("(kb p) n -> p kb n", p=128) for a_ in ph.bind["vowns"]]
        P.dve(lambda e: e.memset(Va[:, :, 256:257], 1.0), writes=["Va1"])
    else:
        kin_v = kin.rearrange("(c p) t -> p c t", p=128)
        vin_v = vin.rearrange("(kb p) h n -> p kb h n", p=128)
    xin_v = xin.rearrange("(c p) t -> p c t", p=128)
    xout_v = xout.rearrange("(c p) t -> p c t", p=128)
    wout_v = wout.rearrange("(c p) n -> p c n", p=128)
    LOOK = 2
    NPT = 4
    PTs = [ph.sb("PTp%d" % i, [128, 2, 256], BF16) for i in range(NPT)]
    Osb = [ph.sb("Osb%d" % i, [128, 257], F32) for i in range(4)]
    zero_b = ph.sb("zero_b", [128, 1], F32)
    P.dve(lambda e: e.memset(zero_b[:], 0.0), writes=["zero_b"])
    gstep = [0]

    def stage_a(hp, qt, kb, sidx):
        kb_rel = kb - (16 + 2 * qt)
        qlo = 128 if kb_rel == 1 else 0
        ps, psk = pb[4 + sidx % 3], pbk[4 + sidx % 3]
        pt, ptk = PTs[sidx % NPT], "PTp%d" % (sidx % NPT)
        for i in range(2):
            P.pe(lambda e, ps=ps, i=i, kb=kb, qt=qt, qlo=qlo: e.matmul(
                ps[:, i * 256 + qlo:(i + 1) * 256], Kt[:, i, kb * 128:(kb + 1) * 128],
                Qt[:, i, qt * 256 + qlo:(qt + 1) * 256], start=True, stop=True),
                reads=["Kt", "Qt"], writes=[psk])
        bias_ap = cpre[:, 0:1] if kb < 16 else zero_b[:, 0:1]
        P.act(lambda e, ps=ps, pt=pt, qlo=qlo, bias_ap=bias_ap: e.activation(
            out=pt[:, :, qlo:256], in_=ps[:].rearrange("p (i q) -> p i q", i=2)[:, :, qlo:256], func=AF.Exp,
            bias=bias_ap, scale=SCALE),
            reads=[psk, "cpre", "zero_b"], writes=[ptk])
        fix = []
        if kb_rel == -1:
            fix.append((0, M1p if kb == 15 else M1, "M1p" if kb == 15 else "M1"))
        elif kb_rel == 0:
            fix.append((0, M0, "M0"))
            fix.append((1, M1, "M1"))
        elif kb_rel == 1:
            fix.append((1, M0, "M0"))
        for (qb, Mt, mk) in fix:
            P.dve(lambda e, pt=pt, qb=qb, Mt=Mt, hp=hp: e.tensor_tensor(
                out=pt[:, :, qb * 128:(qb + 1) * 128], in0=pt[:, :, qb * 128:(qb + 1) * 128],
                in1=Mt[:, 2 * hp:2 * hp + 2, :], op=ALU.mult),
                reads=[ptk, mk], writes=[ptk])

    def stage_b(hp, qt, kb, sidx):
        pt, ptk = PTs[sidx % NPT], "PTp%d" % (sidx % NPT)
        for qb in range(2):
            last = 16 + 2 * qt + qb
            if kb > last:
                continue
            for i in range(2):
                acc = pb[qb * 2 + i]
                P.pe(lambda e, acc=acc, pt=pt, i=i, qb=qb, kb=kb, last=last: e.matmul(
                    acc[:, 0:257], pt[:, i, qb * 128:(qb + 1) * 128], Va[:, kb, :],
                    start=(kb == 0), stop=(kb == last)),
                    reads=[ptk, "Va"], writes=[pbk[qb * 2 + i]])
        if kb == 16 + 2 * qt + 1:
            finalize(hp, qt)

    def finalize(hp, qt):
        for a_i in range(4):
            P.dve(lambda e, a_i=a_i: e.tensor_scalar(out=Osb[a_i][:], in0=pb[a_i][:, 0:257], scalar1=1.0, scalar2=None, op0=ALU.mult),
                  reads=[pbk[a_i]], writes=["Osb%d" % a_i])
        for qb in range(2):
            o1, o1k = Osb[qb * 2], "Osb%d" % (qb * 2)
            o2, o2k = Osb[qb * 2 + 1], "Osb%d" % (qb * 2 + 1)
            P.dve(lambda e, o1=o1: e.reciprocal(out=rc[:, 0:1], in_=o1[:, 256:257]), reads=[o1k], writes=["rc"])
            P.dve(lambda e, o2=o2: e.reciprocal(out=rc[:, 1:2], in_=o2[:, 256:257]), reads=[o2k], writes=["rc"])
            P.dve(lambda e: e.tensor_tensor(out=rc[:, 1:2], in0=rc[:, 1:2], in1=neglam[:], op=ALU.mult),
                  reads=["rc", "neglam"], writes=["rc"])
            P.pool(lambda e, o2=o2: e.tensor_scalar(out=uu[:], in0=o2[:, 0:256], scalar1=rc[:, 1:2], scalar2=None, op0=ALU.mult),
                   reads=[o2k, "rc"], writes=["uu"])
            P.pool(lambda e, o1=o1: e.tensor_scalar(out=att[:], in0=o1[:, 0:256], scalar1=rc[:, 0:1], scalar2=None, op0=ALU.mult),
                   reads=[o1k, "rc"], writes=["att"])
            P.pool(lambda e: e.tensor_tensor(out=att[:], in0=att[:], in1=uu[:], op=ALU.add), reads=["att", "uu"], writes=["att"])
            P.pool(lambda e: e.tensor_tensor(out=asq[:], in0=att[:], in1=att[:], op=ALU.mult), reads=["att"], writes=["asq"])
            P.dve(lambda e: e.reduce_sum(out=ssn[:], in_=asq[:], axis=AX.X), reads=["asq"], writes=["ssn"])
            P.dve(lambda e: e.tensor_scalar(out=ssn[:], in0=ssn[:], scalar1=1.0 / 256.0, scalar2=EPS, op0=ALU.mult, op1=ALU.add),
                  reads=["ssn"], writes=["ssn"])
            P.act(lambda e: e.activation(out=ssn[:], in_=ssn[:], func=AF.Sqrt), reads=["ssn"], writes=["ssn"])
            P.dve(lambda e: e.reciprocal(out=ssn[:], in_=ssn[:]), reads=["ssn"], writes=["ssn"])
            qbg = 2 * qt + qb
            P.pool(lambda e: e.tensor_scalar(out=att[:], in0=att[:], scalar1=ssn[:, 0:1], scalar2=None, op0=ALU.mult),
                   reads=["att", "ssn"], writes=["att"])
            P.pool(lambda e, qbg=qbg, hp=hp: e.tensor_tensor(out=ao[:, qbg, hp * 256:(hp + 1) * 256], in0=att[:], in1=sgs[:],
                                                             op=ALU.mult),
                   reads=["att", "sgs"], writes=[("ao", qbg)])

    for hp in range(8):
        if fused:
            P.dma("pool", lambda e, hp=hp: e.dma_start(out=Kt[:, :, 0:TC], in_=kpre_vs[hp][:, :, :]), writes=["Kt"])
            P.dma("pool", lambda e, hp=hp: e.dma_start(out=Kt[:, :, TC:2 * TC], in_=kown_vs[hp][:, :, :]), writes=["Kt"])
            P.dma("pool", lambda e, hp=hp: e.dma_start(out=Va[:, 0:16, 0:256], in_=vpre_vs[hp][:, :, :]), reads=["Va1"], writes=["Va"])
            P.dma("pool", lambda e, hp=hp: e.dma_start(out=Va[:, 16:32, 0:256], in_=vown_vs[hp][:, :, :]), reads=["Va1"], writes=["Va"])
        else:
            P.dma("pool", lambda e, hp=hp: e.dma_start(out=Kt[:, :, 0:TC], in_=kin_v[:, 2 * hp:2 * hp + 2, 0:TC]), writes=["Kt"])
            P.dma("pool", lambda e, hp=hp: e.dma_start(out=Kt[:, :, TC:2 * TC], in_=kin_v[:, 2 * hp:2 * hp + 2, TC:2 * TC]), writes=["Kt"])
            P.dma("pool", lambda e, hp=hp: e.dma_start(out=Va[:, 0:16, :], in_=vin_v[:, 0:16, hp, :]), writes=["Va"])
            P.dma("pool", lambda e, hp=hp: e.dma_start(out=Va[:, 16:32, :], in_=vin_v[:, 16:32, hp, :]), writes=["Va"])
        P.dma("pool", lambda e, hp=hp: e.dma_start(out=Qt[:], in_=qin_v[:, 2 * hp:2 * hp + 2, :]), writes=["Qt"])
        steps = [(qt, kb) for qt in range(8) for kb in range(16 + 2 * qt + 2)]
        base = gstep[0]
        for n in range(len(steps) + LOOK):
            if n < len(steps):
                stage_a(hp, steps[n][0], steps[n][1], base + n)
            if n >= LOOK:
                stage_b(hp, steps[n - LOOK][0], steps[n - LOOK][1], base + n - LOOK)
        gstep[0] += len(steps)
    xctr = [0]
    wctr = [0]
    pctr = [0]
    for tt in range(4):
        tsl = slice(tt * 512, (tt + 1) * 512)
        for j in range(4):
            qbg = tt * 4 + j
            for fb in range(4):
                ptr, ptrk = pb[6], pbk[6]
                for fi in range(4):
                    fc = fb * 4 + fi
                    P.pe(lambda e, ptr=ptr, fi=fi, fc=fc, qbg=qbg: e.transpose(pbf(ptr)[:, fi * 128:(fi + 1) * 128],
                                                                              ao[:, qbg, fc * 128:(fc + 1) * 128], ident[:]),
                         reads=[("ao", qbg), "ident"], writes=[ptrk])
                P.act(lambda e, ptr=ptr, fb=fb, j=j: e.copy(out=aoT[:, fb * 4:(fb + 1) * 4, j * 128:(j + 1) * 128],
                                                            in_=pbf(ptr)[:, 0:512].rearrange("p (f t) -> p f t", f=4)),
                      reads=[ptrk], writes=[("aoT", fb)])
        for ob in range(4):
            wi = wctr[0] % 2
            wctr[0] += 1
            P.dma("pool", lambda e, wi=wi, ob=ob: e.dma_start(out=Wb[wi][:], in_=wout_v[:, :, ob * 512:(ob + 1) * 512]),
                  writes=["Wb%d" % wi])
            for dd in range(4):
                d = ob * 4 + dd
                pi = pctr[0] % 2
                pctr[0] += 1
                py_, pyk = pb[pi], pbk[pi]
                for fc in range(DC):
                    P.pe(lambda e, fc=fc, dd=dd, py_=py_, wi=wi: e.matmul(py_[:], Wb[wi][:, fc, dd * 128:(dd + 1) * 128], aoT[:, fc, :],
                                                                          start=(fc == 0), stop=(fc == DC - 1)),
                         reads=["Wb%d" % wi, ("aoT", fc // 4)], writes=[pyk])
                xi = xctr[0] % 3
                xctr[0] += 1
                P.dma("sp", lambda e, xi=xi, d=d, tsl=tsl: e.dma_start(out=xs[xi][:], in_=xin_v[:, d, tsl]), writes=["xs%d" % xi])
                P.dve(lambda e, xi=xi, py_=py_: e.tensor_tensor(out=xs[xi][:], in0=xs[xi][:], in1=py_[:], op=ALU.add),
                      reads=[pyk, "xs%d" % xi], writes=["xs%d" % xi])
                P.dma("sp", lambda e, xi=xi, d=d, tsl=tsl: e.dma_start(out=xout_v[:, d, tsl], in_=xs[xi][:]),
                      reads=["xs%d" % xi], writes=[("xo", tt, d)])
                ph.out_keys.append(("xo", tt, d))
    return ph.finish()


def rel_bucket_np(rel):
    n = np.maximum(rel, 0)
    nf = np.maximum(n, 1).astype(np.float32)
    large = 16 + (np.log(nf / np.float32(16)) / np.float32(math.log(128 / 16)) * np.float32(16)).astype(np.int32)
    large = np.minimum(large, 31)
    return np.where(n < 16, n, large)


def diff_bias_tiles(rel_bias, first_half):
    tab = np.concatenate([np.asarray(rel_bias, np.float32), np.full((1, 16), NEG, np.float32),
                          np.full((1, 16), 2 * NEG, np.float32)], axis=0)
    k = np.arange(128)[:, None]
    q = np.arange(128)[None, :]
    rel0 = q - k
    idx0 = np.where(rel0 >= 0, rel_bucket_np(rel0), 32)
    idx1 = rel_bucket_np(128 + q - k)
    B0 = np.ascontiguousarray(tab[idx0].transpose(0, 2, 1))
    B1 = np.ascontiguousarray(tab[idx1].transpose(0, 2, 1))
    if first_half:
        B1p = np.full((128, 16, 128), 2 * NEG, np.float32)
        cpre = np.full((128, 16), NEG, np.float32)
    else:
        B1p = B1.copy()
        cpre = np.zeros((128, 16), np.float32)
    cfar = np.ascontiguousarray(np.broadcast_to(tab[31][None, :], (128, 16)))
    return B0, B1, B1p, cfar, cpre


def diff2_inputs(qT, kT_own, v_own, kT_prev, v_prev, xT, w_out, rel_bias, lam_params, sub_gain, first_half):
    kT_all = np.zeros((D, 2 * TC), np.float32)
    va = np.zeros((2 * TC, 8, 257), np.float32)
    kT_all[:, TC:] = kT_own
    va[TC:, :, :256] = v_own.reshape(TC, 8, 256)
    va[TC:, :, 256] = 1.0
    if not first_half:
        kT_all[:, :TC] = kT_prev
        va[:TC, :, :256] = v_prev.reshape(TC, 8, 256)
        va[:TC, :, 256] = 1.0
    B0, B1, B1p, cfar, cpre = diff_bias_tiles(rel_bias, first_half)
    lpb = np.ascontiguousarray(np.broadcast_to(np.asarray(lam_params, np.float32)[None], (128, 4, 128)))
    sgb = np.ascontiguousarray(np.broadcast_to(np.asarray(sub_gain, np.float32)[None, :], (128, 256)))
    return {"qT": np.ascontiguousarray(qT), "kT": kT_all, "va": va, "xT": np.ascontiguousarray(xT),
            "wout": np.ascontiguousarray(w_out, dtype=np.float32), "B0": B0, "B1": B1, "B1p": B1p, "cfar": cfar, "cpre": cpre,
            "lpb": lpb, "sgb": sgb, "ident": np.eye(128, dtype=np.float32)}


def diff1_inputs(xT, g, w_in, qg, kg):
    qkg = np.zeros((128, 16), np.float32)
    qkg[:, 0] = np.asarray(qg, np.float32)
    qkg[:, 1] = np.asarray(kg, np.float32)
    return {"xT": np.ascontiguousarray(xT), "g": col16(g), "win": np.ascontiguousarray(w_in, dtype=np.float32), "qkg": qkg}


PAIRS = [[0, 1], [2, 3], [4, 5], [6, 7]]
DEPTH = 4


def build_fused(plan=("gla0", "ffn0", "pool", "ffn1", "diff", "ffn2", "gla1", "ffn3")):
    mp = Phase()
    m = mp
    nc, P = mp.nc, mp.P
    mp.btile = mp.sb("btile", [128, 1], F32)
    cache = {}

    def lz(name, shape):
        if name not in cache:
            cache[name] = mp.din(name, shape)
        return cache[name]

    def lzi(name, shape):
        if name not in cache:
            cache[name] = mp.dint(name, shape)
        return cache[name]

    xT_in = mp.din("xT", [D, TC])
    xT_out = mp.dout("xTo", [D, TC])

    def ngf(l, i):
        return lz("ng_%d_%d" % (l, i), [128, DC])

    def barrier():
        P.barrier(lambda e, bt=mp.btile: e.memset(bt[:], 0.0))

    def gather(src, dst, tag):
        P.cc(lambda e: e.collective_compute("AllGather", ALU.bypass, replica_groups=PAIRS, ins=[src], outs=[dst]),
             reads=[], writes=[("cc", tag)])
        barrier()

    def gla_layer(sl, layer, xin, xout):
        st_src = lzi("st_src", [1024, 512])
        st_all = lzi("st_all", [2048, 512])
        common = dict(xT=xin, g=ngf(layer, 0), win=lz("gla%d_win" % sl, [D, GLA_NCOL]), wa2b=lz("gla%d_wa2b" % sl, [17, GLA_DKT]),
                      gnb=lz("gla%d_gnb" % sl, [128, 2048]), wout=lz("gla%d_wout" % sl, [D, D]), tri=lz("tri", [128, 128]),
                      ident=lz("ident", [128, 128]), isb=lz("isb", [128, 16]))
        b1 = dict(common)
        b1.update(st_in=None, st_out=st_src.rearrange("(p c) v -> p c v", c=8))
        build_gla_phase(Phase(mp, "g%da_" % layer, b1), state_only=True)
        gather(st_src[:, :], st_all[:, :], ("st", layer))
        b2 = dict(common)
        b2.update(st_in=st_all[0:1024, :].rearrange("(p c) v -> p c v", c=8), st_out=None, xTo=xout)
        build_gla_phase(Phase(mp, "g%db_" % layer, b2), state_only=False)

    def ffn_layer(layer, xin, xout):
        build_ffn_phase(Phase(mp, "f%d_" % layer, dict(xT=xin, xTo=xout, g=ngf(layer, 1), wgu=lz("ffn%d_wgu" % layer, [D, 2 * FH]),
                                                      wdn=lz("ffn%d_wdn" % layer, [FH, D]))))

    def pool_layer(layer, xin, xout):
        halo_src = lzi("halo_src", [D, 16])
        halo_all = lzi("halo_all", [2 * D, 16])
        P.dma("sp", lambda e: e.dma_start(out=halo_src[:, :], in_=xin[:, TC - 16:TC]), writes=["halo_src"])
        barrier()
        gather(halo_src[:, :], halo_all[:, :], "halo")
        build_pool_phase(Phase(mp, "p1_", dict(xT=xin, halo=halo_all[0:D, :], isb=lz("isb", [128, 16]), g=ngf(layer, 0),
                                               wp=lz("pool_wp", [4, 512, 512]), psc=lz("pool_psc", [128, DC]),
                                               invc=lz("pool_invc", [128, 4, 16]), xTo=xout)))

    def diff_layer(layer, xin, xout):
        q_d = lzi("q_d", [D, TC])
        k_ds = [lzi("k_d%d" % i, [256, TC]) for i in range(8)]
        v_ds = [lzi("v_d%d" % i, [TC, 256]) for i in range(8)]
        k_alls = [lzi("k_all%d" % i, [512, TC]) for i in range(8)]
        v_alls = [lzi("v_all%d" % i, [2 * TC, 256]) for i in range(8)]
        build_diff1_phase(ph=Phase(mp, "d1_", dict(xT=xin, g=ngf(layer, 0), win=lz("diff_win", [D, 3 * D]),
                                                   qkg=lz("diff_qkg", [128, 16]), qkT=q_d, q_d=q_d, k_ds=k_ds, v_ds=v_ds, v=v_ds[0])))
        for i in range(8):
            P.cc(lambda e, i=i: e.collective_compute("AllGather", ALU.bypass, replica_groups=PAIRS, ins=[k_ds[i][:, :]],
                                                     outs=[k_alls[i][:, :]]), reads=[], writes=[("cc", "k", i)])
            P.cc(lambda e, i=i: e.collective_compute("AllGather", ALU.bypass, replica_groups=PAIRS, ins=[v_ds[i][:, :]],
                                                     outs=[v_alls[i][:, :]]), reads=[], writes=[("cc", "v", i)])
        barrier()
        build_diff2_phase(Phase(mp, "d2_", dict(qT=q_d, kpres=[a_[0:256, :] for a_ in k_alls], kowns=k_ds,
                                                vpres=[a_[0:TC, :] for a_ in v_alls], vowns=v_ds, xT=xin,
                                                wout=lz("diff_wout", [D, D]), B0=lz("diff_B0", [128, 16, 128]),
                                                B1=lz("diff_B1", [128, 16, 128]), B1p=lz("diff_B1p", [128, 16, 128]),
                                                cfar=lz("diff_cfar", [128, 16]), cpre=lz("diff_cpre", [128, 16]),
                                                lpb=lz("diff_lpb", [128, 4, 128]), sgb=lz("diff_sgb", [128, 256]),
                                                ident=lz("ident", [128, 128]), xTo=xout)))

    bufs = [lzi("xa", [D, TC]), lzi("xb", [D, TC])]
    cur = xT_in
    for si, step in enumerate(plan):
        dst = xT_out if si == len(plan) - 1 else bufs[si % 2]
        layer = int(step[-1]) if step[:3] == "ffn" else {"gla0": 0, "pool": 1, "diff": 2, "gla1": 3}[step]
        if step[:3] == "ffn":
            ffn_layer(layer, cur, dst)
        elif step[:3] == "gla":
            gla_layer(int(step[3]), layer, cur, dst)
        elif step == "pool":
            pool_layer(layer, cur, dst)
        else:
            diff_layer(layer, cur, dst)
        cur = dst
    mp.input_names = [k for k in cache if not k in ("xa", "xb", "st_src", "st_all", "halo_src", "halo_all", "q_d", "k_d", "v_d", "k_all", "v_all")]
    return mp.finish()


_FUSED = []


def kernel(x, norm_g, gla_w_in, gla_w_a2, gla_b_a, gla_g_norm, gla_w_out, pool_w, pool_scale, diff_w_in,
           diff_q_gain, diff_k_gain, diff_lambda, diff_sub_gain, diff_w_out, rel_bias, ffn_w_gu, ffn_w_down):
    x = np.asarray(x, np.float32)
    B, S, _ = x.shape
    if not _FUSED:
        _FUSED.append(build_fused())
    nc = _FUSED[0]
    f32c = lambda a: np.ascontiguousarray(np.asarray(a, np.float32))
    tri, ident = gla_consts()
    shared = {"tri": tri, "ident": ident}
    for l in range(DEPTH):
        for i in range(2):
            shared["ng_%d_%d" % (l, i)] = col16(norm_g[l, i])
        shared["ffn%d_wgu" % l] = f32c(ffn_w_gu[l])
        shared["ffn%d_wdn" % l] = f32c(ffn_w_down[l])
    for sl in range(2):
        shared["gla%d_win" % sl] = f32c(gla_w_in[sl])
        shared["gla%d_wa2b" % sl] = f32c(np.concatenate([np.asarray(gla_w_a2[sl], np.float32),
                                                         np.asarray(gla_b_a[sl], np.float32)[None, :]], axis=0))
        shared["gla%d_gnb" % sl] = f32c(np.broadcast_to(np.tile(np.asarray(gla_g_norm[sl], np.float32), 4)[None, :], (128, 2048)))
        shared["gla%d_wout" % sl] = f32c(gla_w_out[sl])
    shared["pool_wp"] = f32c(pool_w[0])
    shared["pool_psc"] = col16(pool_scale[0])
    shared["diff_win"] = f32c(diff_w_in[0])
    qkg = np.zeros((128, 16), np.float32)
    qkg[:, 0] = np.asarray(diff_q_gain[0], np.float32)
    qkg[:, 1] = np.asarray(diff_k_gain[0], np.float32)
    shared["diff_qkg"] = qkg
    shared["diff_wout"] = f32c(diff_w_out[0])
    shared["diff_lpb"] = f32c(np.broadcast_to(np.asarray(diff_lambda[0], np.float32)[None], (128, 4, 128)))
    shared["diff_sgb"] = f32c(np.broadcast_to(np.asarray(diff_sub_gain[0], np.float32)[None, :], (128, 256)))
    per_half = []
    for half in range(2):
        B0, B1, B1p, cfar, cpre = diff_bias_tiles(rel_bias, half == 0)
        invc = np.zeros((128, 4, 16), np.float32)
        for gi, w in enumerate(POOL_W):
            for t in range(16):
                invc[:, gi, t] = 1.0 / (min(t + 1, w) if half == 0 else w)
        per_half.append({"diff_B0": B0, "diff_B1": B1, "diff_B1p": B1p, "diff_cfar": cfar, "diff_cpre": cpre,
                         "pool_invc": invc, "isb": np.full((128, 16), float(half), np.float32)})
    in_maps = []
    for c in range(NCORES):
        im = dict(shared)
        im.update(per_half[c % 2])
        im["xT"] = np.ascontiguousarray(x[c // 2, (c % 2) * TC:(c % 2 + 1) * TC].T)
        in_maps.append(im)
    res = run_bass_kernel_spmd(nc, in_maps, core_ids=list(range(NCORES))).results
    out = np.empty((B, S, D), np.float32)
    for c in range(NCORES):
        out[c // 2, (c % 2) * TC:(c % 2 + 1) * TC] = res[c]["xTo"].T
    return out
```

```python
from contextlib import ExitStack
import math
import numpy as np
import concourse.bass as bass
import concourse.mybir as mybir
from concourse.bass_utils import run_bass_kernel_spmd

F32 = mybir.dt.float32
BF16 = mybir.dt.bfloat16
AF = mybir.ActivationFunctionType
ALU = mybir.AluOpType
AX = mybir.AxisListType

D = 2048
DC = D // 128
TC = 2048
FH = 5632
HC = FH // 128
EPS = 1e-6
NCORES = 8

ENGINES = ("pe", "act", "dve", "pool", "sp")


class Op:
    __slots__ = ("eng", "fn", "reads", "writes", "dma", "waits", "sig", "idx", "cc", "barrier")

    def __init__(self, eng, fn, reads, writes, dma):
        self.cc = False
        self.barrier = False
        self.eng = eng
        self.fn = fn
        self.reads = reads
        self.writes = writes
        self.dma = dma
        self.waits = []
        self.sig = None
        self.idx = -1


class Prog:
    NDMA = 32

    def __init__(self, nc):
        self.nc = nc
        self.ops = []

    LOOPVARS = frozenset("tt tsl j jsl h hs dc di dl blk c fc fi fb d dd ob vb rb i ei s sl mi grp b g w at atk kd kdk pq pk_ pv py_ ptr pkv cur oth sh a Wq Wk Wv Wr Wo wb db dp hf step q qk qb kb ki qi hp".split())

    def op(self, eng, fn, reads=(), writes=(), dma=False):
        bad = self.LOOPVARS.intersection(fn.__code__.co_freevars)
        if bad:
            raise RuntimeError("late-bound loop variable(s) %s in lambda at line %d" % (sorted(bad), fn.__code__.co_firstlineno))
        o = Op(eng, fn, tuple(reads), tuple(writes), dma)
        o.idx = len(self.ops)
        self.ops.append(o)
        return o

    def pe(self, fn, reads=(), writes=()):
        return self.op("pe", fn, reads, writes)

    def act(self, fn, reads=(), writes=()):
        return self.op("act", fn, reads, writes)

    def dve(self, fn, reads=(), writes=()):
        return self.op("dve", fn, reads, writes)

    def pool(self, fn, reads=(), writes=()):
        return self.op("pool", fn, reads, writes)

    def dma(self, eng, fn, reads=(), writes=()):
        return self.op(eng, fn, reads, writes, dma=True)

    def cc(self, fn, reads=(), writes=()):
        o = self.op("pool", fn, reads, writes, dma=True)
        o.cc = True
        return o

    def barrier(self, fn):
        o = self.op("dve", fn, (), ())
        o.barrier = True
        return o

    def finalize(self, final_wait_keys=()):
        ops = self.ops
        deps = [set() for _ in ops]
        last_writer = {}
        readers = {}
        since = []
        last_barrier = None
        for o in ops:
            if o.barrier:
                deps[o.idx] |= set(since)
                if last_barrier is not None:
                    deps[o.idx].add(last_barrier)
                since = []
                last_barrier = o.idx
                last_writer_final = dict(last_writer)
                last_writer = {}
                readers = {}
                continue
            since.append(o.idx)
            if last_barrier is not None:
                deps[o.idx].add(last_barrier)
            for k in o.reads:
                w = last_writer.get(k)
                if w is not None:
                    deps[o.idx].add(w.idx)
            for k in o.writes:
                w = last_writer.get(k)
                if w is not None:
                    deps[o.idx].add(w.idx)
                for r in readers.get(k, ()):
                    if r.idx != o.idx:
                        deps[o.idx].add(r.idx)
            for k in o.reads:
                readers.setdefault(k, []).append(o)
            for k in o.writes:
                last_writer[k] = o
                readers[k] = []
        final_ops = [last_writer[k].idx for k in final_wait_keys if k in last_writer]
        if last_barrier is not None:
            final_ops.append(last_barrier)
        needed = [set() for _ in ops]
        for o in ops:
            best = {}
            for d in deps[o.idx]:
                p = ops[d]
                if p.dma:
                    needed[o.idx].add(d)
                    continue
                if p.eng == "pe" and o.eng == "pe" and not o.dma:
                    continue
                if best.get(p.eng, -1) < d:
                    best[p.eng] = d
            needed[o.idx] |= set(best.values())
        signaled = set(final_ops)
        for o in ops:
            signaled |= needed[o.idx]
        eng_count = {e: 0 for e in ENGINES}
        NDMA = self.NDMA
        dma_count = [0] * NDMA
        dma_last = [None] * NDMA
        rr = 0
        cc_count = 0
        cc_last = None
        for o in ops:
            if o.dma and not o.cc:
                signaled.add(o.idx)
            if o.cc:
                signaled.add(o.idx)
                if cc_last is not None:
                    needed[o.idx].add(cc_last)
                cc_count += 1
                cc_last = o.idx
                o.sig = (("cc", 0), None, cc_count)
                continue
            if o.idx not in signaled:
                continue
            if o.dma:
                s = rr % NDMA
                rr += 1
                if dma_last[s] is not None:
                    needed[o.idx].add(dma_last[s])
                dma_count[s] += 1
                dma_last[s] = o.idx
                o.sig = (("dma", s), 16, dma_count[s] * 16)
            else:
                eng_count[o.eng] += 1
                o.sig = (("eng", o.eng), 1, eng_count[o.eng])
        seen = {e: {} for e in ENGINES}
        for o in ops:
            ws = {}
            for d in needed[o.idx]:
                semkey, _, val = ops[d].sig
                if ws.get(semkey, 0) < val:
                    ws[semkey] = val
            for semkey, val in ws.items():
                if seen[o.eng].get(semkey, 0) >= val:
                    continue
                seen[o.eng][semkey] = val
                o.waits.append((semkey, val))
        fw = {}
        for d in final_ops:
            semkey, _, val = ops[d].sig
            fw[semkey] = max(fw.get(semkey, 0), val)
        self.final_waits = fw
        self.eng_count = eng_count

    def emit(self, block, sems):
        per_eng = {e: [] for e in ENGINES}
        for o in self.ops:
            per_eng[o.eng].append(o)
        final_waits = self.final_waits

        def run(engobj, lst, is_last):
            for o in lst:
                for semkey, val in o.waits:
                    engobj.wait_ge(sems[semkey], val)
                ins = o.fn(engobj)
                if o.sig is not None:
                    if o.sig[1] is None:
                        ins.then_inc(sems[o.sig[0]])
                    else:
                        ins.then_inc(sems[o.sig[0]], o.sig[1])
            if is_last:
                for semkey, val in final_waits.items():
                    engobj.wait_ge(sems[semkey], val)

        @block.tensor
        def _(e):
            run(e, per_eng["pe"], False)

        @block.scalar
        def _(e):
            run(e, per_eng["act"], False)

        @block.vector
        def _(e):
            run(e, per_eng["dve"], False)

        @block.gpsimd
        def _(e):
            run(e, per_eng["pool"], False)

        @block.sync
        def _(e):
            run(e, per_eng["sp"], True)


class Phase:
    def __init__(self, master=None, prefix="", bind=None):
        self.master = master
        self.prefix = prefix
        self.bind = bind or {}
        if master is None:
            self.nc = bass.Bass("TRN2", target_bir_lowering=False)
            self.P = Prog(self.nc)
            self.out_keys = []
        else:
            self.nc = master.nc
            self.P = master.P
            self.out_keys = master.out_keys
        self.es = ExitStack()

    def din(self, name, shape, dtype=F32):
        if name in self.bind:
            return self.bind[name]
        return self.nc.dram_tensor(self.prefix + name, list(shape), dtype, kind="ExternalInput").ap()

    def dout(self, name, shape, dtype=F32):
        if name in self.bind:
            return self.bind[name]
        return self.nc.dram_tensor(self.prefix + name, list(shape), dtype, kind="ExternalOutput").ap()

    def dint(self, name, shape, dtype=F32):
        return self.nc.dram_tensor(self.prefix + name, list(shape), dtype).ap()

    def sb(self, name, shape, dtype=F32):
        return self.es.enter_context(self.nc.sbuf_tensor(self.prefix + name, list(shape), dtype))

    def ps(self, name, shape, dtype=F32):
        return self.es.enter_context(self.nc.psum_tensor(self.prefix + name, list(shape), dtype))

    def finish(self):
        if self.master is not None:
            self.es.close()
            bt = self.master.btile
            self.P.barrier(lambda e: e.memset(bt[:], 0.0))
            return None
        P = self.P
        P.finalize(final_wait_keys=self.out_keys)
        sems = {}
        for e in ENGINES:
            sems[("eng", e)] = self.es.enter_context(self.nc.semaphore("s_" + e))
        for i in range(P.NDMA):
            sems[("dma", i)] = self.es.enter_context(self.nc.semaphore("d%d" % i))
        sems[("cc", 0)] = self.es.enter_context(self.nc.semaphore("s_cc"))
        block = self.es.enter_context(self.nc.Block())
        P.emit(block, sems)
        self.es.close()
        return self.nc


def emit_rmsnorm(ph, xT, hT, gcol, ones_bf, sq, pss, rstd, ntok, xkey, hkey, tag):
    P = ph.P
    nsub = ntok // 512
    for s in range(nsub):
        sl = slice(s * 512, (s + 1) * 512)
        pb = pss[s % 2]
        pk = "pss%d" % (s % 2)
        for c in range(DC):
            q = sq[c % 2]
            qk = "sq%d" % (c % 2)
            P.act(lambda e, q=q, c=c, sl=sl: e.activation(out=q[:], in_=xT[:, c, sl], func=AF.Square),
                  reads=[(xkey, c)], writes=[qk])
            P.pe(lambda e, q=q, c=c, pb=pb: e.matmul(pb[:], ones_bf[:], q[:], start=(c == 0), stop=(c == DC - 1)),
                 reads=[qk, "ones"], writes=[pk])
        rk = ("rstd", tag, s)
        P.dve(lambda e, pb=pb, sl=sl: e.tensor_scalar(out=rstd[:, sl], in0=pb[:], scalar1=1.0 / D, scalar2=EPS,
                                                      op0=ALU.mult, op1=ALU.add),
              reads=[pk], writes=[rk])
        P.act(lambda e, sl=sl: e.activation(out=rstd[:, sl], in_=rstd[:, sl], func=AF.Sqrt),
              reads=[rk], writes=[rk])
        P.dve(lambda e, sl=sl: e.reciprocal(out=rstd[:, sl], in_=rstd[:, sl]),
              reads=[rk], writes=[rk])
        for c in range(DC):
            P.dve(lambda e, c=c, sl=sl: e.scalar_tensor_tensor(out=hT[:, c, sl], in0=xT[:, c, sl],
                                                               scalar=gcol[:, c:c + 1], in1=rstd[:, sl],
                                                               op0=ALU.mult, op1=ALU.mult),
                  reads=[(xkey, c), rk, "gcol"], writes=[(hkey, c, s)])


def build_ffn_phase(ph=None):
    ph = ph or Phase()
    P = ph.P
    nc = ph.nc
    xin = ph.din("xT", [D, TC])
    g_in = ph.din("g", [128, DC])
    wgu = ph.din("wgu", [D, 2 * FH])
    wdn = ph.din("wdn", [FH, D])
    xout = ph.dout("xTo", [D, TC])
    NT = 1024
    NS = NT // 512
    HG = 11
    NG = HC // HG
    xT = ph.sb("xTs", [128, DC, NT], F32)
    hT = ph.sb("hTs", [128, DC, NT], BF16)
    aT = ph.sb("aTs", [128, HG, NT], BF16)
    wg = [ph.sb("wg%d" % i, [128, DC, 256], BF16) for i in range(3)]
    wd = [ph.sb("wd%d" % i, [128, HG, 256], BF16) for i in range(2)]
    sq = [ph.sb("sq%d" % i, [128, 512], BF16) for i in range(2)]
    sg = [ph.sb("sg%d" % i, [128, 512], F32) for i in range(2)]
    rstd = ph.sb("rstd", [128, NT], F32)
    gcol = ph.sb("gcol", [128, DC], F32)
    ones = ph.sb("ones", [128, 128], BF16)
    pss = [ph.ps("pss%d" % i, [128, 512]) for i in range(2)]
    pg = [ph.ps("pg%d" % i, [128, 512]) for i in range(2)]
    pu = [ph.ps("pu%d" % i, [128, 512]) for i in range(2)]
    py = [ph.ps("py%d" % i, [128, 512]) for i in range(2)]

    P.dma("sp", lambda e: e.dma_start(out=gcol[:], in_=g_in[:, :]), writes=["gcol"])
    P.dve(lambda e: e.memset(ones[:], 1.0), writes=["ones"])
    xin_v = xin.rearrange("(c p) t -> p c t", p=128)
    xout_v = xout.rearrange("(c p) t -> p c t", p=128)
    wgu_v = wgu.rearrange("(c p) n -> p c n", p=128)
    wdn_v = wdn.rearrange("(m p) n -> p m n", p=128)
    wgi = 0
    wdi = 0
    cnt = 0
    for tt in range(TC // NT):
        tsl = slice(tt * NT, (tt + 1) * NT)
        for c in range(DC):
            P.dma("sp",
                  lambda e, c=c, tsl=tsl: e.dma_start(out=xT[:, c, :], in_=xin_v[:, c, tsl]),
                  writes=[("x", c)])
        emit_rmsnorm(ph, xT, hT, gcol, ones, sq, pss, rstd, NT, "x", "h", tt)
        hkeys = [("h", c, s) for c in range(DC) for s in range(NS)]
        for grp in range(NG):
            for mi in range(HG):
                m = grp * HG + mi
                wb = wg[wgi % 3]
                wk = "wg%d" % (wgi % 3)
                wgi += 1
                P.dma("pool", lambda e, wb=wb, m=m: e.dma_start(out=wb[:, :, 0:128], in_=wgu_v[:, :, m * 128:(m + 1) * 128]),
                      writes=[wk])
                P.dma("pool", lambda e, wb=wb, m=m: e.dma_start(out=wb[:, :, 128:256],
                                                                 in_=wgu_v[:, :, FH + m * 128:FH + (m + 1) * 128]),
                      writes=[wk])
                for s in range(NS):
                    sl = slice(s * 512, (s + 1) * 512)
                    b = cnt % 2
                    cnt += 1
                    for c in range(DC):
                        P.pe(lambda e, wb=wb, c=c, sl=sl, b=b: e.matmul(pg[b][:], wb[:, c, 0:128], hT[:, c, sl],
                                                                        start=(c == 0), stop=(c == DC - 1)),
                             reads=[wk, ("h", c, s)], writes=["pg%d" % b])
                    for c in range(DC):
                        P.pe(lambda e, wb=wb, c=c, sl=sl, b=b: e.matmul(pu[b][:], wb[:, c, 128:256], hT[:, c, sl],
                                                                        start=(c == 0), stop=(c == DC - 1)),
                             reads=[wk, ("h", c, s)], writes=["pu%d" % b])
                    P.act(lambda e, b=b: e.activation(out=sg[b][:], in_=pg[b][:], func=AF.Silu),
                          reads=["pg%d" % b], writes=["sg%d" % b])
                    P.dve(lambda e, b=b, mi=mi, sl=sl: e.tensor_tensor(out=aT[:, mi, sl], in0=sg[b][:], in1=pu[b][:],
                                                                       op=ALU.mult),
                          reads=["sg%d" % b, "pu%d" % b], writes=[("a", mi, s)])
            for dp in range(DC // 2):
                db = wd[wdi % 2]
                dk = "wd%d" % (wdi % 2)
                wdi += 1
                P.dma("pool", lambda e, db=db, dp=dp, grp=grp: e.dma_start(
                    out=db[:], in_=wdn_v[:, grp * HG:(grp + 1) * HG, dp * 256:(dp + 1) * 256]), writes=[dk])
                for dd in range(2):
                    d = dp * 2 + dd
                    for s in range(NS):
                        sl = slice(s * 512, (s + 1) * 512)
                        b = cnt % 2
                        cnt += 1
                        for mi in range(HG):
                            P.pe(lambda e, db=db, mi=mi, dd=dd, sl=sl, b=b: e.matmul(
                                py[b][:], db[:, mi, dd * 128:(dd + 1) * 128], aT[:, mi, sl],
                                start=(mi == 0), stop=(mi == HG - 1)),
                                reads=[dk, ("a", mi, s)], writes=["py%d" % b])
                        P.dve(lambda e, d=d, sl=sl, b=b: e.tensor_tensor(out=xT[:, d, sl], in0=xT[:, d, sl], in1=py[b][:],
                                                                         op=ALU.add),
                              reads=["py%d" % b, ("x", d)], writes=[("x", d)])
        for c in range(DC):
            P.dma("sp",
                  lambda e, c=c, tsl=tsl: e.dma_start(out=xout_v[:, c, tsl], in_=xT[:, c, :]),
                  reads=[("x", c)], writes=[("xo", tt, c)])
            ph.out_keys.append(("xo", tt, c))
    return ph.finish()


def emit_rmsnorm_cols(ph, xT, xoff, hT, hoff, ncols, gcol, ones_bf, sq, pb, pk, rstd, xkeys, hkeys, tag):
    P = ph.P
    xsl_ = slice(xoff, xoff + ncols)
    hsl_ = slice(hoff, hoff + ncols)
    for c in range(DC):
        q = sq[c % 2]
        qk = "sq%d" % (c % 2)
        P.act(lambda e, q=q, c=c: e.activation(out=q[:, 0:ncols], in_=xT[:, c, xsl_], func=AF.Square),
              reads=[xkeys(c)], writes=[qk])
        P.pe(lambda e, q=q, c=c: e.matmul(pb[:, 0:ncols], ones_bf[:], q[:, 0:ncols], start=(c == 0), stop=(c == DC - 1)),
             reads=[qk, "ones"], writes=[pk])
    rk = ("rstd", tag)
    P.dve(lambda e: e.tensor_scalar(out=rstd[:, 0:ncols], in0=pb[:, 0:ncols], scalar1=1.0 / D, scalar2=EPS,
                                    op0=ALU.mult, op1=ALU.add), reads=[pk], writes=[rk])
    P.act(lambda e: e.activation(out=rstd[:, 0:ncols], in_=rstd[:, 0:ncols], func=AF.Sqrt), reads=[rk], writes=[rk])
    P.dve(lambda e: e.reciprocal(out=rstd[:, 0:ncols], in_=rstd[:, 0:ncols]), reads=[rk], writes=[rk])
    for c in range(DC):
        P.dve(lambda e, c=c: e.scalar_tensor_tensor(out=hT[:, c, hsl_], in0=xT[:, c, xsl_], scalar=gcol[:, c:c + 1],
                                                    in1=rstd[:, 0:ncols], op0=ALU.mult, op1=ALU.mult),
              reads=[xkeys(c), rk, "gcol"], writes=[hkeys(c)])


POOL_W = (2, 4, 8, 16)


def build_pool_phase(ph=None):
    ph = ph or Phase()
    P = ph.P
    fused = ph.master is not None
    xin = None if fused else ph.din("xTe", [D, 16 + TC])
    g_in = ph.din("g", [128, DC])
    wp_in = ph.din("wp", [4, 512, 512])
    sc_in = ph.din("psc", [128, DC])
    ic_in = ph.din("invc", [128, 4, 16])
    xout = ph.dout("xTo", [D, TC])
    NT = 512
    xT = ph.sb("xTs", [128, DC, NT], F32)
    xh = ph.sb("xh", [128, DC, 16], F32)
    hx = ph.sb("hx", [128, DC, 16 + NT], F32)
    sA = [ph.sb("sA%d" % i, [128, 16 + NT], F32) for i in range(2)]
    sB = [ph.sb("sB%d" % i, [128, 16 + NT], F32) for i in range(2)]
    yT = ph.sb("yT", [128, DC, NT], BF16)
    wp = ph.sb("wps", [128, 4, 4, 512], BF16)
    sq = [ph.sb("sq%d" % i, [128, 512], BF16) for i in range(2)]
    rstd = ph.sb("rstd", [128, 512], F32)
    gcol = ph.sb("gcol", [128, DC], F32)
    psc = ph.sb("pscs", [128, DC], F32)
    invc = ph.sb("invcs", [128, 4, 16], F32)
    ones = ph.sb("ones", [128, 128], BF16)
    pss = ph.ps("pss", [128, 512])
    pz = [ph.ps("pz%d" % i, [128, 512]) for i in range(2)]

    P.dma("sp", lambda e: e.dma_start(out=gcol[:], in_=g_in[:, :]), writes=["gcol"])
    P.dma("sp", lambda e: e.dma_start(out=psc[:], in_=sc_in[:, :]), writes=["psc"])
    P.dma("sp", lambda e: e.dma_start(out=invc[:], in_=ic_in[:, :, :]), writes=["invc"])
    P.dve(lambda e: e.memset(ones[:], 1.0), writes=["ones"])
    wp_v = wp_in.rearrange("g (ci p) n -> p g ci n", p=128)
    for g in range(4):
        P.dma("pool", lambda e, g=g: e.dma_start(out=wp[:, g, :, :], in_=wp_v[:, g, :, :]), writes=["wp"])
    if fused:
        xmain_v = ph.bind["xT"].rearrange("(c p) t -> p c t", p=128)
        halo_v = ph.bind["halo"].rearrange("(c p) t -> p c t", p=128)
        isb = ph.sb("isb", [128, 16], F32)
        P.dma("sp", lambda e: e.dma_start(out=isb[:], in_=ph.bind["isb"][:, :]), writes=["isb"])
        P.dma("sp", lambda e: e.dma_start(out=xh[:], in_=halo_v[:, :, :]), writes=["xh"])
        P.dve(lambda e: e.tensor_scalar(out=xh[:], in0=xh[:], scalar1=isb[:, 0:1], scalar2=None, op0=ALU.mult),
              reads=["xh", "isb"], writes=["xh"])
        OFF = 0
    else:
        xin_v = xin.rearrange("(c p) t -> p c t", p=128)
        xmain_v = xin_v
        OFF = 16
        P.dma("sp", lambda e: e.dma_start(out=xh[:], in_=xin_v[:, :, 0:16]), writes=["xh"])
    xout_v = xout.rearrange("(c p) t -> p c t", p=128)
    emit_rmsnorm_cols(ph, xh, 0, hx, 0, 16, gcol, ones, sq, pss, "pss", rstd,
                      lambda c: "xh", lambda c: ("hx", c), "halo")
    cnt = 0
    for tt in range(TC // NT):
        for c in range(DC):
            P.dma("sp", lambda e, c=c, tt=tt: e.dma_start(out=xT[:, c, :], in_=xmain_v[:, c, OFF + tt * NT:OFF + (tt + 1) * NT]),
                  writes=[("x", c)])
        emit_rmsnorm_cols(ph, xT, 0, hx, 16, NT, gcol, ones, sq, pss, "pss", rstd,
                          lambda c: ("x", c), lambda c: ("hx", c), ("t", tt))
        W = 16 + NT
        for c in range(DC):
            g = c // 4
            eng = P.dve if c % 2 == 0 else P.pool
            a = sA[c % 2]
            b = sB[c % 2]
            ak = "sA%d" % (c % 2)
            bk = "sB%d" % (c % 2)
            eng(lambda e, a=a, c=c: e.tensor_tensor(out=a[:, 1:W], in0=hx[:, c, 1:W], in1=hx[:, c, 0:W - 1], op=ALU.add),
                reads=[("hx", c)], writes=[ak])
            cur, curk, oth, othk = a, ak, b, bk
            sh = 2
            for step in range(g):
                eng(lambda e, cur=cur, oth=oth, sh=sh: e.tensor_tensor(out=oth[:, 1 + sh:W], in0=cur[:, 1 + sh:W],
                                                                      in1=cur[:, 1:W - sh], op=ALU.add),
                    reads=[curk], writes=[othk])
                cur, curk, oth, othk = oth, othk, cur, curk
                sh *= 2
            w = POOL_W[g]
            P.dve(lambda e, cur=cur, c=c, w=w: e.scalar_tensor_tensor(out=yT[:, c, :], in0=cur[:, 16:W], scalar=1.0 / w,
                                                                    in1=hx[:, c, 16:W], op0=ALU.mult, op1=ALU.subtract),
                reads=[curk, ("hx", c)], writes=[("y", c)])
            if tt == 0:
                eng(lambda e, cur=cur, g=g: e.tensor_tensor(out=cur[:, 16:32], in0=cur[:, 16:32], in1=invc[:, g, :], op=ALU.mult),
                    reads=[curk, "invc", ("y", c)], writes=[curk])
                eng(lambda e, cur=cur, c=c: e.tensor_tensor(out=yT[:, c, 0:16], in0=cur[:, 16:32], in1=hx[:, c, 16:32],
                                                            op=ALU.subtract),
                    reads=[curk, ("hx", c)], writes=[("y", c)])
            if tt + 1 < TC // NT:
                eng(lambda e, c=c: e.tensor_copy(out=hx[:, c, 0:16], in_=hx[:, c, NT:NT + 16]),
                    reads=[("y", c), curk, ak, bk], writes=[("hx", c)])
        for d in range(DC):
            g = d // 4
            b = cnt % 2
            cnt += 1
            for ci in range(4):
                P.pe(lambda e, g=g, ci=ci, d=d, b=b: e.matmul(pz[b][:], wp[:, g, ci, (d % 4) * 128:(d % 4 + 1) * 128],
                                                             yT[:, 4 * g + ci, :], start=(ci == 0), stop=(ci == 3)),
                     reads=["wp", ("y", 4 * g + ci)], writes=["pz%d" % b])
            P.dve(lambda e, d=d, b=b: e.scalar_tensor_tensor(out=xT[:, d, :], in0=pz[b][:], scalar=psc[:, d:d + 1],
                                                            in1=xT[:, d, :], op0=ALU.mult, op1=ALU.add),
                  reads=["pz%d" % b, "psc", ("x", d)], writes=[("x", d)])
        for c in range(DC):
            P.dma("sp", lambda e, c=c, tt=tt: e.dma_start(out=xout_v[:, c, tt * NT:(tt + 1) * NT], in_=xT[:, c, :]),
                  reads=[("x", c)], writes=[("xo", tt, c)])
            ph.out_keys.append(("xo", tt, c))
    return ph.finish()


def col16(v):
    return np.ascontiguousarray(np.asarray(v, np.float32).reshape(DC, 128).T)


def pool_inputs(x_seq, half, g, wp, psc):
    xe = np.zeros((D, 16 + TC), np.float32)
    t0 = half * TC
    xe[:, 16:] = x_seq[t0:t0 + TC].T
    if half == 1:
        xe[:, :16] = x_seq[t0 - 16:t0].T
    invc = np.zeros((128, 4, 16), np.float32)
    for gi, w in enumerate(POOL_W):
        for t in range(16):
            cnt = min(t + 1, w) if half == 0 else w
            invc[:, gi, t] = 1.0 / cnt
    return {"xTe": xe, "g": col16(g), "wp": np.ascontiguousarray(wp, dtype=np.float32), "psc": col16(psc), "invc": invc}


GLA_DKT = 1024
GLA_NCOL = 6160


def build_gla_phase(ph=None, state_only=False):
    ph = ph or Phase()
    P = ph.P
    fused = ph.master is not None
    xin = ph.din("xT", [D, TC])
    g_in = ph.din("g", [128, DC])
    win = ph.din("win", [D, GLA_NCOL])
    wa2b_in = ph.din("wa2b", [17, GLA_DKT])
    gn_in = ph.din("gnb", [128, 2048])
    wout = ph.din("wout", [D, D])
    st_in = ph.bind.get("st_in") if fused else ph.din("st_in", [128, 8, 512])
    tri_in = ph.din("tri", [128, 128])
    id_in = ph.din("ident", [128, 128])
    xout = None if state_only else ph.dout("xTo", [D, TC])
    st_out = ph.bind.get("st_out") if fused else ph.dout("st_out", [128, 8, 512])
    NT = 512
    NJ = 4
    xs = [ph.sb("xs%d" % i, [128, 512], F32) for i in range(3)]
    hT = ph.sb("hTs", [128, DC, NT], BF16)
    Wb = [ph.sb("Wb%d" % i, [128, DC, 512], BF16) for i in range(2)]
    Wa = ph.sb("Wa", [128, DC, 16], BF16)
    qT = ph.sb("qT", [128, 8, NT], BF16)
    kT = ph.sb("kT", [128, 8, NT], BF16)
    kdec = ph.sb("kdec", [128, NJ, 1024], BF16)
    kd_s = [ph.sb("kds%d" % i, [128, 512], BF16) for i in range(2)]
    vt = ph.sb("vt", [128, NJ, 2048], BF16)
    sr = ph.sb("sr", [128, NJ, 2048], BF16)
    gated2 = [ph.sb("gated%d" % i, [128, 2048], BF16) for i in range(2)]
    gT = ph.sb("gT", [128, DC, NT], BF16)
    S = ph.sb("S", [128, 8, 512], F32)
    Sb = ph.sb("Sb", [128, 8, 512], BF16)
    lt = ph.sb("lt", [128, NJ, 1024], F32)
    e1 = ph.sb("e1", [128, 1024], F32)
    Eq = [ph.sb("Eq%d" % i, [128, 512], F32) for i in range(1)]
    Ek = [ph.sb("Ek%d" % i, [128, 512], F32) for i in range(1)]
    Elast = ph.sb("Elast", [128, 8, NJ], F32)
    alr1 = ph.sb("alr1", [32, NT], F32)
    wa2b = ph.sb("wa2bs", [32, GLA_DKT], F32)
    gnb = ph.sb("gnbs", [128, 2048], BF16)
    tri = ph.sb("tris", [128, 128], F32)
    ident = ph.sb("idents", [128, 128], BF16)
    AT4 = [ph.sb("AT4%d" % i, [128, 128], BF16) for i in range(4)]
    osq = ph.sb("osq", [128, 2048], BF16)
    ssq = ph.sb("ssq", [128, 4], F32)
    sq = [ph.sb("sq%d" % i, [128, 512], BF16) for i in range(2)]
    rstd = ph.sb("rstd", [128, 512], F32)
    gcol = ph.sb("gcol", [128, DC], F32)
    ones = ph.sb("ones", [128, 128], BF16)
    pb = [ph.ps("pb%d" % i, [128, 512]) for i in range(8)]
    pbk = ["pb%d" % i for i in range(8)]

    P.dma("sp", lambda e: e.dma_start(out=gcol[:], in_=g_in[:, :]), writes=["gcol"])
    P.dma("sp", lambda e: e.dma_start(out=tri[:], in_=tri_in[:, :]), writes=["tri"])
    P.dma("sp", lambda e: e.dma_start(out=wa2b[0:17, :], in_=wa2b_in[:, :]), writes=["wa2b"])
    if st_in is None:
        P.dve(lambda e: e.memset(S[:], 0.0), writes=[("S", dc) for dc in range(8)])
    else:
        P.dma("sp", lambda e: e.dma_start(out=S[:], in_=st_in[:, :, :]), writes=[("S", dc) for dc in range(8)])
        if fused:
            isb = ph.sb("isb", [128, 16], F32)
            P.dma("sp", lambda e: e.dma_start(out=isb[:], in_=ph.bind["isb"][:, :]), writes=["isb"])
            P.dve(lambda e: e.tensor_scalar(out=S[:], in0=S[:], scalar1=isb[:, 0:1], scalar2=None, op0=ALU.mult),
                  reads=[("S", dc) for dc in range(8)] + ["isb"], writes=[("S", dc) for dc in range(8)])
    P.dma("pool", lambda e: e.dma_start(out=ident[:], in_=id_in[:, :]), writes=["ident"])
    P.dma("pool", lambda e: e.dma_start(out=gnb[:], in_=gn_in[:, :]), writes=["gnb"])
    P.dve(lambda e: e.memset(ones[:], 1.0), writes=["ones"])
    P.dve(lambda e: e.memset(alr1[:], 1.0), writes=["alr1"])
    P.act(lambda e: e.copy(out=Sb[:], in_=S[:]), reads=[("S", dc) for dc in range(8)], writes=[("Sb", dc) for dc in range(8)])
    win_v = win.rearrange("(c p) n -> p c n", p=128)
    wout_v = wout.rearrange("(c p) n -> p c n", p=128)
    xin_v = xin.rearrange("(c p) t -> p c t", p=128)
    xout_v = None if state_only else xout.rearrange("(c p) t -> p c t", p=128)
    P.dma("pool", lambda e: e.dma_start(out=Wa[:], in_=win_v[:, :, 6144:6160]), writes=["Wa"])

    wctr = [0]

    def load_w(src_v, col0):
        i = wctr[0] % 2
        wctr[0] += 1
        P.dma("pool", lambda e, i=i: e.dma_start(out=Wb[i][:], in_=src_v[:, :, col0:col0 + 512]), writes=["Wb%d" % i])
        return Wb[i], "Wb%d" % i

    pctr = [0]

    def next_pb(lo=4, n=4):
        i = lo + pctr[0] % n
        pctr[0] += 1
        return pb[i], pbk[i]

    xctr = [0]
    for tt in range(TC // NT):
        tsl = slice(tt * NT, (tt + 1) * NT)
        p_ss, p_ssk = pb[0], pbk[0]
        for c in range(DC):
            i = xctr[0] % 3
            xctr[0] += 1
            P.dma("sp", lambda e, i=i, c=c, tsl=tsl: e.dma_start(out=xs[i][:], in_=xin_v[:, c, tsl]), writes=["xs%d" % i])
            q = sq[c % 2]
            qk = "sq%d" % (c % 2)
            P.act(lambda e, q=q, i=i: e.activation(out=q[:], in_=xs[i][:], func=AF.Square), reads=["xs%d" % i], writes=[qk])
            P.pe(lambda e, q=q, c=c: e.matmul(p_ss[:], ones[:], q[:], start=(c == 0), stop=(c == DC - 1)),
                 reads=[qk, "ones"], writes=[p_ssk])
        P.dve(lambda e: e.tensor_scalar(out=rstd[:], in0=p_ss[:], scalar1=1.0 / D, scalar2=EPS, op0=ALU.mult, op1=ALU.add),
              reads=[p_ssk], writes=["rstd"])
        P.act(lambda e: e.activation(out=rstd[:], in_=rstd[:], func=AF.Sqrt), reads=["rstd"], writes=["rstd"])
        P.dve(lambda e: e.reciprocal(out=rstd[:], in_=rstd[:]), reads=["rstd"], writes=["rstd"])
        for c in range(DC):
            i = xctr[0] % 3
            xctr[0] += 1
            P.dma("sp", lambda e, i=i, c=c, tsl=tsl: e.dma_start(out=xs[i][:], in_=xin_v[:, c, tsl]), writes=["xs%d" % i])
            P.dve(lambda e, i=i, c=c: e.scalar_tensor_tensor(out=hT[:, c, :], in0=xs[i][:], scalar=gcol[:, c:c + 1],
                                                             in1=rstd[:], op0=ALU.mult, op1=ALU.mult),
                  reads=["xs%d" % i, "rstd", "gcol"], writes=[("h", c)])
        hk = [("h", c) for c in range(DC)]
        pa, pak = pb[1], pbk[1]
        for c in range(DC):
            P.pe(lambda e, c=c: e.matmul(pa[0:16, :], Wa[:, c, :], hT[:, c, :], start=(c == 0), stop=(c == DC - 1)),
                 reads=["Wa", ("h", c)], writes=[pak])
        P.act(lambda e: e.copy(out=alr1[0:16, :], in_=pa[0:16, :]), reads=[pak], writes=["alr1"])
        for j in range(NJ):
            jsl = slice(j * 128, (j + 1) * 128)
            for hf in range(2):
                P.pe(lambda e, jsl=jsl, hf=hf: e.matmul(pb[2 + hf][:], alr1[0:17, jsl], wa2b[0:17, hf * 512:(hf + 1) * 512],
                                                        start=True, stop=True),
                     reads=["alr1", "wa2b"], writes=[pbk[2 + hf]])
                P.act(lambda e, hf=hf: e.activation(out=e1[:, hf * 512:(hf + 1) * 512], in_=pb[2 + hf][:], func=AF.Exp, scale=-1.0),
                      reads=[pbk[2 + hf]], writes=[("e1", hf)])
                P.act(lambda e, hf=hf, j=j: e.activation(out=lt[:, j, hf * 512:(hf + 1) * 512], in_=e1[:, hf * 512:(hf + 1) * 512],
                                                         func=AF.Ln, bias=1.0),
                      reads=[("e1", hf)], writes=[("lt", j)])
        pend_tr = []
        for blk in range(2):
            if not state_only:
                Wq, Wqk = load_w(win_v, blk * 512)
            Wk, Wkk = load_w(win_v, 1024 + blk * 512)
            for dl in range(4):
                dc = blk * 4 + dl
                pbt, pbtk = pb[0], pbk[0]
                for j in range(NJ):
                    jsl = slice(j * 128, (j + 1) * 128)
                    P.pe(lambda e, j=j, jsl=jsl, dc=dc: e.matmul(pbt[:, jsl], lt[:, j, dc * 128:(dc + 1) * 128], tri[:],
                                                                 start=True, stop=True),
                         reads=[("lt", j), "tri"], writes=[pbtk])
                ei = 0
                P.act(lambda e, ei=ei: e.activation(out=Eq[ei][:], in_=pbt[:], func=AF.Exp, scale=-1.0 / 16.0),
                      reads=[pbtk], writes=["Eq%d" % ei])
                P.act(lambda e, ei=ei: e.activation(out=Ek[ei][:], in_=pbt[:], func=AF.Exp, scale=1.0 / 16.0),
                      reads=[pbtk], writes=["Ek%d" % ei])
                P.dve(lambda e, ei=ei, dc=dc: e.tensor_copy(out=Elast[:, dc, :], in_=Eq[ei][:, 127::128]),
                      reads=["Eq%d" % ei], writes=[("Elast", dc)])
                if not state_only:
                    pq, pqk = next_pb()
                    for c in range(DC):
                        P.pe(lambda e, c=c, dl=dl, pq=pq, Wq=Wq: e.matmul(pq[:], Wq[:, c, dl * 128:(dl + 1) * 128], hT[:, c, :],
                                                                   start=(c == 0), stop=(c == DC - 1)),
                             reads=[Wqk, ("h", c)], writes=[pqk])
                    P.dve(lambda e, pq=pq, ei=ei, dc=dc: e.scalar_tensor_tensor(out=qT[:, dc, :], in0=pq[:], scalar=1.0 / 16.0,
                                                                                in1=Eq[ei][:], op0=ALU.mult, op1=ALU.mult),
                          reads=[pqk, "Eq%d" % ei], writes=[("qT", dc)])
                pk_, pkk = next_pb()
                for c in range(DC):
                    P.pe(lambda e, c=c, dl=dl, pk_=pk_, Wk=Wk: e.matmul(pk_[:], Wk[:, c, dl * 128:(dl + 1) * 128], hT[:, c, :],
                                                                 start=(c == 0), stop=(c == DC - 1)),
                         reads=[Wkk, ("h", c)], writes=[pkk])
                P.dve(lambda e, pk_=pk_, ei=ei, dc=dc: e.tensor_tensor(out=kT[:, dc, :], in0=pk_[:], in1=Ek[ei][:], op=ALU.mult),
                      reads=[pkk, "Ek%d" % ei], writes=[("kT", dc)])
                kd = kd_s[dc % 2]
                kdk = "kds%d" % (dc % 2)
                for j in range(NJ):
                    jsl = slice(j * 128, (j + 1) * 128)
                    P.dve(lambda e, kd=kd, dc=dc, j=j, jsl=jsl: e.tensor_scalar(out=kd[:, jsl], in0=kT[:, dc, jsl],
                                                                                scalar1=Elast[:, dc, j:j + 1], scalar2=None,
                                                                                op0=ALU.mult),
                          reads=[("kT", dc), ("Elast", dc)], writes=[kdk])
                def emit_tr(kd=kd, kdk=kdk, dc=dc):
                    ptr, ptrk = next_pb()
                    for j2 in range(NJ):
                        jsl2 = slice(j2 * 128, (j2 + 1) * 128)
                        P.pe(lambda e, kd=kd, jsl2=jsl2, ptr=ptr: e.transpose(pbf(ptr)[:, jsl2], kd[:, jsl2], ident[:]),
                             reads=[kdk, "ident"], writes=[ptrk])
                    P.act(lambda e, ptr=ptr, dc=dc: e.copy(out=kdec[:, :, dc * 128:(dc + 1) * 128],
                                                           in_=pbf(ptr)[:, 0:512].rearrange("p (j d) -> p j d", j=NJ)),
                          reads=[ptrk], writes=[("kdec", dc)])
                pend_tr.append(emit_tr)
                if len(pend_tr) > 1:
                    pend_tr.pop(0)()
        while pend_tr:
            pend_tr.pop(0)()
        for vb in range(4):
            Wv, Wvk = load_w(win_v, 2048 + vb * 512)
            for j in range(NJ):
                jsl = slice(j * 128, (j + 1) * 128)
                pv, pvk = next_pb()
                for c in range(DC):
                    P.pe(lambda e, c=c, jsl=jsl, pv=pv, Wv=Wv: e.matmul(pv[:], hT[:, c, jsl], Wv[:, c, :],
                                                                        start=(c == 0), stop=(c == DC - 1)),
                         reads=[Wvk, ("h", c)], writes=[pvk])
                P.act(lambda e, pv=pv, j=j, vb=vb: e.copy(out=vt[:, j, vb * 512:(vb + 1) * 512], in_=pv[:]),
                      reads=[pvk], writes=[("vt", j, vb)])
        for rb in (range(4) if not state_only else ()):
            Wr, Wrk = load_w(win_v, 4096 + rb * 512)
            for j in range(NJ):
                jsl = slice(j * 128, (j + 1) * 128)
                pv, pvk = next_pb()
                for c in range(DC):
                    P.pe(lambda e, c=c, jsl=jsl, pv=pv, Wr=Wr: e.matmul(pv[:], hT[:, c, jsl], Wr[:, c, :],
                                                                        start=(c == 0), stop=(c == DC - 1)),
                         reads=[Wrk, ("h", c)], writes=[pvk])
                P.act(lambda e, pv=pv, j=j, rb=rb: e.activation(out=sr[:, j, rb * 512:(rb + 1) * 512], in_=pv[:], func=AF.Silu),
                      reads=[pvk], writes=[("sr", j)])
        pend_g = []
        for j in range(NJ):
            jsl = slice(j * 128, (j + 1) * 128)
            gt_ = gated2[j % 2]
            gk_ = "gated%d" % (j % 2)
            if not state_only:
                P.pool(lambda e, j=j: e.tensor_tensor(out=sr[:, j, :], in0=sr[:, j, :], in1=gnb[:], op=ALU.mult),
                       reads=[("sr", j), "gnb"], writes=[("sr", j)])
                pst, pstk = pb[4], pbk[4]
                for h in range(4):
                    hs = slice(h * 128, (h + 1) * 128)
                    for di in range(2):
                        dc = 2 * h + di
                        P.pe(lambda e, dc=dc, di=di, hs=hs, jsl=jsl: e.matmul(pst[:, hs], kT[:, dc, jsl], qT[:, dc, jsl],
                                                                              start=(di == 0), stop=(di == 1)),
                             reads=[("kT", dc), ("qT", dc)], writes=[pstk])
                for h in range(4):
                    hs = slice(h * 128, (h + 1) * 128)
                    P.dve(lambda e, h=h, hs=hs: e.tensor_tensor(out=AT4[h][:], in0=pst[:, hs], in1=tri[:], op=ALU.mult),
                          reads=[pstk, "tri"], writes=["AT4%d" % h])
            for h in range(4):
                vkeys = [("vt", j, h)]
                for di in range(2):
                    dc = 2 * h + di
                    pkv, pkvk = pb[5 + di], pbk[5 + di]
                    P.pe(lambda e, dc=dc, j=j, h=h, pkv=pkv: e.matmul(pkv[:], kdec[:, j, dc * 128:(dc + 1) * 128],
                                                                      vt[:, j, h * 512:(h + 1) * 512], start=True, stop=True),
                         reads=[("kdec", dc)] + vkeys, writes=[pkvk])
                    P.dve(lambda e, dc=dc, j=j, pkv=pkv: e.scalar_tensor_tensor(out=S[:, dc, :], in0=S[:, dc, :],
                                                                                scalar=Elast[:, dc, j:j + 1], in1=pkv[:],
                                                                                op0=ALU.mult, op1=ALU.add),
                          reads=[pkvk, ("Elast", dc), ("S", dc)], writes=[("S", dc)])
                if not state_only:
                    for di in range(2):
                        dc = 2 * h + di
                        P.pe(lambda e, dc=dc, di=di, h=h, jsl=jsl: e.matmul(pb[h][:], qT[:, dc, jsl], Sb[:, dc, :],
                                                                            start=(di == 0), stop=False),
                             reads=[("qT", dc), ("Sb", dc)], writes=[pbk[h]])
                    P.pe(lambda e, h=h, j=j: e.matmul(pb[h][:], AT4[h][:], vt[:, j, h * 512:(h + 1) * 512], start=False, stop=True),
                         reads=["AT4%d" % h] + vkeys, writes=[pbk[h]])
                    for di in range(2):
                        dc = 2 * h + di
                        P.act(lambda e, dc=dc: e.copy(out=Sb[:, dc, :], in_=S[:, dc, :]), reads=[("S", dc)], writes=[("Sb", dc)])
            if state_only:
                continue
            for h in range(4):
                P.act(lambda e, h=h: e.activation(out=osq[:, h * 512:(h + 1) * 512], in_=pb[h][:], func=AF.Square),
                      reads=[pbk[h]], writes=[("osq", h)])
            P.dve(lambda e: e.reduce_sum(out=ssq[:], in_=osq[:].rearrange("p (h v) -> p h v", h=4), axis=AX.X),
                  reads=[("osq", h) for h in range(4)], writes=["ssq"])
            P.dve(lambda e: e.tensor_scalar(out=ssq[:], in0=ssq[:], scalar1=1.0 / 512.0, scalar2=EPS, op0=ALU.mult, op1=ALU.add),
                  reads=["ssq"], writes=["ssq"])
            P.act(lambda e: e.activation(out=ssq[:], in_=ssq[:], func=AF.Sqrt), reads=["ssq"], writes=["ssq"])
            P.dve(lambda e: e.reciprocal(out=ssq[:], in_=ssq[:]), reads=["ssq"], writes=["ssq"])
            for h in range(4):
                P.dve(lambda e, h=h, j=j, gt_=gt_: e.scalar_tensor_tensor(out=gt_[:, h * 512:(h + 1) * 512], in0=pb[h][:],
                                                                         scalar=ssq[:, h:h + 1], in1=sr[:, j, h * 512:(h + 1) * 512],
                                                                         op0=ALU.mult, op1=ALU.mult),
                      reads=[pbk[h], "ssq", ("sr", j)], writes=[(gk_, h)])

            def emit_gtr(gt_=gt_, gk_=gk_, j=j, jsl=jsl):
                for fb in range(4):
                    ptr, ptrk = pb[7], pbk[7]
                    for fi in range(4):
                        fc = fb * 4 + fi
                        P.pe(lambda e, fc=fc, fi=fi, ptr=ptr, gt_=gt_: e.transpose(pbf(ptr)[:, fi * 128:(fi + 1) * 128],
                                                                                  gt_[:, fc * 128:(fc + 1) * 128], ident[:]),
                             reads=[(gk_, fb), "ident"], writes=[ptrk])
                    P.act(lambda e, fb=fb, jsl=jsl, ptr=ptr: e.copy(out=gT[:, fb * 4:(fb + 1) * 4, jsl],
                                                                   in_=pbf(ptr)[:, 0:512].rearrange("p (f t) -> p f t", f=4)),
                          reads=[ptrk], writes=[("gT", fb, j)])
            pend_g.append(emit_gtr)
            if len(pend_g) > 1:
                pend_g.pop(0)()
        while pend_g:
            pend_g.pop(0)()
        for ob in (range(4) if not state_only else ()):
            Wo, Wok = load_w(wout_v, ob * 512)
            for dd in range(4):
                d = ob * 4 + dd
                py_, pyk = next_pb()
                for fc in range(DC):
                    P.pe(lambda e, fc=fc, dd=dd, py_=py_, Wo=Wo: e.matmul(py_[:], Wo[:, fc, dd * 128:(dd + 1) * 128], gT[:, fc, :],
                                                                          start=(fc == 0), stop=(fc == DC - 1)),
                         reads=[Wok] + [("gT", fc // 4, j) for j in range(NJ)], writes=[pyk])
                i = xctr[0] % 3
                xctr[0] += 1
                P.dma("sp", lambda e, i=i, d=d, tsl=tsl: e.dma_start(out=xs[i][:], in_=xin_v[:, d, tsl]), writes=["xs%d" % i])
                P.dve(lambda e, i=i, py_=py_: e.tensor_tensor(out=xs[i][:], in0=xs[i][:], in1=py_[:], op=ALU.add),
                      reads=[pyk, "xs%d" % i], writes=["xs%d" % i])
                P.dma("sp", lambda e, i=i, d=d, tsl=tsl: e.dma_start(out=xout_v[:, d, tsl], in_=xs[i][:]),
                      reads=["xs%d" % i], writes=[("xo", tt, d)])
                ph.out_keys.append(("xo", tt, d))
    if st_out is not None:
        P.dma("sp", lambda e: e.dma_start(out=st_out[:, :, :], in_=S[:]), reads=[("S", dc) for dc in range(8)],
              writes=["st_out"])
        ph.out_keys.append("st_out")
    return ph.finish()


def pbf(ptile):
    return ptile[:].bitcast(BF16) if hasattr(ptile[:], "bitcast") else ptile


def gla_consts():
    tri = np.triu(np.ones((128, 128), np.float32))
    ident = np.eye(128, dtype=np.float32)
    return tri, ident


def gla_inputs(xT_core, g, w_in, w_a2, b_a, g_norm, w_out, state):
    tri, ident = gla_consts()
    st = np.ascontiguousarray(np.asarray(state, np.float32).reshape(4, 2, 128, 512).transpose(2, 0, 1, 3).reshape(128, 8, 512))
    gnb = np.ascontiguousarray(np.broadcast_to(np.tile(np.asarray(g_norm, np.float32), 4)[None, :], (128, 2048)))
    wa2b = np.ascontiguousarray(np.concatenate([np.asarray(w_a2, np.float32), np.asarray(b_a, np.float32)[None, :]], axis=0))
    return {"xT": np.ascontiguousarray(xT_core, dtype=np.float32), "g": col16(g), "win": np.ascontiguousarray(w_in, dtype=np.float32),
            "wa2b": wa2b, "gnb": gnb, "wout": np.ascontiguousarray(w_out, dtype=np.float32), "st_in": st, "tri": tri, "ident": ident}


def gla_state_from_out(st_out):
    return np.ascontiguousarray(st_out.reshape(128, 4, 2, 512).transpose(1, 2, 0, 3).reshape(4, 256, 512))


def build_diff1_phase(do_qk=True, do_v=True, do_norm=True, ph=None):
    ph = ph or Phase()
    P = ph.P
    xin = ph.din("xT", [D, TC])
    g_in = ph.din("g", [128, DC])
    win = ph.din("win", [D, 3 * D])
    qkg_in = ph.din("qkg", [128, 16])
    qko = ph.dout("qkT", [2 * D, TC])
    vo = ph.dout("v", [TC, D])
    NT = 512
    xs = [ph.sb("xs%d" % i, [128, 512], F32) for i in range(3)]
    hT = ph.sb("hTs", [128, DC, NT], BF16)
    Wb = [ph.sb("Wb%d" % i, [128, DC, 512], BF16) for i in range(2)]
    qraw = [ph.sb("qraw%d" % i, [128, 512], F32) for i in range(2)]
    qn = [ph.sb("qn%d" % i, [128, 512], F32) for i in range(3)]
    vs = [ph.sb("vs%d" % i, [128, 512], F32) for i in range(3)]
    sq = [ph.sb("sq%d" % i, [128, 512], BF16) for i in range(2)]
    rstd = ph.sb("rstd", [128, 512], F32)
    rs2 = [ph.sb("rs2%d" % i, [128, 512], F32) for i in range(2)]
    gcol = ph.sb("gcol", [128, DC], F32)
    qkg = ph.sb("qkgs", [128, 16], F32)
    ones = ph.sb("ones", [128, 128], BF16)
    pb = [ph.ps("pb%d" % i, [128, 512]) for i in range(8)]
    pbk = ["pb%d" % i for i in range(8)]
    P.dma("sp", lambda e: e.dma_start(out=gcol[:], in_=g_in[:, :]), writes=["gcol"])
    P.dma("sp", lambda e: e.dma_start(out=qkg[:], in_=qkg_in[:, :]), writes=["qg", "kg"])
    P.dve(lambda e: e.memset(ones[:], 1.0), writes=["ones"])
    win_v = win.rearrange("(c p) n -> p c n", p=128)
    xin_v = xin.rearrange("(c p) t -> p c t", p=128)
    k_ds = ph.bind.get("k_ds")
    v_ds = ph.bind.get("v_ds")
    if k_ds is not None:
        q_v = ph.bind["q_d"].rearrange("(c p) t -> p c t", p=128)
        k_vs = [kd_.rearrange("(c p) t -> p c t", p=128) for kd_ in k_ds]
        v_vs = [vd_.rearrange("(j p) n -> p j n", p=128) for vd_ in v_ds]
    else:
        qko_v = qko.rearrange("(c p) t -> p c t", p=128)
    vo_v = vo.rearrange("(j p) n -> p j n", p=128)
    xctr = [0]
    wctr = [0]
    pctr = [0]
    nctr = [0]

    def load_w(col0):
        i = wctr[0] % 2
        wctr[0] += 1
        P.dma("pool", lambda e, i=i, col0=col0: e.dma_start(out=Wb[i][:], in_=win_v[:, :, col0:col0 + 512]), writes=["Wb%d" % i])
        return Wb[i], "Wb%d" % i

    def next_pb():
        i = 2 + pctr[0] % 4
        pctr[0] += 1
        return pb[i], pbk[i]

    for tt in range(TC // NT):
        tsl = slice(tt * NT, (tt + 1) * NT)
        p_ss, p_ssk = pb[0], pbk[0]
        for c in range(DC):
            i = xctr[0] % 3
            xctr[0] += 1
            P.dma("sp", lambda e, i=i, c=c, tsl=tsl: e.dma_start(out=xs[i][:], in_=xin_v[:, c, tsl]), writes=["xs%d" % i])
            q = sq[c % 2]
            qk = "sq%d" % (c % 2)
            P.act(lambda e, q=q, i=i: e.activation(out=q[:], in_=xs[i][:], func=AF.Square), reads=["xs%d" % i], writes=[qk])
            P.pe(lambda e, q=q, c=c: e.matmul(p_ss[:], ones[:], q[:], start=(c == 0), stop=(c == DC - 1)),
                 reads=[qk, "ones"], writes=[p_ssk])
        P.dve(lambda e: e.tensor_scalar(out=rstd[:], in0=p_ss[:], scalar1=1.0 / D, scalar2=EPS, op0=ALU.mult, op1=ALU.add),
              reads=[p_ssk], writes=["rstd"])
        P.act(lambda e: e.activation(out=rstd[:], in_=rstd[:], func=AF.Sqrt), reads=["rstd"], writes=["rstd"])
        P.dve(lambda e: e.reciprocal(out=rstd[:], in_=rstd[:]), reads=["rstd"], writes=["rstd"])
        for c in range(DC):
            i = xctr[0] % 3
            xctr[0] += 1
            P.dma("sp", lambda e, i=i, c=c, tsl=tsl: e.dma_start(out=xs[i][:], in_=xin_v[:, c, tsl]), writes=["xs%d" % i])
            P.dve(lambda e, i=i, c=c: e.scalar_tensor_tensor(out=hT[:, c, :], in0=xs[i][:], scalar=gcol[:, c:c + 1],
                                                             in1=rstd[:], op0=ALU.mult, op1=ALU.mult),
                  reads=["xs%d" % i, "rstd", "gcol"], writes=[("h", c)])
        for which in (range(2) if do_qk else ()):
            gk = "qg" if which == 0 else "kg"
            for blk in range(4):
                Wq, Wqk = load_w(which * D + blk * 512)
                for dl in range(4):
                    hd = blk * 4 + dl
                    pq, pqk = next_pb()
                    for c in range(DC):
                        P.pe(lambda e, c=c, dl=dl, pq=pq, Wq=Wq: e.matmul(pq[:], Wq[:, c, dl * 128:(dl + 1) * 128], hT[:, c, :],
                                                                          start=(c == 0), stop=(c == DC - 1)),
                             reads=[Wqk, ("h", c)], writes=[pqk])
                    if do_norm:
                        qr = qraw[hd % 2]
                        qrk = "qraw%d" % (hd % 2)
                        sqb = sq[hd % 2]
                        sqk = "sq%d" % (hd % 2)
                        P.act(lambda e, pq=pq, sqb=sqb: e.activation(out=sqb[:], in_=pq[:], func=AF.Square), reads=[pqk], writes=[sqk])
                        P.act(lambda e, pq=pq, qr=qr: e.copy(out=qr[:], in_=pq[:]), reads=[pqk], writes=[qrk])
                        p2, p2k = pb[6 + hd % 2], pbk[6 + hd % 2]
                        P.pe(lambda e, p2=p2, sqb=sqb: e.matmul(p2[:], ones[:], sqb[:], start=True, stop=True),
                             reads=[sqk, "ones"], writes=[p2k])
                        r2 = rs2[hd % 2]
                        r2k = "rs2%d" % (hd % 2)
                        P.dve(lambda e, p2=p2, r2=r2: e.tensor_scalar(out=r2[:], in0=p2[:], scalar1=1.0 / 128.0, scalar2=EPS,
                                                                      op0=ALU.mult, op1=ALU.add), reads=[p2k], writes=[r2k])
                        P.act(lambda e, r2=r2: e.activation(out=r2[:], in_=r2[:], func=AF.Sqrt), reads=[r2k], writes=[r2k])
                        P.dve(lambda e, r2=r2: e.reciprocal(out=r2[:], in_=r2[:]), reads=[r2k], writes=[r2k])
                        ni = nctr[0] % 3
                        nctr[0] += 1
                        P.dve(lambda e, ni=ni, qr=qr, r2=r2, which=which: e.scalar_tensor_tensor(out=qn[ni][:], in0=qr[:], scalar=qkg[:, which:which + 1],
                                                                                              in1=r2[:], op0=ALU.mult, op1=ALU.mult),
                              reads=[qrk, r2k, gk], writes=["qn%d" % ni])
                    else:
                        ni = nctr[0] % 3
                        nctr[0] += 1
                        P.act(lambda e, pq=pq, ni=ni: e.copy(out=qn[ni][:], in_=pq[:]), reads=[pqk], writes=["qn%d" % ni])
                    if k_ds is not None:
                        dst_ap = q_v[:, hd, tsl] if which == 0 else k_vs[hd // 2][:, hd % 2, tsl]
                    else:
                        dst_ap = qko_v[:, which * 16 + hd, tsl]
                    P.dma("sp", lambda e, ni=ni, dst_ap=dst_ap: e.dma_start(out=dst_ap, in_=qn[ni][:]),
                          reads=["qn%d" % ni], writes=[("qo", which, tt, hd)])
                    ph.out_keys.append(("qo", which, tt, hd))
        for vb in (range(4) if do_v else ()):
            Wv, Wvk = load_w(2 * D + vb * 512)
            for j in range(4):
                jsl = slice(j * 128, (j + 1) * 128)
                pv, pvk = next_pb()
                for c in range(DC):
                    P.pe(lambda e, c=c, jsl=jsl, pv=pv, Wv=Wv: e.matmul(pv[:], hT[:, c, jsl], Wv[:, c, :],
                                                                        start=(c == 0), stop=(c == DC - 1)),
                         reads=[Wvk, ("h", c)], writes=[pvk])
                vi = nctr[0] % 3
                nctr[0] += 1
                P.act(lambda e, pv=pv, vi=vi: e.copy(out=vs[vi][:], in_=pv[:]), reads=[pvk], writes=["vs%d" % vi])
                if v_ds is not None:
                    for hh in range(2):
                        P.dma("sp", lambda e, vi=vi, tt=tt, j=j, vb=vb, hh=hh: e.dma_start(
                            out=v_vs[2 * vb + hh][:, tt * 4 + j, :], in_=vs[vi][:, hh * 256:(hh + 1) * 256]),
                            reads=["vs%d" % vi], writes=[("vo", tt, j, vb, hh)])
                else:
                    P.dma("sp", lambda e, vi=vi, tt=tt, j=j, vb=vb: e.dma_start(out=vo_v[:, tt * 4 + j, vb * 512:(vb + 1) * 512], in_=vs[vi][:]),
                          reads=["vs%d" % vi], writes=[("vo", tt, j, vb)])
                    ph.out_keys.append(("vo", tt, j, vb))
    return ph.finish()


LAM_INIT2 = 0.8 - 0.6 * math.exp(-0.3 * 2)
NEG = -30000.0


def build_diff2_phase(ph=None):
    ph = ph or Phase()
    P = ph.P
    fused = ph.master is not None
    qin = ph.din("qT", [D, TC])
    kin = None if fused else ph.din("kT", [D, 2 * TC])
    vin = None if fused else ph.din("va", [2 * TC, 8, 257])
    xin = ph.din("xT", [D, TC])
    wout = ph.din("wout", [D, D])
    b0_in = ph.din("B0", [128, 16, 128])
    b1_in = ph.din("B1", [128, 16, 128])
    b1p_in = ph.din("B1p", [128, 16, 128])
    cf_in = ph.din("cfar", [128, 16])
    cp_in = ph.din("cpre", [128, 16])
    lp_in = ph.din("lpb", [128, 4, 128])
    sg_in = ph.din("sgb", [128, 256])
    id_in = ph.din("ident", [128, 128])
    xout = ph.dout("xTo", [D, TC])
    SCALE = 128 ** -0.5
    Kt = ph.sb("Kt", [128, 2, 2 * TC], BF16)
    Va = ph.sb("Va", [128, 32, 257], BF16)
    Qt = ph.sb("Qt", [128, 2, TC], BF16)
    ao = ph.sb("ao", [128, 16, 2048], BF16)
    M0 = ph.sb("M0", [128, 16, 128], BF16)
    M1 = ph.sb("M1", [128, 16, 128], BF16)
    M1p = ph.sb("M1p", [128, 16, 128], BF16)
    btmp = ph.sb("btmp", [128, 16, 128], F32)
    cfar = ph.sb("cfars", [128, 16], F32)
    cpre = ph.sb("cpres", [128, 16], F32)
    negc = ph.sb("negc", [128, 16], F32)
    negcp = ph.sb("negcp", [128, 16], F32)
    lpb = ph.sb("lpbs", [128, 4, 128], F32)
    lt1 = ph.sb("lt1", [128, 128], F32)
    lsum = ph.sb("lsum", [128, 2], F32)
    neglam = ph.sb("neglam", [128, 1], F32)
    sgs = ph.sb("sgs", [128, 256], F32)
    ident = ph.sb("idents", [128, 128], BF16)
    PT = [ph.sb("PT%d" % i, [128, 2, 256], BF16) for i in range(3)]
    rc = ph.sb("rc", [128, 4], F32)
    uu = ph.sb("uu", [128, 256], F32)
    att = ph.sb("att", [128, 256], F32)
    asq = ph.sb("asq", [128, 256], F32)
    ssn = ph.sb("ssn", [128, 1], F32)
    aoT = ph.sb("aoT", [128, DC, 512], BF16)
    Wb = [ph.sb("Wb%d" % i, [128, DC, 512], BF16) for i in range(2)]
    xs = [ph.sb("xs%d" % i, [128, 512], F32) for i in range(3)]
    pb = [ph.ps("pb%d" % i, [128, 512]) for i in range(8)]
    pbk = ["pb%d" % i for i in range(8)]

    for (dst, src, k) in ((cfar, cf_in, "cfar"), (cpre, cp_in, "cpre"), (sgs, sg_in, "sgs")):
        P.dma("sp", lambda e, dst=dst, src=src: e.dma_start(out=dst[:], in_=src[:, :]), writes=[k])
    P.dma("sp", lambda e: e.dma_start(out=lpb[:], in_=lp_in[:, :, :]), writes=["lpb"])
    P.dma("pool", lambda e: e.dma_start(out=ident[:], in_=id_in[:, :]), writes=["ident"])
    P.dve(lambda e: e.tensor_scalar(out=negc[:], in0=cfar[:], scalar1=-1.0, scalar2=None, op0=ALU.mult), reads=["cfar"], writes=["negc"])
    P.dve(lambda e: e.tensor_scalar(out=negcp[:], in0=cpre[:], scalar1=-1.0, scalar2=None, op0=ALU.mult), reads=["cpre"], writes=["negcp"])
    for (Mt, src, nb, k) in ((M0, b0_in, negc, "M0"), (M1, b1_in, negc, "M1"), (M1p, b1p_in, negc, "M1p")):
        P.dma("sp", lambda e, src=src: e.dma_start(out=btmp[:], in_=src[:, :, :]), writes=["btmp"])
        for h in range(16):
            P.act(lambda e, Mt=Mt, nb=nb, h=h: e.activation(out=Mt[:, h, :], in_=btmp[:, h, :], func=AF.Exp, bias=nb[:, h:h + 1]),
                  reads=["btmp", "negc", "negcp"], writes=[k])
    for pi in range(2):
        P.dve(lambda e, pi=pi: e.tensor_tensor(out=lt1[:], in0=lpb[:, 2 * pi, :], in1=lpb[:, 2 * pi + 1, :], op=ALU.mult),
              reads=["lpb"], writes=["lt1"])
        P.dve(lambda e, pi=pi: e.reduce_sum(out=lsum[:, pi:pi + 1], in_=lt1[:], axis=AX.X), reads=["lt1"], writes=["lsum"])
    P.act(lambda e: e.activation(out=lsum[:], in_=lsum[:], func=AF.Exp), reads=["lsum"], writes=["lsum"])
    P.dve(lambda e: e.tensor_tensor(out=neglam[:], in0=lsum[:, 1:2], in1=lsum[:, 0:1], op=ALU.subtract), reads=["lsum"], writes=["neglam"])
    P.dve(lambda e: e.tensor_scalar(out=neglam[:], in0=neglam[:], scalar1=-LAM_INIT2, scalar2=None, op0=ALU.add),
          reads=["neglam"], writes=["neglam"])
    P.dve(lambda e: e.tensor_scalar(out=sgs[:], in0=sgs[:], scalar1=1.0 - LAM_INIT2, scalar2=None, op0=ALU.mult),
          reads=["sgs"], writes=["sgs"])

    qin_v = qin.rearrange("(c p) t -> p c t", p=128)
    if fused:
        kpre_vs = [a_.rearrange("(c p) t -> p c t", p=128) for a_ in ph.bind["kpres"]]
        kown_vs = [a_.rearrange("(c p) t -> p c t", p=128) for a_ in ph.bind["kowns"]]
        vpre_vs = [a_.rearrange("(kb p) n -> p kb n", p=128) for a_ in ph.bind["vpres"]]
        vown_vs = [a_.rearrange("(kb p) n -> p kb n", p=128) for a_ in ph.bind["vowns"]]
        P.dve(lambda e: e.memset(Va[:, :, 256:257], 1.0), writes=["Va1"])
    else:
        kin_v = kin.rearrange("(c p) t -> p c t", p=128)
        vin_v = vin.rearrange("(kb p) h n -> p kb h n", p=128)
    xin_v = xin.rearrange("(c p) t -> p c t", p=128)
    xout_v = xout.rearrange("(c p) t -> p c t", p=128)
    wout_v = wout.rearrange("(c p) n -> p c n", p=128)
    LOOK = 2
    NPT = 4
    PTs = [ph.sb("PTp%d" % i, [128, 2, 256], BF16) for i in range(NPT)]
    Osb = [ph.sb("Osb%d" % i, [128, 257], F32) for i in range(8)]
    zero_b = ph.sb("zero_b", [128, 1], F32)
    P.dve(lambda e: e.memset(zero_b[:], 0.0), writes=["zero_b"])
    gstep = [0]

    def stage_a(hp, qt, kb, sidx):
        kb_rel = kb - (16 + 2 * qt)
        qlo = 128 if kb_rel == 1 else 0
        ps, psk = pb[4 + sidx % 3], pbk[4 + sidx % 3]
        pt, ptk = PTs[sidx % NPT], "PTp%d" % (sidx % NPT)
        for i in range(2):
            P.pe(lambda e, ps=ps, i=i, kb=kb, qt=qt, qlo=qlo: e.matmul(
                ps[:, i * 256 + qlo:(i + 1) * 256], Kt[:, i, kb * 128:(kb + 1) * 128],
                Qt[:, i, qt * 256 + qlo:(qt + 1) * 256], start=True, stop=True),
                reads=["Kt", "Qt"], writes=[psk])
        bias_ap = cpre[:, 0:1] if kb < 16 else zero_b[:, 0:1]
        P.act(lambda e, ps=ps, pt=pt, qlo=qlo, bias_ap=bias_ap: e.activation(
            out=pt[:, :, qlo:256], in_=ps[:].rearrange("p (i q) -> p i q", i=2)[:, :, qlo:256], func=AF.Exp,
            bias=bias_ap, scale=SCALE),
            reads=[psk, "cpre", "zero_b"], writes=[ptk])
        fix = []
        if kb_rel == -1:
            fix.append((0, M1p if kb == 15 else M1, "M1p" if kb == 15 else "M1"))
        elif kb_rel == 0:
            fix.append((0, M0, "M0"))
            fix.append((1, M1, "M1"))
        elif kb_rel == 1:
            fix.append((1, M0, "M0"))
        for (qb, Mt, mk) in fix:
            P.dve(lambda e, pt=pt, qb=qb, Mt=Mt, hp=hp: e.tensor_tensor(
                out=pt[:, :, qb * 128:(qb + 1) * 128], in0=pt[:, :, qb * 128:(qb + 1) * 128],
                in1=Mt[:, 2 * hp:2 * hp + 2, :], op=ALU.mult),
                reads=[ptk, mk], writes=[ptk])

    def stage_b(hp, qt, kb, sidx):
        pt, ptk = PTs[sidx % NPT], "PTp%d" % (sidx % NPT)
        for qb in range(2):
            last = 16 + 2 * qt + qb
            if kb > last:
                continue
            for i in range(2):
                acc = pb[qb * 2 + i]
                P.pe(lambda e, acc=acc, pt=pt, i=i, qb=qb, kb=kb, last=last: e.matmul(
                    acc[:, 0:257], pt[:, i, qb * 128:(qb + 1) * 128], Va[:, kb, :],
                    start=(kb == 0), stop=(kb == last)),
                    reads=[ptk, "Va"], writes=[pbk[qb * 2 + i]])
        if kb == 16 + 2 * qt + 1:
            finalize(hp, qt)

    fctr = [0]

    def finalize(hp, qt):
        fs = (fctr[0] % 2) * 4
        fctr[0] += 1
        for a_i in range(4):
            P.dve(lambda e, a_i=a_i, fs=fs: e.tensor_scalar(out=Osb[fs + a_i][:], in0=pb[a_i][:, 0:257], scalar1=1.0, scalar2=None,
                                                            op0=ALU.mult),
                  reads=[pbk[a_i]], writes=["Osb%d" % (fs + a_i)])
        for qb in range(2):
            o1, o1k = Osb[fs + qb * 2], "Osb%d" % (fs + qb * 2)
            o2, o2k = Osb[fs + qb * 2 + 1], "Osb%d" % (fs + qb * 2 + 1)
            P.dve(lambda e, o1=o1: e.reciprocal(out=rc[:, 0:1], in_=o1[:, 256:257]), reads=[o1k], writes=["rc"])
            P.dve(lambda e, o2=o2: e.reciprocal(out=rc[:, 1:2], in_=o2[:, 256:257]), reads=[o2k], writes=["rc"])
            P.dve(lambda e: e.tensor_tensor(out=rc[:, 1:2], in0=rc[:, 1:2], in1=neglam[:], op=ALU.mult),
                  reads=["rc", "neglam"], writes=["rc"])
            P.dve(lambda e, o2=o2: e.tensor_scalar(out=uu[:], in0=o2[:, 0:256], scalar1=rc[:, 1:2], scalar2=None, op0=ALU.mult),
                  reads=[o2k, "rc"], writes=["uu"])
            P.dve(lambda e, o1=o1: e.scalar_tensor_tensor(out=att[:], in0=o1[:, 0:256], scalar=rc[:, 0:1], in1=uu[:],
                                                          op0=ALU.mult, op1=ALU.add),
                  reads=[o1k, "rc", "uu"], writes=["att"])
            P.dve(lambda e: e.tensor_tensor(out=asq[:], in0=att[:], in1=att[:], op=ALU.mult), reads=["att"], writes=["asq"])
            P.dve(lambda e: e.reduce_sum(out=ssn[:], in_=asq[:], axis=AX.X), reads=["asq"], writes=["ssn"])
            P.dve(lambda e: e.tensor_scalar(out=ssn[:], in0=ssn[:], scalar1=1.0 / 256.0, scalar2=EPS, op0=ALU.mult, op1=ALU.add),
                  reads=["ssn"], writes=["ssn"])
            P.act(lambda e: e.activation(out=ssn[:], in_=ssn[:], func=AF.Sqrt), reads=["ssn"], writes=["ssn"])
            P.dve(lambda e: e.reciprocal(out=ssn[:], in_=ssn[:]), reads=["ssn"], writes=["ssn"])
            qbg = 2 * qt + qb
            P.dve(lambda e, qbg=qbg, hp=hp: e.scalar_tensor_tensor(out=ao[:, qbg, hp * 256:(hp + 1) * 256], in0=att[:],
                                                                   scalar=ssn[:, 0:1], in1=sgs[:], op0=ALU.mult, op1=ALU.mult),
                  reads=["att", "ssn", "sgs"], writes=[("ao", qbg)])

    for hp in range(8):
        if fused:
            P.dma("pool", lambda e, hp=hp: e.dma_start(out=Kt[:, :, 0:TC], in_=kpre_vs[hp][:, :, :]), writes=["Kt"])
            P.dma("pool", lambda e, hp=hp: e.dma_start(out=Kt[:, :, TC:2 * TC], in_=kown_vs[hp][:, :, :]), writes=["Kt"])
            P.dma("pool", lambda e, hp=hp: e.dma_start(out=Va[:, 0:16, 0:256], in_=vpre_vs[hp][:, :, :]), reads=["Va1"], writes=["Va"])
            P.dma("pool", lambda e, hp=hp: e.dma_start(out=Va[:, 16:32, 0:256], in_=vown_vs[hp][:, :, :]), reads=["Va1"], writes=["Va"])
        else:
            P.dma("pool", lambda e, hp=hp: e.dma_start(out=Kt[:, :, 0:TC], in_=kin_v[:, 2 * hp:2 * hp + 2, 0:TC]), writes=["Kt"])
            P.dma("pool", lambda e, hp=hp: e.dma_start(out=Kt[:, :, TC:2 * TC], in_=kin_v[:, 2 * hp:2 * hp + 2, TC:2 * TC]), writes=["Kt"])
            P.dma("pool", lambda e, hp=hp: e.dma_start(out=Va[:, 0:16, :], in_=vin_v[:, 0:16, hp, :]), writes=["Va"])
            P.dma("pool", lambda e, hp=hp: e.dma_start(out=Va[:, 16:32, :], in_=vin_v[:, 16:32, hp, :]), writes=["Va"])
        P.dma("pool", lambda e, hp=hp: e.dma_start(out=Qt[:], in_=qin_v[:, 2 * hp:2 * hp + 2, :]), writes=["Qt"])
        steps = [(qt, kb) for qt in range(8) for kb in range(16 + 2 * qt + 2)]
        base = gstep[0]
        for n in range(len(steps) + LOOK):
            if n < len(steps):
                stage_a(hp, steps[n][0], steps[n][1], base + n)
            if n >= LOOK:
                stage_b(hp, steps[n - LOOK][0], steps[n - LOOK][1], base + n - LOOK)
        gstep[0] += len(steps)
    xctr = [0]
    wctr = [0]
    pctr = [0]
    for tt in range(4):
        tsl = slice(tt * 512, (tt + 1) * 512)
        for j in range(4):
            qbg = tt * 4 + j
            for fb in range(4):
                ptr, ptrk = pb[6], pbk[6]
                for fi in range(4):
                    fc = fb * 4 + fi
                    P.pe(lambda e, ptr=ptr, fi=fi, fc=fc, qbg=qbg: e.transpose(pbf(ptr)[:, fi * 128:(fi + 1) * 128],
                                                                              ao[:, qbg, fc * 128:(fc + 1) * 128], ident[:]),
                         reads=[("ao", qbg), "ident"], writes=[ptrk])
                P.act(lambda e, ptr=ptr, fb=fb, j=j: e.copy(out=aoT[:, fb * 4:(fb + 1) * 4, j * 128:(j + 1) * 128],
                                                            in_=pbf(ptr)[:, 0:512].rearrange("p (f t) -> p f t", f=4)),
                      reads=[ptrk], writes=[("aoT", fb)])
        for ob in range(4):
            wi = wctr[0] % 2
            wctr[0] += 1
            P.dma("pool", lambda e, wi=wi, ob=ob: e.dma_start(out=Wb[wi][:], in_=wout_v[:, :, ob * 512:(ob + 1) * 512]),
                  writes=["Wb%d" % wi])
            for dd in range(4):
                d = ob * 4 + dd
                pi = pctr[0] % 2
                pctr[0] += 1
                py_, pyk = pb[pi], pbk[pi]
                for fc in range(DC):
                    P.pe(lambda e, fc=fc, dd=dd, py_=py_, wi=wi: e.matmul(py_[:], Wb[wi][:, fc, dd * 128:(dd + 1) * 128], aoT[:, fc, :],
                                                                          start=(fc == 0), stop=(fc == DC - 1)),
                         reads=["Wb%d" % wi, ("aoT", fc // 4)], writes=[pyk])
                xi = xctr[0] % 3
                xctr[0] += 1
                P.dma("sp", lambda e, xi=xi, d=d, tsl=tsl: e.dma_start(out=xs[xi][:], in_=xin_v[:, d, tsl]), writes=["xs%d" % xi])
                P.dve(lambda e, xi=xi, py_=py_: e.tensor_tensor(out=xs[xi][:], in0=xs[xi][:], in1=py_[:], op=ALU.add),
                      reads=[pyk, "xs%d" % xi], writes=["xs%d" % xi])
                P.dma("sp", lambda e, xi=xi, d=d, tsl=tsl: e.dma_start(out=xout_v[:, d, tsl], in_=xs[xi][:]),
                      reads=["xs%d" % xi], writes=[("xo", tt, d)])
                ph.out_keys.append(("xo", tt, d))
    return ph.finish()


def rel_bucket_np(rel):
    n = np.maximum(rel, 0)
    nf = np.maximum(n, 1).astype(np.float32)
    large = 16 + (np.log(nf / np.float32(16)) / np.float32(math.log(128 / 16)) * np.float32(16)).astype(np.int32)
    large = np.minimum(large, 31)
    return np.where(n < 16, n, large)


def diff_bias_tiles(rel_bias, first_half):
    tab = np.concatenate([np.asarray(rel_bias, np.float32), np.full((1, 16), NEG, np.float32),
                          np.full((1, 16), 2 * NEG, np.float32)], axis=0)
    k = np.arange(128)[:, None]
    q = np.arange(128)[None, :]
    rel0 = q - k
    idx0 = np.where(rel0 >= 0, rel_bucket_np(rel0), 32)
    idx1 = rel_bucket_np(128 + q - k)
    B0 = np.ascontiguousarray(tab[idx0].transpose(0, 2, 1))
    B1 = np.ascontiguousarray(tab[idx1].transpose(0, 2, 1))
    if first_half:
        B1p = np.full((128, 16, 128), 2 * NEG, np.float32)
        cpre = np.full((128, 16), NEG, np.float32)
    else:
        B1p = B1.copy()
        cpre = np.zeros((128, 16), np.float32)
    cfar = np.ascontiguousarray(np.broadcast_to(tab[31][None, :], (128, 16)))
    return B0, B1, B1p, cfar, cpre


def diff2_inputs(qT, kT_own, v_own, kT_prev, v_prev, xT, w_out, rel_bias, lam_params, sub_gain, first_half):
    kT_all = np.zeros((D, 2 * TC), np.float32)
    va = np.zeros((2 * TC, 8, 257), np.float32)
    kT_all[:, TC:] = kT_own
    va[TC:, :, :256] = v_own.reshape(TC, 8, 256)
    va[TC:, :, 256] = 1.0
    if not first_half:
        kT_all[:, :TC] = kT_prev
        va[:TC, :, :256] = v_prev.reshape(TC, 8, 256)
        va[:TC, :, 256] = 1.0
    B0, B1, B1p, cfar, cpre = diff_bias_tiles(rel_bias, first_half)
    lpb = np.ascontiguousarray(np.broadcast_to(np.asarray(lam_params, np.float32)[None], (128, 4, 128)))
    sgb = np.ascontiguousarray(np.broadcast_to(np.asarray(sub_gain, np.float32)[None, :], (128, 256)))
    return {"qT": np.ascontiguousarray(qT), "kT": kT_all, "va": va, "xT": np.ascontiguousarray(xT),
            "wout": np.ascontiguousarray(w_out, dtype=np.float32), "B0": B0, "B1": B1, "B1p": B1p, "cfar": cfar, "cpre": cpre,
            "lpb": lpb, "sgb": sgb, "ident": np.eye(128, dtype=np.float32)}


def diff1_inputs(xT, g, w_in, qg, kg):
    qkg = np.zeros((128, 16), np.float32)
    qkg[:, 0] = np.asarray(qg, np.float32)
    qkg[:, 1] = np.asarray(kg, np.float32)
    return {"xT": np.ascontiguousarray(xT), "g": col16(g), "win": np.ascontiguousarray(w_in, dtype=np.float32), "qkg": qkg}


PAIRS = [[0, 1], [2, 3], [4, 5], [6, 7]]
DEPTH = 4


def build_fused(plan=("gla0", "ffn0", "pool", "ffn1", "diff", "ffn2", "gla1", "ffn3")):
    mp = Phase()
    m = mp
    nc, P = mp.nc, mp.P
    mp.btile = mp.sb("btile", [128, 1], F32)
    cache = {}

    def lz(name, shape):
        if name not in cache:
            cache[name] = mp.din(name, shape)
        return cache[name]

    def lzi(name, shape):
        if name not in cache:
            cache[name] = mp.dint(name, shape)
        return cache[name]

    xT_in = mp.din("xT", [D, TC])
    xT_out = mp.dout("xTo", [D, TC])

    def ngf(l, i):
        return lz("ng_%d_%d" % (l, i), [128, DC])

    def barrier():
        P.barrier(lambda e, bt=mp.btile: e.memset(bt[:], 0.0))

    def gather(src, dst, tag):
        P.cc(lambda e: e.collective_compute("AllGather", ALU.bypass, replica_groups=PAIRS, ins=[src], outs=[dst]),
             reads=[], writes=[("cc", tag)])
        barrier()

    def gla_layer(sl, layer, xin, xout):
        st_src = lzi("st_src", [1024, 512])
        st_all = lzi("st_all", [2048, 512])
        common = dict(xT=xin, g=ngf(layer, 0), win=lz("gla%d_win" % sl, [D, GLA_NCOL]), wa2b=lz("gla%d_wa2b" % sl, [17, GLA_DKT]),
                      gnb=lz("gla%d_gnb" % sl, [128, 2048]), wout=lz("gla%d_wout" % sl, [D, D]), tri=lz("tri", [128, 128]),
                      ident=lz("ident", [128, 128]), isb=lz("isb", [128, 16]))
        b1 = dict(common)
        b1.update(st_in=None, st_out=st_src.rearrange("(p c) v -> p c v", c=8))
        build_gla_phase(Phase(mp, "g%da_" % layer, b1), state_only=True)
        gather(st_src[:, :], st_all[:, :], ("st", layer))
        b2 = dict(common)
        b2.update(st_in=st_all[0:1024, :].rearrange("(p c) v -> p c v", c=8), st_out=None, xTo=xout)
        build_gla_phase(Phase(mp, "g%db_" % layer, b2), state_only=False)

    def ffn_layer(layer, xin, xout):
        build_ffn_phase(Phase(mp, "f%d_" % layer, dict(xT=xin, xTo=xout, g=ngf(layer, 1), wgu=lz("ffn%d_wgu" % layer, [D, 2 * FH]),
                                                      wdn=lz("ffn%d_wdn" % layer, [FH, D]))))

    def pool_layer(layer, xin, xout):
        halo_src = lzi("halo_src", [D, 16])
        halo_all = lzi("halo_all", [2 * D, 16])
        P.dma("sp", lambda e: e.dma_start(out=halo_src[:, :], in_=xin[:, TC - 16:TC]), writes=["halo_src"])
        barrier()
        gather(halo_src[:, :], halo_all[:, :], "halo")
        build_pool_phase(Phase(mp, "p1_", dict(xT=xin, halo=halo_all[0:D, :], isb=lz("isb", [128, 16]), g=ngf(layer, 0),
                                               wp=lz("pool_wp", [4, 512, 512]), psc=lz("pool_psc", [128, DC]),
                                               invc=lz("pool_invc", [128, 4, 16]), xTo=xout)))

    def diff_layer(layer, xin, xout):
        q_d = lzi("q_d", [D, TC])
        k_ds = [lzi("k_d%d" % i, [256, TC]) for i in range(8)]
        v_ds = [lzi("v_d%d" % i, [TC, 256]) for i in range(8)]
        k_alls = [lzi("k_all%d" % i, [512, TC]) for i in range(8)]
        v_alls = [lzi("v_all%d" % i, [2 * TC, 256]) for i in range(8)]
        build_diff1_phase(ph=Phase(mp, "d1_", dict(xT=xin, g=ngf(layer, 0), win=lz("diff_win", [D, 3 * D]),
                                                   qkg=lz("diff_qkg", [128, 16]), qkT=q_d, q_d=q_d, k_ds=k_ds, v_ds=v_ds, v=v_ds[0])))
        for i in range(8):
            P.cc(lambda e, i=i: e.collective_compute("AllGather", ALU.bypass, replica_groups=PAIRS, ins=[k_ds[i][:, :]],
                                                     outs=[k_alls[i][:, :]]), reads=[], writes=[("cc", "k", i)])
            P.cc(lambda e, i=i: e.collective_compute("AllGather", ALU.bypass, replica_groups=PAIRS, ins=[v_ds[i][:, :]],
                                                     outs=[v_alls[i][:, :]]), reads=[], writes=[("cc", "v", i)])
        barrier()
        build_diff2_phase(Phase(mp, "d2_", dict(qT=q_d, kpres=[a_[0:256, :] for a_ in k_alls], kowns=k_ds,
                                                vpres=[a_[0:TC, :] for a_ in v_alls], vowns=v_ds, xT=xin,
                                                wout=lz("diff_wout", [D, D]), B0=lz("diff_B0", [128, 16, 128]),
                                                B1=lz("diff_B1", [128, 16, 128]), B1p=lz("diff_B1p", [128, 16, 128]),
                                                cfar=lz("diff_cfar", [128, 16]), cpre=lz("diff_cpre", [128, 16]),
                                                lpb=lz("diff_lpb", [128, 4, 128]), sgb=lz("diff_sgb", [128, 256]),
                                                ident=lz("ident", [128, 128]), xTo=xout)))

    bufs = [lzi("xa", [D, TC]), lzi("xb", [D, TC])]
    cur = xT_in
    for si, step in enumerate(plan):
        dst = xT_out if si == len(plan) - 1 else bufs[si % 2]
        layer = int(step[-1]) if step[:3] == "ffn" else {"gla0": 0, "pool": 1, "diff": 2, "gla1": 3}[step]
        if step[:3] == "ffn":
            ffn_layer(layer, cur, dst)
        elif step[:3] == "gla":
            gla_layer(int(step[3]), layer, cur, dst)
        elif step == "pool":
            pool_layer(layer, cur, dst)
        else:
            diff_layer(layer, cur, dst)
        cur = dst
    mp.input_names = [k for k in cache if not k in ("xa", "xb", "st_src", "st_all", "halo_src", "halo_all", "q_d", "k_d", "v_d", "k_all", "v_all")]
    return mp.finish()


_FUSED = []


def kernel(x, norm_g, gla_w_in, gla_w_a2, gla_b_a, gla_g_norm, gla_w_out, pool_w, pool_scale, diff_w_in,
           diff_q_gain, diff_k_gain, diff_lambda, diff_sub_gain, diff_w_out, rel_bias, ffn_w_gu, ffn_w_down):
    x = np.asarray(x, np.float32)
    B, S, _ = x.shape
    if not _FUSED:
        _FUSED.append(build_fused())
    nc = _FUSED[0]
    f32c = lambda a: np.ascontiguousarray(np.asarray(a, np.float32))
    tri, ident = gla_consts()
    shared = {"tri": tri, "ident": ident}
    for l in range(DEPTH):
        for i in range(2):
            shared["ng_%d_%d" % (l, i)] = col16(norm_g[l, i])
        shared["ffn%d_wgu" % l] = f32c(ffn_w_gu[l])
        shared["ffn%d_wdn" % l] = f32c(ffn_w_down[l])
    for sl in range(2):
        shared["gla%d_win" % sl] = f32c(gla_w_in[sl])
        shared["gla%d_wa2b" % sl] = f32c(np.concatenate([np.asarray(gla_w_a2[sl], np.float32),
                                                         np.asarray(gla_b_a[sl], np.float32)[None, :]], axis=0))
        shared["gla%d_gnb" % sl] = f32c(np.broadcast_to(np.tile(np.asarray(gla_g_norm[sl], np.float32), 4)[None, :], (128, 2048)))
        shared["gla%d_wout" % sl] = f32c(gla_w_out[sl])
    shared["pool_wp"] = f32c(pool_w[0])
    shared["pool_psc"] = col16(pool_scale[0])
    shared["diff_win"] = f32c(diff_w_in[0])
    qkg = np.zeros((128, 16), np.float32)
    qkg[:, 0] = np.asarray(diff_q_gain[0], np.float32)
    qkg[:, 1] = np.asarray(diff_k_gain[0], np.float32)
    shared["diff_qkg"] = qkg
    shared["diff_wout"] = f32c(diff_w_out[0])
    shared["diff_lpb"] = f32c(np.broadcast_to(np.asarray(diff_lambda[0], np.float32)[None], (128, 4, 128)))
    shared["diff_sgb"] = f32c(np.broadcast_to(np.asarray(diff_sub_gain[0], np.float32)[None, :], (128, 256)))
    per_half = []
    for half in range(2):
        B0, B1, B1p, cfar, cpre = diff_bias_tiles(rel_bias, half == 0)
        invc = np.zeros((128, 4, 16), np.float32)
        for gi, w in enumerate(POOL_W):
            for t in range(16):
                invc[:, gi, t] = 1.0 / (min(t + 1, w) if half == 0 else w)
        per_half.append({"diff_B0": B0, "diff_B1": B1, "diff_B1p": B1p, "diff_cfar": cfar, "diff_cpre": cpre,
                         "pool_invc": invc, "isb": np.full((128, 16), float(half), np.float32)})
    in_maps = []
    for c in range(NCORES):
        im = dict(shared)
        im.update(per_half[c % 2])
        im["xT"] = np.ascontiguousarray(x[c // 2, (c % 2) * TC:(c % 2 + 1) * TC].T)
        in_maps.append(im)
    res = run_bass_kernel_spmd(nc, in_maps, core_ids=list(range(NCORES))).results
    out = np.empty((B, S, D), np.float32)
    for c in range(NCORES):
        out[c // 2, (c % 2) * TC:(c % 2 + 1) * TC] = res[c]["xTo"].T
    return out
```

```python
from contextlib import ExitStack
import math
import numpy as np
import concourse.bass as bass
import concourse.mybir as mybir
from concourse.bass_utils import run_bass_kernel_spmd

F32 = mybir.dt.float32
BF16 = mybir.dt.bfloat16
AF = mybir.ActivationFunctionType
ALU = mybir.AluOpType
AX = mybir.AxisListType

D = 2048
DC = D // 128
TC = 2048
FH = 5632
HC = FH // 128
EPS = 1e-6
NCORES = 8

ENGINES = ("pe", "act", "dve", "pool", "sp")


class Op:
    __slots__ = ("eng", "fn", "reads", "writes", "dma", "waits", "sig", "idx", "cc", "barrier")

    def __init__(self, eng, fn, reads, writes, dma):
        self.cc = False
        self.barrier = False
        self.eng = eng
        self.fn = fn
        self.reads = reads
        self.writes = writes
        self.dma = dma
        self.waits = []
        self.sig = None
        self.idx = -1


class Prog:
    NDMA = 32

    def __init__(self, nc):
        self.nc = nc
        self.ops = []

    LOOPVARS = frozenset("tt tsl j jsl h hs dc di dl blk c fc fi fb d dd ob vb rb i ei s sl mi grp b g w at atk kd kdk pq pk_ pv py_ ptr pkv cur oth sh a Wq Wk Wv Wr Wo wb db dp hf step q qk qb kb ki qi hp".split())

    def op(self, eng, fn, reads=(), writes=(), dma=False):
        bad = self.LOOPVARS.intersection(fn.__code__.co_freevars)
        if bad:
            raise RuntimeError("late-bound loop variable(s) %s in lambda at line %d" % (sorted(bad), fn.__code__.co_firstlineno))
        o = Op(eng, fn, tuple(reads), tuple(writes), dma)
        o.idx = len(self.ops)
        self.ops.append(o)
        return o

    def pe(self, fn, reads=(), writes=()):
        return self.op("pe", fn, reads, writes)

    def act(self, fn, reads=(), writes=()):
        return self.op("act", fn, reads, writes)

    def dve(self, fn, reads=(), writes=()):
        return self.op("dve", fn, reads, writes)

    def pool(self, fn, reads=(), writes=()):
        return self.op("pool", fn, reads, writes)

    def dma(self, eng, fn, reads=(), writes=()):
        return self.op(eng, fn, reads, writes, dma=True)

    def cc(self, fn, reads=(), writes=()):
        o = self.op("pool", fn, reads, writes, dma=True)
        o.cc = True
        return o

    def barrier(self, fn):
        o = self.op("dve", fn, (), ())
        o.barrier = True
        return o

    def finalize(self, final_wait_keys=()):
        ops = self.ops
        deps = [set() for _ in ops]
        last_writer = {}
        readers = {}
        since = []
        last_barrier = None
        for o in ops:
            if o.barrier:
                deps[o.idx] |= set(since)
                if last_barrier is not None:
                    deps[o.idx].add(last_barrier)
                since = []
                last_barrier = o.idx
                last_writer_final = dict(last_writer)
                last_writer = {}
                readers = {}
                continue
            since.append(o.idx)
            if last_barrier is not None:
                deps[o.idx].add(last_barrier)
            for k in o.reads:
                w = last_writer.get(k)
                if w is not None:
                    deps[o.idx].add(w.idx)
            for k in o.writes:
                w = last_writer.get(k)
                if w is not None:
                    deps[o.idx].add(w.idx)
                for r in readers.get(k, ()):
                    if r.idx != o.idx:
                        deps[o.idx].add(r.idx)
            for k in o.reads:
                readers.setdefault(k, []).append(o)
            for k in o.writes:
                last_writer[k] = o
                readers[k] = []
        final_ops = [last_writer[k].idx for k in final_wait_keys if k in last_writer]
        if last_barrier is not None:
            final_ops.append(last_barrier)
        needed = [set() for _ in ops]
        for o in ops:
            best = {}
            for d in deps[o.idx]:
                p = ops[d]
                if p.dma:
                    needed[o.idx].add(d)
                    continue
                if p.eng == "pe" and o.eng == "pe" and not o.dma:
                    continue
                if best.get(p.eng, -1) < d:
                    best[p.eng] = d
            needed[o.idx] |= set(best.values())
        signaled = set(final_ops)
        for o in ops:
            signaled |= needed[o.idx]
        eng_count = {e: 0 for e in ENGINES}
        NDMA = self.NDMA
        dma_count = [0] * NDMA
        dma_last = [None] * NDMA
        rr = 0
        rr_sw = 0
        cc_count = 0
        cc_last = None
        for o in ops:
            if o.dma and not o.cc:
                signaled.add(o.idx)
            if o.cc:
                signaled.add(o.idx)
                if cc_last is not None:
                    needed[o.idx].add(cc_last)
                cc_count += 1
                cc_last = o.idx
                o.sig = (("cc", 0), None, cc_count)
                continue
            if o.idx not in signaled:
                continue
            if o.dma:
                half = NDMA // 2
                if o.eng == "pool":
                    s = half + rr_sw % half
                    rr_sw += 1
                else:
                    s = rr % half
                    rr += 1
                if dma_last[s] is not None:
                    needed[o.idx].add(dma_last[s])
                dma_count[s] += 1
                dma_last[s] = o.idx
                o.sig = (("dma", s), 16, dma_count[s] * 16)
            else:
                eng_count[o.eng] += 1
                o.sig = (("eng", o.eng), 1, eng_count[o.eng])
        seen = {e: {} for e in ENGINES}
        for o in ops:
            ws = {}
            for d in needed[o.idx]:
                semkey, _, val = ops[d].sig
                if ws.get(semkey, 0) < val:
                    ws[semkey] = val
            for semkey, val in ws.items():
                if seen[o.eng].get(semkey, 0) >= val:
                    continue
                seen[o.eng][semkey] = val
                o.waits.append((semkey, val))
        fw = {}
        for d in final_ops:
            semkey, _, val = ops[d].sig
            fw[semkey] = max(fw.get(semkey, 0), val)
        self.final_waits = fw
        self.eng_count = eng_count

    def emit(self, block, sems):
        per_eng = {e: [] for e in ENGINES}
        for o in self.ops:
            per_eng[o.eng].append(o)
        final_waits = self.final_waits

        def run(engobj, lst, is_last):
            for o in lst:
                for semkey, val in o.waits:
                    engobj.wait_ge(sems[semkey], val)
                ins = o.fn(engobj)
                if o.sig is not None:
                    if o.sig[1] is None:
                        ins.then_inc(sems[o.sig[0]])
                    else:
                        ins.then_inc(sems[o.sig[0]], o.sig[1])
            if is_last:
                for semkey, val in final_waits.items():
                    engobj.wait_ge(sems[semkey], val)

        @block.tensor
        def _(e):
            run(e, per_eng["pe"], False)

        @block.scalar
        def _(e):
            run(e, per_eng["act"], False)

        @block.vector
        def _(e):
            run(e, per_eng["dve"], False)

        @block.gpsimd
        def _(e):
            run(e, per_eng["pool"], False)

        @block.sync
        def _(e):
            run(e, per_eng["sp"], True)


class Phase:
    def __init__(self, master=None, prefix="", bind=None):
        self.master = master
        self.prefix = prefix
        self.bind = bind or {}
        if master is None:
            self.nc = bass.Bass("TRN2", target_bir_lowering=False)
            self.P = Prog(self.nc)
            self.out_keys = []
        else:
            self.nc = master.nc
            self.P = master.P
            self.out_keys = master.out_keys
        self.es = ExitStack()

    def din(self, name, shape, dtype=F32):
        if name in self.bind:
            return self.bind[name]
        return self.nc.dram_tensor(self.prefix + name, list(shape), dtype, kind="ExternalInput").ap()

    def dout(self, name, shape, dtype=F32):
        if name in self.bind:
            return self.bind[name]
        return self.nc.dram_tensor(self.prefix + name, list(shape), dtype, kind="ExternalOutput").ap()

    def dint(self, name, shape, dtype=F32):
        return self.nc.dram_tensor(self.prefix + name, list(shape), dtype).ap()

    def sb(self, name, shape, dtype=F32):
        return self.es.enter_context(self.nc.sbuf_tensor(self.prefix + name, list(shape), dtype))

    def ps(self, name, shape, dtype=F32):
        return self.es.enter_context(self.nc.psum_tensor(self.prefix + name, list(shape), dtype))

    def finish(self):
        if self.master is not None:
            self.es.close()
            bt = self.master.btile
            self.P.barrier(lambda e: e.memset(bt[:], 0.0))
            return None
        P = self.P
        P.finalize(final_wait_keys=self.out_keys)
        sems = {}
        for e in ENGINES:
            sems[("eng", e)] = self.es.enter_context(self.nc.semaphore("s_" + e))
        for i in range(P.NDMA):
            sems[("dma", i)] = self.es.enter_context(self.nc.semaphore("d%d" % i))
        sems[("cc", 0)] = self.es.enter_context(self.nc.semaphore("s_cc"))
        block = self.es.enter_context(self.nc.Block())
        P.emit(block, sems)
        self.es.close()
        return self.nc


def emit_rmsnorm(ph, xT, hT, gcol, ones_bf, sq, pss, rstd, ntok, xkey, hkey, tag):
    P = ph.P
    nsub = ntok // 512
    for s in range(nsub):
        sl = slice(s * 512, (s + 1) * 512)
        pb = pss[s % 2]
        pk = "pss%d" % (s % 2)
        for c in range(DC):
            q = sq[c % 2]
            qk = "sq%d" % (c % 2)
            P.act(lambda e, q=q, c=c, sl=sl: e.activation(out=q[:], in_=xT[:, c, sl], func=AF.Square),
                  reads=[(xkey, c)], writes=[qk])
            P.pe(lambda e, q=q, c=c, pb=pb: e.matmul(pb[:], ones_bf[:], q[:], start=(c == 0), stop=(c == DC - 1)),
                 reads=[qk, "ones"], writes=[pk])
        rk = ("rstd", tag, s)
        P.dve(lambda e, pb=pb, sl=sl: e.tensor_scalar(out=rstd[:, sl], in0=pb[:], scalar1=1.0 / D, scalar2=EPS,
                                                      op0=ALU.mult, op1=ALU.add),
              reads=[pk], writes=[rk])
        P.act(lambda e, sl=sl: e.activation(out=rstd[:, sl], in_=rstd[:, sl], func=AF.Sqrt),
              reads=[rk], writes=[rk])
        P.dve(lambda e, sl=sl: e.reciprocal(out=rstd[:, sl], in_=rstd[:, sl]),
              reads=[rk], writes=[rk])
        for c in range(DC):
            P.dve(lambda e, c=c, sl=sl: e.scalar_tensor_tensor(out=hT[:, c, sl], in0=xT[:, c, sl],
                                                               scalar=gcol[:, c:c + 1], in1=rstd[:, sl],
                                                               op0=ALU.mult, op1=ALU.mult),
                  reads=[(xkey, c), rk, "gcol"], writes=[(hkey, c, s)])


def build_ffn_phase(ph=None):
    ph = ph or Phase()
    P = ph.P
    nc = ph.nc
    xin = ph.din("xT", [D, TC])
    g_in = ph.din("g", [128, DC])
    wgu = ph.din("wgu", [D, 2 * FH])
    wdn = ph.din("wdn", [FH, D])
    xout = ph.dout("xTo", [D, TC])
    NT = 1024
    NS = NT // 512
    HG = 11
    NG = HC // HG
    xT = ph.sb("xTs", [128, DC, NT], F32)
    hT = ph.sb("hTs", [128, DC, NT], BF16)
    aT = ph.sb("aTs", [128, HG, NT], BF16)
    wg = [ph.sb("wg%d" % i, [128, DC, 256], BF16) for i in range(3)]
    wd = [ph.sb("wd%d" % i, [128, HG, 256], BF16) for i in range(2)]
    sq = [ph.sb("sq%d" % i, [128, 512], BF16) for i in range(2)]
    sg = [ph.sb("sg%d" % i, [128, 512], F32) for i in range(2)]
    rstd = ph.sb("rstd", [128, NT], F32)
    gcol = ph.sb("gcol", [128, DC], F32)
    ones = ph.sb("ones", [128, 128], BF16)
    pss = [ph.ps("pss%d" % i, [128, 512]) for i in range(2)]
    pg = [ph.ps("pg%d" % i, [128, 512]) for i in range(2)]
    pu = [ph.ps("pu%d" % i, [128, 512]) for i in range(2)]
    py = [ph.ps("py%d" % i, [128, 512]) for i in range(2)]

    P.dma("sp", lambda e: e.dma_start(out=gcol[:], in_=g_in[:, :]), writes=["gcol"])
    P.dve(lambda e: e.memset(ones[:], 1.0), writes=["ones"])
    xin_v = xin.rearrange("(c p) t -> p c t", p=128)
    xout_v = xout.rearrange("(c p) t -> p c t", p=128)
    wgu_v = wgu.rearrange("(c p) n -> p c n", p=128)
    wdn_v = wdn.rearrange("(m p) n -> p m n", p=128)
    wgi = 0
    wdi = 0
    cnt = 0
    for tt in range(TC // NT):
        tsl = slice(tt * NT, (tt + 1) * NT)
        for c in range(DC):
            P.dma("sp",
                  lambda e, c=c, tsl=tsl: e.dma_start(out=xT[:, c, :], in_=xin_v[:, c, tsl]),
                  writes=[("x", c)])
        emit_rmsnorm(ph, xT, hT, gcol, ones, sq, pss, rstd, NT, "x", "h", tt)
        hkeys = [("h", c, s) for c in range(DC) for s in range(NS)]
        for grp in range(NG):
            for mi in range(HG):
                m = grp * HG + mi
                wb = wg[wgi % 3]
                wk = "wg%d" % (wgi % 3)
                wgi += 1
                P.dma("pool", lambda e, wb=wb, m=m: e.dma_start(out=wb[:, :, 0:128], in_=wgu_v[:, :, m * 128:(m + 1) * 128]),
                      writes=[wk])
                P.dma("pool", lambda e, wb=wb, m=m: e.dma_start(out=wb[:, :, 128:256],
                                                                 in_=wgu_v[:, :, FH + m * 128:FH + (m + 1) * 128]),
                      writes=[wk])
                for s in range(NS):
                    sl = slice(s * 512, (s + 1) * 512)
                    b = cnt % 2
                    cnt += 1
                    for c in range(DC):
                        P.pe(lambda e, wb=wb, c=c, sl=sl, b=b: e.matmul(pg[b][:], wb[:, c, 0:128], hT[:, c, sl],
                                                                        start=(c == 0), stop=(c == DC - 1)),
                             reads=[wk, ("h", c, s)], writes=["pg%d" % b])
                    for c in range(DC):
                        P.pe(lambda e, wb=wb, c=c, sl=sl, b=b: e.matmul(pu[b][:], wb[:, c, 128:256], hT[:, c, sl],
                                                                        start=(c == 0), stop=(c == DC - 1)),
                             reads=[wk, ("h", c, s)], writes=["pu%d" % b])
                    P.act(lambda e, b=b: e.activation(out=sg[b][:], in_=pg[b][:], func=AF.Silu),
                          reads=["pg%d" % b], writes=["sg%d" % b])
                    P.dve(lambda e, b=b, mi=mi, sl=sl: e.tensor_tensor(out=aT[:, mi, sl], in0=sg[b][:], in1=pu[b][:],
                                                                       op=ALU.mult),
                          reads=["sg%d" % b, "pu%d" % b], writes=[("a", mi, s)])
            for dp in range(DC // 2):
                db = wd[wdi % 2]
                dk = "wd%d" % (wdi % 2)
                wdi += 1
                P.dma("pool", lambda e, db=db, dp=dp, grp=grp: e.dma_start(
                    out=db[:], in_=wdn_v[:, grp * HG:(grp + 1) * HG, dp * 256:(dp + 1) * 256]), writes=[dk])
                for dd in range(2):
                    d = dp * 2 + dd
                    for s in range(NS):
                        sl = slice(s * 512, (s + 1) * 512)
                        b = cnt % 2
                        cnt += 1
                        for mi in range(HG):
                            P.pe(lambda e, db=db, mi=mi, dd=dd, sl=sl, b=b: e.matmul(
                                py[b][:], db[:, mi, dd * 128:(dd + 1) * 128], aT[:, mi, sl],
                                start=(mi == 0), stop=(mi == HG - 1)),
                                reads=[dk, ("a", mi, s)], writes=["py%d" % b])
                        P.dve(lambda e, d=d, sl=sl, b=b: e.tensor_tensor(out=xT[:, d, sl], in0=xT[:, d, sl], in1=py[b][:],
                                                                         op=ALU.add),
                              reads=["py%d" % b, ("x", d)], writes=[("x", d)])
        for c in range(DC):
            P.dma("sp",
                  lambda e, c=c, tsl=tsl: e.dma_start(out=xout_v[:, c, tsl], in_=xT[:, c, :]),
                  reads=[("x", c)], writes=[("xo", tt, c)])
            ph.out_keys.append(("xo", tt, c))
    return ph.finish()


def emit_rmsnorm_cols(ph, xT, xoff, hT, hoff, ncols, gcol, ones_bf, sq, pb, pk, rstd, xkeys, hkeys, tag):
    P = ph.P
    xsl_ = slice(xoff, xoff + ncols)
    hsl_ = slice(hoff, hoff + ncols)
    for c in range(DC):
        q = sq[c % 2]
        qk = "sq%d" % (c % 2)
        P.act(lambda e, q=q, c=c: e.activation(out=q[:, 0:ncols], in_=xT[:, c, xsl_], func=AF.Square),
              reads=[xkeys(c)], writes=[qk])
        P.pe(lambda e, q=q, c=c: e.matmul(pb[:, 0:ncols], ones_bf[:], q[:, 0:ncols], start=(c == 0), stop=(c == DC - 1)),
             reads=[qk, "ones"], writes=[pk])
    rk = ("rstd", tag)
    P.dve(lambda e: e.tensor_scalar(out=rstd[:, 0:ncols], in0=pb[:, 0:ncols], scalar1=1.0 / D, scalar2=EPS,
                                    op0=ALU.mult, op1=ALU.add), reads=[pk], writes=[rk])
    P.act(lambda e: e.activation(out=rstd[:, 0:ncols], in_=rstd[:, 0:ncols], func=AF.Sqrt), reads=[rk], writes=[rk])
    P.dve(lambda e: e.reciprocal(out=rstd[:, 0:ncols], in_=rstd[:, 0:ncols]), reads=[rk], writes=[rk])
    for c in range(DC):
        P.dve(lambda e, c=c: e.scalar_tensor_tensor(out=hT[:, c, hsl_], in0=xT[:, c, xsl_], scalar=gcol[:, c:c + 1],
                                                    in1=rstd[:, 0:ncols], op0=ALU.mult, op1=ALU.mult),
              reads=[xkeys(c), rk, "gcol"], writes=[hkeys(c)])


POOL_W = (2, 4, 8, 16)


def build_pool_phase(ph=None):
    ph = ph or Phase()
    P = ph.P
    fused = ph.master is not None
    xin = None if fused else ph.din("xTe", [D, 16 + TC])
    g_in = ph.din("g", [128, DC])
    wp_in = ph.din("wp", [4, 512, 512])
    sc_in = ph.din("psc", [128, DC])
    ic_in = ph.din("invc", [128, 4, 16])
    xout = ph.dout("xTo", [D, TC])
    NT = 512
    xT = ph.sb("xTs", [128, DC, NT], F32)
    xh = ph.sb("xh", [128, DC, 16], F32)
    hx = ph.sb("hx", [128, DC, 16 + NT], F32)
    sA = [ph.sb("sA%d" % i, [128, 16 + NT], F32) for i in range(2)]
    sB = [ph.sb("sB%d" % i, [128, 16 + NT], F32) for i in range(2)]
    yT = ph.sb("yT", [128, DC, NT], BF16)
    wp = ph.sb("wps", [128, 4, 4, 512], BF16)
    sq = [ph.sb("sq%d" % i, [128, 512], BF16) for i in range(2)]
    rstd = ph.sb("rstd", [128, 512], F32)
    gcol = ph.sb("gcol", [128, DC], F32)
    psc = ph.sb("pscs", [128, DC], F32)
    invc = ph.sb("invcs", [128, 4, 16], F32)
    ones = ph.sb("ones", [128, 128], BF16)
    pss = ph.ps("pss", [128, 512])
    pz = [ph.ps("pz%d" % i, [128, 512]) for i in range(2)]

    P.dma("sp", lambda e: e.dma_start(out=gcol[:], in_=g_in[:, :]), writes=["gcol"])
    P.dma("sp", lambda e: e.dma_start(out=psc[:], in_=sc_in[:, :]), writes=["psc"])
    P.dma("sp", lambda e: e.dma_start(out=invc[:], in_=ic_in[:, :, :]), writes=["invc"])
    P.dve(lambda e: e.memset(ones[:], 1.0), writes=["ones"])
    wp_v = wp_in.rearrange("g (ci p) n -> p g ci n", p=128)
    for g in range(4):
        P.dma("pool", lambda e, g=g: e.dma_start(out=wp[:, g, :, :], in_=wp_v[:, g, :, :]), writes=["wp"])
    if fused:
        xmain_v = ph.bind["xT"].rearrange("(c p) t -> p c t", p=128)
        halo_v = ph.bind["halo"].rearrange("(c p) t -> p c t", p=128)
        isb = ph.sb("isb", [128, 16], F32)
        P.dma("sp", lambda e: e.dma_start(out=isb[:], in_=ph.bind["isb"][:, :]), writes=["isb"])
        P.dma("sp", lambda e: e.dma_start(out=xh[:], in_=halo_v[:, :, :]), writes=["xh"])
        P.dve(lambda e: e.tensor_scalar(out=xh[:], in0=xh[:], scalar1=isb[:, 0:1], scalar2=None, op0=ALU.mult),
              reads=["xh", "isb"], writes=["xh"])
        OFF = 0
    else:
        xin_v = xin.rearrange("(c p) t -> p c t", p=128)
        xmain_v = xin_v
        OFF = 16
        P.dma("sp", lambda e: e.dma_start(out=xh[:], in_=xin_v[:, :, 0:16]), writes=["xh"])
    xout_v = xout.rearrange("(c p) t -> p c t", p=128)
    emit_rmsnorm_cols(ph, xh, 0, hx, 0, 16, gcol, ones, sq, pss, "pss", rstd,
                      lambda c: "xh", lambda c: ("hx", c), "halo")
    cnt = 0
    for tt in range(TC // NT):
        for c in range(DC):
            P.dma("sp", lambda e, c=c, tt=tt: e.dma_start(out=xT[:, c, :], in_=xmain_v[:, c, OFF + tt * NT:OFF + (tt + 1) * NT]),
                  writes=[("x", c)])
        emit_rmsnorm_cols(ph, xT, 0, hx, 16, NT, gcol, ones, sq, pss, "pss", rstd,
                          lambda c: ("x", c), lambda c: ("hx", c), ("t", tt))
        W = 16 + NT
        for c in range(DC):
            g = c // 4
            eng = P.dve if c % 2 == 0 else P.pool
            a = sA[c % 2]
            b = sB[c % 2]
            ak = "sA%d" % (c % 2)
            bk = "sB%d" % (c % 2)
            eng(lambda e, a=a, c=c: e.tensor_tensor(out=a[:, 1:W], in0=hx[:, c, 1:W], in1=hx[:, c, 0:W - 1], op=ALU.add),
                reads=[("hx", c)], writes=[ak])
            cur, curk, oth, othk = a, ak, b, bk
            sh = 2
            for step in range(g):
                eng(lambda e, cur=cur, oth=oth, sh=sh: e.tensor_tensor(out=oth[:, 1 + sh:W], in0=cur[:, 1 + sh:W],
                                                                      in1=cur[:, 1:W - sh], op=ALU.add),
                    reads=[curk], writes=[othk])
                cur, curk, oth, othk = oth, othk, cur, curk
                sh *= 2
            w = POOL_W[g]
            P.dve(lambda e, cur=cur, c=c, w=w: e.scalar_tensor_tensor(out=yT[:, c, :], in0=cur[:, 16:W], scalar=1.0 / w,
                                                                    in1=hx[:, c, 16:W], op0=ALU.mult, op1=ALU.subtract),
                reads=[curk, ("hx", c)], writes=[("y", c)])
            if tt == 0:
                eng(lambda e, cur=cur, g=g: e.tensor_tensor(out=cur[:, 16:32], in0=cur[:, 16:32], in1=invc[:, g, :], op=ALU.mult),
                    reads=[curk, "invc", ("y", c)], writes=[curk])
                eng(lambda e, cur=cur, c=c: e.tensor_tensor(out=yT[:, c, 0:16], in0=cur[:, 16:32], in1=hx[:, c, 16:32],
                                                            op=ALU.subtract),
                    reads=[curk, ("hx", c)], writes=[("y", c)])
            if tt + 1 < TC // NT:
                eng(lambda e, c=c: e.tensor_copy(out=hx[:, c, 0:16], in_=hx[:, c, NT:NT + 16]),
                    reads=[("y", c), curk, ak, bk], writes=[("hx", c)])
        for d in range(DC):
            g = d // 4
            b = cnt % 2
            cnt += 1
            for ci in range(4):
                P.pe(lambda e, g=g, ci=ci, d=d, b=b: e.matmul(pz[b][:], wp[:, g, ci, (d % 4) * 128:(d % 4 + 1) * 128],
                                                             yT[:, 4 * g + ci, :], start=(ci == 0), stop=(ci == 3)),
                     reads=["wp", ("y", 4 * g + ci)], writes=["pz%d" % b])
            P.dve(lambda e, d=d, b=b: e.scalar_tensor_tensor(out=xT[:, d, :], in0=pz[b][:], scalar=psc[:, d:d + 1],
                                                            in1=xT[:, d, :], op0=ALU.mult, op1=ALU.add),
                  reads=["pz%d" % b, "psc", ("x", d)], writes=[("x", d)])
        for c in range(DC):
            P.dma("sp", lambda e, c=c, tt=tt: e.dma_start(out=xout_v[:, c, tt * NT:(tt + 1) * NT], in_=xT[:, c, :]),
                  reads=[("x", c)], writes=[("xo", tt, c)])
            ph.out_keys.append(("xo", tt, c))
    return ph.finish()


def col16(v):
    return np.ascontiguousarray(np.asarray(v, np.float32).reshape(DC, 128).T)


def pool_inputs(x_seq, half, g, wp, psc):
    xe = np.zeros((D, 16 + TC), np.float32)
    t0 = half * TC
    xe[:, 16:] = x_seq[t0:t0 + TC].T
    if half == 1:
        xe[:, :16] = x_seq[t0 - 16:t0].T
    invc = np.zeros((128, 4, 16), np.float32)
    for gi, w in enumerate(POOL_W):
        for t in range(16):
            cnt = min(t + 1, w) if half == 0 else w
            invc[:, gi, t] = 1.0 / cnt
    return {"xTe": xe, "g": col16(g), "wp": np.ascontiguousarray(wp, dtype=np.float32), "psc": col16(psc), "invc": invc}


GLA_DKT = 1024
GLA_NCOL = 6160


def build_gla_phase(ph=None, state_only=False, mode="full"):
    ph = ph or Phase()
    P = ph.P
    fused = ph.master is not None
    xin = ph.din("xT", [D, TC])
    g_in = ph.din("g", [128, DC])
    win = ph.din("win", [D, GLA_NCOL])
    wa2b_in = ph.din("wa2b", [17, GLA_DKT])
    gn_in = ph.din("gnb", [128, 2048])
    wout = ph.din("wout", [D, D])
    st_in = ph.bind.get("st_in") if fused else ph.din("st_in", [128, 8, 512])
    tri_in = ph.din("tri", [128, 128])
    id_in = ph.din("ident", [128, 128])
    xout = None if (state_only or mode == "proj") else ph.dout("xTo", [D, TC])
    st_out = ph.bind.get("st_out") if fused else ph.dout("st_out", [128, 8, 512])
    NT = 512
    NJ = 4
    xs = [ph.sb("xs%d" % i, [128, 512], F32) for i in range(3)]
    hT = ph.sb("hTs", [128, DC, NT], BF16)
    Wb = [ph.sb("Wb%d" % i, [128, DC, 512], BF16) for i in range(2)]
    Wa = ph.sb("Wa", [128, DC, 16], BF16)
    qT = ph.sb("qT", [128, 8, NT], BF16)
    kT = ph.sb("kT", [128, 8, NT], BF16)
    kdec = ph.sb("kdec", [128, NJ, 1024], BF16)
    kd_s = [ph.sb("kds%d" % i, [128, 512], BF16) for i in range(2)]
    vt = ph.sb("vt", [128, NJ, 2048], BF16)
    sr = ph.sb("sr", [128, NJ, 2048], BF16)
    gated2 = [ph.sb("gated%d" % i, [128, 2048], BF16) for i in range(2)]
    gT = ph.sb("gT", [128, DC, NT], BF16)
    S = ph.sb("S", [128, 8, 512], F32)
    Sb = ph.sb("Sb", [128, 8, 512], BF16)
    lt = ph.sb("lt", [128, NJ, 1024], F32)
    e1 = ph.sb("e1", [128, 1024], F32)
    Eq = [ph.sb("Eq%d" % i, [128, 512], F32) for i in range(1)]
    Ek = [ph.sb("Ek%d" % i, [128, 512], F32) for i in range(1)]
    Elast = ph.sb("Elast", [128, 8, TC // 128], F32)
    dr = ph.bind.get("dr")
    alr1 = ph.sb("alr1", [32, NT], F32)
    wa2b = ph.sb("wa2bs", [32, GLA_DKT], F32)
    gnb = ph.sb("gnbs", [128, 2048], BF16)
    tri = ph.sb("tris", [128, 128], F32)
    ident = ph.sb("idents", [128, 128], BF16)
    AT4 = [ph.sb("AT4%d" % i, [128, 128], BF16) for i in range(4)]
    osq = ph.sb("osq", [128, 2048], BF16)
    ssq = ph.sb("ssq", [128, 4], F32)
    sq = [ph.sb("sq%d" % i, [128, 512], BF16) for i in range(2)]
    rstd = ph.sb("rstd", [128, 512], F32)
    gcol = ph.sb("gcol", [128, DC], F32)
    ones = ph.sb("ones", [128, 128], BF16)
    pb = [ph.ps("pb%d" % i, [128, 512]) for i in range(8)]
    pbk = ["pb%d" % i for i in range(8)]

    P.dma("sp", lambda e: e.dma_start(out=gcol[:], in_=g_in[:, :]), writes=["gcol"])
    P.dma("sp", lambda e: e.dma_start(out=tri[:], in_=tri_in[:, :]), writes=["tri"])
    P.dma("sp", lambda e: e.dma_start(out=wa2b[0:17, :], in_=wa2b_in[:, :]), writes=["wa2b"])
    if st_in is None:
        P.dve(lambda e: e.memset(S[:], 0.0), writes=[("S", dc) for dc in range(8)])
    else:
        P.dma("sp", lambda e: e.dma_start(out=S[:], in_=st_in[:, :, :]), writes=[("S", dc) for dc in range(8)])
        if fused:
            isb = ph.sb("isb", [128, 16], F32)
            P.dma("sp", lambda e: e.dma_start(out=isb[:], in_=ph.bind["isb"][:, :]), writes=["isb"])
            P.dve(lambda e: e.tensor_scalar(out=S[:], in0=S[:], scalar1=isb[:, 0:1], scalar2=None, op0=ALU.mult),
                  reads=[("S", dc) for dc in range(8)] + ["isb"], writes=[("S", dc) for dc in range(8)])
    P.dma("pool", lambda e: e.dma_start(out=ident[:], in_=id_in[:, :]), writes=["ident"])
    P.dma("pool", lambda e: e.dma_start(out=gnb[:], in_=gn_in[:, :]), writes=["gnb"])
    P.dve(lambda e: e.memset(ones[:], 1.0), writes=["ones"])
    P.dve(lambda e: e.memset(alr1[:], 1.0), writes=["alr1"])
    P.act(lambda e: e.copy(out=Sb[:], in_=S[:]), reads=[("S", dc) for dc in range(8)], writes=[("Sb", dc) for dc in range(8)])
    win_v = win.rearrange("(c p) n -> p c n", p=128)
    wout_v = wout.rearrange("(c p) n -> p c n", p=128)
    xin_v = xin.rearrange("(c p) t -> p c t", p=128)
    xout_v = None if xout is None else xout.rearrange("(c p) t -> p c t", p=128)
    P.dma("pool", lambda e: e.dma_start(out=Wa[:], in_=win_v[:, :, 6144:6160]), writes=["Wa"])

    if mode == "scanf":
        P.dma("sp", lambda e: e.dma_start(out=Elast[:], in_=dr["elast"].rearrange("p (dc jj) -> p dc jj", dc=8)),
              writes=[("Elast", dc) for dc in range(8)])
    wctr = [0]

    def load_w(src_v, col0):
        i = wctr[0] % 2
        wctr[0] += 1
        P.dma("pool", lambda e, i=i: e.dma_start(out=Wb[i][:], in_=src_v[:, :, col0:col0 + 512]), writes=["Wb%d" % i])
        return Wb[i], "Wb%d" % i

    pctr = [0]

    def next_pb(lo=4, n=4):
        i = lo + pctr[0] % n
        pctr[0] += 1
        return pb[i], pbk[i]

    xctr = [0]
    for tt in range(TC // NT):
        tsl = slice(tt * NT, (tt + 1) * NT)
        if mode == "scanf":
            P.dma("sp", lambda e, tsl=tsl: e.dma_start(out=qT[:], in_=dr["q"].rearrange("(dc p) t -> p dc t", p=128)[:, :, tsl]),
                  writes=[("qT", dc) for dc in range(8)])
            P.dma("sp", lambda e, tsl=tsl: e.dma_start(out=kT[:], in_=dr["k"].rearrange("(dc p) t -> p dc t", p=128)[:, :, tsl]),
                  writes=[("kT", dc) for dc in range(8)])
            P.dma("sp", lambda e, tt=tt: e.dma_start(out=kdec[:], in_=dr["kdec"].rearrange("(jj p) d -> p jj d", p=128)[:, tt * NJ:(tt + 1) * NJ, :]),
                  writes=[("kdec", dc) for dc in range(8)])
            P.dma("sp", lambda e, tt=tt: e.dma_start(out=vt[:], in_=dr["v"].rearrange("(jj p) n -> p jj n", p=128)[:, tt * NJ:(tt + 1) * NJ, :]),
                  writes=[("vt", j, h) for j in range(NJ) for h in range(4)])
            P.dma("sp", lambda e, tt=tt: e.dma_start(out=sr[:], in_=dr["sr"].rearrange("(jj p) n -> p jj n", p=128)[:, tt * NJ:(tt + 1) * NJ, :]),
                  writes=[("sr", j) for j in range(NJ)])
        else:
            p_ss, p_ssk = pb[0], pbk[0]
            for c in range(DC):
                i = xctr[0] % 3
                xctr[0] += 1
                P.dma("sp", lambda e, i=i, c=c, tsl=tsl: e.dma_start(out=xs[i][:], in_=xin_v[:, c, tsl]), writes=["xs%d" % i])
                q = sq[c % 2]
                qk = "sq%d" % (c % 2)
                P.act(lambda e, q=q, i=i: e.activation(out=q[:], in_=xs[i][:], func=AF.Square), reads=["xs%d" % i], writes=[qk])
                P.pe(lambda e, q=q, c=c: e.matmul(p_ss[:], ones[:], q[:], start=(c == 0), stop=(c == DC - 1)),
                     reads=[qk, "ones"], writes=[p_ssk])
            P.dve(lambda e: e.tensor_scalar(out=rstd[:], in0=p_ss[:], scalar1=1.0 / D, scalar2=EPS, op0=ALU.mult, op1=ALU.add),
                  reads=[p_ssk], writes=["rstd"])
            P.act(lambda e: e.activation(out=rstd[:], in_=rstd[:], func=AF.Sqrt), reads=["rstd"], writes=["rstd"])
            P.dve(lambda e: e.reciprocal(out=rstd[:], in_=rstd[:]), reads=["rstd"], writes=["rstd"])
            for c in range(DC):
                i = xctr[0] % 3
                xctr[0] += 1
                P.dma("sp", lambda e, i=i, c=c, tsl=tsl: e.dma_start(out=xs[i][:], in_=xin_v[:, c, tsl]), writes=["xs%d" % i])
                P.dve(lambda e, i=i, c=c: e.scalar_tensor_tensor(out=hT[:, c, :], in0=xs[i][:], scalar=gcol[:, c:c + 1],
                                                                 in1=rstd[:], op0=ALU.mult, op1=ALU.mult),
                      reads=["xs%d" % i, "rstd", "gcol"], writes=[("h", c)])
            hk = [("h", c) for c in range(DC)]
            pa, pak = pb[1], pbk[1]
            for c in range(DC):
                P.pe(lambda e, c=c: e.matmul(pa[0:16, :], Wa[:, c, :], hT[:, c, :], start=(c == 0), stop=(c == DC - 1)),
                     reads=["Wa", ("h", c)], writes=[pak])
            P.act(lambda e: e.copy(out=alr1[0:16, :], in_=pa[0:16, :]), reads=[pak], writes=["alr1"])
            for j in range(NJ):
                jsl = slice(j * 128, (j + 1) * 128)
                for hf in range(2):
                    P.pe(lambda e, jsl=jsl, hf=hf: e.matmul(pb[2 + hf][:], alr1[0:17, jsl], wa2b[0:17, hf * 512:(hf + 1) * 512],
                                                            start=True, stop=True),
                         reads=["alr1", "wa2b"], writes=[pbk[2 + hf]])
                    P.act(lambda e, hf=hf: e.activation(out=e1[:, hf * 512:(hf + 1) * 512], in_=pb[2 + hf][:], func=AF.Exp, scale=-1.0),
                          reads=[pbk[2 + hf]], writes=[("e1", hf)])
                    P.act(lambda e, hf=hf, j=j: e.activation(out=lt[:, j, hf * 512:(hf + 1) * 512], in_=e1[:, hf * 512:(hf + 1) * 512],
                                                             func=AF.Ln, bias=1.0),
                          reads=[("e1", hf)], writes=[("lt", j)])
            pend_tr = []
            for blk in range(2):
                if not state_only:
                    Wq, Wqk = load_w(win_v, blk * 512)
                Wk, Wkk = load_w(win_v, 1024 + blk * 512)
                for dl in range(4):
                    dc = blk * 4 + dl
                    pbt, pbtk = pb[0], pbk[0]
                    for j in range(NJ):
                        jsl = slice(j * 128, (j + 1) * 128)
                        P.pe(lambda e, j=j, jsl=jsl, dc=dc: e.matmul(pbt[:, jsl], lt[:, j, dc * 128:(dc + 1) * 128], tri[:],
                                                                     start=True, stop=True),
                             reads=[("lt", j), "tri"], writes=[pbtk])
                    ei = 0
                    P.act(lambda e, ei=ei: e.activation(out=Eq[ei][:], in_=pbt[:], func=AF.Exp, scale=-1.0 / 16.0),
                          reads=[pbtk], writes=["Eq%d" % ei])
                    P.act(lambda e, ei=ei: e.activation(out=Ek[ei][:], in_=pbt[:], func=AF.Exp, scale=1.0 / 16.0),
                          reads=[pbtk], writes=["Ek%d" % ei])
                    P.dve(lambda e, ei=ei, dc=dc, tt=tt: e.tensor_copy(out=Elast[:, dc, tt * NJ:(tt + 1) * NJ], in_=Eq[ei][:, 127::128]),
                          reads=["Eq%d" % ei], writes=[("Elast", dc)])
                    if not state_only:
                        pq, pqk = next_pb()
                        for c in range(DC):
                            P.pe(lambda e, c=c, dl=dl, pq=pq, Wq=Wq: e.matmul(pq[:], Wq[:, c, dl * 128:(dl + 1) * 128], hT[:, c, :],
                                                                       start=(c == 0), stop=(c == DC - 1)),
                                 reads=[Wqk, ("h", c)], writes=[pqk])
                        P.dve(lambda e, pq=pq, ei=ei, dc=dc: e.scalar_tensor_tensor(out=qT[:, dc, :], in0=pq[:], scalar=1.0 / 16.0,
                                                                                    in1=Eq[ei][:], op0=ALU.mult, op1=ALU.mult),
                              reads=[pqk, "Eq%d" % ei], writes=[("qT", dc)])
                    pk_, pkk = next_pb()
                    for c in range(DC):
                        P.pe(lambda e, c=c, dl=dl, pk_=pk_, Wk=Wk: e.matmul(pk_[:], Wk[:, c, dl * 128:(dl + 1) * 128], hT[:, c, :],
                                                                     start=(c == 0), stop=(c == DC - 1)),
                             reads=[Wkk, ("h", c)], writes=[pkk])
                    P.dve(lambda e, pk_=pk_, ei=ei, dc=dc: e.tensor_tensor(out=kT[:, dc, :], in0=pk_[:], in1=Ek[ei][:], op=ALU.mult),
                          reads=[pkk, "Ek%d" % ei], writes=[("kT", dc)])
                    kd = kd_s[dc % 2]
                    kdk = "kds%d" % (dc % 2)
                    for j in range(NJ):
                        jsl = slice(j * 128, (j + 1) * 128)
                        P.dve(lambda e, kd=kd, dc=dc, j=j, jsl=jsl, tt=tt: e.tensor_scalar(out=kd[:, jsl], in0=kT[:, dc, jsl],
                                                                                    scalar1=Elast[:, dc, tt * NJ + j:tt * NJ + j + 1], scalar2=None,
                                                                                    op0=ALU.mult),
                              reads=[("kT", dc), ("Elast", dc)], writes=[kdk])
                    def emit_tr(kd=kd, kdk=kdk, dc=dc):
                        ptr, ptrk = next_pb()
                        for j2 in range(NJ):
                            jsl2 = slice(j2 * 128, (j2 + 1) * 128)
                            P.pe(lambda e, kd=kd, jsl2=jsl2, ptr=ptr: e.transpose(pbf(ptr)[:, jsl2], kd[:, jsl2], ident[:]),
                                 reads=[kdk, "ident"], writes=[ptrk])
                        P.act(lambda e, ptr=ptr, dc=dc: e.copy(out=kdec[:, :, dc * 128:(dc + 1) * 128],
                                                               in_=pbf(ptr)[:, 0:512].rearrange("p (j d) -> p j d", j=NJ)),
                              reads=[ptrk], writes=[("kdec", dc)])
                    pend_tr.append(emit_tr)
                    if len(pend_tr) > 1:
                        pend_tr.pop(0)()
            while pend_tr:
                pend_tr.pop(0)()
            for vb in range(4):
                Wv, Wvk = load_w(win_v, 2048 + vb * 512)
                for j in range(NJ):
                    jsl = slice(j * 128, (j + 1) * 128)
                    pv, pvk = next_pb()
                    for c in range(DC):
                        P.pe(lambda e, c=c, jsl=jsl, pv=pv, Wv=Wv: e.matmul(pv[:], hT[:, c, jsl], Wv[:, c, :],
                                                                            start=(c == 0), stop=(c == DC - 1)),
                             reads=[Wvk, ("h", c)], writes=[pvk])
                    P.act(lambda e, pv=pv, j=j, vb=vb: e.copy(out=vt[:, j, vb * 512:(vb + 1) * 512], in_=pv[:]),
                          reads=[pvk], writes=[("vt", j, vb)])
            for rb in (range(4) if not state_only else ()):
                Wr, Wrk = load_w(win_v, 4096 + rb * 512)
                for j in range(NJ):
                    jsl = slice(j * 128, (j + 1) * 128)
                    pv, pvk = next_pb()
                    for c in range(DC):
                        P.pe(lambda e, c=c, jsl=jsl, pv=pv, Wr=Wr: e.matmul(pv[:], hT[:, c, jsl], Wr[:, c, :],
                                                                            start=(c == 0), stop=(c == DC - 1)),
                             reads=[Wrk, ("h", c)], writes=[pvk])
                    P.act(lambda e, pv=pv, j=j, rb=rb: e.activation(out=sr[:, j, rb * 512:(rb + 1) * 512], in_=pv[:], func=AF.Silu),
                          reads=[pvk], writes=[("sr", j)])
        if mode == "proj":
            for j in range(NJ):
                P.pool(lambda e, j=j: e.tensor_tensor(out=sr[:, j, :], in0=sr[:, j, :], in1=gnb[:], op=ALU.mult),
                       reads=[("sr", j), "gnb"], writes=[("sr", j)])
            P.dma("sp", lambda e, tsl=tsl: e.dma_start(out=dr["q"].rearrange("(dc p) t -> p dc t", p=128)[:, :, tsl], in_=qT[:]),
                  reads=[("qT", dc) for dc in range(8)], writes=[("dq", tt)])
            P.dma("sp", lambda e, tsl=tsl: e.dma_start(out=dr["k"].rearrange("(dc p) t -> p dc t", p=128)[:, :, tsl], in_=kT[:]),
                  reads=[("kT", dc) for dc in range(8)], writes=[("dk", tt)])
            P.dma("sp", lambda e, tt=tt: e.dma_start(out=dr["kdec"].rearrange("(jj p) d -> p jj d", p=128)[:, tt * NJ:(tt + 1) * NJ, :], in_=kdec[:]),
                  reads=[("kdec", dc) for dc in range(8)], writes=[("dkd", tt)])
            P.dma("sp", lambda e, tt=tt: e.dma_start(out=dr["v"].rearrange("(jj p) n -> p jj n", p=128)[:, tt * NJ:(tt + 1) * NJ, :], in_=vt[:]),
                  reads=[("vt", j, h) for j in range(NJ) for h in range(4)], writes=[("dv", tt)])
            P.dma("sp", lambda e, tt=tt: e.dma_start(out=dr["sr"].rearrange("(jj p) n -> p jj n", p=128)[:, tt * NJ:(tt + 1) * NJ, :], in_=sr[:]),
                  reads=[("sr", j) for j in range(NJ)], writes=[("dsr", tt)])
            continue
        pend_g = []
        for j in range(NJ):
            jsl = slice(j * 128, (j + 1) * 128)
            gt_ = gated2[j % 2]
            gk_ = "gated%d" % (j % 2)
            if not state_only:
                if mode != "scanf":
                    P.pool(lambda e, j=j: e.tensor_tensor(out=sr[:, j, :], in0=sr[:, j, :], in1=gnb[:], op=ALU.mult),
                           reads=[("sr", j), "gnb"], writes=[("sr", j)])
                pst, pstk = pb[4], pbk[4]
                for h in range(4):
                    hs = slice(h * 128, (h + 1) * 128)
                    for di in range(2):
                        dc = 2 * h + di
                        P.pe(lambda e, dc=dc, di=di, hs=hs, jsl=jsl: e.matmul(pst[:, hs], kT[:, dc, jsl], qT[:, dc, jsl],
                                                                              start=(di == 0), stop=(di == 1)),
                             reads=[("kT", dc), ("qT", dc)], writes=[pstk])
                for h in range(4):
                    hs = slice(h * 128, (h + 1) * 128)
                    P.dve(lambda e, h=h, hs=hs: e.tensor_tensor(out=AT4[h][:], in0=pst[:, hs], in1=tri[:], op=ALU.mult),
                          reads=[pstk, "tri"], writes=["AT4%d" % h])
            for h in range(4):
                vkeys = [("vt", j, h)]
                for di in range(2):
                    dc = 2 * h + di
                    pkv, pkvk = pb[5 + di], pbk[5 + di]
                    P.pe(lambda e, dc=dc, j=j, h=h, pkv=pkv: e.matmul(pkv[:], kdec[:, j, dc * 128:(dc + 1) * 128],
                                                                      vt[:, j, h * 512:(h + 1) * 512], start=True, stop=True),
                         reads=[("kdec", dc)] + vkeys, writes=[pkvk])
                    P.dve(lambda e, dc=dc, j=j, pkv=pkv, tt=tt: e.scalar_tensor_tensor(out=S[:, dc, :], in0=S[:, dc, :],
                                                                                scalar=Elast[:, dc, tt * NJ + j:tt * NJ + j + 1], in1=pkv[:],
                                                                                op0=ALU.mult, op1=ALU.add),
                          reads=[pkvk, ("Elast", dc), ("S", dc)], writes=[("S", dc)])
                if not state_only:
                    for di in range(2):
                        dc = 2 * h + di
                        P.pe(lambda e, dc=dc, di=di, h=h, jsl=jsl: e.matmul(pb[h][:], qT[:, dc, jsl], Sb[:, dc, :],
                                                                            start=(di == 0), stop=False),
                             reads=[("qT", dc), ("Sb", dc)], writes=[pbk[h]])
                    P.pe(lambda e, h=h, j=j: e.matmul(pb[h][:], AT4[h][:], vt[:, j, h * 512:(h + 1) * 512], start=False, stop=True),
                         reads=["AT4%d" % h] + vkeys, writes=[pbk[h]])
                    for di in range(2):
                        dc = 2 * h + di
                        P.act(lambda e, dc=dc: e.copy(out=Sb[:, dc, :], in_=S[:, dc, :]), reads=[("S", dc)], writes=[("Sb", dc)])
            if state_only:
                continue
            for h in range(4):
                P.act(lambda e, h=h: e.activation(out=osq[:, h * 512:(h + 1) * 512], in_=pb[h][:], func=AF.Square),
                      reads=[pbk[h]], writes=[("osq", h)])
            P.dve(lambda e: e.reduce_sum(out=ssq[:], in_=osq[:].rearrange("p (h v) -> p h v", h=4), axis=AX.X),
                  reads=[("osq", h) for h in range(4)], writes=["ssq"])
            P.dve(lambda e: e.tensor_scalar(out=ssq[:], in0=ssq[:], scalar1=1.0 / 512.0, scalar2=EPS, op0=ALU.mult, op1=ALU.add),
                  reads=["ssq"], writes=["ssq"])
            P.act(lambda e: e.activation(out=ssq[:], in_=ssq[:], func=AF.Sqrt), reads=["ssq"], writes=["ssq"])
            P.dve(lambda e: e.reciprocal(out=ssq[:], in_=ssq[:]), reads=["ssq"], writes=["ssq"])
            for h in range(4):
                P.dve(lambda e, h=h, j=j, gt_=gt_: e.scalar_tensor_tensor(out=gt_[:, h * 512:(h + 1) * 512], in0=pb[h][:],
                                                                         scalar=ssq[:, h:h + 1], in1=sr[:, j, h * 512:(h + 1) * 512],
                                                                         op0=ALU.mult, op1=ALU.mult),
                      reads=[pbk[h], "ssq", ("sr", j)], writes=[(gk_, h)])

            def emit_gtr(gt_=gt_, gk_=gk_, j=j, jsl=jsl):
                for fb in range(4):
                    ptr, ptrk = pb[7], pbk[7]
                    for fi in range(4):
                        fc = fb * 4 + fi
                        P.pe(lambda e, fc=fc, fi=fi, ptr=ptr, gt_=gt_: e.transpose(pbf(ptr)[:, fi * 128:(fi + 1) * 128],
                                                                                  gt_[:, fc * 128:(fc + 1) * 128], ident[:]),
                             reads=[(gk_, fb), "ident"], writes=[ptrk])
                    P.act(lambda e, fb=fb, jsl=jsl, ptr=ptr: e.copy(out=gT[:, fb * 4:(fb + 1) * 4, jsl],
                                                                   in_=pbf(ptr)[:, 0:512].rearrange("p (f t) -> p f t", f=4)),
                          reads=[ptrk], writes=[("gT", fb, j)])
            pend_g.append(emit_gtr)
            if len(pend_g) > 1:
                pend_g.pop(0)()
        while pend_g:
            pend_g.pop(0)()
        for ob in (range(4) if not state_only else ()):
            Wo, Wok = load_w(wout_v, ob * 512)
            for dd in range(4):
                d = ob * 4 + dd
                py_, pyk = next_pb()
                for fc in range(DC):
                    P.pe(lambda e, fc=fc, dd=dd, py_=py_, Wo=Wo: e.matmul(py_[:], Wo[:, fc, dd * 128:(dd + 1) * 128], gT[:, fc, :],
                                                                          start=(fc == 0), stop=(fc == DC - 1)),
                         reads=[Wok] + [("gT", fc // 4, j) for j in range(NJ)], writes=[pyk])
                i = xctr[0] % 3
                xctr[0] += 1
                P.dma("sp", lambda e, i=i, d=d, tsl=tsl: e.dma_start(out=xs[i][:], in_=xin_v[:, d, tsl]), writes=["xs%d" % i])
                P.dve(lambda e, i=i, py_=py_: e.tensor_tensor(out=xs[i][:], in0=xs[i][:], in1=py_[:], op=ALU.add),
                      reads=[pyk, "xs%d" % i], writes=["xs%d" % i])
                P.dma("sp", lambda e, i=i, d=d, tsl=tsl: e.dma_start(out=xout_v[:, d, tsl], in_=xs[i][:]),
                      reads=["xs%d" % i], writes=[("xo", tt, d)])
                ph.out_keys.append(("xo", tt, d))
    if mode == "proj":
        P.dma("sp", lambda e: e.dma_start(out=dr["elast"].rearrange("p (dc jj) -> p dc jj", dc=8), in_=Elast[:]),
              reads=[("Elast", dc) for dc in range(8)], writes=["del"])
    if st_out is not None:
        P.dma("sp", lambda e: e.dma_start(out=st_out[:, :, :], in_=S[:]), reads=[("S", dc) for dc in range(8)],
              writes=["st_out"])
        ph.out_keys.append("st_out")
    return ph.finish()


def build_gla_scan0(ph):
    P = ph.P
    dr = ph.bind["dr"]
    st_out = ph.bind["st_out"]
    NCH = TC // 128
    kd_all = ph.sb("kd_all", [128, NCH, 1024], BF16)
    v_all = ph.sb("v_all", [128, NCH, 2048], BF16)
    El = ph.sb("El", [128, 8, NCH], F32)
    S = ph.sb("S", [128, 8, 512], F32)
    pb = [ph.ps("pb%d" % i, [128, 512]) for i in range(8)]
    kd_v = dr["kdec"].rearrange("(jj p) d -> p jj d", p=128)
    v_v = dr["v"].rearrange("(jj p) n -> p jj n", p=128)
    for q4 in range(4):
        P.dma("sp", lambda e, q4=q4: e.dma_start(out=kd_all[:, q4 * 4:(q4 + 1) * 4, :], in_=kd_v[:, q4 * 4:(q4 + 1) * 4, :]),
              writes=[("kd", q4)])
        P.dma("sp", lambda e, q4=q4: e.dma_start(out=v_all[:, q4 * 4:(q4 + 1) * 4, :], in_=v_v[:, q4 * 4:(q4 + 1) * 4, :]),
              writes=[("v", q4)])
    P.dma("sp", lambda e: e.dma_start(out=El[:], in_=dr["elast"].rearrange("p (dc jj) -> p dc jj", dc=8)), writes=["El"])
    P.dve(lambda e: e.memset(S[:], 0.0), writes=[("S", dc) for dc in range(8)])
    cnt = 0
    for jg in range(NCH):
        for h in range(4):
            for di in range(2):
                dc = 2 * h + di
                pkv, pkvk = pb[cnt % 8], "pb%d" % (cnt % 8)
                cnt += 1
                P.pe(lambda e, dc=dc, jg=jg, h=h, pkv=pkv: e.matmul(pkv[:], kd_all[:, jg, dc * 128:(dc + 1) * 128],
                                                                  v_all[:, jg, h * 512:(h + 1) * 512], start=True, stop=True),
                     reads=[("kd", jg // 4), ("v", jg // 4)], writes=[pkvk])
                P.dve(lambda e, dc=dc, jg=jg, pkv=pkv: e.scalar_tensor_tensor(out=S[:, dc, :], in0=S[:, dc, :],
                                                                             scalar=El[:, dc, jg:jg + 1], in1=pkv[:],
                                                                             op0=ALU.mult, op1=ALU.add),
                      reads=[pkvk, "El", ("S", dc)], writes=[("S", dc)])
    P.dma("sp", lambda e: e.dma_start(out=st_out[:, :, :], in_=S[:]), reads=[("S", dc) for dc in range(8)], writes=["st_out"])
    return ph.finish()


def pbf(ptile):
    return ptile[:].bitcast(BF16) if hasattr(ptile[:], "bitcast") else ptile


def gla_consts():
    tri = np.triu(np.ones((128, 128), np.float32))
    ident = np.eye(128, dtype=np.float32)
    return tri, ident


def gla_inputs(xT_core, g, w_in, w_a2, b_a, g_norm, w_out, state):
    tri, ident = gla_consts()
    st = np.ascontiguousarray(np.asarray(state, np.float32).reshape(4, 2, 128, 512).transpose(2, 0, 1, 3).reshape(128, 8, 512))
    gnb = np.ascontiguousarray(np.broadcast_to(np.tile(np.asarray(g_norm, np.float32), 4)[None, :], (128, 2048)))
    wa2b = np.ascontiguousarray(np.concatenate([np.asarray(w_a2, np.float32), np.asarray(b_a, np.float32)[None, :]], axis=0))
    return {"xT": np.ascontiguousarray(xT_core, dtype=np.float32), "g": col16(g), "win": np.ascontiguousarray(w_in, dtype=np.float32),
            "wa2b": wa2b, "gnb": gnb, "wout": np.ascontiguousarray(w_out, dtype=np.float32), "st_in": st, "tri": tri, "ident": ident}


def gla_state_from_out(st_out):
    return np.ascontiguousarray(st_out.reshape(128, 4, 2, 512).transpose(1, 2, 0, 3).reshape(4, 256, 512))


def build_diff1_phase(do_qk=True, do_v=True, do_norm=True, ph=None):
    ph = ph or Phase()
    P = ph.P
    xin = ph.din("xT", [D, TC])
    g_in = ph.din("g", [128, DC])
    win = ph.din("win", [D, 3 * D])
    qkg_in = ph.din("qkg", [128, 16])
    qko = ph.dout("qkT", [2 * D, TC])
    vo = ph.dout("v", [TC, D])
    NT = 512
    xs = [ph.sb("xs%d" % i, [128, 512], F32) for i in range(3)]
    hT = ph.sb("hTs", [128, DC, NT], BF16)
    Wb = [ph.sb("Wb%d" % i, [128, DC, 512], BF16) for i in range(2)]
    qraw = [ph.sb("qraw%d" % i, [128, 512], F32) for i in range(2)]
    qn = [ph.sb("qn%d" % i, [128, 512], F32) for i in range(3)]
    vs = [ph.sb("vs%d" % i, [128, 512], F32) for i in range(3)]
    sq = [ph.sb("sq%d" % i, [128, 512], BF16) for i in range(2)]
    rstd = ph.sb("rstd", [128, 512], F32)
    rs2 = [ph.sb("rs2%d" % i, [128, 512], F32) for i in range(2)]
    gcol = ph.sb("gcol", [128, DC], F32)
    qkg = ph.sb("qkgs", [128, 16], F32)
    ones = ph.sb("ones", [128, 128], BF16)
    pb = [ph.ps("pb%d" % i, [128, 512]) for i in range(8)]
    pbk = ["pb%d" % i for i in range(8)]
    P.dma("sp", lambda e: e.dma_start(out=gcol[:], in_=g_in[:, :]), writes=["gcol"])
    P.dma("sp", lambda e: e.dma_start(out=qkg[:], in_=qkg_in[:, :]), writes=["qg", "kg"])
    P.dve(lambda e: e.memset(ones[:], 1.0), writes=["ones"])
    win_v = win.rearrange("(c p) n -> p c n", p=128)
    xin_v = xin.rearrange("(c p) t -> p c t", p=128)
    k_ds = ph.bind.get("k_ds")
    v_ds = ph.bind.get("v_ds")
    if k_ds is not None:
        q_v = ph.bind["q_d"].rearrange("(c p) t -> p c t", p=128)
        k_vs = [kd_.rearrange("(c p) t -> p c t", p=128) for kd_ in k_ds]
        v_vs = [vd_.rearrange("(j p) n -> p j n", p=128) for vd_ in v_ds]
    else:
        qko_v = qko.rearrange("(c p) t -> p c t", p=128)
    vo_v = vo.rearrange("(j p) n -> p j n", p=128)
    xctr = [0]
    wctr = [0]
    pctr = [0]
    nctr = [0]

    def load_w(col0):
        i = wctr[0] % 2
        wctr[0] += 1
        P.dma("pool", lambda e, i=i, col0=col0: e.dma_start(out=Wb[i][:], in_=win_v[:, :, col0:col0 + 512]), writes=["Wb%d" % i])
        return Wb[i], "Wb%d" % i

    def next_pb():
        i = 2 + pctr[0] % 4
        pctr[0] += 1
        return pb[i], pbk[i]

    for tt in range(TC // NT):
        tsl = slice(tt * NT, (tt + 1) * NT)
        p_ss, p_ssk = pb[0], pbk[0]
        for c in range(DC):
            i = xctr[0] % 3
            xctr[0] += 1
            P.dma("sp", lambda e, i=i, c=c, tsl=tsl: e.dma_start(out=xs[i][:], in_=xin_v[:, c, tsl]), writes=["xs%d" % i])
            q = sq[c % 2]
            qk = "sq%d" % (c % 2)
            P.act(lambda e, q=q, i=i: e.activation(out=q[:], in_=xs[i][:], func=AF.Square), reads=["xs%d" % i], writes=[qk])
            P.pe(lambda e, q=q, c=c: e.matmul(p_ss[:], ones[:], q[:], start=(c == 0), stop=(c == DC - 1)),
                 reads=[qk, "ones"], writes=[p_ssk])
        P.dve(lambda e: e.tensor_scalar(out=rstd[:], in0=p_ss[:], scalar1=1.0 / D, scalar2=EPS, op0=ALU.mult, op1=ALU.add),
              reads=[p_ssk], writes=["rstd"])
        P.act(lambda e: e.activation(out=rstd[:], in_=rstd[:], func=AF.Sqrt), reads=["rstd"], writes=["rstd"])
        P.dve(lambda e: e.reciprocal(out=rstd[:], in_=rstd[:]), reads=["rstd"], writes=["rstd"])
        for c in range(DC):
            i = xctr[0] % 3
            xctr[0] += 1
            P.dma("sp", lambda e, i=i, c=c, tsl=tsl: e.dma_start(out=xs[i][:], in_=xin_v[:, c, tsl]), writes=["xs%d" % i])
            P.dve(lambda e, i=i, c=c: e.scalar_tensor_tensor(out=hT[:, c, :], in0=xs[i][:], scalar=gcol[:, c:c + 1],
                                                             in1=rstd[:], op0=ALU.mult, op1=ALU.mult),
                  reads=["xs%d" % i, "rstd", "gcol"], writes=[("h", c)])
        for which in (range(2) if do_qk else ()):
            gk = "qg" if which == 0 else "kg"
            for blk in range(4):
                Wq, Wqk = load_w(which * D + blk * 512)
                for dl in range(4):
                    hd = blk * 4 + dl
                    pq, pqk = next_pb()
                    for c in range(DC):
                        P.pe(lambda e, c=c, dl=dl, pq=pq, Wq=Wq: e.matmul(pq[:], Wq[:, c, dl * 128:(dl + 1) * 128], hT[:, c, :],
                                                                          start=(c == 0), stop=(c == DC - 1)),
                             reads=[Wqk, ("h", c)], writes=[pqk])
                    if do_norm:
                        qr = qraw[hd % 2]
                        qrk = "qraw%d" % (hd % 2)
                        sqb = sq[hd % 2]
                        sqk = "sq%d" % (hd % 2)
                        P.act(lambda e, pq=pq, sqb=sqb: e.activation(out=sqb[:], in_=pq[:], func=AF.Square), reads=[pqk], writes=[sqk])
                        P.act(lambda e, pq=pq, qr=qr: e.copy(out=qr[:], in_=pq[:]), reads=[pqk], writes=[qrk])
                        p2, p2k = pb[6 + hd % 2], pbk[6 + hd % 2]
                        P.pe(lambda e, p2=p2, sqb=sqb: e.matmul(p2[:], ones[:], sqb[:], start=True, stop=True),
                             reads=[sqk, "ones"], writes=[p2k])
                        r2 = rs2[hd % 2]
                        r2k = "rs2%d" % (hd % 2)
                        P.dve(lambda e, p2=p2, r2=r2: e.tensor_scalar(out=r2[:], in0=p2[:], scalar1=1.0 / 128.0, scalar2=EPS,
                                                                      op0=ALU.mult, op1=ALU.add), reads=[p2k], writes=[r2k])
                        P.act(lambda e, r2=r2: e.activation(out=r2[:], in_=r2[:], func=AF.Sqrt), reads=[r2k], writes=[r2k])
                        P.dve(lambda e, r2=r2: e.reciprocal(out=r2[:], in_=r2[:]), reads=[r2k], writes=[r2k])
                        ni = nctr[0] % 3
                        nctr[0] += 1
                        P.dve(lambda e, ni=ni, qr=qr, r2=r2, which=which: e.scalar_tensor_tensor(out=qn[ni][:], in0=qr[:], scalar=qkg[:, which:which + 1],
                                                                                              in1=r2[:], op0=ALU.mult, op1=ALU.mult),
                              reads=[qrk, r2k, gk], writes=["qn%d" % ni])
                    else:
                        ni = nctr[0] % 3
                        nctr[0] += 1
                        P.act(lambda e, pq=pq, ni=ni: e.copy(out=qn[ni][:], in_=pq[:]), reads=[pqk], writes=["qn%d" % ni])
                    if k_ds is not None:
                        dst_ap = q_v[:, hd, tsl] if which == 0 else k_vs[hd // 2][:, hd % 2, tsl]
                    else:
                        dst_ap = qko_v[:, which * 16 + hd, tsl]
                    P.dma("sp", lambda e, ni=ni, dst_ap=dst_ap: e.dma_start(out=dst_ap, in_=qn[ni][:]),
                          reads=["qn%d" % ni], writes=[("qo", which, tt, hd)])
                    ph.out_keys.append(("qo", which, tt, hd))
        for vb in (range(4) if do_v else ()):
            Wv, Wvk = load_w(2 * D + vb * 512)
            for j in range(4):
                jsl = slice(j * 128, (j + 1) * 128)
                pv, pvk = next_pb()
                for c in range(DC):
                    P.pe(lambda e, c=c, jsl=jsl, pv=pv, Wv=Wv: e.matmul(pv[:], hT[:, c, jsl], Wv[:, c, :],
                                                                        start=(c == 0), stop=(c == DC - 1)),
                         reads=[Wvk, ("h", c)], writes=[pvk])
                vi = nctr[0] % 3
                nctr[0] += 1
                P.act(lambda e, pv=pv, vi=vi: e.copy(out=vs[vi][:], in_=pv[:]), reads=[pvk], writes=["vs%d" % vi])
                if v_ds is not None:
                    for hh in range(2):
                        P.dma("sp", lambda e, vi=vi, tt=tt, j=j, vb=vb, hh=hh: e.dma_start(
                            out=v_vs[2 * vb + hh][:, tt * 4 + j, :], in_=vs[vi][:, hh * 256:(hh + 1) * 256]),
                            reads=["vs%d" % vi], writes=[("vo", tt, j, vb, hh)])
                else:
                    P.dma("sp", lambda e, vi=vi, tt=tt, j=j, vb=vb: e.dma_start(out=vo_v[:, tt * 4 + j, vb * 512:(vb + 1) * 512], in_=vs[vi][:]),
                          reads=["vs%d" % vi], writes=[("vo", tt, j, vb)])
                    ph.out_keys.append(("vo", tt, j, vb))
    return ph.finish()


LAM_INIT2 = 0.8 - 0.6 * math.exp(-0.3 * 2)
NEG = -30000.0


def build_diff2_phase(ph=None):
    ph = ph or Phase()
    P = ph.P
    fused = ph.master is not None
    qin = ph.din("qT", [D, TC])
    kin = None if fused else ph.din("kT", [D, 2 * TC])
    vin = None if fused else ph.din("va", [2 * TC, 8, 257])
    xin = ph.din("xT", [D, TC])
    wout = ph.din("wout", [D, D])
    b0_in = ph.din("B0", [128, 16, 128])
    b1_in = ph.din("B1", [128, 16, 128])
    b1p_in = ph.din("B1p", [128, 16, 128])
    cf_in = ph.din("cfar", [128, 16])
    cp_in = ph.din("cpre", [128, 16])
    lp_in = ph.din("lpb", [128, 4, 128])
    sg_in = ph.din("sgb", [128, 256])
    id_in = ph.din("ident", [128, 128])
    xout = ph.dout("xTo", [D, TC])
    SCALE = 128 ** -0.5
    Kt = ph.sb("Kt", [128, 2, 2 * TC], BF16)
    Va = ph.sb("Va", [128, 32, 257], BF16)
    Qt = ph.sb("Qt", [128, 2, TC], BF16)
    ao = ph.sb("ao", [128, 16, 2048], BF16)
    M0 = ph.sb("M0", [128, 16, 128], BF16)
    M1 = ph.sb("M1", [128, 16, 128], BF16)
    M1p = ph.sb("M1p", [128, 16, 128], BF16)
    btmp = ph.sb("btmp", [128, 16, 128], F32)
    cfar = ph.sb("cfars", [128, 16], F32)
    cpre = ph.sb("cpres", [128, 16], F32)
    negc = ph.sb("negc", [128, 16], F32)
    negcp = ph.sb("negcp", [128, 16], F32)
    lpb = ph.sb("lpbs", [128, 4, 128], F32)
    lt1 = ph.sb("lt1", [128, 128], F32)
    lsum = ph.sb("lsum", [128, 2], F32)
    neglam = ph.sb("neglam", [128, 1], F32)
    sgs = ph.sb("sgs", [128, 256], F32)
    ident = ph.sb("idents", [128, 128], BF16)
    PT = [ph.sb("PT%d" % i, [128, 2, 256], BF16) for i in range(3)]
    rc = ph.sb("rc", [128, 4], F32)
    uu = ph.sb("uu", [128, 256], F32)
    att = ph.sb("att", [128, 256], F32)
    asq = ph.sb("asq", [128, 256], F32)
    ssn = ph.sb("ssn", [128, 1], F32)
    aoT = ph.sb("aoT", [128, DC, 512], BF16)
    Wb = [ph.sb("Wb%d" % i, [128, DC, 512], BF16) for i in range(2)]
    xs = [ph.sb("xs%d" % i, [128, 512], F32) for i in range(3)]
    pb = [ph.ps("pb%d" % i, [128, 512]) for i in range(8)]
    pbk = ["pb%d" % i for i in range(8)]

    for (dst, src, k) in ((cfar, cf_in, "cfar"), (cpre, cp_in, "cpre"), (sgs, sg_in, "sgs")):
        P.dma("sp", lambda e, dst=dst, src=src: e.dma_start(out=dst[:], in_=src[:, :]), writes=[k])
    P.dma("sp", lambda e: e.dma_start(out=lpb[:], in_=lp_in[:, :, :]), writes=["lpb"])
    P.dma("pool", lambda e: e.dma_start(out=ident[:], in_=id_in[:, :]), writes=["ident"])
    P.dve(lambda e: e.tensor_scalar(out=negc[:], in0=cfar[:], scalar1=-1.0, scalar2=None, op0=ALU.mult), reads=["cfar"], writes=["negc"])
    P.dve(lambda e: e.tensor_scalar(out=negcp[:], in0=cpre[:], scalar1=-1.0, scalar2=None, op0=ALU.mult), reads=["cpre"], writes=["negcp"])
    for (Mt, src, nb, k) in ((M0, b0_in, negc, "M0"), (M1, b1_in, negc, "M1"), (M1p, b1p_in, negc, "M1p")):
        P.dma("sp", lambda e, src=src: e.dma_start(out=btmp[:], in_=src[:, :, :]), writes=["btmp"])
        for h in range(16):
            P.act(lambda e, Mt=Mt, nb=nb, h=h: e.activation(out=Mt[:, h, :], in_=btmp[:, h, :], func=AF.Exp, bias=nb[:, h:h + 1]),
                  reads=["btmp", "negc", "negcp"], writes=[k])
    for pi in range(2):
        P.dve(lambda e, pi=pi: e.tensor_tensor(out=lt1[:], in0=lpb[:, 2 * pi, :], in1=lpb[:, 2 * pi + 1, :], op=ALU.mult),
              reads=["lpb"], writes=["lt1"])
        P.dve(lambda e, pi=pi: e.reduce_sum(out=lsum[:, pi:pi + 1], in_=lt1[:], axis=AX.X), reads=["lt1"], writes=["lsum"])
    P.act(lambda e: e.activation(out=lsum[:], in_=lsum[:], func=AF.Exp), reads=["lsum"], writes=["lsum"])
    P.dve(lambda e: e.tensor_tensor(out=neglam[:], in0=lsum[:, 1:2], in1=lsum[:, 0:1], op=ALU.subtract), reads=["lsum"], writes=["neglam"])
    P.dve(lambda e: e.tensor_scalar(out=neglam[:], in0=neglam[:], scalar1=-LAM_INIT2, scalar2=None, op0=ALU.add),
          reads=["neglam"], writes=["neglam"])
    P.dve(lambda e: e.tensor_scalar(out=sgs[:], in0=sgs[:], scalar1=1.0 - LAM_INIT2, scalar2=None, op0=ALU.mult),
          reads=["sgs"], writes=["sgs"])

    qin_v = qin.rearrange("(c p) t -> p c t", p=128)
    if fused:
        kpre_vs = [a_.rearrange("(c p) t -> p c t", p=128) for a_ in ph.bind["kpres"]]
        kown_vs = [a_.rearrange("(c p) t -> p c t", p=128) for a_ in ph.bind["kowns"]]
        vpre_vs = [a_.rearrange("(kb p) n -> p kb n", p=128) for a_ in ph.bind["vpres"]]
        vown_vs = [a_.rearrange("(kb p) n -> p kb n", p=128) for a_ in ph.bind["vowns"]]
        P.dve(lambda e: e.memset(Va[:, :, 256:257], 1.0), writes=["Va1"])
    else:
        kin_v = kin.rearrange("(c p) t -> p c t", p=128)
        vin_v = vin.rearrange("(kb p) h n -> p kb h n", p=128)
    xin_v = xin.rearrange("(c p) t -> p c t", p=128)
    xout_v = xout.rearrange("(c p) t -> p c t", p=128)
    wout_v = wout.rearrange("(c p) n -> p c n", p=128)
    LOOK = 2
    NPT = 4
    PTs = [ph.sb("PTp%d" % i, [128, 2, 256], BF16) for i in range(NPT)]
    Osb = [ph.sb("Osb%d" % i, [128, 257], F32) for i in range(8)]
    zero_b = ph.sb("zero_b", [128, 1], F32)
    P.dve(lambda e: e.memset(zero_b[:], 0.0), writes=["zero_b"])
    gstep = [0]

    def stage_a(hp, qt, kb, sidx):
        kb_rel = kb - (16 + 2 * qt)
        qlo = 128 if kb_rel == 1 else 0
        ps, psk = pb[4 + sidx % 3], pbk[4 + sidx % 3]
        pt, ptk = PTs[sidx % NPT], "PTp%d" % (sidx % NPT)
        for i in range(2):
            P.pe(lambda e, ps=ps, i=i, kb=kb, qt=qt, qlo=qlo: e.matmul(
                ps[:, i * 256 + qlo:(i + 1) * 256], Kt[:, i, kb * 128:(kb + 1) * 128],
                Qt[:, i, qt * 256 + qlo:(qt + 1) * 256], start=True, stop=True),
                reads=["Kt", "Qt"], writes=[psk])
        bias_ap = cpre[:, 0:1] if kb < 16 else zero_b[:, 0:1]
        P.act(lambda e, ps=ps, pt=pt, qlo=qlo, bias_ap=bias_ap: e.activation(
            out=pt[:, :, qlo:256], in_=ps[:].rearrange("p (i q) -> p i q", i=2)[:, :, qlo:256], func=AF.Exp,
            bias=bias_ap, scale=SCALE),
            reads=[psk, "cpre", "zero_b"], writes=[ptk])
        fix = []
        if kb_rel == -1:
            fix.append((0, M1p if kb == 15 else M1, "M1p" if kb == 15 else "M1"))
        elif kb_rel == 0:
            fix.append((0, M0, "M0"))
            fix.append((1, M1, "M1"))
        elif kb_rel == 1:
            fix.append((1, M0, "M0"))
        for (qb, Mt, mk) in fix:
            P.dve(lambda e, pt=pt, qb=qb, Mt=Mt, hp=hp: e.tensor_tensor(
                out=pt[:, :, qb * 128:(qb + 1) * 128], in0=pt[:, :, qb * 128:(qb + 1) * 128],
                in1=Mt[:, 2 * hp:2 * hp + 2, :], op=ALU.mult),
                reads=[ptk, mk], writes=[ptk])

    def stage_b(hp, qt, kb, sidx):
        pt, ptk = PTs[sidx % NPT], "PTp%d" % (sidx % NPT)
        for qb in range(2):
            last = 16 + 2 * qt + qb
            if kb > last:
                continue
            for i in range(2):
                acc = pb[qb * 2 + i]
                P.pe(lambda e, acc=acc, pt=pt, i=i, qb=qb, kb=kb, last=last: e.matmul(
                    acc[:, 0:257], pt[:, i, qb * 128:(qb + 1) * 128], Va[:, kb, :],
                    start=(kb == 0), stop=(kb == last)),
                    reads=[ptk, "Va"], writes=[pbk[qb * 2 + i]])
        if kb == 16 + 2 * qt + 1:
            finalize(hp, qt)

    fctr = [0]

    def finalize(hp, qt):
        fs = (fctr[0] % 2) * 4
        fctr[0] += 1
        for a_i in range(4):
            P.dve(lambda e, a_i=a_i, fs=fs: e.tensor_scalar(out=Osb[fs + a_i][:], in0=pb[a_i][:, 0:257], scalar1=1.0, scalar2=None,
                                                            op0=ALU.mult),
                  reads=[pbk[a_i]], writes=["Osb%d" % (fs + a_i)])
        for qb in range(2):
            o1, o1k = Osb[fs + qb * 2], "Osb%d" % (fs + qb * 2)
            o2, o2k = Osb[fs + qb * 2 + 1], "Osb%d" % (fs + qb * 2 + 1)
            P.dve(lambda e, o1=o1: e.reciprocal(out=rc[:, 0:1], in_=o1[:, 256:257]), reads=[o1k], writes=["rc"])
            P.dve(lambda e, o2=o2: e.reciprocal(out=rc[:, 1:2], in_=o2[:, 256:257]), reads=[o2k], writes=["rc"])
            P.dve(lambda e: e.tensor_tensor(out=rc[:, 1:2], in0=rc[:, 1:2], in1=neglam[:], op=ALU.mult),
                  reads=["rc", "neglam"], writes=["rc"])
            P.dve(lambda e, o2=o2: e.tensor_scalar(out=uu[:], in0=o2[:, 0:256], scalar1=rc[:, 1:2], scalar2=None, op0=ALU.mult),
                  reads=[o2k, "rc"], writes=["uu"])
            P.dve(lambda e, o1=o1: e.scalar_tensor_tensor(out=att[:], in0=o1[:, 0:256], scalar=rc[:, 0:1], in1=uu[:],
                                                          op0=ALU.mult, op1=ALU.add),
                  reads=[o1k, "rc", "uu"], writes=["att"])
            P.dve(lambda e: e.tensor_tensor(out=asq[:], in0=att[:], in1=att[:], op=ALU.mult), reads=["att"], writes=["asq"])
            P.dve(lambda e: e.reduce_sum(out=ssn[:], in_=asq[:], axis=AX.X), reads=["asq"], writes=["ssn"])
            P.dve(lambda e: e.tensor_scalar(out=ssn[:], in0=ssn[:], scalar1=1.0 / 256.0, scalar2=EPS, op0=ALU.mult, op1=ALU.add),
                  reads=["ssn"], writes=["ssn"])
            P.act(lambda e: e.activation(out=ssn[:], in_=ssn[:], func=AF.Sqrt), reads=["ssn"], writes=["ssn"])
            P.dve(lambda e: e.reciprocal(out=ssn[:], in_=ssn[:]), reads=["ssn"], writes=["ssn"])
            qbg = 2 * qt + qb
            P.dve(lambda e, qbg=qbg, hp=hp: e.scalar_tensor_tensor(out=ao[:, qbg, hp * 256:(hp + 1) * 256], in0=att[:],
                                                                   scalar=ssn[:, 0:1], in1=sgs[:], op0=ALU.mult, op1=ALU.mult),
                  reads=["att", "ssn", "sgs"], writes=[("ao", qbg)])

    for hp in range(8):
        if fused:
            P.dma("pool", lambda e, hp=hp: e.dma_start(out=Kt[:, :, 0:TC], in_=kpre_vs[hp][:, :, :]), writes=["Kt"])
            P.dma("pool", lambda e, hp=hp: e.dma_start(out=Kt[:, :, TC:2 * TC], in_=kown_vs[hp][:, :, :]), writes=["Kt"])
            P.dma("pool", lambda e, hp=hp: e.dma_start(out=Va[:, 0:16, 0:256], in_=vpre_vs[hp][:, :, :]), reads=["Va1"], writes=["Va"])
            P.dma("pool", lambda e, hp=hp: e.dma_start(out=Va[:, 16:32, 0:256], in_=vown_vs[hp][:, :, :]), reads=["Va1"], writes=["Va"])
        else:
            P.dma("pool", lambda e, hp=hp: e.dma_start(out=Kt[:, :, 0:TC], in_=kin_v[:, 2 * hp:2 * hp + 2, 0:TC]), writes=["Kt"])
            P.dma("pool", lambda e, hp=hp: e.dma_start(out=Kt[:, :, TC:2 * TC], in_=kin_v[:, 2 * hp:2 * hp + 2, TC:2 * TC]), writes=["Kt"])
            P.dma("pool", lambda e, hp=hp: e.dma_start(out=Va[:, 0:16, :], in_=vin_v[:, 0:16, hp, :]), writes=["Va"])
            P.dma("pool", lambda e, hp=hp: e.dma_start(out=Va[:, 16:32, :], in_=vin_v[:, 16:32, hp, :]), writes=["Va"])
        P.dma("pool", lambda e, hp=hp: e.dma_start(out=Qt[:], in_=qin_v[:, 2 * hp:2 * hp + 2, :]), writes=["Qt"])
        steps = [(qt, kb) for qt in range(8) for kb in range(16 + 2 * qt + 2)]
        base = gstep[0]
        for n in range(len(steps) + LOOK):
            if n < len(steps):
                stage_a(hp, steps[n][0], steps[n][1], base + n)
            if n >= LOOK:
                stage_b(hp, steps[n - LOOK][0], steps[n - LOOK][1], base + n - LOOK)
        gstep[0] += len(steps)
    xctr = [0]
    wctr = [0]
    pctr = [0]
    for tt in range(4):
        tsl = slice(tt * 512, (tt + 1) * 512)
        for j in range(4):
            qbg = tt * 4 + j
            for fb in range(4):
                ptr, ptrk = pb[6], pbk[6]
                for fi in range(4):
                    fc = fb * 4 + fi
                    P.pe(lambda e, ptr=ptr, fi=fi, fc=fc, qbg=qbg: e.transpose(pbf(ptr)[:, fi * 128:(fi + 1) * 128],
                                                                              ao[:, qbg, fc * 128:(fc + 1) * 128], ident[:]),
                         reads=[("ao", qbg), "ident"], writes=[ptrk])
                P.act(lambda e, ptr=ptr, fb=fb, j=j: e.copy(out=aoT[:, fb * 4:(fb + 1) * 4, j * 128:(j + 1) * 128],
                                                            in_=pbf(ptr)[:, 0:512].rearrange("p (f t) -> p f t", f=4)),
                      reads=[ptrk], writes=[("aoT", fb)])
        for ob in range(4):
            wi = wctr[0] % 2
            wctr[0] += 1
            P.dma("pool", lambda e, wi=wi, ob=ob: e.dma_start(out=Wb[wi][:], in_=wout_v[:, :, ob * 512:(ob + 1) * 512]),
                  writes=["Wb%d" % wi])
            for dd in range(4):
                d = ob * 4 + dd
                pi = pctr[0] % 2
                pctr[0] += 1
                py_, pyk = pb[pi], pbk[pi]
                for fc in range(DC):
                    P.pe(lambda e, fc=fc, dd=dd, py_=py_, wi=wi: e.matmul(py_[:], Wb[wi][:, fc, dd * 128:(dd + 1) * 128], aoT[:, fc, :],
                                                                          start=(fc == 0), stop=(fc == DC - 1)),
                         reads=["Wb%d" % wi, ("aoT", fc // 4)], writes=[pyk])
                xi = xctr[0] % 3
                xctr[0] += 1
                P.dma("sp", lambda e, xi=xi, d=d, tsl=tsl: e.dma_start(out=xs[xi][:], in_=xin_v[:, d, tsl]), writes=["xs%d" % xi])
                P.dve(lambda e, xi=xi, py_=py_: e.tensor_tensor(out=xs[xi][:], in0=xs[xi][:], in1=py_[:], op=ALU.add),
                      reads=[pyk, "xs%d" % xi], writes=["xs%d" % xi])
                P.dma("sp", lambda e, xi=xi, d=d, tsl=tsl: e.dma_start(out=xout_v[:, d, tsl], in_=xs[xi][:]),
                      reads=["xs%d" % xi], writes=[("xo", tt, d)])
                ph.out_keys.append(("xo", tt, d))
    return ph.finish()


def rel_bucket_np(rel):
    n = np.maximum(rel, 0)
    nf = np.maximum(n, 1).astype(np.float32)
    large = 16 + (np.log(nf / np.float32(16)) / np.float32(math.log(128 / 16)) * np.float32(16)).astype(np.int32)
    large = np.minimum(large, 31)
    return np.where(n < 16, n, large)


def diff_bias_tiles(rel_bias, first_half):
    tab = np.concatenate([np.asarray(rel_bias, np.float32), np.full((1, 16), NEG, np.float32),
                          np.full((1, 16), 2 * NEG, np.float32)], axis=0)
    k = np.arange(128)[:, None]
    q = np.arange(128)[None, :]
    rel0 = q - k
    idx0 = np.where(rel0 >= 0, rel_bucket_np(rel0), 32)
    idx1 = rel_bucket_np(128 + q - k)
    B0 = np.ascontiguousarray(tab[idx0].transpose(0, 2, 1))
    B1 = np.ascontiguousarray(tab[idx1].transpose(0, 2, 1))
    if first_half:
        B1p = np.full((128, 16, 128), 2 * NEG, np.float32)
        cpre = np.full((128, 16), NEG, np.float32)
    else:
        B1p = B1.copy()
        cpre = np.zeros((128, 16), np.float32)
    cfar = np.ascontiguousarray(np.broadcast_to(tab[31][None, :], (128, 16)))
    return B0, B1, B1p, cfar, cpre


def diff2_inputs(qT, kT_own, v_own, kT_prev, v_prev, xT, w_out, rel_bias, lam_params, sub_gain, first_half):
    kT_all = np.zeros((D, 2 * TC), np.float32)
    va = np.zeros((2 * TC, 8, 257), np.float32)
    kT_all[:, TC:] = kT_own
    va[TC:, :, :256] = v_own.reshape(TC, 8, 256)
    va[TC:, :, 256] = 1.0
    if not first_half:
        kT_all[:, :TC] = kT_prev
        va[:TC, :, :256] = v_prev.reshape(TC, 8, 256)
        va[:TC, :, 256] = 1.0
    B0, B1, B1p, cfar, cpre = diff_bias_tiles(rel_bias, first_half)
    lpb = np.ascontiguousarray(np.broadcast_to(np.asarray(lam_params, np.float32)[None], (128, 4, 128)))
    sgb = np.ascontiguousarray(np.broadcast_to(np.asarray(sub_gain, np.float32)[None, :], (128, 256)))
    return {"qT": np.ascontiguousarray(qT), "kT": kT_all, "va": va, "xT": np.ascontiguousarray(xT),
            "wout": np.ascontiguousarray(w_out, dtype=np.float32), "B0": B0, "B1": B1, "B1p": B1p, "cfar": cfar, "cpre": cpre,
            "lpb": lpb, "sgb": sgb, "ident": np.eye(128, dtype=np.float32)}


def diff1_inputs(xT, g, w_in, qg, kg):
    qkg = np.zeros((128, 16), np.float32)
    qkg[:, 0] = np.asarray(qg, np.float32)
    qkg[:, 1] = np.asarray(kg, np.float32)
    return {"xT": np.ascontiguousarray(xT), "g": col16(g), "win": np.ascontiguousarray(w_in, dtype=np.float32), "qkg": qkg}


PAIRS = [[0, 1], [2, 3], [4, 5], [6, 7]]
DEPTH = 4


def build_fused(plan=("gla0", "ffn0", "pool", "ffn1", "diff", "ffn2", "gla1", "ffn3")):
    mp = Phase()
    m = mp
    nc, P = mp.nc, mp.P
    mp.btile = mp.sb("btile", [128, 1], F32)
    cache = {}

    def lz(name, shape):
        if name not in cache:
            cache[name] = mp.din(name, shape)
        return cache[name]

    def lzi(name, shape):
        if name not in cache:
            cache[name] = mp.dint(name, shape)
        return cache[name]

    xT_in = mp.din("xT", [D, TC])
    xT_out = mp.dout("xTo", [D, TC])

    def ngf(l, i):
        return lz("ng_%d_%d" % (l, i), [128, DC])

    def barrier():
        P.barrier(lambda e, bt=mp.btile: e.memset(bt[:], 0.0))

    def gather(src, dst, tag):
        P.cc(lambda e: e.collective_compute("AllGather", ALU.bypass, replica_groups=PAIRS, ins=[src], outs=[dst]),
             reads=[], writes=[("cc", tag)])
        barrier()

    def lzb(name, shape):
        if name not in cache:
            cache[name] = mp.dint(name, shape, BF16)
        return cache[name]

    def gla_layer(sl, layer, xin, xout):
        st_src = lzi("st_src", [1024, 512])
        st_all = lzi("st_all", [2048, 512])
        dr = dict(q=lzb("g_q", [1024, TC]), k=lzb("g_k", [1024, TC]), kdec=lzb("g_kd", [TC, 1024]), v=lzb("g_v", [TC, 2048]),
                  sr=lzb("g_sr", [TC, 2048]), elast=lzi("g_el", [128, 128]))
        common = dict(xT=xin, g=ngf(layer, 0), win=lz("gla%d_win" % sl, [D, GLA_NCOL]), wa2b=lz("gla%d_wa2b" % sl, [17, GLA_DKT]),
                      gnb=lz("gla%d_gnb" % sl, [128, 2048]), wout=lz("gla%d_wout" % sl, [D, D]), tri=lz("tri", [128, 128]),
                      ident=lz("ident", [128, 128]), isb=lz("isb", [128, 16]), dr=dr)
        b1 = dict(common)
        b1.update(st_in=None, st_out=None)
        build_gla_phase(Phase(mp, "g%dp_" % layer, b1), mode="proj")
        build_gla_scan0(Phase(mp, "g%ds_" % layer, dict(dr=dr, st_out=st_src.rearrange("(p c) v -> p c v", c=8))))
        gather(st_src[:, :], st_all[:, :], ("st", layer))
        b2 = dict(common)
        b2.update(st_in=st_all[0:1024, :].rearrange("(p c) v -> p c v", c=8), st_out=None, xTo=xout)
        build_gla_phase(Phase(mp, "g%df_" % layer, b2), mode="scanf")

    def ffn_layer(layer, xin, xout):
        build_ffn_phase(Phase(mp, "f%d_" % layer, dict(xT=xin, xTo=xout, g=ngf(layer, 1), wgu=lz("ffn%d_wgu" % layer, [D, 2 * FH]),
                                                      wdn=lz("ffn%d_wdn" % layer, [FH, D]))))

    def pool_layer(layer, xin, xout):
        halo_src = lzi("halo_src", [D, 16])
        halo_all = lzi("halo_all", [2 * D, 16])
        P.dma("sp", lambda e: e.dma_start(out=halo_src[:, :], in_=xin[:, TC - 16:TC]), writes=["halo_src"])
        barrier()
        gather(halo_src[:, :], halo_all[:, :], "halo")
        build_pool_phase(Phase(mp, "p1_", dict(xT=xin, halo=halo_all[0:D, :], isb=lz("isb", [128, 16]), g=ngf(layer, 0),
                                               wp=lz("pool_wp", [4, 512, 512]), psc=lz("pool_psc", [128, DC]),
                                               invc=lz("pool_invc", [128, 4, 16]), xTo=xout)))

    def diff_layer(layer, xin, xout):
        q_d = lzi("q_d", [D, TC])
        k_ds = [lzi("k_d%d" % i, [256, TC]) for i in range(8)]
        v_ds = [lzi("v_d%d" % i, [TC, 256]) for i in range(8)]
        k_alls = [lzi("k_all%d" % i, [512, TC]) for i in range(8)]
        v_alls = [lzi("v_all%d" % i, [2 * TC, 256]) for i in range(8)]
        build_diff1_phase(ph=Phase(mp, "d1_", dict(xT=xin, g=ngf(layer, 0), win=lz("diff_win", [D, 3 * D]),
                                                   qkg=lz("diff_qkg", [128, 16]), qkT=q_d, q_d=q_d, k_ds=k_ds, v_ds=v_ds, v=v_ds[0])))
        for i in range(8):
            P.cc(lambda e, i=i: e.collective_compute("AllGather", ALU.bypass, replica_groups=PAIRS, ins=[k_ds[i][:, :]],
                                                     outs=[k_alls[i][:, :]]), reads=[], writes=[("cc", "k", i)])
            P.cc(lambda e, i=i: e.collective_compute("AllGather", ALU.bypass, replica_groups=PAIRS, ins=[v_ds[i][:, :]],
                                                     outs=[v_alls[i][:, :]]), reads=[], writes=[("cc", "v", i)])
        barrier()
        build_diff2_phase(Phase(mp, "d2_", dict(qT=q_d, kpres=[a_[0:256, :] for a_ in k_alls], kowns=k_ds,
                                                vpres=[a_[0:TC, :] for a_ in v_alls], vowns=v_ds, xT=xin,
                                                wout=lz("diff_wout", [D, D]), B0=lz("diff_B0", [128, 16, 128]),
                                                B1=lz("diff_B1", [128, 16, 128]), B1p=lz("diff_B1p", [128, 16, 128]),
                                                cfar=lz("diff_cfar", [128, 16]), cpre=lz("diff_cpre", [128, 16]),
                                                lpb=lz("diff_lpb", [128, 4, 128]), sgb=lz("diff_sgb", [128, 256]),
                                                ident=lz("ident", [128, 128]), xTo=xout)))

    bufs = [lzi("xa", [D, TC]), lzi("xb", [D, TC])]
    cur = xT_in
    for si, step in enumerate(plan):
        dst = xT_out if si == len(plan) - 1 else bufs[si % 2]
        layer = int(step[-1]) if step[:3] == "ffn" else {"gla0": 0, "pool": 1, "diff": 2, "gla1": 3}[step]
        if step[:3] == "ffn":
            ffn_layer(layer, cur, dst)
        elif step[:3] == "gla":
            gla_layer(int(step[3]), layer, cur, dst)
        elif step == "pool":
            pool_layer(layer, cur, dst)
        else:
            diff_layer(layer, cur, dst)
        cur = dst
    mp.input_names = [k for k in cache if not k.startswith("g_") and not k in ("xa", "xb", "st_src", "st_all", "halo_src", "halo_all", "q_d", "k_d", "v_d", "k_all", "v_all")]
    return mp.finish()


_FUSED = []


def kernel(x, norm_g, gla_w_in, gla_w_a2, gla_b_a, gla_g_norm, gla_w_out, pool_w, pool_scale, diff_w_in,
           diff_q_gain, diff_k_gain, diff_lambda, diff_sub_gain, diff_w_out, rel_bias, ffn_w_gu, ffn_w_down):
    x = np.asarray(x, np.float32)
    B, S, _ = x.shape
    if not _FUSED:
        _FUSED.append(build_fused())
    nc = _FUSED[0]
    f32c = lambda a: np.ascontiguousarray(np.asarray(a, np.float32))
    tri, ident = gla_consts()
    shared = {"tri": tri, "ident": ident}
    for l in range(DEPTH):
        for i in range(2):
            shared["ng_%d_%d" % (l, i)] = col16(norm_g[l, i])
        shared["ffn%d_wgu" % l] = f32c(ffn_w_gu[l])
        shared["ffn%d_wdn" % l] = f32c(ffn_w_down[l])
    for sl in range(2):
        shared["gla%d_win" % sl] = f32c(gla_w_in[sl])
        shared["gla%d_wa2b" % sl] = f32c(np.concatenate([np.asarray(gla_w_a2[sl], np.float32),
                                                         np.asarray(gla_b_a[sl], np.float32)[None, :]], axis=0))
        shared["gla%d_gnb" % sl] = f32c(np.broadcast_to(np.tile(np.asarray(gla_g_norm[sl], np.float32), 4)[None, :], (128, 2048)))
        shared["gla%d_wout" % sl] = f32c(gla_w_out[sl])
    shared["pool_wp"] = f32c(pool_w[0])
    shared["pool_psc"] = col16(pool_scale[0])
    shared["diff_win"] = f32c(diff_w_in[0])
    qkg = np.zeros((128, 16), np.float32)
    qkg[:, 0] = np.asarray(diff_q_gain[0], np.float32)
    qkg[:, 1] = np.asarray(diff_k_gain[0], np.float32)
    shared["diff_qkg"] = qkg
    shared["diff_wout"] = f32c(diff_w_out[0])
    shared["diff_lpb"] = f32c(np.broadcast_to(np.asarray(diff_lambda[0], np.float32)[None], (128, 4, 128)))
    shared["diff_sgb"] = f32c(np.broadcast_to(np.asarray(diff_sub_gain[0], np.float32)[None, :], (128, 256)))
    per_half = []
    for half in range(2):
        B0, B1, B1p, cfar, cpre = diff_bias_tiles(rel_bias, half == 0)
        invc = np.zeros((128, 4, 16), np.float32)
        for gi, w in enumerate(POOL_W):
            for t in range(16):
                invc[:, gi, t] = 1.0 / (min(t + 1, w) if half == 0 else w)
        per_half.append({"diff_B0": B0, "diff_B1": B1, "diff_B1p": B1p, "diff_cfar": cfar, "diff_cpre": cpre,
                         "pool_invc": invc, "isb": np.full((128, 16), float(half), np.float32)})
    in_maps = []
    for c in range(NCORES):
        im = dict(shared)
        im.update(per_half[c % 2])
        im["xT"] = np.ascontiguousarray(x[c // 2, (c % 2) * TC:(c % 2 + 1) * TC].T)
        in_maps.append(im)
    res = run_bass_kernel_spmd(nc, in_maps, core_ids=list(range(NCORES))).results
    out = np.empty((B, S, D), np.float32)
    for c in range(NCORES):
        out[c // 2, (c % 2) * TC:(c % 2 + 1) * TC] = res[c]["xTo"].T
    return out
```

```python
from contextlib import ExitStack
import math
import numpy as np
import concourse.bass as bass
import concourse.mybir as mybir
from concourse.bass_utils import run_bass_kernel_spmd

F32 = mybir.dt.float32
BF16 = mybir.dt.bfloat16
AF = mybir.ActivationFunctionType
ALU = mybir.AluOpType
AX = mybir.AxisListType

D = 2048
DC = D // 128
TC = 2048
FH = 5632
HC = FH // 128
EPS = 1e-6
NCORES = 8

ENGINES = ("pe", "act", "dve", "pool", "sp")


class Op:
    __slots__ = ("eng", "fn", "reads", "writes", "dma", "waits", "sig", "idx", "cc", "barrier")

    def __init__(self, eng, fn, reads, writes, dma):
        self.cc = False
        self.barrier = False
        self.eng = eng
        self.fn = fn
        self.reads = reads
        self.writes = writes
        self.dma = dma
        self.waits = []
        self.sig = None
        self.idx = -1


class Prog:
    NDMA = 32

    def __init__(self, nc):
        self.nc = nc
        self.ops = []

    LOOPVARS = frozenset("tt tsl j jsl h hs dc di dl blk c fc fi fb d dd ob vb rb i ei s sl mi grp b g w at atk kd kdk pq pk_ pv py_ ptr pkv cur oth sh a Wq Wk Wv Wr Wo wb db dp hf step q qk qb kb ki qi hp".split())

    def op(self, eng, fn, reads=(), writes=(), dma=False):
        bad = self.LOOPVARS.intersection(fn.__code__.co_freevars)
        if bad:
            raise RuntimeError("late-bound loop variable(s) %s in lambda at line %d" % (sorted(bad), fn.__code__.co_firstlineno))
        o = Op(eng, fn, tuple(reads), tuple(writes), dma)
        o.idx = len(self.ops)
        self.ops.append(o)
        return o

    def pe(self, fn, reads=(), writes=()):
        return self.op("pe", fn, reads, writes)

    def act(self, fn, reads=(), writes=()):
        return self.op("act", fn, reads, writes)

    def dve(self, fn, reads=(), writes=()):
        return self.op("dve", fn, reads, writes)

    def pool(self, fn, reads=(), writes=()):
        return self.op("pool", fn, reads, writes)

    def dma(self, eng, fn, reads=(), writes=()):
        return self.op(eng, fn, reads, writes, dma=True)

    def cc(self, fn, reads=(), writes=()):
        o = self.op("pool", fn, reads, writes, dma=True)
        o.cc = True
        return o

    def barrier(self, fn):
        o = self.op("dve", fn, (), ())
        o.barrier = True
        return o

    def finalize(self, final_wait_keys=()):
        ops = self.ops
        deps = [set() for _ in ops]
        last_writer = {}
        readers = {}
        since = []
        last_barrier = None
        for o in ops:
            if o.barrier:
                deps[o.idx] |= set(since)
                if last_barrier is not None:
                    deps[o.idx].add(last_barrier)
                since = []
                last_barrier = o.idx
                last_writer_final = dict(last_writer)
                last_writer = {}
                readers = {}
                continue
            since.append(o.idx)
            if last_barrier is not None:
                deps[o.idx].add(last_barrier)
            for k in o.reads:
                w = last_writer.get(k)
                if w is not None:
                    deps[o.idx].add(w.idx)
            for k in o.writes:
                w = last_writer.get(k)
                if w is not None:
                    deps[o.idx].add(w.idx)
                for r in readers.get(k, ()):
                    if r.idx != o.idx:
                        deps[o.idx].add(r.idx)
            for k in o.reads:
                readers.setdefault(k, []).append(o)
            for k in o.writes:
                last_writer[k] = o
                readers[k] = []
        final_ops = [last_writer[k].idx for k in final_wait_keys if k in last_writer]
        if last_barrier is not None:
            final_ops.append(last_barrier)
        needed = [set() for _ in ops]
        for o in ops:
            best = {}
            for d in deps[o.idx]:
                p = ops[d]
                if p.dma:
                    needed[o.idx].add(d)
                    continue
                if p.eng == "pe" and o.eng == "pe" and not o.dma:
                    continue
                if best.get(p.eng, -1) < d:
                    best[p.eng] = d
            needed[o.idx] |= set(best.values())
        signaled = set(final_ops)
        for o in ops:
            signaled |= needed[o.idx]
        eng_count = {e: 0 for e in ENGINES}
        NDMA = self.NDMA
        dma_count = [0] * NDMA
        dma_last = [None] * NDMA
        rr = 0
        rr_sw = 0
        cc_count = 0
        cc_last = None
        for o in ops:
            if o.dma and not o.cc:
                signaled.add(o.idx)
            if o.cc:
                signaled.add(o.idx)
                if cc_last is not None:
                    needed[o.idx].add(cc_last)
                cc_count += 1
                cc_last = o.idx
                o.sig = (("cc", 0), None, cc_count)
                continue
            if o.idx not in signaled:
                continue
            if o.dma:
                half = NDMA // 2
                if o.eng == "pool":
                    s = half + rr_sw % half
                    rr_sw += 1
                else:
                    s = rr % half
                    rr += 1
                if dma_last[s] is not None:
                    needed[o.idx].add(dma_last[s])
                dma_count[s] += 1
                dma_last[s] = o.idx
                o.sig = (("dma", s), 16, dma_count[s] * 16)
            else:
                eng_count[o.eng] += 1
                o.sig = (("eng", o.eng), 1, eng_count[o.eng])
        seen = {e: {} for e in ENGINES}
        for o in ops:
            ws = {}
            for d in needed[o.idx]:
                semkey, _, val = ops[d].sig
                if ws.get(semkey, 0) < val:
                    ws[semkey] = val
            for semkey, val in ws.items():
                if seen[o.eng].get(semkey, 0) >= val:
                    continue
                seen[o.eng][semkey] = val
                o.waits.append((semkey, val))
        fw = {}
        for d in final_ops:
            semkey, _, val = ops[d].sig
            fw[semkey] = max(fw.get(semkey, 0), val)
        self.final_waits = fw
        self.eng_count = eng_count

    def emit(self, block, sems):
        per_eng = {e: [] for e in ENGINES}
        for o in self.ops:
            per_eng[o.eng].append(o)
        final_waits = self.final_waits

        def run(engobj, lst, is_last):
            for o in lst:
                for semkey, val in o.waits:
                    engobj.wait_ge(sems[semkey], val)
                ins = o.fn(engobj)
                if o.sig is not None:
                    if o.sig[1] is None:
                        ins.then_inc(sems[o.sig[0]])
                    else:
                        ins.then_inc(sems[o.sig[0]], o.sig[1])
            if is_last:
                for semkey, val in final_waits.items():
                    engobj.wait_ge(sems[semkey], val)

        @block.tensor
        def _(e):
            run(e, per_eng["pe"], False)

        @block.scalar
        def _(e):
            run(e, per_eng["act"], False)

        @block.vector
        def _(e):
            run(e, per_eng["dve"], False)

        @block.gpsimd
        def _(e):
            run(e, per_eng["pool"], False)

        @block.sync
        def _(e):
            run(e, per_eng["sp"], True)


class Phase:
    def __init__(self, master=None, prefix="", bind=None):
        self.master = master
        self.prefix = prefix
        self.bind = bind or {}
        if master is None:
            self.nc = bass.Bass("TRN2", target_bir_lowering=False)
            self.P = Prog(self.nc)
            self.out_keys = []
        else:
            self.nc = master.nc
            self.P = master.P
            self.out_keys = master.out_keys
        self.es = ExitStack()

    def din(self, name, shape, dtype=F32):
        if name in self.bind:
            return self.bind[name]
        return self.nc.dram_tensor(self.prefix + name, list(shape), dtype, kind="ExternalInput").ap()

    def dout(self, name, shape, dtype=F32):
        if name in self.bind:
            return self.bind[name]
        return self.nc.dram_tensor(self.prefix + name, list(shape), dtype, kind="ExternalOutput").ap()

    def dint(self, name, shape, dtype=F32):
        return self.nc.dram_tensor(self.prefix + name, list(shape), dtype).ap()

    def sb(self, name, shape, dtype=F32):
        return self.es.enter_context(self.nc.sbuf_tensor(self.prefix + name, list(shape), dtype))

    def ps(self, name, shape, dtype=F32):
        return self.es.enter_context(self.nc.psum_tensor(self.prefix + name, list(shape), dtype))

    def finish(self):
        if self.master is not None:
            self.es.close()
            bt = self.master.btile
            self.P.barrier(lambda e: e.memset(bt[:], 0.0))
            return None
        P = self.P
        P.finalize(final_wait_keys=self.out_keys)
        sems = {}
        for e in ENGINES:
            sems[("eng", e)] = self.es.enter_context(self.nc.semaphore("s_" + e))
        for i in range(P.NDMA):
            sems[("dma", i)] = self.es.enter_context(self.nc.semaphore("d%d" % i))
        sems[("cc", 0)] = self.es.enter_context(self.nc.semaphore("s_cc"))
        block = self.es.enter_context(self.nc.Block())
        P.emit(block, sems)
        self.es.close()
        return self.nc


def emit_rmsnorm(ph, xT, hT, gcol, ones_bf, sq, pss, rstd, ntok, xkey, hkey, tag):
    P = ph.P
    nsub = ntok // 512
    for s in range(nsub):
        sl = slice(s * 512, (s + 1) * 512)
        pb = pss[s % 2]
        pk = "pss%d" % (s % 2)
        for c in range(DC):
            q = sq[c % 2]
            qk = "sq%d" % (c % 2)
            P.act(lambda e, q=q, c=c, sl=sl: e.activation(out=q[:], in_=xT[:, c, sl], func=AF.Square),
                  reads=[(xkey, c)], writes=[qk])
            P.pe(lambda e, q=q, c=c, pb=pb: e.matmul(pb[:], ones_bf[:], q[:], start=(c == 0), stop=(c == DC - 1)),
                 reads=[qk, "ones"], writes=[pk])
        rk = ("rstd", tag, s)
        P.dve(lambda e, pb=pb, sl=sl: e.tensor_scalar(out=rstd[:, sl], in0=pb[:], scalar1=1.0 / D, scalar2=EPS,
                                                      op0=ALU.mult, op1=ALU.add),
              reads=[pk], writes=[rk])
        P.act(lambda e, sl=sl: e.activation(out=rstd[:, sl], in_=rstd[:, sl], func=AF.Sqrt),
              reads=[rk], writes=[rk])
        P.dve(lambda e, sl=sl: e.reciprocal(out=rstd[:, sl], in_=rstd[:, sl]),
              reads=[rk], writes=[rk])
        for c in range(DC):
            P.dve(lambda e, c=c, sl=sl: e.scalar_tensor_tensor(out=hT[:, c, sl], in0=xT[:, c, sl],
                                                               scalar=gcol[:, c:c + 1], in1=rstd[:, sl],
                                                               op0=ALU.mult, op1=ALU.mult),
                  reads=[(xkey, c), rk, "gcol"], writes=[(hkey, c, s)])


def build_ffn_phase(ph=None):
    ph = ph or Phase()
    P = ph.P
    nc = ph.nc
    xin = ph.din("xT", [D, TC])
    g_in = ph.din("g", [128, DC])
    wgu = ph.din("wgu", [D, 2 * FH])
    wdn = ph.din("wdn", [FH, D])
    xout = ph.dout("xTo", [D, TC])
    NT = 1024
    NS = NT // 512
    HG = 11
    NG = HC // HG
    xT = ph.sb("xTs", [128, DC, NT], F32)
    hT = ph.sb("hTs", [128, DC, NT], BF16)
    aT = ph.sb("aTs", [128, HG, NT], BF16)
    wg = [ph.sb("wg%d" % i, [128, DC, 256], BF16) for i in range(3)]
    wd = [ph.sb("wd%d" % i, [128, HG, 256], BF16) for i in range(2)]
    sq = [ph.sb("sq%d" % i, [128, 512], BF16) for i in range(2)]
    sg = [ph.sb("sg%d" % i, [128, 512], F32) for i in range(2)]
    rstd = ph.sb("rstd", [128, NT], F32)
    gcol = ph.sb("gcol", [128, DC], F32)
    ones = ph.sb("ones", [128, 128], BF16)
    pss = [ph.ps("pss%d" % i, [128, 512]) for i in range(2)]
    pg = [ph.ps("pg%d" % i, [128, 512]) for i in range(2)]
    pu = [ph.ps("pu%d" % i, [128, 512]) for i in range(2)]
    py = [ph.ps("py%d" % i, [128, 512]) for i in range(2)]

    P.dma("sp", lambda e: e.dma_start(out=gcol[:], in_=g_in[:, :]), writes=["gcol"])
    P.dve(lambda e: e.memset(ones[:], 1.0), writes=["ones"])
    xin_v = xin.rearrange("(c p) t -> p c t", p=128)
    xout_v = xout.rearrange("(c p) t -> p c t", p=128)
    wgu_v = wgu.rearrange("(c p) n -> p c n", p=128)
    wdn_v = wdn.rearrange("(m p) n -> p m n", p=128)
    wgi = 0
    wdi = 0
    cnt = 0
    for tt in range(TC // NT):
        tsl = slice(tt * NT, (tt + 1) * NT)
        for c in range(DC):
            P.dma("sp",
                  lambda e, c=c, tsl=tsl: e.dma_start(out=xT[:, c, :], in_=xin_v[:, c, tsl]),
                  writes=[("x", c)])
        emit_rmsnorm(ph, xT, hT, gcol, ones, sq, pss, rstd, NT, "x", "h", tt)
        hkeys = [("h", c, s) for c in range(DC) for s in range(NS)]
        for grp in range(NG):
            for mi in range(HG):
                m = grp * HG + mi
                wb = wg[wgi % 3]
                wk = "wg%d" % (wgi % 3)
                wgi += 1
                P.dma("pool", lambda e, wb=wb, m=m: e.dma_start(out=wb[:, :, 0:128], in_=wgu_v[:, :, m * 128:(m + 1) * 128]),
                      writes=[wk])
                P.dma("pool", lambda e, wb=wb, m=m: e.dma_start(out=wb[:, :, 128:256],
                                                                 in_=wgu_v[:, :, FH + m * 128:FH + (m + 1) * 128]),
                      writes=[wk])
                for s in range(NS):
                    sl = slice(s * 512, (s + 1) * 512)
                    b = cnt % 2
                    cnt += 1
                    for c in range(DC):
                        P.pe(lambda e, wb=wb, c=c, sl=sl, b=b: e.matmul(pg[b][:], wb[:, c, 0:128], hT[:, c, sl],
                                                                        start=(c == 0), stop=(c == DC - 1)),
                             reads=[wk, ("h", c, s)], writes=["pg%d" % b])
                    for c in range(DC):
                        P.pe(lambda e, wb=wb, c=c, sl=sl, b=b: e.matmul(pu[b][:], wb[:, c, 128:256], hT[:, c, sl],
                                                                        start=(c == 0), stop=(c == DC - 1)),
                             reads=[wk, ("h", c, s)], writes=["pu%d" % b])
                    P.act(lambda e, b=b: e.activation(out=sg[b][:], in_=pg[b][:], func=AF.Silu),
                          reads=["pg%d" % b], writes=["sg%d" % b])
                    P.dve(lambda e, b=b, mi=mi, sl=sl: e.tensor_tensor(out=aT[:, mi, sl], in0=sg[b][:], in1=pu[b][:],
                                                                       op=ALU.mult),
                          reads=["sg%d" % b, "pu%d" % b], writes=[("a", mi, s)])
            for dp in range(DC // 2):
                db = wd[wdi % 2]
                dk = "wd%d" % (wdi % 2)
                wdi += 1
                P.dma("pool", lambda e, db=db, dp=dp, grp=grp: e.dma_start(
                    out=db[:], in_=wdn_v[:, grp * HG:(grp + 1) * HG, dp * 256:(dp + 1) * 256]), writes=[dk])
                for dd in range(2):
                    d = dp * 2 + dd
                    for s in range(NS):
                        sl = slice(s * 512, (s + 1) * 512)
                        b = cnt % 2
                        cnt += 1
                        for mi in range(HG):
                            P.pe(lambda e, db=db, mi=mi, dd=dd, sl=sl, b=b: e.matmul(
                                py[b][:], db[:, mi, dd * 128:(dd + 1) * 128], aT[:, mi, sl],
                                start=(mi == 0), stop=(mi == HG - 1)),
                                reads=[dk, ("a", mi, s)], writes=["py%d" % b])
                        P.dve(lambda e, d=d, sl=sl, b=b: e.tensor_tensor(out=xT[:, d, sl], in0=xT[:, d, sl], in1=py[b][:],
                                                                         op=ALU.add),
                              reads=["py%d" % b, ("x", d)], writes=[("x", d)])
        for c in range(DC):
            P.dma("sp",
                  lambda e, c=c, tsl=tsl: e.dma_start(out=xout_v[:, c, tsl], in_=xT[:, c, :]),
                  reads=[("x", c)], writes=[("xo", tt, c)])
            ph.out_keys.append(("xo", tt, c))
    return ph.finish()


def emit_rmsnorm_cols(ph, xT, xoff, hT, hoff, ncols, gcol, ones_bf, sq, pb, pk, rstd, xkeys, hkeys, tag):
    P = ph.P
    xsl_ = slice(xoff, xoff + ncols)
    hsl_ = slice(hoff, hoff + ncols)
    for c in range(DC):
        q = sq[c % 2]
        qk = "sq%d" % (c % 2)
        P.act(lambda e, q=q, c=c: e.activation(out=q[:, 0:ncols], in_=xT[:, c, xsl_], func=AF.Square),
              reads=[xkeys(c)], writes=[qk])
        P.pe(lambda e, q=q, c=c: e.matmul(pb[:, 0:ncols], ones_bf[:], q[:, 0:ncols], start=(c == 0), stop=(c == DC - 1)),
             reads=[qk, "ones"], writes=[pk])
    rk = ("rstd", tag)
    P.dve(lambda e: e.tensor_scalar(out=rstd[:, 0:ncols], in0=pb[:, 0:ncols], scalar1=1.0 / D, scalar2=EPS,
                                    op0=ALU.mult, op1=ALU.add), reads=[pk], writes=[rk])
    P.act(lambda e: e.activation(out=rstd[:, 0:ncols], in_=rstd[:, 0:ncols], func=AF.Sqrt), reads=[rk], writes=[rk])
    P.dve(lambda e: e.reciprocal(out=rstd[:, 0:ncols], in_=rstd[:, 0:ncols]), reads=[rk], writes=[rk])
    for c in range(DC):
        P.dve(lambda e, c=c: e.scalar_tensor_tensor(out=hT[:, c, hsl_], in0=xT[:, c, xsl_], scalar=gcol[:, c:c + 1],
                                                    in1=rstd[:, 0:ncols], op0=ALU.mult, op1=ALU.mult),
              reads=[xkeys(c), rk, "gcol"], writes=[hkeys(c)])


POOL_W = (2, 4, 8, 16)


def build_pool_phase(ph=None):
    ph = ph or Phase()
    P = ph.P
    fused = ph.master is not None
    xin = None if fused else ph.din("xTe", [D, 16 + TC])
    g_in = ph.din("g", [128, DC])
    wp_in = ph.din("wp", [4, 512, 512])
    sc_in = ph.din("psc", [128, DC])
    ic_in = ph.din("invc", [128, 4, 16])
    xout = ph.dout("xTo", [D, TC])
    NT = 512
    xT = ph.sb("xTs", [128, DC, NT], F32)
    xh = ph.sb("xh", [128, DC, 16], F32)
    hx = ph.sb("hx", [128, DC, 16 + NT], F32)
    sA = [ph.sb("sA%d" % i, [128, 16 + NT], F32) for i in range(2)]
    sB = [ph.sb("sB%d" % i, [128, 16 + NT], F32) for i in range(2)]
    yT = ph.sb("yT", [128, DC, NT], BF16)
    wp = ph.sb("wps", [128, 4, 4, 512], BF16)
    sq = [ph.sb("sq%d" % i, [128, 512], BF16) for i in range(2)]
    rstd = ph.sb("rstd", [128, 512], F32)
    gcol = ph.sb("gcol", [128, DC], F32)
    psc = ph.sb("pscs", [128, DC], F32)
    invc = ph.sb("invcs", [128, 4, 16], F32)
    ones = ph.sb("ones", [128, 128], BF16)
    pss = ph.ps("pss", [128, 512])
    pz = [ph.ps("pz%d" % i, [128, 512]) for i in range(2)]

    P.dma("sp", lambda e: e.dma_start(out=gcol[:], in_=g_in[:, :]), writes=["gcol"])
    P.dma("sp", lambda e: e.dma_start(out=psc[:], in_=sc_in[:, :]), writes=["psc"])
    P.dma("sp", lambda e: e.dma_start(out=invc[:], in_=ic_in[:, :, :]), writes=["invc"])
    P.dve(lambda e: e.memset(ones[:], 1.0), writes=["ones"])
    wp_v = wp_in.rearrange("g (ci p) n -> p g ci n", p=128)
    for g in range(4):
        P.dma("pool", lambda e, g=g: e.dma_start(out=wp[:, g, :, :], in_=wp_v[:, g, :, :]), writes=["wp"])
    if fused:
        xmain_v = ph.bind["xT"].rearrange("(c p) t -> p c t", p=128)
        halo_v = ph.bind["halo"].rearrange("(c p) t -> p c t", p=128)
        isb = ph.sb("isb", [128, 16], F32)
        P.dma("sp", lambda e: e.dma_start(out=isb[:], in_=ph.bind["isb"][:, :]), writes=["isb"])
        P.dma("sp", lambda e: e.dma_start(out=xh[:], in_=halo_v[:, :, :]), writes=["xh"])
        P.dve(lambda e: e.tensor_scalar(out=xh[:], in0=xh[:], scalar1=isb[:, 0:1], scalar2=None, op0=ALU.mult),
              reads=["xh", "isb"], writes=["xh"])
        OFF = 0
    else:
        xin_v = xin.rearrange("(c p) t -> p c t", p=128)
        xmain_v = xin_v
        OFF = 16
        P.dma("sp", lambda e: e.dma_start(out=xh[:], in_=xin_v[:, :, 0:16]), writes=["xh"])
    xout_v = xout.rearrange("(c p) t -> p c t", p=128)
    emit_rmsnorm_cols(ph, xh, 0, hx, 0, 16, gcol, ones, sq, pss, "pss", rstd,
                      lambda c: "xh", lambda c: ("hx", c), "halo")
    cnt = 0
    for tt in range(TC // NT):
        for c in range(DC):
            P.dma("sp", lambda e, c=c, tt=tt: e.dma_start(out=xT[:, c, :], in_=xmain_v[:, c, OFF + tt * NT:OFF + (tt + 1) * NT]),
                  writes=[("x", c)])
        emit_rmsnorm_cols(ph, xT, 0, hx, 16, NT, gcol, ones, sq, pss, "pss", rstd,
                          lambda c: ("x", c), lambda c: ("hx", c), ("t", tt))
        W = 16 + NT
        for c in range(DC):
            g = c // 4
            eng = P.dve if c % 2 == 0 else P.pool
            a = sA[c % 2]
            b = sB[c % 2]
            ak = "sA%d" % (c % 2)
            bk = "sB%d" % (c % 2)
            eng(lambda e, a=a, c=c: e.tensor_tensor(out=a[:, 1:W], in0=hx[:, c, 1:W], in1=hx[:, c, 0:W - 1], op=ALU.add),
                reads=[("hx", c)], writes=[ak])
            cur, curk, oth, othk = a, ak, b, bk
            sh = 2
            for step in range(g):
                eng(lambda e, cur=cur, oth=oth, sh=sh: e.tensor_tensor(out=oth[:, 1 + sh:W], in0=cur[:, 1 + sh:W],
                                                                      in1=cur[:, 1:W - sh], op=ALU.add),
                    reads=[curk], writes=[othk])
                cur, curk, oth, othk = oth, othk, cur, curk
                sh *= 2
            w = POOL_W[g]
            P.dve(lambda e, cur=cur, c=c, w=w: e.scalar_tensor_tensor(out=yT[:, c, :], in0=cur[:, 16:W], scalar=1.0 / w,
                                                                    in1=hx[:, c, 16:W], op0=ALU.mult, op1=ALU.subtract),
                reads=[curk, ("hx", c)], writes=[("y", c)])
            if tt == 0:
                eng(lambda e, cur=cur, g=g: e.tensor_tensor(out=cur[:, 16:32], in0=cur[:, 16:32], in1=invc[:, g, :], op=ALU.mult),
                    reads=[curk, "invc", ("y", c)], writes=[curk])
                eng(lambda e, cur=cur, c=c: e.tensor_tensor(out=yT[:, c, 0:16], in0=cur[:, 16:32], in1=hx[:, c, 16:32],
                                                            op=ALU.subtract),
                    reads=[curk, ("hx", c)], writes=[("y", c)])
            if tt + 1 < TC // NT:
                eng(lambda e, c=c: e.tensor_copy(out=hx[:, c, 0:16], in_=hx[:, c, NT:NT + 16]),
                    reads=[("y", c), curk, ak, bk], writes=[("hx", c)])
        for d in range(DC):
            g = d // 4
            b = cnt % 2
            cnt += 1
            for ci in range(4):
                P.pe(lambda e, g=g, ci=ci, d=d, b=b: e.matmul(pz[b][:], wp[:, g, ci, (d % 4) * 128:(d % 4 + 1) * 128],
                                                             yT[:, 4 * g + ci, :], start=(ci == 0), stop=(ci == 3)),
                     reads=["wp", ("y", 4 * g + ci)], writes=["pz%d" % b])
            P.dve(lambda e, d=d, b=b: e.scalar_tensor_tensor(out=xT[:, d, :], in0=pz[b][:], scalar=psc[:, d:d + 1],
                                                            in1=xT[:, d, :], op0=ALU.mult, op1=ALU.add),
                  reads=["pz%d" % b, "psc", ("x", d)], writes=[("x", d)])
        for c in range(DC):
            P.dma("sp", lambda e, c=c, tt=tt: e.dma_start(out=xout_v[:, c, tt * NT:(tt + 1) * NT], in_=xT[:, c, :]),
                  reads=[("x", c)], writes=[("xo", tt, c)])
            ph.out_keys.append(("xo", tt, c))
    return ph.finish()


def col16(v):
    return np.ascontiguousarray(np.asarray(v, np.float32).reshape(DC, 128).T)


def pool_inputs(x_seq, half, g, wp, psc):
    xe = np.zeros((D, 16 + TC), np.float32)
    t0 = half * TC
    xe[:, 16:] = x_seq[t0:t0 + TC].T
    if half == 1:
        xe[:, :16] = x_seq[t0 - 16:t0].T
    invc = np.zeros((128, 4, 16), np.float32)
    for gi, w in enumerate(POOL_W):
        for t in range(16):
            cnt = min(t + 1, w) if half == 0 else w
            invc[:, gi, t] = 1.0 / cnt
    return {"xTe": xe, "g": col16(g), "wp": np.ascontiguousarray(wp, dtype=np.float32), "psc": col16(psc), "invc": invc}


GLA_DKT = 1024
GLA_NCOL = 6160


def build_gla_phase(ph=None, state_only=False, mode="full"):
    ph = ph or Phase()
    P = ph.P
    fused = ph.master is not None
    xin = ph.din("xT", [D, TC])
    g_in = ph.din("g", [128, DC])
    win = ph.din("win", [D, GLA_NCOL])
    wa2b_in = ph.din("wa2b", [17, GLA_DKT])
    gn_in = ph.din("gnb", [128, 2048])
    wout = ph.din("wout", [D, D])
    st_in = ph.bind.get("st_in") if fused else ph.din("st_in", [128, 8, 512])
    tri_in = ph.din("tri", [128, 128])
    id_in = ph.din("ident", [128, 128])
    xout = None if (state_only or mode == "proj") else ph.dout("xTo", [D, TC])
    st_out = ph.bind.get("st_out") if fused else ph.dout("st_out", [128, 8, 512])
    NT = 512
    NJ = 4
    xs = [ph.sb("xs%d" % i, [128, 512], F32) for i in range(3)]
    hT = ph.sb("hTs", [128, DC, NT], BF16)
    Wb = [ph.sb("Wb%d" % i, [128, DC, 512], BF16) for i in range(2)]
    Wa = ph.sb("Wa", [128, DC, 16], BF16)
    qT = ph.sb("qT", [128, 8, NT], BF16)
    kT = ph.sb("kT", [128, 8, NT], BF16)
    kdec = ph.sb("kdec", [128, NJ, 1024], BF16)
    kd_s = [ph.sb("kds%d" % i, [128, 512], BF16) for i in range(2)]
    vt = ph.sb("vt", [128, NJ, 2048], BF16)
    sr = ph.sb("sr", [128, NJ, 2048], BF16)
    gated2 = [ph.sb("gated%d" % i, [128, 2048], BF16) for i in range(2)]
    gT = ph.sb("gT", [128, DC, NT], BF16)
    S = ph.sb("S", [128, 8, 512], F32)
    Sb = ph.sb("Sb", [128, 8, 512], BF16)
    lt = ph.sb("lt", [128, NJ, 1024], F32)
    e1 = ph.sb("e1", [128, 1024], F32)
    Eq = [ph.sb("Eq%d" % i, [128, 512], F32) for i in range(1)]
    Ek = [ph.sb("Ek%d" % i, [128, 512], F32) for i in range(1)]
    Elast = ph.sb("Elast", [128, 8, TC // 128], F32)
    dr = ph.bind.get("dr")
    alr1 = ph.sb("alr1", [32, NT], F32)
    wa2b = ph.sb("wa2bs", [32, GLA_DKT], F32)
    gnb = ph.sb("gnbs", [128, 2048], BF16)
    tri = ph.sb("tris", [128, 128], F32)
    ident = ph.sb("idents", [128, 128], BF16)
    AT4 = [ph.sb("AT4%d" % i, [128, 128], BF16) for i in range(4)]
    osq = ph.sb("osq", [128, 2048], BF16)
    ssq = ph.sb("ssq", [128, 4], F32)
    sq = [ph.sb("sq%d" % i, [128, 512], BF16) for i in range(2)]
    rstd = ph.sb("rstd", [128, 512], F32)
    gcol = ph.sb("gcol", [128, DC], F32)
    ones = ph.sb("ones", [128, 128], BF16)
    pb = [ph.ps("pb%d" % i, [128, 512]) for i in range(8)]
    pbk = ["pb%d" % i for i in range(8)]

    P.dma("sp", lambda e: e.dma_start(out=gcol[:], in_=g_in[:, :]), writes=["gcol"])
    P.dma("sp", lambda e: e.dma_start(out=tri[:], in_=tri_in[:, :]), writes=["tri"])
    P.dma("sp", lambda e: e.dma_start(out=wa2b[0:17, :], in_=wa2b_in[:, :]), writes=["wa2b"])
    if st_in is None:
        P.dve(lambda e: e.memset(S[:], 0.0), writes=[("S", dc) for dc in range(8)])
    else:
        P.dma("sp", lambda e: e.dma_start(out=S[:], in_=st_in[:, :, :]), writes=[("S", dc) for dc in range(8)])
        if fused:
            isb = ph.sb("isb", [128, 16], F32)
            P.dma("sp", lambda e: e.dma_start(out=isb[:], in_=ph.bind["isb"][:, :]), writes=["isb"])
            P.dve(lambda e: e.tensor_scalar(out=S[:], in0=S[:], scalar1=isb[:, 0:1], scalar2=None, op0=ALU.mult),
                  reads=[("S", dc) for dc in range(8)] + ["isb"], writes=[("S", dc) for dc in range(8)])
    P.dma("pool", lambda e: e.dma_start(out=ident[:], in_=id_in[:, :]), writes=["ident"])
    P.dma("pool", lambda e: e.dma_start(out=gnb[:], in_=gn_in[:, :]), writes=["gnb"])
    P.dve(lambda e: e.memset(ones[:], 1.0), writes=["ones"])
    P.dve(lambda e: e.memset(alr1[:], 1.0), writes=["alr1"])
    P.act(lambda e: e.copy(out=Sb[:], in_=S[:]), reads=[("S", dc) for dc in range(8)], writes=[("Sb", dc) for dc in range(8)])
    win_v = win.rearrange("(c p) n -> p c n", p=128)
    wout_v = wout.rearrange("(c p) n -> p c n", p=128)
    xin_v = xin.rearrange("(c p) t -> p c t", p=128)
    xout_v = None if xout is None else xout.rearrange("(c p) t -> p c t", p=128)
    P.dma("pool", lambda e: e.dma_start(out=Wa[:], in_=win_v[:, :, 6144:6160]), writes=["Wa"])

    if mode == "scanf":
        P.dma("sp", lambda e: e.dma_start(out=Elast[:], in_=dr["elast"].rearrange("p (dc jj) -> p dc jj", dc=8)),
              writes=[("Elast", dc) for dc in range(8)])
    wctr = [0]

    def load_w(src_v, col0):
        i = wctr[0] % 2
        wctr[0] += 1
        P.dma("pool", lambda e, i=i: e.dma_start(out=Wb[i][:], in_=src_v[:, :, col0:col0 + 512]), writes=["Wb%d" % i])
        return Wb[i], "Wb%d" % i

    pctr = [0]

    def next_pb(lo=4, n=4):
        i = lo + pctr[0] % n
        pctr[0] += 1
        return pb[i], pbk[i]

    xctr = [0]
    for tt in range(TC // NT):
        tsl = slice(tt * NT, (tt + 1) * NT)
        if mode == "scanf":
            P.dma("sp", lambda e, tsl=tsl: e.dma_start(out=qT[:], in_=dr["q"].rearrange("(dc p) t -> p dc t", p=128)[:, :, tsl]),
                  writes=[("qT", dc) for dc in range(8)])
            P.dma("sp", lambda e, tsl=tsl: e.dma_start(out=kT[:], in_=dr["k"].rearrange("(dc p) t -> p dc t", p=128)[:, :, tsl]),
                  writes=[("kT", dc) for dc in range(8)])
            P.dma("sp", lambda e, tt=tt: e.dma_start(out=kdec[:], in_=dr["kdec"].rearrange("(jj p) d -> p jj d", p=128)[:, tt * NJ:(tt + 1) * NJ, :]),
                  writes=[("kdec", dc) for dc in range(8)])
            P.dma("sp", lambda e, tt=tt: e.dma_start(out=vt[:], in_=dr["v"].rearrange("(jj p) n -> p jj n", p=128)[:, tt * NJ:(tt + 1) * NJ, :]),
                  writes=[("vt", j, h) for j in range(NJ) for h in range(4)])
            P.dma("sp", lambda e, tt=tt: e.dma_start(out=sr[:], in_=dr["sr"].rearrange("(jj p) n -> p jj n", p=128)[:, tt * NJ:(tt + 1) * NJ, :]),
                  writes=[("sr", j) for j in range(NJ)])
        else:
            p_ss, p_ssk = pb[0], pbk[0]
            for c in range(DC):
                i = xctr[0] % 3
                xctr[0] += 1
                P.dma("sp", lambda e, i=i, c=c, tsl=tsl: e.dma_start(out=xs[i][:], in_=xin_v[:, c, tsl]), writes=["xs%d" % i])
                q = sq[c % 2]
                qk = "sq%d" % (c % 2)
                P.act(lambda e, q=q, i=i: e.activation(out=q[:], in_=xs[i][:], func=AF.Square), reads=["xs%d" % i], writes=[qk])
                P.pe(lambda e, q=q, c=c: e.matmul(p_ss[:], ones[:], q[:], start=(c == 0), stop=(c == DC - 1)),
                     reads=[qk, "ones"], writes=[p_ssk])
            P.dve(lambda e: e.tensor_scalar(out=rstd[:], in0=p_ss[:], scalar1=1.0 / D, scalar2=EPS, op0=ALU.mult, op1=ALU.add),
                  reads=[p_ssk], writes=["rstd"])
            P.act(lambda e: e.activation(out=rstd[:], in_=rstd[:], func=AF.Sqrt), reads=["rstd"], writes=["rstd"])
            P.dve(lambda e: e.reciprocal(out=rstd[:], in_=rstd[:]), reads=["rstd"], writes=["rstd"])
            for c in range(DC):
                i = xctr[0] % 3
                xctr[0] += 1
                P.dma("sp", lambda e, i=i, c=c, tsl=tsl: e.dma_start(out=xs[i][:], in_=xin_v[:, c, tsl]), writes=["xs%d" % i])
                P.dve(lambda e, i=i, c=c: e.scalar_tensor_tensor(out=hT[:, c, :], in0=xs[i][:], scalar=gcol[:, c:c + 1],
                                                                 in1=rstd[:], op0=ALU.mult, op1=ALU.mult),
                      reads=["xs%d" % i, "rstd", "gcol"], writes=[("h", c)])
            hk = [("h", c) for c in range(DC)]
            pa, pak = pb[1], pbk[1]
            for c in range(DC):
                P.pe(lambda e, c=c: e.matmul(pa[0:16, :], Wa[:, c, :], hT[:, c, :], start=(c == 0), stop=(c == DC - 1)),
                     reads=["Wa", ("h", c)], writes=[pak])
            P.act(lambda e: e.copy(out=alr1[0:16, :], in_=pa[0:16, :]), reads=[pak], writes=["alr1"])
            for j in range(NJ):
                jsl = slice(j * 128, (j + 1) * 128)
                for hf in range(2):
                    P.pe(lambda e, jsl=jsl, hf=hf: e.matmul(pb[2 + hf][:], alr1[0:17, jsl], wa2b[0:17, hf * 512:(hf + 1) * 512],
                                                            start=True, stop=True),
                         reads=["alr1", "wa2b"], writes=[pbk[2 + hf]])
                    P.act(lambda e, hf=hf: e.activation(out=e1[:, hf * 512:(hf + 1) * 512], in_=pb[2 + hf][:], func=AF.Exp, scale=-1.0),
                          reads=[pbk[2 + hf]], writes=[("e1", hf)])
                    P.act(lambda e, hf=hf, j=j: e.activation(out=lt[:, j, hf * 512:(hf + 1) * 512], in_=e1[:, hf * 512:(hf + 1) * 512],
                                                             func=AF.Ln, bias=1.0),
                          reads=[("e1", hf)], writes=[("lt", j)])
            pend_tr = []
            for blk in range(2):
                if not state_only:
                    Wq, Wqk = load_w(win_v, blk * 512)
                Wk, Wkk = load_w(win_v, 1024 + blk * 512)
                for dl in range(4):
                    dc = blk * 4 + dl
                    pbt, pbtk = pb[0], pbk[0]
                    for j in range(NJ):
                        jsl = slice(j * 128, (j + 1) * 128)
                        P.pe(lambda e, j=j, jsl=jsl, dc=dc: e.matmul(pbt[:, jsl], lt[:, j, dc * 128:(dc + 1) * 128], tri[:],
                                                                     start=True, stop=True),
                             reads=[("lt", j), "tri"], writes=[pbtk])
                    ei = 0
                    P.act(lambda e, ei=ei: e.activation(out=Eq[ei][:], in_=pbt[:], func=AF.Exp, scale=-1.0 / 16.0),
                          reads=[pbtk], writes=["Eq%d" % ei])
                    P.act(lambda e, ei=ei: e.activation(out=Ek[ei][:], in_=pbt[:], func=AF.Exp, scale=1.0 / 16.0),
                          reads=[pbtk], writes=["Ek%d" % ei])
                    P.dve(lambda e, ei=ei, dc=dc, tt=tt: e.tensor_copy(out=Elast[:, dc, tt * NJ:(tt + 1) * NJ], in_=Eq[ei][:, 127::128]),
                          reads=["Eq%d" % ei], writes=[("Elast", dc)])
                    if not state_only:
                        pq, pqk = next_pb()
                        for c in range(DC):
                            P.pe(lambda e, c=c, dl=dl, pq=pq, Wq=Wq: e.matmul(pq[:], Wq[:, c, dl * 128:(dl + 1) * 128], hT[:, c, :],
                                                                       start=(c == 0), stop=(c == DC - 1)),
                                 reads=[Wqk, ("h", c)], writes=[pqk])
                        P.dve(lambda e, pq=pq, ei=ei, dc=dc: e.scalar_tensor_tensor(out=qT[:, dc, :], in0=pq[:], scalar=1.0 / 16.0,
                                                                                    in1=Eq[ei][:], op0=ALU.mult, op1=ALU.mult),
                              reads=[pqk, "Eq%d" % ei], writes=[("qT", dc)])
                    pk_, pkk = next_pb()
                    for c in range(DC):
                        P.pe(lambda e, c=c, dl=dl, pk_=pk_, Wk=Wk: e.matmul(pk_[:], Wk[:, c, dl * 128:(dl + 1) * 128], hT[:, c, :],
                                                                     start=(c == 0), stop=(c == DC - 1)),
                             reads=[Wkk, ("h", c)], writes=[pkk])
                    P.dve(lambda e, pk_=pk_, ei=ei, dc=dc: e.tensor_tensor(out=kT[:, dc, :], in0=pk_[:], in1=Ek[ei][:], op=ALU.mult),
                          reads=[pkk, "Ek%d" % ei], writes=[("kT", dc)])
                    kd = kd_s[dc % 2]
                    kdk = "kds%d" % (dc % 2)
                    for j in range(NJ):
                        jsl = slice(j * 128, (j + 1) * 128)
                        P.dve(lambda e, kd=kd, dc=dc, j=j, jsl=jsl, tt=tt: e.tensor_scalar(out=kd[:, jsl], in0=kT[:, dc, jsl],
                                                                                    scalar1=Elast[:, dc, tt * NJ + j:tt * NJ + j + 1], scalar2=None,
                                                                                    op0=ALU.mult),
                              reads=[("kT", dc), ("Elast", dc)], writes=[kdk])
                    def emit_tr(kd=kd, kdk=kdk, dc=dc):
                        ptr, ptrk = next_pb()
                        for j2 in range(NJ):
                            jsl2 = slice(j2 * 128, (j2 + 1) * 128)
                            P.pe(lambda e, kd=kd, jsl2=jsl2, ptr=ptr: e.transpose(pbf(ptr)[:, jsl2], kd[:, jsl2], ident[:]),
                                 reads=[kdk, "ident"], writes=[ptrk])
                        P.act(lambda e, ptr=ptr, dc=dc: e.copy(out=kdec[:, :, dc * 128:(dc + 1) * 128],
                                                               in_=pbf(ptr)[:, 0:512].rearrange("p (j d) -> p j d", j=NJ)),
                              reads=[ptrk], writes=[("kdec", dc)])
                    pend_tr.append(emit_tr)
                    if len(pend_tr) > 1:
                        pend_tr.pop(0)()
            while pend_tr:
                pend_tr.pop(0)()
            for vb in range(4):
                Wv, Wvk = load_w(win_v, 2048 + vb * 512)
                for j in range(NJ):
                    jsl = slice(j * 128, (j + 1) * 128)
                    pv, pvk = next_pb()
                    for c in range(DC):
                        P.pe(lambda e, c=c, jsl=jsl, pv=pv, Wv=Wv: e.matmul(pv[:], hT[:, c, jsl], Wv[:, c, :],
                                                                            start=(c == 0), stop=(c == DC - 1)),
                             reads=[Wvk, ("h", c)], writes=[pvk])
                    P.act(lambda e, pv=pv, j=j, vb=vb: e.copy(out=vt[:, j, vb * 512:(vb + 1) * 512], in_=pv[:]),
                          reads=[pvk], writes=[("vt", j, vb)])
            for rb in (range(4) if not state_only else ()):
                Wr, Wrk = load_w(win_v, 4096 + rb * 512)
                for j in range(NJ):
                    jsl = slice(j * 128, (j + 1) * 128)
                    pv, pvk = next_pb()
                    for c in range(DC):
                        P.pe(lambda e, c=c, jsl=jsl, pv=pv, Wr=Wr: e.matmul(pv[:], hT[:, c, jsl], Wr[:, c, :],
                                                                            start=(c == 0), stop=(c == DC - 1)),
                             reads=[Wrk, ("h", c)], writes=[pvk])
                    P.act(lambda e, pv=pv, j=j, rb=rb: e.activation(out=sr[:, j, rb * 512:(rb + 1) * 512], in_=pv[:], func=AF.Silu),
                          reads=[pvk], writes=[("sr", j)])
        if mode == "proj":
            for j in range(NJ):
                P.pool(lambda e, j=j: e.tensor_tensor(out=sr[:, j, :], in0=sr[:, j, :], in1=gnb[:], op=ALU.mult),
                       reads=[("sr", j), "gnb"], writes=[("sr", j)])
            P.dma("sp", lambda e, tsl=tsl: e.dma_start(out=dr["q"].rearrange("(dc p) t -> p dc t", p=128)[:, :, tsl], in_=qT[:]),
                  reads=[("qT", dc) for dc in range(8)], writes=[("dq", tt)])
            P.dma("sp", lambda e, tsl=tsl: e.dma_start(out=dr["k"].rearrange("(dc p) t -> p dc t", p=128)[:, :, tsl], in_=kT[:]),
                  reads=[("kT", dc) for dc in range(8)], writes=[("dk", tt)])
            P.dma("sp", lambda e, tt=tt: e.dma_start(out=dr["kdec"].rearrange("(jj p) d -> p jj d", p=128)[:, tt * NJ:(tt + 1) * NJ, :], in_=kdec[:]),
                  reads=[("kdec", dc) for dc in range(8)], writes=[("dkd", tt)])
            P.dma("sp", lambda e, tt=tt: e.dma_start(out=dr["v"].rearrange("(jj p) n -> p jj n", p=128)[:, tt * NJ:(tt + 1) * NJ, :], in_=vt[:]),
                  reads=[("vt", j, h) for j in range(NJ) for h in range(4)], writes=[("dv", tt)])
            P.dma("sp", lambda e, tt=tt: e.dma_start(out=dr["sr"].rearrange("(jj p) n -> p jj n", p=128)[:, tt * NJ:(tt + 1) * NJ, :], in_=sr[:]),
                  reads=[("sr", j) for j in range(NJ)], writes=[("dsr", tt)])
            continue
        pend_g = []
        for j in range(NJ):
            jsl = slice(j * 128, (j + 1) * 128)
            gt_ = gated2[j % 2]
            gk_ = "gated%d" % (j % 2)
            if not state_only:
                if mode != "scanf":
                    P.pool(lambda e, j=j: e.tensor_tensor(out=sr[:, j, :], in0=sr[:, j, :], in1=gnb[:], op=ALU.mult),
                           reads=[("sr", j), "gnb"], writes=[("sr", j)])
                pst, pstk = pb[4], pbk[4]
                for h in range(4):
                    hs = slice(h * 128, (h + 1) * 128)
                    for di in range(2):
                        dc = 2 * h + di
                        P.pe(lambda e, dc=dc, di=di, hs=hs, jsl=jsl: e.matmul(pst[:, hs], kT[:, dc, jsl], qT[:, dc, jsl],
                                                                              start=(di == 0), stop=(di == 1)),
                             reads=[("kT", dc), ("qT", dc)], writes=[pstk])
                for h in range(4):
                    hs = slice(h * 128, (h + 1) * 128)
                    P.dve(lambda e, h=h, hs=hs: e.tensor_tensor(out=AT4[h][:], in0=pst[:, hs], in1=tri[:], op=ALU.mult),
                          reads=[pstk, "tri"], writes=["AT4%d" % h])
            for h in range(4):
                vkeys = [("vt", j, h)]
                for di in range(2):
                    dc = 2 * h + di
                    pkv, pkvk = pb[5 + di], pbk[5 + di]
                    P.pe(lambda e, dc=dc, j=j, h=h, pkv=pkv: e.matmul(pkv[:], kdec[:, j, dc * 128:(dc + 1) * 128],
                                                                      vt[:, j, h * 512:(h + 1) * 512], start=True, stop=True),
                         reads=[("kdec", dc)] + vkeys, writes=[pkvk])
                    P.dve(lambda e, dc=dc, j=j, pkv=pkv, tt=tt: e.scalar_tensor_tensor(out=S[:, dc, :], in0=S[:, dc, :],
                                                                                scalar=Elast[:, dc, tt * NJ + j:tt * NJ + j + 1], in1=pkv[:],
                                                                                op0=ALU.mult, op1=ALU.add),
                          reads=[pkvk, ("Elast", dc), ("S", dc)], writes=[("S", dc)])
                if not state_only:
                    for di in range(2):
                        dc = 2 * h + di
                        P.pe(lambda e, dc=dc, di=di, h=h, jsl=jsl: e.matmul(pb[h][:], qT[:, dc, jsl], Sb[:, dc, :],
                                                                            start=(di == 0), stop=False),
                             reads=[("qT", dc), ("Sb", dc)], writes=[pbk[h]])
                    P.pe(lambda e, h=h, j=j: e.matmul(pb[h][:], AT4[h][:], vt[:, j, h * 512:(h + 1) * 512], start=False, stop=True),
                         reads=["AT4%d" % h] + vkeys, writes=[pbk[h]])
                    for di in range(2):
                        dc = 2 * h + di
                        P.act(lambda e, dc=dc: e.copy(out=Sb[:, dc, :], in_=S[:, dc, :]), reads=[("S", dc)], writes=[("Sb", dc)])
            if state_only:
                continue
            for h in range(4):
                P.act(lambda e, h=h: e.activation(out=osq[:, h * 512:(h + 1) * 512], in_=pb[h][:], func=AF.Square),
                      reads=[pbk[h]], writes=[("osq", h)])
            P.dve(lambda e: e.reduce_sum(out=ssq[:], in_=osq[:].rearrange("p (h v) -> p h v", h=4), axis=AX.X),
                  reads=[("osq", h) for h in range(4)], writes=["ssq"])
            P.dve(lambda e: e.tensor_scalar(out=ssq[:], in0=ssq[:], scalar1=1.0 / 512.0, scalar2=EPS, op0=ALU.mult, op1=ALU.add),
                  reads=["ssq"], writes=["ssq"])
            P.act(lambda e: e.activation(out=ssq[:], in_=ssq[:], func=AF.Sqrt), reads=["ssq"], writes=["ssq"])
            P.dve(lambda e: e.reciprocal(out=ssq[:], in_=ssq[:]), reads=["ssq"], writes=["ssq"])
            for h in range(4):
                P.dve(lambda e, h=h, j=j, gt_=gt_: e.scalar_tensor_tensor(out=gt_[:, h * 512:(h + 1) * 512], in0=pb[h][:],
                                                                         scalar=ssq[:, h:h + 1], in1=sr[:, j, h * 512:(h + 1) * 512],
                                                                         op0=ALU.mult, op1=ALU.mult),
                      reads=[pbk[h], "ssq", ("sr", j)], writes=[(gk_, h)])

            def emit_gtr(gt_=gt_, gk_=gk_, j=j, jsl=jsl):
                for fb in range(4):
                    ptr, ptrk = pb[7], pbk[7]
                    for fi in range(4):
                        fc = fb * 4 + fi
                        P.pe(lambda e, fc=fc, fi=fi, ptr=ptr, gt_=gt_: e.transpose(pbf(ptr)[:, fi * 128:(fi + 1) * 128],
                                                                                  gt_[:, fc * 128:(fc + 1) * 128], ident[:]),
                             reads=[(gk_, fb), "ident"], writes=[ptrk])
                    P.act(lambda e, fb=fb, jsl=jsl, ptr=ptr: e.copy(out=gT[:, fb * 4:(fb + 1) * 4, jsl],
                                                                   in_=pbf(ptr)[:, 0:512].rearrange("p (f t) -> p f t", f=4)),
                          reads=[ptrk], writes=[("gT", fb, j)])
            pend_g.append(emit_gtr)
            if len(pend_g) > 1:
                pend_g.pop(0)()
        while pend_g:
            pend_g.pop(0)()
        for ob in (range(4) if not state_only else ()):
            Wo, Wok = load_w(wout_v, ob * 512)
            for dd in range(4):
                d = ob * 4 + dd
                py_, pyk = next_pb()
                for fc in range(DC):
                    P.pe(lambda e, fc=fc, dd=dd, py_=py_, Wo=Wo: e.matmul(py_[:], Wo[:, fc, dd * 128:(dd + 1) * 128], gT[:, fc, :],
                                                                          start=(fc == 0), stop=(fc == DC - 1)),
                         reads=[Wok] + [("gT", fc // 4, j) for j in range(NJ)], writes=[pyk])
                i = xctr[0] % 3
                xctr[0] += 1
                P.dma("sp", lambda e, i=i, d=d, tsl=tsl: e.dma_start(out=xs[i][:], in_=xin_v[:, d, tsl]), writes=["xs%d" % i])
                P.dve(lambda e, i=i, py_=py_: e.tensor_tensor(out=xs[i][:], in0=xs[i][:], in1=py_[:], op=ALU.add),
                      reads=[pyk, "xs%d" % i], writes=["xs%d" % i])
                P.dma("sp", lambda e, i=i, d=d, tsl=tsl: e.dma_start(out=xout_v[:, d, tsl], in_=xs[i][:]),
                      reads=["xs%d" % i], writes=[("xo", tt, d)])
                ph.out_keys.append(("xo", tt, d))
    if mode == "proj":
        P.dma("sp", lambda e: e.dma_start(out=dr["elast"].rearrange("p (dc jj) -> p dc jj", dc=8), in_=Elast[:]),
              reads=[("Elast", dc) for dc in range(8)], writes=["del"])
    if st_out is not None:
        P.dma("sp", lambda e: e.dma_start(out=st_out[:, :, :], in_=S[:]), reads=[("S", dc) for dc in range(8)],
              writes=["st_out"])
        ph.out_keys.append("st_out")
    return ph.finish()


def build_gla_scan0(ph):
    P = ph.P
    dr = ph.bind["dr"]
    st_out = ph.bind["st_out"]
    NCH = TC // 128
    kd_all = ph.sb("kd_all", [128, NCH, 1024], BF16)
    v_all = ph.sb("v_all", [128, NCH, 2048], BF16)
    El = ph.sb("El", [128, 8, NCH], F32)
    S = ph.sb("S", [128, 8, 512], F32)
    pb = [ph.ps("pb%d" % i, [128, 512]) for i in range(8)]
    kd_v = dr["kdec"].rearrange("(jj p) d -> p jj d", p=128)
    v_v = dr["v"].rearrange("(jj p) n -> p jj n", p=128)
    for q4 in range(4):
        P.dma("sp", lambda e, q4=q4: e.dma_start(out=kd_all[:, q4 * 4:(q4 + 1) * 4, :], in_=kd_v[:, q4 * 4:(q4 + 1) * 4, :]),
              writes=[("kd", q4)])
        P.dma("sp", lambda e, q4=q4: e.dma_start(out=v_all[:, q4 * 4:(q4 + 1) * 4, :], in_=v_v[:, q4 * 4:(q4 + 1) * 4, :]),
              writes=[("v", q4)])
    P.dma("sp", lambda e: e.dma_start(out=El[:], in_=dr["elast"].rearrange("p (dc jj) -> p dc jj", dc=8)), writes=["El"])
    P.dve(lambda e: e.memset(S[:], 0.0), writes=[("S", dc) for dc in range(8)])
    cnt = 0
    for jg in range(NCH):
        for h in range(4):
            for di in range(2):
                dc = 2 * h + di
                pkv, pkvk = pb[cnt % 8], "pb%d" % (cnt % 8)
                cnt += 1
                P.pe(lambda e, dc=dc, jg=jg, h=h, pkv=pkv: e.matmul(pkv[:], kd_all[:, jg, dc * 128:(dc + 1) * 128],
                                                                  v_all[:, jg, h * 512:(h + 1) * 512], start=True, stop=True),
                     reads=[("kd", jg // 4), ("v", jg // 4)], writes=[pkvk])
                P.dve(lambda e, dc=dc, jg=jg, pkv=pkv: e.scalar_tensor_tensor(out=S[:, dc, :], in0=S[:, dc, :],
                                                                             scalar=El[:, dc, jg:jg + 1], in1=pkv[:],
                                                                             op0=ALU.mult, op1=ALU.add),
                      reads=[pkvk, "El", ("S", dc)], writes=[("S", dc)])
    P.dma("sp", lambda e: e.dma_start(out=st_out[:, :, :], in_=S[:]), reads=[("S", dc) for dc in range(8)], writes=["st_out"])
    return ph.finish()


def pbf(ptile):
    return ptile[:].bitcast(BF16) if hasattr(ptile[:], "bitcast") else ptile


def gla_consts():
    tri = np.triu(np.ones((128, 128), np.float32))
    ident = np.eye(128, dtype=np.float32)
    return tri, ident


def gla_inputs(xT_core, g, w_in, w_a2, b_a, g_norm, w_out, state):
    tri, ident = gla_consts()
    st = np.ascontiguousarray(np.asarray(state, np.float32).reshape(4, 2, 128, 512).transpose(2, 0, 1, 3).reshape(128, 8, 512))
    gnb = np.ascontiguousarray(np.broadcast_to(np.tile(np.asarray(g_norm, np.float32), 4)[None, :], (128, 2048)))
    wa2b = np.ascontiguousarray(np.concatenate([np.asarray(w_a2, np.float32), np.asarray(b_a, np.float32)[None, :]], axis=0))
    return {"xT": np.ascontiguousarray(xT_core, dtype=np.float32), "g": col16(g), "win": np.ascontiguousarray(w_in, dtype=np.float32),
            "wa2b": wa2b, "gnb": gnb, "wout": np.ascontiguousarray(w_out, dtype=np.float32), "st_in": st, "tri": tri, "ident": ident}


def gla_state_from_out(st_out):
    return np.ascontiguousarray(st_out.reshape(128, 4, 2, 512).transpose(1, 2, 0, 3).reshape(4, 256, 512))


def build_diff1_phase(do_qk=True, do_v=True, do_norm=True, ph=None):
    ph = ph or Phase()
    P = ph.P
    xin = ph.din("xT", [D, TC])
    g_in = ph.din("g", [128, DC])
    win = ph.din("win", [D, 3 * D])
    qkg_in = ph.din("qkg", [128, 16])
    qko = ph.dout("qkT", [2 * D, TC])
    vo = ph.dout("v", [TC, D])
    NT = 512
    xs = [ph.sb("xs%d" % i, [128, 512], F32) for i in range(3)]
    hT = ph.sb("hTs", [128, DC, NT], BF16)
    Wb = [ph.sb("Wb%d" % i, [128, DC, 512], BF16) for i in range(2)]
    qraw = [ph.sb("qraw%d" % i, [128, 512], F32) for i in range(2)]
    qn = [ph.sb("qn%d" % i, [128, 512], F32) for i in range(3)]
    vs = [ph.sb("vs%d" % i, [128, 512], F32) for i in range(3)]
    sq = [ph.sb("sq%d" % i, [128, 512], BF16) for i in range(2)]
    rstd = ph.sb("rstd", [128, 512], F32)
    rs2 = [ph.sb("rs2%d" % i, [128, 512], F32) for i in range(2)]
    gcol = ph.sb("gcol", [128, DC], F32)
    qkg = ph.sb("qkgs", [128, 16], F32)
    ones = ph.sb("ones", [128, 128], BF16)
    pb = [ph.ps("pb%d" % i, [128, 512]) for i in range(8)]
    pbk = ["pb%d" % i for i in range(8)]
    P.dma("sp", lambda e: e.dma_start(out=gcol[:], in_=g_in[:, :]), writes=["gcol"])
    P.dma("sp", lambda e: e.dma_start(out=qkg[:], in_=qkg_in[:, :]), writes=["qg", "kg"])
    P.dve(lambda e: e.memset(ones[:], 1.0), writes=["ones"])
    win_v = win.rearrange("(c p) n -> p c n", p=128)
    xin_v = xin.rearrange("(c p) t -> p c t", p=128)
    k_ds = ph.bind.get("k_ds")
    v_ds = ph.bind.get("v_ds")
    if k_ds is not None:
        q_v = ph.bind["q_d"].rearrange("(c p) t -> p c t", p=128)
        k_vs = [kd_.rearrange("(c p) t -> p c t", p=128) for kd_ in k_ds]
        v_vs = [vd_.rearrange("(j p) n -> p j n", p=128) for vd_ in v_ds]
    else:
        qko_v = qko.rearrange("(c p) t -> p c t", p=128)
    vo_v = vo.rearrange("(j p) n -> p j n", p=128)
    xctr = [0]
    wctr = [0]
    pctr = [0]
    nctr = [0]

    def load_w(col0):
        i = wctr[0] % 2
        wctr[0] += 1
        P.dma("pool", lambda e, i=i, col0=col0: e.dma_start(out=Wb[i][:], in_=win_v[:, :, col0:col0 + 512]), writes=["Wb%d" % i])
        return Wb[i], "Wb%d" % i

    def next_pb():
        i = 2 + pctr[0] % 4
        pctr[0] += 1
        return pb[i], pbk[i]

    for tt in range(TC // NT):
        tsl = slice(tt * NT, (tt + 1) * NT)
        p_ss, p_ssk = pb[0], pbk[0]
        for c in range(DC):
            i = xctr[0] % 3
            xctr[0] += 1
            P.dma("sp", lambda e, i=i, c=c, tsl=tsl: e.dma_start(out=xs[i][:], in_=xin_v[:, c, tsl]), writes=["xs%d" % i])
            q = sq[c % 2]
            qk = "sq%d" % (c % 2)
            P.act(lambda e, q=q, i=i: e.activation(out=q[:], in_=xs[i][:], func=AF.Square), reads=["xs%d" % i], writes=[qk])
            P.pe(lambda e, q=q, c=c: e.matmul(p_ss[:], ones[:], q[:], start=(c == 0), stop=(c == DC - 1)),
                 reads=[qk, "ones"], writes=[p_ssk])
        P.dve(lambda e: e.tensor_scalar(out=rstd[:], in0=p_ss[:], scalar1=1.0 / D, scalar2=EPS, op0=ALU.mult, op1=ALU.add),
              reads=[p_ssk], writes=["rstd"])
        P.act(lambda e: e.activation(out=rstd[:], in_=rstd[:], func=AF.Sqrt), reads=["rstd"], writes=["rstd"])
        P.dve(lambda e: e.reciprocal(out=rstd[:], in_=rstd[:]), reads=["rstd"], writes=["rstd"])
        for c in range(DC):
            i = xctr[0] % 3
            xctr[0] += 1
            P.dma("sp", lambda e, i=i, c=c, tsl=tsl: e.dma_start(out=xs[i][:], in_=xin_v[:, c, tsl]), writes=["xs%d" % i])
            P.dve(lambda e, i=i, c=c: e.scalar_tensor_tensor(out=hT[:, c, :], in0=xs[i][:], scalar=gcol[:, c:c + 1],
                                                             in1=rstd[:], op0=ALU.mult, op1=ALU.mult),
                  reads=["xs%d" % i, "rstd", "gcol"], writes=[("h", c)])
        for which in (range(2) if do_qk else ()):
            gk = "qg" if which == 0 else "kg"
            for blk in range(4):
                Wq, Wqk = load_w(which * D + blk * 512)
                for dl in range(4):
                    hd = blk * 4 + dl
                    pq, pqk = next_pb()
                    for c in range(DC):
                        P.pe(lambda e, c=c, dl=dl, pq=pq, Wq=Wq: e.matmul(pq[:], Wq[:, c, dl * 128:(dl + 1) * 128], hT[:, c, :],
                                                                          start=(c == 0), stop=(c == DC - 1)),
                             reads=[Wqk, ("h", c)], writes=[pqk])
                    if do_norm:
                        qr = qraw[hd % 2]
                        qrk = "qraw%d" % (hd % 2)
                        sqb = sq[hd % 2]
                        sqk = "sq%d" % (hd % 2)
                        P.act(lambda e, pq=pq, sqb=sqb: e.activation(out=sqb[:], in_=pq[:], func=AF.Square), reads=[pqk], writes=[sqk])
                        P.act(lambda e, pq=pq, qr=qr: e.copy(out=qr[:], in_=pq[:]), reads=[pqk], writes=[qrk])
                        p2, p2k = pb[6 + hd % 2], pbk[6 + hd % 2]
                        P.pe(lambda e, p2=p2, sqb=sqb: e.matmul(p2[:], ones[:], sqb[:], start=True, stop=True),
                             reads=[sqk, "ones"], writes=[p2k])
                        r2 = rs2[hd % 2]
                        r2k = "rs2%d" % (hd % 2)
                        P.dve(lambda e, p2=p2, r2=r2: e.tensor_scalar(out=r2[:], in0=p2[:], scalar1=1.0 / 128.0, scalar2=EPS,
                                                                      op0=ALU.mult, op1=ALU.add), reads=[p2k], writes=[r2k])
                        P.act(lambda e, r2=r2: e.activation(out=r2[:], in_=r2[:], func=AF.Sqrt), reads=[r2k], writes=[r2k])
                        P.dve(lambda e, r2=r2: e.reciprocal(out=r2[:], in_=r2[:]), reads=[r2k], writes=[r2k])
                        ni = nctr[0] % 3
                        nctr[0] += 1
                        P.dve(lambda e, ni=ni, qr=qr, r2=r2, which=which: e.scalar_tensor_tensor(out=qn[ni][:], in0=qr[:], scalar=qkg[:, which:which + 1],
                                                                                              in1=r2[:], op0=ALU.mult, op1=ALU.mult),
                              reads=[qrk, r2k, gk], writes=["qn%d" % ni])
                    else:
                        ni = nctr[0] % 3
                        nctr[0] += 1
                        P.act(lambda e, pq=pq, ni=ni: e.copy(out=qn[ni][:], in_=pq[:]), reads=[pqk], writes=["qn%d" % ni])
                    if k_ds is not None:
                        dst_ap = q_v[:, hd, tsl] if which == 0 else k_vs[hd // 2][:, hd % 2, tsl]
                    else:
                        dst_ap = qko_v[:, which * 16 + hd, tsl]
                    P.dma("sp", lambda e, ni=ni, dst_ap=dst_ap: e.dma_start(out=dst_ap, in_=qn[ni][:]),
                          reads=["qn%d" % ni], writes=[("qo", which, tt, hd)])
                    ph.out_keys.append(("qo", which, tt, hd))
        for vb in (range(4) if do_v else ()):
            Wv, Wvk = load_w(2 * D + vb * 512)
            for j in range(4):
                jsl = slice(j * 128, (j + 1) * 128)
                pv, pvk = next_pb()
                for c in range(DC):
                    P.pe(lambda e, c=c, jsl=jsl, pv=pv, Wv=Wv: e.matmul(pv[:], hT[:, c, jsl], Wv[:, c, :],
                                                                        start=(c == 0), stop=(c == DC - 1)),
                         reads=[Wvk, ("h", c)], writes=[pvk])
                vi = nctr[0] % 3
                nctr[0] += 1
                P.act(lambda e, pv=pv, vi=vi: e.copy(out=vs[vi][:], in_=pv[:]), reads=[pvk], writes=["vs%d" % vi])
                if v_ds is not None:
                    for hh in range(2):
                        P.dma("sp", lambda e, vi=vi, tt=tt, j=j, vb=vb, hh=hh: e.dma_start(
                            out=v_vs[2 * vb + hh][:, tt * 4 + j, :], in_=vs[vi][:, hh * 256:(hh + 1) * 256]),
                            reads=["vs%d" % vi], writes=[("vo", tt, j, vb, hh)])
                else:
                    P.dma("sp", lambda e, vi=vi, tt=tt, j=j, vb=vb: e.dma_start(out=vo_v[:, tt * 4 + j, vb * 512:(vb + 1) * 512], in_=vs[vi][:]),
                          reads=["vs%d" % vi], writes=[("vo", tt, j, vb)])
                    ph.out_keys.append(("vo", tt, j, vb))
    return ph.finish()


LAM_INIT2 = 0.8 - 0.6 * math.exp(-0.3 * 2)
NEG = -30000.0


def build_diff2_phase(ph=None):
    ph = ph or Phase()
    P = ph.P
    fused = ph.master is not None
    qin = ph.din("qT", [D, TC])
    kin = None if fused else ph.din("kT", [D, 2 * TC])
    vin = None if fused else ph.din("va", [2 * TC, 8, 257])
    xin = ph.din("xT", [D, TC])
    wout = ph.din("wout", [D, D])
    b0_in = ph.din("B0", [128, 16, 128])
    b1_in = ph.din("B1", [128, 16, 128])
    b1p_in = ph.din("B1p", [128, 16, 128])
    cf_in = ph.din("cfar", [128, 16])
    cp_in = ph.din("cpre", [128, 16])
    lp_in = ph.din("lpb", [128, 4, 128])
    sg_in = ph.din("sgb", [128, 256])
    id_in = ph.din("ident", [128, 128])
    xout = ph.dout("xTo", [D, TC])
    SCALE = 128 ** -0.5
    Kt = ph.sb("Kt", [128, 2, 2 * TC], BF16)
    Va = ph.sb("Va", [128, 32, 257], BF16)
    Qt = ph.sb("Qt", [128, 2, TC], BF16)
    ao = ph.sb("ao", [128, 16, 2048], BF16)
    M0 = ph.sb("M0", [128, 16, 128], BF16)
    M1 = ph.sb("M1", [128, 16, 128], BF16)
    M1p = ph.sb("M1p", [128, 16, 128], BF16)
    btmp = ph.sb("btmp", [128, 16, 128], F32)
    cfar = ph.sb("cfars", [128, 16], F32)
    cpre = ph.sb("cpres", [128, 16], F32)
    negc = ph.sb("negc", [128, 16], F32)
    negcp = ph.sb("negcp", [128, 16], F32)
    lpb = ph.sb("lpbs", [128, 4, 128], F32)
    lt1 = ph.sb("lt1", [128, 128], F32)
    lsum = ph.sb("lsum", [128, 2], F32)
    neglam = ph.sb("neglam", [128, 1], F32)
    sgs = ph.sb("sgs", [128, 256], F32)
    ident = ph.sb("idents", [128, 128], BF16)
    PT = [ph.sb("PT%d" % i, [128, 2, 256], BF16) for i in range(3)]
    rc = ph.sb("rc", [128, 4], F32)
    uu = ph.sb("uu", [128, 256], F32)
    att = ph.sb("att", [128, 256], F32)
    asq = ph.sb("asq", [128, 256], F32)
    ssn = ph.sb("ssn", [128, 1], F32)
    ssn16 = ph.sb("ssn16", [128, 16], F32)
    aoT = ph.sb("aoT", [128, DC, 512], BF16)
    Wb = [ph.sb("Wb%d" % i, [128, DC, 512], BF16) for i in range(2)]
    xs = [ph.sb("xs%d" % i, [128, 512], F32) for i in range(3)]
    pb = [ph.ps("pb%d" % i, [128, 512]) for i in range(8)]
    pbk = ["pb%d" % i for i in range(8)]

    for (dst, src, k) in ((cfar, cf_in, "cfar"), (cpre, cp_in, "cpre"), (sgs, sg_in, "sgs")):
        P.dma("sp", lambda e, dst=dst, src=src: e.dma_start(out=dst[:], in_=src[:, :]), writes=[k])
    P.dma("sp", lambda e: e.dma_start(out=lpb[:], in_=lp_in[:, :, :]), writes=["lpb"])
    P.dma("pool", lambda e: e.dma_start(out=ident[:], in_=id_in[:, :]), writes=["ident"])
    P.dve(lambda e: e.tensor_scalar(out=negc[:], in0=cfar[:], scalar1=-1.0, scalar2=None, op0=ALU.mult), reads=["cfar"], writes=["negc"])
    P.dve(lambda e: e.tensor_scalar(out=negcp[:], in0=cpre[:], scalar1=-1.0, scalar2=None, op0=ALU.mult), reads=["cpre"], writes=["negcp"])
    for (Mt, src, nb, k) in ((M0, b0_in, negc, "M0"), (M1, b1_in, negc, "M1"), (M1p, b1p_in, negc, "M1p")):
        P.dma("sp", lambda e, src=src: e.dma_start(out=btmp[:], in_=src[:, :, :]), writes=["btmp"])
        for h in range(16):
            P.act(lambda e, Mt=Mt, nb=nb, h=h: e.activation(out=Mt[:, h, :], in_=btmp[:, h, :], func=AF.Exp, bias=nb[:, h:h + 1]),
                  reads=["btmp", "negc", "negcp"], writes=[k])
    for pi in range(2):
        P.dve(lambda e, pi=pi: e.tensor_tensor(out=lt1[:], in0=lpb[:, 2 * pi, :], in1=lpb[:, 2 * pi + 1, :], op=ALU.mult),
              reads=["lpb"], writes=["lt1"])
        P.dve(lambda e, pi=pi: e.reduce_sum(out=lsum[:, pi:pi + 1], in_=lt1[:], axis=AX.X), reads=["lt1"], writes=["lsum"])
    P.act(lambda e: e.activation(out=lsum[:], in_=lsum[:], func=AF.Exp), reads=["lsum"], writes=["lsum"])
    P.dve(lambda e: e.tensor_tensor(out=neglam[:], in0=lsum[:, 1:2], in1=lsum[:, 0:1], op=ALU.subtract), reads=["lsum"], writes=["neglam"])
    P.dve(lambda e: e.tensor_scalar(out=neglam[:], in0=neglam[:], scalar1=-LAM_INIT2, scalar2=None, op0=ALU.add),
          reads=["neglam"], writes=["neglam"])
    P.dve(lambda e: e.tensor_scalar(out=sgs[:], in0=sgs[:], scalar1=1.0 - LAM_INIT2, scalar2=None, op0=ALU.mult),
          reads=["sgs"], writes=["sgs"])

    qin_v = qin.rearrange("(c p) t -> p c t", p=128)
    if fused:
        kpre_vs = [a_.rearrange("(c p) t -> p c t", p=128) for a_ in ph.bind["kpres"]]
        kown_vs = [a_.rearrange("(c p) t -> p c t", p=128) for a_ in ph.bind["kowns"]]
        vpre_vs = [a_.rearrange("(kb p) n -> p kb n", p=128) for a_ in ph.bind["vpres"]]
        vown_vs = [a_.rearrange("(kb p) n -> p kb n", p=128) for a_ in ph.bind["vowns"]]
        P.dve(lambda e: e.memset(Va[:, :, 256:257], 1.0), writes=["Va1"])
    else:
        kin_v = kin.rearrange("(c p) t -> p c t", p=128)
        vin_v = vin.rearrange("(kb p) h n -> p kb h n", p=128)
    xin_v = xin.rearrange("(c p) t -> p c t", p=128)
    xout_v = xout.rearrange("(c p) t -> p c t", p=128)
    wout_v = wout.rearrange("(c p) n -> p c n", p=128)
    LOOK = 2
    NPT = 4
    PTs = [ph.sb("PTp%d" % i, [128, 2, 256], BF16) for i in range(NPT)]
    Osb = [ph.sb("Osb%d" % i, [128, 257], F32) for i in range(8)]
    zero_b = ph.sb("zero_b", [128, 1], F32)
    P.dve(lambda e: e.memset(zero_b[:], 0.0), writes=["zero_b"])
    gstep = [0]

    def stage_a(hp, qt, kb, sidx):
        kb_rel = kb - (16 + 2 * qt)
        qlo = 128 if kb_rel == 1 else 0
        ps, psk = pb[4 + sidx % 3], pbk[4 + sidx % 3]
        pt, ptk = PTs[sidx % NPT], "PTp%d" % (sidx % NPT)
        for i in range(2):
            P.pe(lambda e, ps=ps, i=i, kb=kb, qt=qt, qlo=qlo: e.matmul(
                ps[:, i * 256 + qlo:(i + 1) * 256], Kt[:, i, kb * 128:(kb + 1) * 128],
                Qt[:, i, qt * 256 + qlo:(qt + 1) * 256], start=True, stop=True),
                reads=["Kt", "Qt"], writes=[psk])
        bias_ap = cpre[:, 0:1] if kb < 16 else zero_b[:, 0:1]
        P.act(lambda e, ps=ps, pt=pt, qlo=qlo, bias_ap=bias_ap: e.activation(
            out=pt[:, :, qlo:256], in_=ps[:].rearrange("p (i q) -> p i q", i=2)[:, :, qlo:256], func=AF.Exp,
            bias=bias_ap, scale=SCALE),
            reads=[psk, "cpre", "zero_b"], writes=[ptk])
        fix = []
        if kb_rel == -1:
            fix.append((0, M1p if kb == 15 else M1, "M1p" if kb == 15 else "M1"))
        elif kb_rel == 0:
            fix.append((0, M0, "M0"))
            fix.append((1, M1, "M1"))
        elif kb_rel == 1:
            fix.append((1, M0, "M0"))
        for (qb, Mt, mk) in fix:
            P.dve(lambda e, pt=pt, qb=qb, Mt=Mt, hp=hp: e.tensor_tensor(
                out=pt[:, :, qb * 128:(qb + 1) * 128], in0=pt[:, :, qb * 128:(qb + 1) * 128],
                in1=Mt[:, 2 * hp:2 * hp + 2, :], op=ALU.mult),
                reads=[ptk, mk], writes=[ptk])

    def stage_b(hp, qt, kb, sidx):
        pt, ptk = PTs[sidx % NPT], "PTp%d" % (sidx % NPT)
        for qb in range(2):
            last = 16 + 2 * qt + qb
            if kb > last:
                continue
            for i in range(2):
                acc = pb[qb * 2 + i]
                P.pe(lambda e, acc=acc, pt=pt, i=i, qb=qb, kb=kb, last=last: e.matmul(
                    acc[:, 0:257], pt[:, i, qb * 128:(qb + 1) * 128], Va[:, kb, :],
                    start=(kb == 0), stop=(kb == last)),
                    reads=[ptk, "Va"], writes=[pbk[qb * 2 + i]])
        if kb == 16 + 2 * qt + 1:
            finalize(hp, qt)

    fctr = [0]

    def finalize(hp, qt):
        fs = (fctr[0] % 2) * 4
        fctr[0] += 1
        for a_i in range(4):
            P.dve(lambda e, a_i=a_i, fs=fs: e.tensor_scalar(out=Osb[fs + a_i][:], in0=pb[a_i][:, 0:257], scalar1=1.0, scalar2=None,
                                                            op0=ALU.mult),
                  reads=[pbk[a_i]], writes=["Osb%d" % (fs + a_i)])
        for qb in range(2):
            o1, o1k = Osb[fs + qb * 2], "Osb%d" % (fs + qb * 2)
            o2, o2k = Osb[fs + qb * 2 + 1], "Osb%d" % (fs + qb * 2 + 1)
            P.dve(lambda e, o1=o1: e.reciprocal(out=rc[:, 0:1], in_=o1[:, 256:257]), reads=[o1k], writes=["rc"])
            P.dve(lambda e, o2=o2: e.reciprocal(out=rc[:, 1:2], in_=o2[:, 256:257]), reads=[o2k], writes=["rc"])
            P.dve(lambda e: e.tensor_tensor(out=rc[:, 1:2], in0=rc[:, 1:2], in1=neglam[:], op=ALU.mult),
                  reads=["rc", "neglam"], writes=["rc"])
            P.dve(lambda e, o2=o2: e.tensor_scalar(out=uu[:], in0=o2[:, 0:256], scalar1=rc[:, 1:2], scalar2=None, op0=ALU.mult),
                  reads=[o2k, "rc"], writes=["uu"])
            P.dve(lambda e, o1=o1: e.scalar_tensor_tensor(out=att[:], in0=o1[:, 0:256], scalar=rc[:, 0:1], in1=uu[:],
                                                          op0=ALU.mult, op1=ALU.add),
                  reads=[o1k, "rc", "uu"], writes=["att"])
            P.dve(lambda e: e.tensor_tensor(out=asq[:], in0=att[:], in1=att[:], op=ALU.mult), reads=["att"], writes=["asq"])
            qbg = 2 * qt + qb
            P.dve(lambda e, qbg=qbg: e.reduce_sum(out=ssn16[:, qbg:qbg + 1], in_=asq[:], axis=AX.X), reads=["asq"], writes=["ssn16"])
            P.dve(lambda e, qbg=qbg, hp=hp: e.tensor_scalar(out=ao[:, qbg, hp * 256:(hp + 1) * 256], in0=att[:], scalar1=1.0,
                                                            scalar2=None, op0=ALU.mult),
                  reads=["att"], writes=[("ao", qbg)])

    def post_hp(hp):
        P.dve(lambda e: e.tensor_scalar(out=ssn16[:], in0=ssn16[:], scalar1=1.0 / 256.0, scalar2=EPS, op0=ALU.mult, op1=ALU.add),
              reads=["ssn16"], writes=["ssn16"])
        P.act(lambda e: e.activation(out=ssn16[:], in_=ssn16[:], func=AF.Sqrt), reads=["ssn16"], writes=["ssn16"])
        P.dve(lambda e: e.reciprocal(out=ssn16[:], in_=ssn16[:]), reads=["ssn16"], writes=["ssn16"])
        for qbg in range(16):
            P.dve(lambda e, qbg=qbg, hp=hp: e.scalar_tensor_tensor(out=ao[:, qbg, hp * 256:(hp + 1) * 256],
                                                                   in0=ao[:, qbg, hp * 256:(hp + 1) * 256],
                                                                   scalar=ssn16[:, qbg:qbg + 1], in1=sgs[:], op0=ALU.mult, op1=ALU.mult),
                  reads=[("ao", qbg), "ssn16", "sgs"], writes=[("ao", qbg)])

    for hp in range(8):
        if fused:
            P.dma("pool", lambda e, hp=hp: e.dma_start(out=Kt[:, :, 0:TC], in_=kpre_vs[hp][:, :, :]), writes=["Kt"])
            P.dma("pool", lambda e, hp=hp: e.dma_start(out=Kt[:, :, TC:2 * TC], in_=kown_vs[hp][:, :, :]), writes=["Kt"])
            P.dma("pool", lambda e, hp=hp: e.dma_start(out=Va[:, 0:16, 0:256], in_=vpre_vs[hp][:, :, :]), reads=["Va1"], writes=["Va"])
            P.dma("pool", lambda e, hp=hp: e.dma_start(out=Va[:, 16:32, 0:256], in_=vown_vs[hp][:, :, :]), reads=["Va1"], writes=["Va"])
        else:
            P.dma("pool", lambda e, hp=hp: e.dma_start(out=Kt[:, :, 0:TC], in_=kin_v[:, 2 * hp:2 * hp + 2, 0:TC]), writes=["Kt"])
            P.dma("pool", lambda e, hp=hp: e.dma_start(out=Kt[:, :, TC:2 * TC], in_=kin_v[:, 2 * hp:2 * hp + 2, TC:2 * TC]), writes=["Kt"])
            P.dma("pool", lambda e, hp=hp: e.dma_start(out=Va[:, 0:16, :], in_=vin_v[:, 0:16, hp, :]), writes=["Va"])
            P.dma("pool", lambda e, hp=hp: e.dma_start(out=Va[:, 16:32, :], in_=vin_v[:, 16:32, hp, :]), writes=["Va"])
        P.dma("pool", lambda e, hp=hp: e.dma_start(out=Qt[:], in_=qin_v[:, 2 * hp:2 * hp + 2, :]), writes=["Qt"])
        steps = [(qt, kb) for qt in range(8) for kb in range(16 + 2 * qt + 2)]
        base = gstep[0]
        for n in range(len(steps) + LOOK):
            if n < len(steps):
                stage_a(hp, steps[n][0], steps[n][1], base + n)
            if n >= LOOK:
                stage_b(hp, steps[n - LOOK][0], steps[n - LOOK][1], base + n - LOOK)
        gstep[0] += len(steps)
        post_hp(hp)
    xctr = [0]
    wctr = [0]
    pctr = [0]
    for tt in range(4):
        tsl = slice(tt * 512, (tt + 1) * 512)
        for j in range(4):
            qbg = tt * 4 + j
            for fb in range(4):
                ptr, ptrk = pb[6], pbk[6]
                for fi in range(4):
                    fc = fb * 4 + fi
                    P.pe(lambda e, ptr=ptr, fi=fi, fc=fc, qbg=qbg: e.transpose(pbf(ptr)[:, fi * 128:(fi + 1) * 128],
                                                                              ao[:, qbg, fc * 128:(fc + 1) * 128], ident[:]),
                         reads=[("ao", qbg), "ident"], writes=[ptrk])
                P.act(lambda e, ptr=ptr, fb=fb, j=j: e.copy(out=aoT[:, fb * 4:(fb + 1) * 4, j * 128:(j + 1) * 128],
                                                            in_=pbf(ptr)[:, 0:512].rearrange("p (f t) -> p f t", f=4)),
                      reads=[ptrk], writes=[("aoT", fb)])
        for ob in range(4):
            wi = wctr[0] % 2
            wctr[0] += 1
            P.dma("pool", lambda e, wi=wi, ob=ob: e.dma_start(out=Wb[wi][:], in_=wout_v[:, :, ob * 512:(ob + 1) * 512]),
                  writes=["Wb%d" % wi])
            for dd in range(4):
                d = ob * 4 + dd
                pi = pctr[0] % 2
                pctr[0] += 1
                py_, pyk = pb[pi], pbk[pi]
                for fc in range(DC):
                    P.pe(lambda e, fc=fc, dd=dd, py_=py_, wi=wi: e.matmul(py_[:], Wb[wi][:, fc, dd * 128:(dd + 1) * 128], aoT[:, fc, :],
                                                                          start=(fc == 0), stop=(fc == DC - 1)),
                         reads=["Wb%d" % wi, ("aoT", fc // 4)], writes=[pyk])
                xi = xctr[0] % 3
                xctr[0] += 1
                P.dma("sp", lambda e, xi=xi, d=d, tsl=tsl: e.dma_start(out=xs[xi][:], in_=xin_v[:, d, tsl]), writes=["xs%d" % xi])
                P.dve(lambda e, xi=xi, py_=py_: e.tensor_tensor(out=xs[xi][:], in0=xs[xi][:], in1=py_[:], op=ALU.add),
                      reads=[pyk, "xs%d" % xi], writes=["xs%d" % xi])
                P.dma("sp", lambda e, xi=xi, d=d, tsl=tsl: e.dma_start(out=xout_v[:, d, tsl], in_=xs[xi][:]),
                      reads=["xs%d" % xi], writes=[("xo", tt, d)])
                ph.out_keys.append(("xo", tt, d))
    return ph.finish()


def rel_bucket_np(rel):
    n = np.maximum(rel, 0)
    nf = np.maximum(n, 1).astype(np.float32)
    large = 16 + (np.log(nf / np.float32(16)) / np.float32(math.log(128 / 16)) * np.float32(16)).astype(np.int32)
    large = np.minimum(large, 31)
    return np.where(n < 16, n, large)


def diff_bias_tiles(rel_bias, first_half):
    tab = np.concatenate([np.asarray(rel_bias, np.float32), np.full((1, 16), NEG, np.float32),
                          np.full((1, 16), 2 * NEG, np.float32)], axis=0)
    k = np.arange(128)[:, None]
    q = np.arange(128)[None, :]
    rel0 = q - k
    idx0 = np.where(rel0 >= 0, rel_bucket_np(rel0), 32)
    idx1 = rel_bucket_np(128 + q - k)
    B0 = np.ascontiguousarray(tab[idx0].transpose(0, 2, 1))
    B1 = np.ascontiguousarray(tab[idx1].transpose(0, 2, 1))
    if first_half:
        B1p = np.full((128, 16, 128), 2 * NEG, np.float32)
        cpre = np.full((128, 16), NEG, np.float32)
    else:
        B1p = B1.copy()
        cpre = np.zeros((128, 16), np.float32)
    cfar = np.ascontiguousarray(np.broadcast_to(tab[31][None, :], (128, 16)))
    return B0, B1, B1p, cfar, cpre


def diff2_inputs(qT, kT_own, v_own, kT_prev, v_prev, xT, w_out, rel_bias, lam_params, sub_gain, first_half):
    kT_all = np.zeros((D, 2 * TC), np.float32)
    va = np.zeros((2 * TC, 8, 257), np.float32)
    kT_all[:, TC:] = kT_own
    va[TC:, :, :256] = v_own.reshape(TC, 8, 256)
    va[TC:, :, 256] = 1.0
    if not first_half:
        kT_all[:, :TC] = kT_prev
        va[:TC, :, :256] = v_prev.reshape(TC, 8, 256)
        va[:TC, :, 256] = 1.0
    B0, B1, B1p, cfar, cpre = diff_bias_tiles(rel_bias, first_half)
    lpb = np.ascontiguousarray(np.broadcast_to(np.asarray(lam_params, np.float32)[None], (128, 4, 128)))
    sgb = np.ascontiguousarray(np.broadcast_to(np.asarray(sub_gain, np.float32)[None, :], (128, 256)))
    return {"qT": np.ascontiguousarray(qT), "kT": kT_all, "va": va, "xT": np.ascontiguousarray(xT),
            "wout": np.ascontiguousarray(w_out, dtype=np.float32), "B0": B0, "B1": B1, "B1p": B1p, "cfar": cfar, "cpre": cpre,
            "lpb": lpb, "sgb": sgb, "ident": np.eye(128, dtype=np.float32)}


def diff1_inputs(xT, g, w_in, qg, kg):
    qkg = np.zeros((128, 16), np.float32)
    qkg[:, 0] = np.asarray(qg, np.float32)
    qkg[:, 1] = np.asarray(kg, np.float32)
    return {"xT": np.ascontiguousarray(xT), "g": col16(g), "win": np.ascontiguousarray(w_in, dtype=np.float32), "qkg": qkg}


PAIRS = [[0, 1], [2, 3], [4, 5], [6, 7]]
DEPTH = 4


def build_fused(plan=("gla0", "ffn0", "pool", "ffn1", "diff", "ffn2", "gla1", "ffn3")):
    mp = Phase()
    m = mp
    nc, P = mp.nc, mp.P
    mp.btile = mp.sb("btile", [128, 1], F32)
    cache = {}

    def lz(name, shape):
        if name not in cache:
            cache[name] = mp.din(name, shape)
        return cache[name]

    def lzi(name, shape):
        if name not in cache:
            cache[name] = mp.dint(name, shape)
        return cache[name]

    xT_in = mp.din("xT", [D, TC])
    xT_out = mp.dout("xTo", [D, TC])

    def ngf(l, i):
        return lz("ng_%d_%d" % (l, i), [128, DC])

    def barrier():
        P.barrier(lambda e, bt=mp.btile: e.memset(bt[:], 0.0))

    def gather(src, dst, tag):
        P.cc(lambda e: e.collective_compute("AllGather", ALU.bypass, replica_groups=PAIRS, ins=[src], outs=[dst]),
             reads=[], writes=[("cc", tag)])
        barrier()

    def lzb(name, shape):
        if name not in cache:
            cache[name] = mp.dint(name, shape, BF16)
        return cache[name]

    def gla_layer(sl, layer, xin, xout):
        st_src = lzi("st_src", [1024, 512])
        st_all = lzi("st_all", [2048, 512])
        dr = dict(q=lzb("g_q", [1024, TC]), k=lzb("g_k", [1024, TC]), kdec=lzb("g_kd", [TC, 1024]), v=lzb("g_v", [TC, 2048]),
                  sr=lzb("g_sr", [TC, 2048]), elast=lzi("g_el", [128, 128]))
        common = dict(xT=xin, g=ngf(layer, 0), win=lz("gla%d_win" % sl, [D, GLA_NCOL]), wa2b=lz("gla%d_wa2b" % sl, [17, GLA_DKT]),
                      gnb=lz("gla%d_gnb" % sl, [128, 2048]), wout=lz("gla%d_wout" % sl, [D, D]), tri=lz("tri", [128, 128]),
                      ident=lz("ident", [128, 128]), isb=lz("isb", [128, 16]), dr=dr)
        b1 = dict(common)
        b1.update(st_in=None, st_out=None)
        build_gla_phase(Phase(mp, "g%dp_" % layer, b1), mode="proj")
        build_gla_scan0(Phase(mp, "g%ds_" % layer, dict(dr=dr, st_out=st_src.rearrange("(p c) v -> p c v", c=8))))
        gather(st_src[:, :], st_all[:, :], ("st", layer))
        b2 = dict(common)
        b2.update(st_in=st_all[0:1024, :].rearrange("(p c) v -> p c v", c=8), st_out=None, xTo=xout)
        build_gla_phase(Phase(mp, "g%df_" % layer, b2), mode="scanf")

    def ffn_layer(layer, xin, xout):
        build_ffn_phase(Phase(mp, "f%d_" % layer, dict(xT=xin, xTo=xout, g=ngf(layer, 1), wgu=lz("ffn%d_wgu" % layer, [D, 2 * FH]),
                                                      wdn=lz("ffn%d_wdn" % layer, [FH, D]))))

    def pool_layer(layer, xin, xout):
        halo_src = lzi("halo_src", [D, 16])
        halo_all = lzi("halo_all", [2 * D, 16])
        P.dma("sp", lambda e: e.dma_start(out=halo_src[:, :], in_=xin[:, TC - 16:TC]), writes=["halo_src"])
        barrier()
        gather(halo_src[:, :], halo_all[:, :], "halo")
        build_pool_phase(Phase(mp, "p1_", dict(xT=xin, halo=halo_all[0:D, :], isb=lz("isb", [128, 16]), g=ngf(layer, 0),
                                               wp=lz("pool_wp", [4, 512, 512]), psc=lz("pool_psc", [128, DC]),
                                               invc=lz("pool_invc", [128, 4, 16]), xTo=xout)))

    def diff_layer(layer, xin, xout):
        q_d = lzi("q_d", [D, TC])
        k_ds = [lzi("k_d%d" % i, [256, TC]) for i in range(8)]
        v_ds = [lzi("v_d%d" % i, [TC, 256]) for i in range(8)]
        k_alls = [lzi("k_all%d" % i, [512, TC]) for i in range(8)]
        v_alls = [lzi("v_all%d" % i, [2 * TC, 256]) for i in range(8)]
        build_diff1_phase(ph=Phase(mp, "d1_", dict(xT=xin, g=ngf(layer, 0), win=lz("diff_win", [D, 3 * D]),
                                                   qkg=lz("diff_qkg", [128, 16]), qkT=q_d, q_d=q_d, k_ds=k_ds, v_ds=v_ds, v=v_ds[0])))
        for i in range(8):
            P.cc(lambda e, i=i: e.collective_compute("AllGather", ALU.bypass, replica_groups=PAIRS, ins=[k_ds[i][:, :]],
                                                     outs=[k_alls[i][:, :]]), reads=[], writes=[("cc", "k", i)])
            P.cc(lambda e, i=i: e.collective_compute("AllGather", ALU.bypass, replica_groups=PAIRS, ins=[v_ds[i][:, :]],
                                                     outs=[v_alls[i][:, :]]), reads=[], writes=[("cc", "v", i)])
        barrier()
        build_diff2_phase(Phase(mp, "d2_", dict(qT=q_d, kpres=[a_[0:256, :] for a_ in k_alls], kowns=k_ds,
                                                vpres=[a_[0:TC, :] for a_ in v_alls], vowns=v_ds, xT=xin,
                                                wout=lz("diff_wout", [D, D]), B0=lz("diff_B0", [128, 16, 128]),
                                                B1=lz("diff_B1", [128, 16, 128]), B1p=lz("diff_B1p", [128, 16, 128]),
                                                cfar=lz("diff_cfar", [128, 16]), cpre=lz("diff_cpre", [128, 16]),
                                                lpb=lz("diff_lpb", [128, 4, 128]), sgb=lz("diff_sgb", [128, 256]),
                                                ident=lz("ident", [128, 128]), xTo=xout)))

    bufs = [lzi("xa", [D, TC]), lzi("xb", [D, TC])]
    cur = xT_in
    for si, step in enumerate(plan):
        dst = xT_out if si == len(plan) - 1 else bufs[si % 2]
        layer = int(step[-1]) if step[:3] == "ffn" else {"gla0": 0, "pool": 1, "diff": 2, "gla1": 3}[step]
        if step[:3] == "ffn":
            ffn_layer(layer, cur, dst)
        elif step[:3] == "gla":
            gla_layer(int(step[3]), layer, cur, dst)
        elif step == "pool":
            pool_layer(layer, cur, dst)
        else:
            diff_layer(layer, cur, dst)
        cur = dst
    mp.input_names = [k for k in cache if not k.startswith("g_") and not k in ("xa", "xb", "st_src", "st_all", "halo_src", "halo_all", "q_d", "k_d", "v_d", "k_all", "v_all")]
    return mp.finish()


_FUSED = []


def kernel(x, norm_g, gla_w_in, gla_w_a2, gla_b_a, gla_g_norm, gla_w_out, pool_w, pool_scale, diff_w_in,
           diff_q_gain, diff_k_gain, diff_lambda, diff_sub_gain, diff_w_out, rel_bias, ffn_w_gu, ffn_w_down):
    x = np.asarray(x, np.float32)
    B, S, _ = x.shape
    if not _FUSED:
        _FUSED.append(build_fused())
    nc = _FUSED[0]
    f32c = lambda a: np.ascontiguousarray(np.asarray(a, np.float32))
    tri, ident = gla_consts()
    shared = {"tri": tri, "ident": ident}
    for l in range(DEPTH):
        for i in range(2):
            shared["ng_%d_%d" % (l, i)] = col16(norm_g[l, i])
        shared["ffn%d_wgu" % l] = f32c(ffn_w_gu[l])
        shared["ffn%d_wdn" % l] = f32c(ffn_w_down[l])
    for sl in range(2):
        shared["gla%d_win" % sl] = f32c(gla_w_in[sl])
        shared["gla%d_wa2b" % sl] = f32c(np.concatenate([np.asarray(gla_w_a2[sl], np.float32),
                                                         np.asarray(gla_b_a[sl], np.float32)[None, :]], axis=0))
        shared["gla%d_gnb" % sl] = f32c(np.broadcast_to(np.tile(np.asarray(gla_g_norm[sl], np.float32), 4)[None, :], (128, 2048)))
        shared["gla%d_wout" % sl] = f32c(gla_w_out[sl])
    shared["pool_wp"] = f32c(pool_w[0])
    shared["pool_psc"] = col16(pool_scale[0])
    shared["diff_win"] = f32c(diff_w_in[0])
    qkg = np.zeros((128, 16), np.float32)
    qkg[:, 0] = np.asarray(diff_q_gain[0], np.float32)
    qkg[:, 1] = np.asarray(diff_k_gain[0], np.float32)
    shared["diff_qkg"] = qkg
    shared["diff_wout"] = f32c(diff_w_out[0])
    shared["diff_lpb"] = f32c(np.broadcast_to(np.asarray(diff_lambda[0], np.float32)[None], (128, 4, 128)))
    shared["diff_sgb"] = f32c(np.broadcast_to(np.asarray(diff_sub_gain[0], np.float32)[None, :], (128, 256)))
    per_half = []
    for half in range(2):
        B0, B1, B1p, cfar, cpre = diff_bias_tiles(rel_bias, half == 0)
        invc = np.zeros((128, 4, 16), np.float32)
        for gi, w in enumerate(POOL_W):
            for t in range(16):
                invc[:, gi, t] = 1.0 / (min(t + 1, w) if half == 0 else w)
        per_half.append({"diff_B0": B0, "diff_B1": B1, "diff_B1p": B1p, "diff_cfar": cfar, "diff_cpre": cpre,
                         "pool_invc": invc, "isb": np.full((128, 16), float(half), np.float32)})
    in_maps = []
    for c in range(NCORES):
        im = dict(shared)
        im.update(per_half[c % 2])
        im["xT"] = np.ascontiguousarray(x[c // 2, (c % 2) * TC:(c % 2 + 1) * TC].T)
        in_maps.append(im)
    res = run_bass_kernel_spmd(nc, in_maps, core_ids=list(range(NCORES))).results
    out = np.empty((B, S, D), np.float32)
    for c in range(NCORES):
        out[c // 2, (c % 2) * TC:(c % 2 + 1) * TC] = res[c]["xTo"].T
    return out
```

```python
from contextlib import ExitStack
import math
import numpy as np
import concourse.bass as bass
import concourse.mybir as mybir
from concourse.bass_utils import run_bass_kernel_spmd

F32 = mybir.dt.float32
BF16 = mybir.dt.bfloat16
AF = mybir.ActivationFunctionType
ALU = mybir.AluOpType
AX = mybir.AxisListType

D = 2048
DC = D // 128
TC = 2048
FH = 5632
HC = FH // 128
EPS = 1e-6
NCORES = 8

ENGINES = ("pe", "act", "dve", "pool", "sp")


class Op:
    __slots__ = ("eng", "fn", "reads", "writes", "dma", "waits", "sig", "idx", "cc", "barrier")

    def __init__(self, eng, fn, reads, writes, dma):
        self.cc = False
        self.barrier = False
        self.eng = eng
        self.fn = fn
        self.reads = reads
        self.writes = writes
        self.dma = dma
        self.waits = []
        self.sig = None
        self.idx = -1


class Prog:
    NDMA = 32

    def __init__(self, nc):
        self.nc = nc
        self.ops = []

    LOOPVARS = frozenset("tt tsl j jsl h hs dc di dl blk c fc fi fb d dd ob vb rb i ei s sl mi grp b g w at atk kd kdk pq pk_ pv py_ ptr pkv cur oth sh a Wq Wk Wv Wr Wo wb db dp hf step q qk qb kb ki qi hp".split())

    def op(self, eng, fn, reads=(), writes=(), dma=False):
        bad = self.LOOPVARS.intersection(fn.__code__.co_freevars)
        if bad:
            raise RuntimeError("late-bound loop variable(s) %s in lambda at line %d" % (sorted(bad), fn.__code__.co_firstlineno))
        o = Op(eng, fn, tuple(reads), tuple(writes), dma)
        o.idx = len(self.ops)
        self.ops.append(o)
        return o

    def pe(self, fn, reads=(), writes=()):
        return self.op("pe", fn, reads, writes)

    def act(self, fn, reads=(), writes=()):
        return self.op("act", fn, reads, writes)

    def dve(self, fn, reads=(), writes=()):
        return self.op("dve", fn, reads, writes)

    def pool(self, fn, reads=(), writes=()):
        return self.op("pool", fn, reads, writes)

    def dma(self, eng, fn, reads=(), writes=()):
        return self.op(eng, fn, reads, writes, dma=True)

    def cc(self, fn, reads=(), writes=()):
        o = self.op("pool", fn, reads, writes, dma=True)
        o.cc = True
        return o

    def barrier(self, fn):
        o = self.op("dve", fn, (), ())
        o.barrier = True
        return o

    def finalize(self, final_wait_keys=()):
        ops = self.ops
        deps = [set() for _ in ops]
        last_writer = {}
        readers = {}
        since = []
        last_barrier = None
        for o in ops:
            if o.barrier:
                deps[o.idx] |= set(since)
                if last_barrier is not None:
                    deps[o.idx].add(last_barrier)
                since = []
                last_barrier = o.idx
                last_writer_final = dict(last_writer)
                last_writer = {}
                readers = {}
                continue
            since.append(o.idx)
            if last_barrier is not None:
                deps[o.idx].add(last_barrier)
            for k in o.reads:
                w = last_writer.get(k)
                if w is not None:
                    deps[o.idx].add(w.idx)
            for k in o.writes:
                w = last_writer.get(k)
                if w is not None:
                    deps[o.idx].add(w.idx)
                for r in readers.get(k, ()):
                    if r.idx != o.idx:
                        deps[o.idx].add(r.idx)
            for k in o.reads:
                readers.setdefault(k, []).append(o)
            for k in o.writes:
                last_writer[k] = o
                readers[k] = []
        final_ops = [last_writer[k].idx for k in final_wait_keys if k in last_writer]
        if last_barrier is not None:
            final_ops.append(last_barrier)
        needed = [set() for _ in ops]
        for o in ops:
            best = {}
            for d in deps[o.idx]:
                p = ops[d]
                if p.dma:
                    needed[o.idx].add(d)
                    continue
                if p.eng == "pe" and o.eng == "pe" and not o.dma:
                    continue
                if best.get(p.eng, -1) < d:
                    best[p.eng] = d
            needed[o.idx] |= set(best.values())
        signaled = set(final_ops)
        for o in ops:
            signaled |= needed[o.idx]
        eng_count = {e: 0 for e in ENGINES}
        NDMA = self.NDMA
        dma_count = [0] * NDMA
        dma_last = [None] * NDMA
        rr = 0
        rr_sw = 0
        cc_count = 0
        cc_last = None
        for o in ops:
            if o.dma and not o.cc:
                signaled.add(o.idx)
            if o.cc:
                signaled.add(o.idx)
                if cc_last is not None:
                    needed[o.idx].add(cc_last)
                cc_count += 1
                cc_last = o.idx
                o.sig = (("cc", 0), None, cc_count)
                continue
            if o.idx not in signaled:
                continue
            if o.dma:
                half = NDMA // 2
                if o.eng == "pool":
                    s = half + rr_sw % half
                    rr_sw += 1
                else:
                    s = rr % half
                    rr += 1
                if dma_last[s] is not None:
                    needed[o.idx].add(dma_last[s])
                dma_count[s] += 1
                dma_last[s] = o.idx
                o.sig = (("dma", s), 16, dma_count[s] * 16)
            else:
                eng_count[o.eng] += 1
                o.sig = (("eng", o.eng), 1, eng_count[o.eng])
        seen = {e: {} for e in ENGINES}
        for o in ops:
            ws = {}
            for d in needed[o.idx]:
                semkey, _, val = ops[d].sig
                if ws.get(semkey, 0) < val:
                    ws[semkey] = val
            for semkey, val in ws.items():
                if seen[o.eng].get(semkey, 0) >= val:
                    continue
                seen[o.eng][semkey] = val
                o.waits.append((semkey, val))
        fw = {}
        for d in final_ops:
            semkey, _, val = ops[d].sig
            fw[semkey] = max(fw.get(semkey, 0), val)
        self.final_waits = fw
        self.eng_count = eng_count

    def emit(self, block, sems):
        per_eng = {e: [] for e in ENGINES}
        for o in self.ops:
            per_eng[o.eng].append(o)
        final_waits = self.final_waits

        def run(engobj, lst, is_last):
            for o in lst:
                for semkey, val in o.waits:
                    engobj.wait_ge(sems[semkey], val)
                ins = o.fn(engobj)
                if o.sig is not None:
                    if o.sig[1] is None:
                        ins.then_inc(sems[o.sig[0]])
                    else:
                        ins.then_inc(sems[o.sig[0]], o.sig[1])
            if is_last:
                for semkey, val in final_waits.items():
                    engobj.wait_ge(sems[semkey], val)

        @block.tensor
        def _(e):
            run(e, per_eng["pe"], False)

        @block.scalar
        def _(e):
            run(e, per_eng["act"], False)

        @block.vector
        def _(e):
            run(e, per_eng["dve"], False)

        @block.gpsimd
        def _(e):
            run(e, per_eng["pool"], False)

        @block.sync
        def _(e):
            run(e, per_eng["sp"], True)


class Phase:
    def __init__(self, master=None, prefix="", bind=None):
        self.master = master
        self.prefix = prefix
        self.bind = bind or {}
        if master is None:
            self.nc = bass.Bass("TRN2", target_bir_lowering=False)
            self.P = Prog(self.nc)
            self.out_keys = []
        else:
            self.nc = master.nc
            self.P = master.P
            self.out_keys = master.out_keys
        self.es = ExitStack()

    def din(self, name, shape, dtype=F32):
        if name in self.bind:
            return self.bind[name]
        return self.nc.dram_tensor(self.prefix + name, list(shape), dtype, kind="ExternalInput").ap()

    def dout(self, name, shape, dtype=F32):
        if name in self.bind:
            return self.bind[name]
        return self.nc.dram_tensor(self.prefix + name, list(shape), dtype, kind="ExternalOutput").ap()

    def dint(self, name, shape, dtype=F32):
        return self.nc.dram_tensor(self.prefix + name, list(shape), dtype).ap()

    def sb(self, name, shape, dtype=F32):
        return self.es.enter_context(self.nc.sbuf_tensor(self.prefix + name, list(shape), dtype))

    def ps(self, name, shape, dtype=F32):
        return self.es.enter_context(self.nc.psum_tensor(self.prefix + name, list(shape), dtype))

    def finish(self):
        if self.master is not None:
            self.es.close()
            bt = self.master.btile
            self.P.barrier(lambda e: e.memset(bt[:], 0.0))
            return None
        P = self.P
        P.finalize(final_wait_keys=self.out_keys)
        sems = {}
        for e in ENGINES:
            sems[("eng", e)] = self.es.enter_context(self.nc.semaphore("s_" + e))
        for i in range(P.NDMA):
            sems[("dma", i)] = self.es.enter_context(self.nc.semaphore("d%d" % i))
        sems[("cc", 0)] = self.es.enter_context(self.nc.semaphore("s_cc"))
        block = self.es.enter_context(self.nc.Block())
        P.emit(block, sems)
        self.es.close()
        return self.nc


def emit_rmsnorm(ph, xT, hT, gcol, ones_bf, sq, pss, rstd, ntok, xkey, hkey, tag):
    P = ph.P
    nsub = ntok // 512
    for s in range(nsub):
        sl = slice(s * 512, (s + 1) * 512)
        pb = pss[s % 2]
        pk = "pss%d" % (s % 2)
        for c in range(DC):
            q = sq[c % 2]
            qk = "sq%d" % (c % 2)
            P.act(lambda e, q=q, c=c, sl=sl: e.activation(out=q[:], in_=xT[:, c, sl], func=AF.Square),
                  reads=[(xkey, c)], writes=[qk])
            P.pe(lambda e, q=q, c=c, pb=pb: e.matmul(pb[:], ones_bf[:], q[:], start=(c == 0), stop=(c == DC - 1)),
                 reads=[qk, "ones"], writes=[pk])
        rk = ("rstd", tag, s)
        P.dve(lambda e, pb=pb, sl=sl: e.tensor_scalar(out=rstd[:, sl], in0=pb[:], scalar1=1.0 / D, scalar2=EPS,
                                                      op0=ALU.mult, op1=ALU.add),
              reads=[pk], writes=[rk])
        P.act(lambda e, sl=sl: e.activation(out=rstd[:, sl], in_=rstd[:, sl], func=AF.Sqrt),
              reads=[rk], writes=[rk])
        P.dve(lambda e, sl=sl: e.reciprocal(out=rstd[:, sl], in_=rstd[:, sl]),
              reads=[rk], writes=[rk])
        for c in range(DC):
            P.dve(lambda e, c=c, sl=sl: e.scalar_tensor_tensor(out=hT[:, c, sl], in0=xT[:, c, sl],
                                                               scalar=gcol[:, c:c + 1], in1=rstd[:, sl],
                                                               op0=ALU.mult, op1=ALU.mult),
                  reads=[(xkey, c), rk, "gcol"], writes=[(hkey, c, s)])


def build_ffn_phase(ph=None):
    ph = ph or Phase()
    P = ph.P
    nc = ph.nc
    xin = ph.din("xT", [D, TC])
    g_in = ph.din("g", [128, DC])
    wgu = ph.din("wgu", [D, 2 * FH])
    wdn = ph.din("wdn", [FH, D])
    xout = ph.dout("xTo", [D, TC])
    NT = 1024
    NS = NT // 512
    HG = 11
    NG = HC // HG
    xT = ph.sb("xTs", [128, DC, NT], F32)
    hT = ph.sb("hTs", [128, DC, NT], BF16)
    aT = ph.sb("aTs", [128, HG, NT], BF16)
    wg = [ph.sb("wg%d" % i, [128, DC, 256], BF16) for i in range(3)]
    wd = [ph.sb("wd%d" % i, [128, HG, 256], BF16) for i in range(2)]
    sq = [ph.sb("sq%d" % i, [128, 512], BF16) for i in range(2)]
    sg = [ph.sb("sg%d" % i, [128, 512], F32) for i in range(2)]
    rstd = ph.sb("rstd", [128, NT], F32)
    gcol = ph.sb("gcol", [128, DC], F32)
    ones = ph.sb("ones", [128, 128], BF16)
    pss = [ph.ps("pss%d" % i, [128, 512]) for i in range(2)]
    pg = [ph.ps("pg%d" % i, [128, 512]) for i in range(2)]
    pu = [ph.ps("pu%d" % i, [128, 512]) for i in range(2)]
    py = [ph.ps("py%d" % i, [128, 512]) for i in range(2)]

    P.dma("sp", lambda e: e.dma_start(out=gcol[:], in_=g_in[:, :]), writes=["gcol"])
    P.dve(lambda e: e.memset(ones[:], 1.0), writes=["ones"])
    xin_v = xin.rearrange("(c p) t -> p c t", p=128)
    xout_v = xout.rearrange("(c p) t -> p c t", p=128)
    wgu_v = wgu.rearrange("(c p) n -> p c n", p=128)
    wdn_v = wdn.rearrange("(m p) n -> p m n", p=128)
    wgi = 0
    wdi = 0
    cnt = 0
    for tt in range(TC // NT):
        tsl = slice(tt * NT, (tt + 1) * NT)
        for c in range(DC):
            P.dma("sp",
                  lambda e, c=c, tsl=tsl: e.dma_start(out=xT[:, c, :], in_=xin_v[:, c, tsl]),
                  writes=[("x", c)])
        emit_rmsnorm(ph, xT, hT, gcol, ones, sq, pss, rstd, NT, "x", "h", tt)
        hkeys = [("h", c, s) for c in range(DC) for s in range(NS)]
        for grp in range(NG):
            for mi in range(HG):
                m = grp * HG + mi
                wb = wg[wgi % 3]
                wk = "wg%d" % (wgi % 3)
                wgi += 1
                P.dma("pool", lambda e, wb=wb, m=m: e.dma_start(out=wb[:, :, 0:128], in_=wgu_v[:, :, m * 128:(m + 1) * 128]),
                      writes=[wk])
                P.dma("pool", lambda e, wb=wb, m=m: e.dma_start(out=wb[:, :, 128:256],
                                                                 in_=wgu_v[:, :, FH + m * 128:FH + (m + 1) * 128]),
                      writes=[wk])
                for s in range(NS):
                    sl = slice(s * 512, (s + 1) * 512)
                    b = cnt % 2
                    cnt += 1
                    for c in range(DC):
                        P.pe(lambda e, wb=wb, c=c, sl=sl, b=b: e.matmul(pg[b][:], wb[:, c, 0:128], hT[:, c, sl],
                                                                        start=(c == 0), stop=(c == DC - 1)),
                             reads=[wk, ("h", c, s)], writes=["pg%d" % b])
                    for c in range(DC):
                        P.pe(lambda e, wb=wb, c=c, sl=sl, b=b: e.matmul(pu[b][:], wb[:, c, 128:256], hT[:, c, sl],
                                                                        start=(c == 0), stop=(c == DC - 1)),
                             reads=[wk, ("h", c, s)], writes=["pu%d" % b])
                    P.act(lambda e, b=b: e.activation(out=sg[b][:], in_=pg[b][:], func=AF.Silu),
                          reads=["pg%d" % b], writes=["sg%d" % b])
                    P.dve(lambda e, b=b, mi=mi, sl=sl: e.tensor_tensor(out=aT[:, mi, sl], in0=sg[b][:], in1=pu[b][:],
                                                                       op=ALU.mult),
                          reads=["sg%d" % b, "pu%d" % b], writes=[("a", mi, s)])
            for dp in range(DC // 2):
                db = wd[wdi % 2]
                dk = "wd%d" % (wdi % 2)
                wdi += 1
                P.dma("pool", lambda e, db=db, dp=dp, grp=grp: e.dma_start(
                    out=db[:], in_=wdn_v[:, grp * HG:(grp + 1) * HG, dp * 256:(dp + 1) * 256]), writes=[dk])
                for dd in range(2):
                    d = dp * 2 + dd
                    for s in range(NS):
                        sl = slice(s * 512, (s + 1) * 512)
                        b = cnt % 2
                        cnt += 1
                        for mi in range(HG):
                            P.pe(lambda e, db=db, mi=mi, dd=dd, sl=sl, b=b: e.matmul(
                                py[b][:], db[:, mi, dd * 128:(dd + 1) * 128], aT[:, mi, sl],
                                start=(mi == 0), stop=(mi == HG - 1)),
                                reads=[dk, ("a", mi, s)], writes=["py%d" % b])
                        P.dve(lambda e, d=d, sl=sl, b=b: e.tensor_tensor(out=xT[:, d, sl], in0=xT[:, d, sl], in1=py[b][:],
                                                                         op=ALU.add),
                              reads=["py%d" % b, ("x", d)], writes=[("x", d)])
        for c in range(DC):
            P.dma("sp",
                  lambda e, c=c, tsl=tsl: e.dma_start(out=xout_v[:, c, tsl], in_=xT[:, c, :]),
                  reads=[("x", c)], writes=[("xo", tt, c)])
            ph.out_keys.append(("xo", tt, c))
    return ph.finish()


def emit_rmsnorm_cols(ph, xT, xoff, hT, hoff, ncols, gcol, ones_bf, sq, pb, pk, rstd, xkeys, hkeys, tag):
    P = ph.P
    xsl_ = slice(xoff, xoff + ncols)
    hsl_ = slice(hoff, hoff + ncols)
    for c in range(DC):
        q = sq[c % 2]
        qk = "sq%d" % (c % 2)
        P.act(lambda e, q=q, c=c: e.activation(out=q[:, 0:ncols], in_=xT[:, c, xsl_], func=AF.Square),
              reads=[xkeys(c)], writes=[qk])
        P.pe(lambda e, q=q, c=c: e.matmul(pb[:, 0:ncols], ones_bf[:], q[:, 0:ncols], start=(c == 0), stop=(c == DC - 1)),
             reads=[qk, "ones"], writes=[pk])
    rk = ("rstd", tag)
    P.dve(lambda e: e.tensor_scalar(out=rstd[:, 0:ncols], in0=pb[:, 0:ncols], scalar1=1.0 / D, scalar2=EPS,
                                    op0=ALU.mult, op1=ALU.add), reads=[pk], writes=[rk])
    P.act(lambda e: e.activation(out=rstd[:, 0:ncols], in_=rstd[:, 0:ncols], func=AF.Sqrt), reads=[rk], writes=[rk])
    P.dve(lambda e: e.reciprocal(out=rstd[:, 0:ncols], in_=rstd[:, 0:ncols]), reads=[rk], writes=[rk])
    for c in range(DC):
        P.dve(lambda e, c=c: e.scalar_tensor_tensor(out=hT[:, c, hsl_], in0=xT[:, c, xsl_], scalar=gcol[:, c:c + 1],
                                                    in1=rstd[:, 0:ncols], op0=ALU.mult, op1=ALU.mult),
              reads=[xkeys(c), rk, "gcol"], writes=[hkeys(c)])


POOL_W = (2, 4, 8, 16)


def build_pool_phase(ph=None):
    ph = ph or Phase()
    P = ph.P
    fused = ph.master is not None
    xin = None if fused else ph.din("xTe", [D, 16 + TC])
    g_in = ph.din("g", [128, DC])
    wp_in = ph.din("wp", [4, 512, 512])
    sc_in = ph.din("psc", [128, DC])
    ic_in = ph.din("invc", [128, 4, 16])
    xout = ph.dout("xTo", [D, TC])
    NT = 512
    xT = ph.sb("xTs", [128, DC, NT], F32)
    xh = ph.sb("xh", [128, DC, 16], F32)
    hx = ph.sb("hx", [128, DC, 16 + NT], F32)
    sA = [ph.sb("sA%d" % i, [128, 16 + NT], F32) for i in range(2)]
    sB = [ph.sb("sB%d" % i, [128, 16 + NT], F32) for i in range(2)]
    yT = ph.sb("yT", [128, DC, NT], BF16)
    wp = ph.sb("wps", [128, 4, 4, 512], BF16)
    sq = [ph.sb("sq%d" % i, [128, 512], BF16) for i in range(2)]
    rstd = ph.sb("rstd", [128, 512], F32)
    gcol = ph.sb("gcol", [128, DC], F32)
    psc = ph.sb("pscs", [128, DC], F32)
    invc = ph.sb("invcs", [128, 4, 16], F32)
    ones = ph.sb("ones", [128, 128], BF16)
    pss = ph.ps("pss", [128, 512])
    pz = [ph.ps("pz%d" % i, [128, 512]) for i in range(2)]

    P.dma("sp", lambda e: e.dma_start(out=gcol[:], in_=g_in[:, :]), writes=["gcol"])
    P.dma("sp", lambda e: e.dma_start(out=psc[:], in_=sc_in[:, :]), writes=["psc"])
    P.dma("sp", lambda e: e.dma_start(out=invc[:], in_=ic_in[:, :, :]), writes=["invc"])
    P.dve(lambda e: e.memset(ones[:], 1.0), writes=["ones"])
    wp_v = wp_in.rearrange("g (ci p) n -> p g ci n", p=128)
    for g in range(4):
        P.dma("pool", lambda e, g=g: e.dma_start(out=wp[:, g, :, :], in_=wp_v[:, g, :, :]), writes=["wp"])
    if fused:
        xmain_v = ph.bind["xT"].rearrange("(c p) t -> p c t", p=128)
        halo_v = ph.bind["halo"].rearrange("(c p) t -> p c t", p=128)
        isb = ph.sb("isb", [128, 16], F32)
        P.dma("sp", lambda e: e.dma_start(out=isb[:], in_=ph.bind["isb"][:, :]), writes=["isb"])
        P.dma("sp", lambda e: e.dma_start(out=xh[:], in_=halo_v[:, :, :]), writes=["xh"])
        P.dve(lambda e: e.tensor_scalar(out=xh[:], in0=xh[:], scalar1=isb[:, 0:1], scalar2=None, op0=ALU.mult),
              reads=["xh", "isb"], writes=["xh"])
        OFF = 0
    else:
        xin_v = xin.rearrange("(c p) t -> p c t", p=128)
        xmain_v = xin_v
        OFF = 16
        P.dma("sp", lambda e: e.dma_start(out=xh[:], in_=xin_v[:, :, 0:16]), writes=["xh"])
    xout_v = xout.rearrange("(c p) t -> p c t", p=128)
    emit_rmsnorm_cols(ph, xh, 0, hx, 0, 16, gcol, ones, sq, pss, "pss", rstd,
                      lambda c: "xh", lambda c: ("hx", c), "halo")
    cnt = 0
    for tt in range(TC // NT):
        for c in range(DC):
            P.dma("sp", lambda e, c=c, tt=tt: e.dma_start(out=xT[:, c, :], in_=xmain_v[:, c, OFF + tt * NT:OFF + (tt + 1) * NT]),
                  writes=[("x", c)])
        emit_rmsnorm_cols(ph, xT, 0, hx, 16, NT, gcol, ones, sq, pss, "pss", rstd,
                          lambda c: ("x", c), lambda c: ("hx", c), ("t", tt))
        W = 16 + NT
        for c in range(DC):
            g = c // 4
            eng = P.dve if c % 2 == 0 else P.pool
            a = sA[c % 2]
            b = sB[c % 2]
            ak = "sA%d" % (c % 2)
            bk = "sB%d" % (c % 2)
            eng(lambda e, a=a, c=c: e.tensor_tensor(out=a[:, 1:W], in0=hx[:, c, 1:W], in1=hx[:, c, 0:W - 1], op=ALU.add),
                reads=[("hx", c)], writes=[ak])
            cur, curk, oth, othk = a, ak, b, bk
            sh = 2
            for step in range(g):
                eng(lambda e, cur=cur, oth=oth, sh=sh: e.tensor_tensor(out=oth[:, 1 + sh:W], in0=cur[:, 1 + sh:W],
                                                                      in1=cur[:, 1:W - sh], op=ALU.add),
                    reads=[curk], writes=[othk])
                cur, curk, oth, othk = oth, othk, cur, curk
                sh *= 2
            w = POOL_W[g]
            P.dve(lambda e, cur=cur, c=c, w=w: e.scalar_tensor_tensor(out=yT[:, c, :], in0=cur[:, 16:W], scalar=1.0 / w,
                                                                    in1=hx[:, c, 16:W], op0=ALU.mult, op1=ALU.subtract),
                reads=[curk, ("hx", c)], writes=[("y", c)])
            if tt == 0:
                eng(lambda e, cur=cur, g=g: e.tensor_tensor(out=cur[:, 16:32], in0=cur[:, 16:32], in1=invc[:, g, :], op=ALU.mult),
                    reads=[curk, "invc", ("y", c)], writes=[curk])
                eng(lambda e, cur=cur, c=c: e.tensor_tensor(out=yT[:, c, 0:16], in0=cur[:, 16:32], in1=hx[:, c, 16:32],
                                                            op=ALU.subtract),
                    reads=[curk, ("hx", c)], writes=[("y", c)])
            if tt + 1 < TC // NT:
                eng(lambda e, c=c: e.tensor_copy(out=hx[:, c, 0:16], in_=hx[:, c, NT:NT + 16]),
                    reads=[("y", c), curk, ak, bk], writes=[("hx", c)])
        for d in range(DC):
            g = d // 4
            b = cnt % 2
            cnt += 1
            for ci in range(4):
                P.pe(lambda e, g=g, ci=ci, d=d, b=b: e.matmul(pz[b][:], wp[:, g, ci, (d % 4) * 128:(d % 4 + 1) * 128],
                                                             yT[:, 4 * g + ci, :], start=(ci == 0), stop=(ci == 3)),
                     reads=["wp", ("y", 4 * g + ci)], writes=["pz%d" % b])
            P.dve(lambda e, d=d, b=b: e.scalar_tensor_tensor(out=xT[:, d, :], in0=pz[b][:], scalar=psc[:, d:d + 1],
                                                            in1=xT[:, d, :], op0=ALU.mult, op1=ALU.add),
                  reads=["pz%d" % b, "psc", ("x", d)], writes=[("x", d)])
        for c in range(DC):
            P.dma("sp", lambda e, c=c, tt=tt: e.dma_start(out=xout_v[:, c, tt * NT:(tt + 1) * NT], in_=xT[:, c, :]),
                  reads=[("x", c)], writes=[("xo", tt, c)])
            ph.out_keys.append(("xo", tt, c))
    return ph.finish()


def col16(v):
    return np.ascontiguousarray(np.asarray(v, np.float32).reshape(DC, 128).T)


def pool_inputs(x_seq, half, g, wp, psc):
    xe = np.zeros((D, 16 + TC), np.float32)
    t0 = half * TC
    xe[:, 16:] = x_seq[t0:t0 + TC].T
    if half == 1:
        xe[:, :16] = x_seq[t0 - 16:t0].T
    invc = np.zeros((128, 4, 16), np.float32)
    for gi, w in enumerate(POOL_W):
        for t in range(16):
            cnt = min(t + 1, w) if half == 0 else w
            invc[:, gi, t] = 1.0 / cnt
    return {"xTe": xe, "g": col16(g), "wp": np.ascontiguousarray(wp, dtype=np.float32), "psc": col16(psc), "invc": invc}


GLA_DKT = 1024
GLA_NCOL = 6160


def build_gla_phase(ph=None, state_only=False, mode="full"):
    ph = ph or Phase()
    P = ph.P
    fused = ph.master is not None
    xin = ph.din("xT", [D, TC])
    g_in = ph.din("g", [128, DC])
    win = ph.din("win", [D, GLA_NCOL])
    wa2b_in = ph.din("wa2b", [17, GLA_DKT])
    gn_in = ph.din("gnb", [128, 2048])
    wout = ph.din("wout", [D, D])
    st_in = ph.bind.get("st_in") if fused else ph.din("st_in", [128, 8, 512])
    tri_in = ph.din("tri", [128, 128])
    id_in = ph.din("ident", [128, 128])
    xout = None if (state_only or mode == "proj") else ph.dout("xTo", [D, TC])
    st_out = ph.bind.get("st_out") if fused else ph.dout("st_out", [128, 8, 512])
    NT = 512
    NJ = 4
    xs = [ph.sb("xs%d" % i, [128, 512], F32) for i in range(3)]
    hT = ph.sb("hTs", [128, DC, NT], BF16)
    Wb = [ph.sb("Wb%d" % i, [128, DC, 512], BF16) for i in range(2)]
    Wa = ph.sb("Wa", [128, DC, 16], BF16)
    qT = ph.sb("qT", [128, 8, NT], BF16)
    kT = ph.sb("kT", [128, 8, NT], BF16)
    kdec = ph.sb("kdec", [128, NJ, 1024], BF16)
    kd_s = [ph.sb("kds%d" % i, [128, 512], BF16) for i in range(2)]
    vt = ph.sb("vt", [128, NJ, 2048], BF16)
    sr = ph.sb("sr", [128, NJ, 2048], BF16)
    gated2 = [ph.sb("gated%d" % i, [128, 2048], BF16) for i in range(2)]
    gT = ph.sb("gT", [128, DC, NT], BF16)
    S = ph.sb("S", [128, 8, 512], F32)
    Sb = ph.sb("Sb", [128, 8, 512], BF16)
    lt = ph.sb("lt", [128, NJ, 1024], F32)
    e1 = ph.sb("e1", [128, 1024], F32)
    Eq = [ph.sb("Eq%d" % i, [128, 512], F32) for i in range(1)]
    Ek = [ph.sb("Ek%d" % i, [128, 512], F32) for i in range(1)]
    Elast = ph.sb("Elast", [128, 8, TC // 128], F32)
    dr = ph.bind.get("dr")
    alr1 = ph.sb("alr1", [32, NT], F32)
    wa2b = ph.sb("wa2bs", [32, GLA_DKT], F32)
    gnb = ph.sb("gnbs", [128, 2048], BF16)
    tri = ph.sb("tris", [128, 128], F32)
    ident = ph.sb("idents", [128, 128], BF16)
    AT4 = [ph.sb("AT4%d" % i, [128, 128], BF16) for i in range(4)]
    osq = ph.sb("osq", [128, 2048], BF16)
    ssq = ph.sb("ssq", [128, 4], F32)
    sq = [ph.sb("sq%d" % i, [128, 512], BF16) for i in range(2)]
    rstd = ph.sb("rstd", [128, 512], F32)
    gcol = ph.sb("gcol", [128, DC], F32)
    ones = ph.sb("ones", [128, 128], BF16)
    pb = [ph.ps("pb%d" % i, [128, 512]) for i in range(8)]
    pbk = ["pb%d" % i for i in range(8)]

    P.dma("sp", lambda e: e.dma_start(out=gcol[:], in_=g_in[:, :]), writes=["gcol"])
    P.dma("sp", lambda e: e.dma_start(out=tri[:], in_=tri_in[:, :]), writes=["tri"])
    P.dma("sp", lambda e: e.dma_start(out=wa2b[0:17, :], in_=wa2b_in[:, :]), writes=["wa2b"])
    if st_in is None:
        P.dve(lambda e: e.memset(S[:], 0.0), writes=[("S", dc) for dc in range(8)])
    else:
        P.dma("sp", lambda e: e.dma_start(out=S[:], in_=st_in[:, :, :]), writes=[("S", dc) for dc in range(8)])
        if fused:
            isb = ph.sb("isb", [128, 16], F32)
            P.dma("sp", lambda e: e.dma_start(out=isb[:], in_=ph.bind["isb"][:, :]), writes=["isb"])
            P.dve(lambda e: e.tensor_scalar(out=S[:], in0=S[:], scalar1=isb[:, 0:1], scalar2=None, op0=ALU.mult),
                  reads=[("S", dc) for dc in range(8)] + ["isb"], writes=[("S", dc) for dc in range(8)])
    P.dma("pool", lambda e: e.dma_start(out=ident[:], in_=id_in[:, :]), writes=["ident"])
    P.dma("pool", lambda e: e.dma_start(out=gnb[:], in_=gn_in[:, :]), writes=["gnb"])
    P.dve(lambda e: e.memset(ones[:], 1.0), writes=["ones"])
    P.dve(lambda e: e.memset(alr1[:], 1.0), writes=["alr1"])
    P.act(lambda e: e.copy(out=Sb[:], in_=S[:]), reads=[("S", dc) for dc in range(8)], writes=[("Sb", dc) for dc in range(8)])
    win_v = win.rearrange("(c p) n -> p c n", p=128)
    wout_v = wout.rearrange("(c p) n -> p c n", p=128)
    xin_v = xin.rearrange("(c p) t -> p c t", p=128)
    xout_v = None if xout is None else xout.rearrange("(c p) t -> p c t", p=128)
    P.dma("pool", lambda e: e.dma_start(out=Wa[:], in_=win_v[:, :, 6144:6160]), writes=["Wa"])

    if mode == "scanf":
        P.dma("sp", lambda e: e.dma_start(out=Elast[:], in_=dr["elast"].rearrange("p (dc jj) -> p dc jj", dc=8)),
              writes=[("Elast", dc) for dc in range(8)])
    wctr = [0]

    def load_w(src_v, col0):
        i = wctr[0] % 2
        wctr[0] += 1
        P.dma("pool", lambda e, i=i: e.dma_start(out=Wb[i][:], in_=src_v[:, :, col0:col0 + 512]), writes=["Wb%d" % i])
        return Wb[i], "Wb%d" % i

    pctr = [0]

    def next_pb(lo=4, n=4):
        i = lo + pctr[0] % n
        pctr[0] += 1
        return pb[i], pbk[i]

    xctr = [0]
    for tt in range(TC // NT):
        tsl = slice(tt * NT, (tt + 1) * NT)
        if mode == "scanf":
            P.dma("sp", lambda e, tsl=tsl: e.dma_start(out=qT[:], in_=dr["q"].rearrange("(dc p) t -> p dc t", p=128)[:, :, tsl]),
                  writes=[("qT", dc) for dc in range(8)])
            P.dma("sp", lambda e, tsl=tsl: e.dma_start(out=kT[:], in_=dr["k"].rearrange("(dc p) t -> p dc t", p=128)[:, :, tsl]),
                  writes=[("kT", dc) for dc in range(8)])
            P.dma("sp", lambda e, tt=tt: e.dma_start(out=kdec[:], in_=dr["kdec"].rearrange("(jj p) d -> p jj d", p=128)[:, tt * NJ:(tt + 1) * NJ, :]),
                  writes=[("kdec", dc) for dc in range(8)])
            P.dma("sp", lambda e, tt=tt: e.dma_start(out=vt[:], in_=dr["v"].rearrange("(jj p) n -> p jj n", p=128)[:, tt * NJ:(tt + 1) * NJ, :]),
                  writes=[("vt", j, h) for j in range(NJ) for h in range(4)])
            P.dma("sp", lambda e, tt=tt: e.dma_start(out=sr[:], in_=dr["sr"].rearrange("(jj p) n -> p jj n", p=128)[:, tt * NJ:(tt + 1) * NJ, :]),
                  writes=[("sr", j) for j in range(NJ)])
        else:
            p_ss, p_ssk = pb[0], pbk[0]
            for c in range(DC):
                i = xctr[0] % 3
                xctr[0] += 1
                P.dma("sp", lambda e, i=i, c=c, tsl=tsl: e.dma_start(out=xs[i][:], in_=xin_v[:, c, tsl]), writes=["xs%d" % i])
                q = sq[c % 2]
                qk = "sq%d" % (c % 2)
                P.act(lambda e, q=q, i=i: e.activation(out=q[:], in_=xs[i][:], func=AF.Square), reads=["xs%d" % i], writes=[qk])
                P.pe(lambda e, q=q, c=c: e.matmul(p_ss[:], ones[:], q[:], start=(c == 0), stop=(c == DC - 1)),
                     reads=[qk, "ones"], writes=[p_ssk])
            P.dve(lambda e: e.tensor_scalar(out=rstd[:], in0=p_ss[:], scalar1=1.0 / D, scalar2=EPS, op0=ALU.mult, op1=ALU.add),
                  reads=[p_ssk], writes=["rstd"])
            P.act(lambda e: e.activation(out=rstd[:], in_=rstd[:], func=AF.Sqrt), reads=["rstd"], writes=["rstd"])
            P.dve(lambda e: e.reciprocal(out=rstd[:], in_=rstd[:]), reads=["rstd"], writes=["rstd"])
            for c in range(DC):
                i = xctr[0] % 3
                xctr[0] += 1
                P.dma("sp", lambda e, i=i, c=c, tsl=tsl: e.dma_start(out=xs[i][:], in_=xin_v[:, c, tsl]), writes=["xs%d" % i])
                P.dve(lambda e, i=i, c=c: e.scalar_tensor_tensor(out=hT[:, c, :], in0=xs[i][:], scalar=gcol[:, c:c + 1],
                                                                 in1=rstd[:], op0=ALU.mult, op1=ALU.mult),
                      reads=["xs%d" % i, "rstd", "gcol"], writes=[("h", c)])
            hk = [("h", c) for c in range(DC)]
            pa, pak = pb[1], pbk[1]
            for c in range(DC):
                P.pe(lambda e, c=c: e.matmul(pa[0:16, :], Wa[:, c, :], hT[:, c, :], start=(c == 0), stop=(c == DC - 1)),
                     reads=["Wa", ("h", c)], writes=[pak])
            P.act(lambda e: e.copy(out=alr1[0:16, :], in_=pa[0:16, :]), reads=[pak], writes=["alr1"])
            for j in range(NJ):
                jsl = slice(j * 128, (j + 1) * 128)
                for hf in range(2):
                    P.pe(lambda e, jsl=jsl, hf=hf: e.matmul(pb[2 + hf][:], alr1[0:17, jsl], wa2b[0:17, hf * 512:(hf + 1) * 512],
                                                            start=True, stop=True),
                         reads=["alr1", "wa2b"], writes=[pbk[2 + hf]])
                    P.act(lambda e, hf=hf: e.activation(out=e1[:, hf * 512:(hf + 1) * 512], in_=pb[2 + hf][:], func=AF.Exp, scale=-1.0),
                          reads=[pbk[2 + hf]], writes=[("e1", hf)])
                    P.act(lambda e, hf=hf, j=j: e.activation(out=lt[:, j, hf * 512:(hf + 1) * 512], in_=e1[:, hf * 512:(hf + 1) * 512],
                                                             func=AF.Ln, bias=1.0),
                          reads=[("e1", hf)], writes=[("lt", j)])
            pend_tr = []
            for blk in range(2):
                if not state_only:
                    Wq, Wqk = load_w(win_v, blk * 512)
                Wk, Wkk = load_w(win_v, 1024 + blk * 512)
                for dl in range(4):
                    dc = blk * 4 + dl
                    pbt, pbtk = pb[0], pbk[0]
                    for j in range(NJ):
                        jsl = slice(j * 128, (j + 1) * 128)
                        P.pe(lambda e, j=j, jsl=jsl, dc=dc: e.matmul(pbt[:, jsl], lt[:, j, dc * 128:(dc + 1) * 128], tri[:],
                                                                     start=True, stop=True),
                             reads=[("lt", j), "tri"], writes=[pbtk])
                    ei = 0
                    P.act(lambda e, ei=ei: e.activation(out=Eq[ei][:], in_=pbt[:], func=AF.Exp, scale=-1.0 / 16.0),
                          reads=[pbtk], writes=["Eq%d" % ei])
                    P.act(lambda e, ei=ei: e.activation(out=Ek[ei][:], in_=pbt[:], func=AF.Exp, scale=1.0 / 16.0),
                          reads=[pbtk], writes=["Ek%d" % ei])
                    P.dve(lambda e, ei=ei, dc=dc, tt=tt: e.tensor_copy(out=Elast[:, dc, tt * NJ:(tt + 1) * NJ], in_=Eq[ei][:, 127::128]),
                          reads=["Eq%d" % ei], writes=[("Elast", dc)])
                    if not state_only:
                        pq, pqk = next_pb()
                        for c in range(DC):
                            P.pe(lambda e, c=c, dl=dl, pq=pq, Wq=Wq: e.matmul(pq[:], Wq[:, c, dl * 128:(dl + 1) * 128], hT[:, c, :],
                                                                       start=(c == 0), stop=(c == DC - 1)),
                                 reads=[Wqk, ("h", c)], writes=[pqk])
                        P.dve(lambda e, pq=pq, ei=ei, dc=dc: e.scalar_tensor_tensor(out=qT[:, dc, :], in0=pq[:], scalar=1.0 / 16.0,
                                                                                    in1=Eq[ei][:], op0=ALU.mult, op1=ALU.mult),
                              reads=[pqk, "Eq%d" % ei], writes=[("qT", dc)])
                    pk_, pkk = next_pb()
                    for c in range(DC):
                        P.pe(lambda e, c=c, dl=dl, pk_=pk_, Wk=Wk: e.matmul(pk_[:], Wk[:, c, dl * 128:(dl + 1) * 128], hT[:, c, :],
                                                                     start=(c == 0), stop=(c == DC - 1)),
                             reads=[Wkk, ("h", c)], writes=[pkk])
                    P.dve(lambda e, pk_=pk_, ei=ei, dc=dc: e.tensor_tensor(out=kT[:, dc, :], in0=pk_[:], in1=Ek[ei][:], op=ALU.mult),
                          reads=[pkk, "Ek%d" % ei], writes=[("kT", dc)])
                    kd = kd_s[dc % 2]
                    kdk = "kds%d" % (dc % 2)
                    for j in range(NJ):
                        jsl = slice(j * 128, (j + 1) * 128)
                        P.dve(lambda e, kd=kd, dc=dc, j=j, jsl=jsl, tt=tt: e.tensor_scalar(out=kd[:, jsl], in0=kT[:, dc, jsl],
                                                                                    scalar1=Elast[:, dc, tt * NJ + j:tt * NJ + j + 1], scalar2=None,
                                                                                    op0=ALU.mult),
                              reads=[("kT", dc), ("Elast", dc)], writes=[kdk])
                    def emit_tr(kd=kd, kdk=kdk, dc=dc):
                        ptr, ptrk = next_pb()
                        for j2 in range(NJ):
                            jsl2 = slice(j2 * 128, (j2 + 1) * 128)
                            P.pe(lambda e, kd=kd, jsl2=jsl2, ptr=ptr: e.transpose(pbf(ptr)[:, jsl2], kd[:, jsl2], ident[:]),
                                 reads=[kdk, "ident"], writes=[ptrk])
                        P.act(lambda e, ptr=ptr, dc=dc: e.copy(out=kdec[:, :, dc * 128:(dc + 1) * 128],
                                                               in_=pbf(ptr)[:, 0:512].rearrange("p (j d) -> p j d", j=NJ)),
                              reads=[ptrk], writes=[("kdec", dc)])
                    pend_tr.append(emit_tr)
                    if len(pend_tr) > 1:
                        pend_tr.pop(0)()
            while pend_tr:
                pend_tr.pop(0)()
            for vb in range(4):
                Wv, Wvk = load_w(win_v, 2048 + vb * 512)
                for j in range(NJ):
                    jsl = slice(j * 128, (j + 1) * 128)
                    pv, pvk = next_pb()
                    for c in range(DC):
                        P.pe(lambda e, c=c, jsl=jsl, pv=pv, Wv=Wv: e.matmul(pv[:], hT[:, c, jsl], Wv[:, c, :],
                                                                            start=(c == 0), stop=(c == DC - 1)),
                             reads=[Wvk, ("h", c)], writes=[pvk])
                    P.act(lambda e, pv=pv, j=j, vb=vb: e.copy(out=vt[:, j, vb * 512:(vb + 1) * 512], in_=pv[:]),
                          reads=[pvk], writes=[("vt", j, vb)])
            for rb in (range(4) if not state_only else ()):
                Wr, Wrk = load_w(win_v, 4096 + rb * 512)
                for j in range(NJ):
                    jsl = slice(j * 128, (j + 1) * 128)
                    pv, pvk = next_pb()
                    for c in range(DC):
                        P.pe(lambda e, c=c, jsl=jsl, pv=pv, Wr=Wr: e.matmul(pv[:], hT[:, c, jsl], Wr[:, c, :],
                                                                            start=(c == 0), stop=(c == DC - 1)),
                             reads=[Wrk, ("h", c)], writes=[pvk])
                    P.act(lambda e, pv=pv, j=j, rb=rb: e.activation(out=sr[:, j, rb * 512:(rb + 1) * 512], in_=pv[:], func=AF.Silu),
                          reads=[pvk], writes=[("sr", j)])
        if mode == "proj":
            for j in range(NJ):
                P.pool(lambda e, j=j: e.tensor_tensor(out=sr[:, j, :], in0=sr[:, j, :], in1=gnb[:], op=ALU.mult),
                       reads=[("sr", j), "gnb"], writes=[("sr", j)])
            P.dma("sp", lambda e, tsl=tsl: e.dma_start(out=dr["q"].rearrange("(dc p) t -> p dc t", p=128)[:, :, tsl], in_=qT[:]),
                  reads=[("qT", dc) for dc in range(8)], writes=[("dq", tt)])
            P.dma("sp", lambda e, tsl=tsl: e.dma_start(out=dr["k"].rearrange("(dc p) t -> p dc t", p=128)[:, :, tsl], in_=kT[:]),
                  reads=[("kT", dc) for dc in range(8)], writes=[("dk", tt)])
            P.dma("sp", lambda e, tt=tt: e.dma_start(out=dr["kdec"].rearrange("(jj p) d -> p jj d", p=128)[:, tt * NJ:(tt + 1) * NJ, :], in_=kdec[:]),
                  reads=[("kdec", dc) for dc in range(8)], writes=[("dkd", tt)])
            P.dma("sp", lambda e, tt=tt: e.dma_start(out=dr["v"].rearrange("(jj p) n -> p jj n", p=128)[:, tt * NJ:(tt + 1) * NJ, :], in_=vt[:]),
                  reads=[("vt", j, h) for j in range(NJ) for h in range(4)], writes=[("dv", tt)])
            P.dma("sp", lambda e, tt=tt: e.dma_start(out=dr["sr"].rearrange("(jj p) n -> p jj n", p=128)[:, tt * NJ:(tt + 1) * NJ, :], in_=sr[:]),
                  reads=[("sr", j) for j in range(NJ)], writes=[("dsr", tt)])
            continue
        pend_g = []
        for j in range(NJ):
            jsl = slice(j * 128, (j + 1) * 128)
            gt_ = gated2[j % 2]
            gk_ = "gated%d" % (j % 2)
            if not state_only:
                if mode != "scanf":
                    P.pool(lambda e, j=j: e.tensor_tensor(out=sr[:, j, :], in0=sr[:, j, :], in1=gnb[:], op=ALU.mult),
                           reads=[("sr", j), "gnb"], writes=[("sr", j)])
                pst, pstk = pb[4], pbk[4]
                for h in range(4):
                    hs = slice(h * 128, (h + 1) * 128)
                    for di in range(2):
                        dc = 2 * h + di
                        P.pe(lambda e, dc=dc, di=di, hs=hs, jsl=jsl: e.matmul(pst[:, hs], kT[:, dc, jsl], qT[:, dc, jsl],
                                                                              start=(di == 0), stop=(di == 1)),
                             reads=[("kT", dc), ("qT", dc)], writes=[pstk])
                for h in range(4):
                    hs = slice(h * 128, (h + 1) * 128)
                    P.dve(lambda e, h=h, hs=hs: e.tensor_tensor(out=AT4[h][:], in0=pst[:, hs], in1=tri[:], op=ALU.mult),
                          reads=[pstk, "tri"], writes=["AT4%d" % h])
            for h in range(4):
                vkeys = [("vt", j, h)]
                for di in range(2):
                    dc = 2 * h + di
                    pkv, pkvk = pb[5 + di], pbk[5 + di]
                    P.pe(lambda e, dc=dc, j=j, h=h, pkv=pkv: e.matmul(pkv[:], kdec[:, j, dc * 128:(dc + 1) * 128],
                                                                      vt[:, j, h * 512:(h + 1) * 512], start=True, stop=True),
                         reads=[("kdec", dc)] + vkeys, writes=[pkvk])
                    P.dve(lambda e, dc=dc, j=j, pkv=pkv, tt=tt: e.scalar_tensor_tensor(out=S[:, dc, :], in0=S[:, dc, :],
                                                                                scalar=Elast[:, dc, tt * NJ + j:tt * NJ + j + 1], in1=pkv[:],
                                                                                op0=ALU.mult, op1=ALU.add),
                          reads=[pkvk, ("Elast", dc), ("S", dc)], writes=[("S", dc)])
                if not state_only:
                    for di in range(2):
                        dc = 2 * h + di
                        P.pe(lambda e, dc=dc, di=di, h=h, jsl=jsl: e.matmul(pb[h][:], qT[:, dc, jsl], Sb[:, dc, :],
                                                                            start=(di == 0), stop=False),
                             reads=[("qT", dc), ("Sb", dc)], writes=[pbk[h]])
                    P.pe(lambda e, h=h, j=j: e.matmul(pb[h][:], AT4[h][:], vt[:, j, h * 512:(h + 1) * 512], start=False, stop=True),
                         reads=["AT4%d" % h] + vkeys, writes=[pbk[h]])
                    for di in range(2):
                        dc = 2 * h + di
                        P.act(lambda e, dc=dc: e.copy(out=Sb[:, dc, :], in_=S[:, dc, :]), reads=[("S", dc)], writes=[("Sb", dc)])
            if state_only:
                continue
            for h in range(4):
                P.act(lambda e, h=h: e.activation(out=osq[:, h * 512:(h + 1) * 512], in_=pb[h][:], func=AF.Square),
                      reads=[pbk[h]], writes=[("osq", h)])
            P.dve(lambda e: e.reduce_sum(out=ssq[:], in_=osq[:].rearrange("p (h v) -> p h v", h=4), axis=AX.X),
                  reads=[("osq", h) for h in range(4)], writes=["ssq"])
            P.dve(lambda e: e.tensor_scalar(out=ssq[:], in0=ssq[:], scalar1=1.0 / 512.0, scalar2=EPS, op0=ALU.mult, op1=ALU.add),
                  reads=["ssq"], writes=["ssq"])
            P.act(lambda e: e.activation(out=ssq[:], in_=ssq[:], func=AF.Sqrt), reads=["ssq"], writes=["ssq"])
            P.dve(lambda e: e.reciprocal(out=ssq[:], in_=ssq[:]), reads=["ssq"], writes=["ssq"])
            for h in range(4):
                P.dve(lambda e, h=h, j=j, gt_=gt_: e.scalar_tensor_tensor(out=gt_[:, h * 512:(h + 1) * 512], in0=pb[h][:],
                                                                         scalar=ssq[:, h:h + 1], in1=sr[:, j, h * 512:(h + 1) * 512],
                                                                         op0=ALU.mult, op1=ALU.mult),
                      reads=[pbk[h], "ssq", ("sr", j)], writes=[(gk_, h)])

            def emit_gtr(gt_=gt_, gk_=gk_, j=j, jsl=jsl):
                for fb in range(4):
                    ptr, ptrk = pb[7], pbk[7]
                    for fi in range(4):
                        fc = fb * 4 + fi
                        P.pe(lambda e, fc=fc, fi=fi, ptr=ptr, gt_=gt_: e.transpose(pbf(ptr)[:, fi * 128:(fi + 1) * 128],
                                                                                  gt_[:, fc * 128:(fc + 1) * 128], ident[:]),
                             reads=[(gk_, fb), "ident"], writes=[ptrk])
                    P.act(lambda e, fb=fb, jsl=jsl, ptr=ptr: e.copy(out=gT[:, fb * 4:(fb + 1) * 4, jsl],
                                                                   in_=pbf(ptr)[:, 0:512].rearrange("p (f t) -> p f t", f=4)),
                          reads=[ptrk], writes=[("gT", fb, j)])
            pend_g.append(emit_gtr)
            if len(pend_g) > 1:
                pend_g.pop(0)()
        while pend_g:
            pend_g.pop(0)()
        for ob in (range(4) if not state_only else ()):
            Wo, Wok = load_w(wout_v, ob * 512)
            for dd in range(4):
                d = ob * 4 + dd
                py_, pyk = next_pb()
                for fc in range(DC):
                    P.pe(lambda e, fc=fc, dd=dd, py_=py_, Wo=Wo: e.matmul(py_[:], Wo[:, fc, dd * 128:(dd + 1) * 128], gT[:, fc, :],
                                                                          start=(fc == 0), stop=(fc == DC - 1)),
                         reads=[Wok] + [("gT", fc // 4, j) for j in range(NJ)], writes=[pyk])
                i = xctr[0] % 3
                xctr[0] += 1
                P.dma("sp", lambda e, i=i, d=d, tsl=tsl: e.dma_start(out=xs[i][:], in_=xin_v[:, d, tsl]), writes=["xs%d" % i])
                P.dve(lambda e, i=i, py_=py_: e.tensor_tensor(out=xs[i][:], in0=xs[i][:], in1=py_[:], op=ALU.add),
                      reads=[pyk, "xs%d" % i], writes=["xs%d" % i])
                P.dma("sp", lambda e, i=i, d=d, tsl=tsl: e.dma_start(out=xout_v[:, d, tsl], in_=xs[i][:]),
                      reads=["xs%d" % i], writes=[("xo", tt, d)])
                ph.out_keys.append(("xo", tt, d))
    if mode == "proj":
        P.dma("sp", lambda e: e.dma_start(out=dr["elast"].rearrange("p (dc jj) -> p dc jj", dc=8), in_=Elast[:]),
              reads=[("Elast", dc) for dc in range(8)], writes=["del"])
    if st_out is not None:
        P.dma("sp", lambda e: e.dma_start(out=st_out[:, :, :], in_=S[:]), reads=[("S", dc) for dc in range(8)],
              writes=["st_out"])
        ph.out_keys.append("st_out")
    return ph.finish()


def build_gla_scan0(ph):
    P = ph.P
    dr = ph.bind["dr"]
    st_out = ph.bind["st_out"]
    NCH = TC // 128
    kd_all = ph.sb("kd_all", [128, NCH, 1024], BF16)
    v_all = ph.sb("v_all", [128, NCH, 2048], BF16)
    El = ph.sb("El", [128, 8, NCH], F32)
    S = ph.sb("S", [128, 8, 512], F32)
    pb = [ph.ps("pb%d" % i, [128, 512]) for i in range(8)]
    kd_v = dr["kdec"].rearrange("(jj p) d -> p jj d", p=128)
    v_v = dr["v"].rearrange("(jj p) n -> p jj n", p=128)
    for q4 in range(4):
        P.dma("sp", lambda e, q4=q4: e.dma_start(out=kd_all[:, q4 * 4:(q4 + 1) * 4, :], in_=kd_v[:, q4 * 4:(q4 + 1) * 4, :]),
              writes=[("kd", q4)])
        P.dma("sp", lambda e, q4=q4: e.dma_start(out=v_all[:, q4 * 4:(q4 + 1) * 4, :], in_=v_v[:, q4 * 4:(q4 + 1) * 4, :]),
              writes=[("v", q4)])
    P.dma("sp", lambda e: e.dma_start(out=El[:], in_=dr["elast"].rearrange("p (dc jj) -> p dc jj", dc=8)), writes=["El"])
    P.dve(lambda e: e.memset(S[:], 0.0), writes=[("S", dc) for dc in range(8)])
    cnt = 0
    for jg in range(NCH):
        for h in range(4):
            for di in range(2):
                dc = 2 * h + di
                pkv, pkvk = pb[cnt % 8], "pb%d" % (cnt % 8)
                cnt += 1
                P.pe(lambda e, dc=dc, jg=jg, h=h, pkv=pkv: e.matmul(pkv[:], kd_all[:, jg, dc * 128:(dc + 1) * 128],
                                                                  v_all[:, jg, h * 512:(h + 1) * 512], start=True, stop=True),
                     reads=[("kd", jg // 4), ("v", jg // 4)], writes=[pkvk])
                P.dve(lambda e, dc=dc, jg=jg, pkv=pkv: e.scalar_tensor_tensor(out=S[:, dc, :], in0=S[:, dc, :],
                                                                             scalar=El[:, dc, jg:jg + 1], in1=pkv[:],
                                                                             op0=ALU.mult, op1=ALU.add),
                      reads=[pkvk, "El", ("S", dc)], writes=[("S", dc)])
    P.dma("sp", lambda e: e.dma_start(out=st_out[:, :, :], in_=S[:]), reads=[("S", dc) for dc in range(8)], writes=["st_out"])
    return ph.finish()


def pbf(ptile):
    return ptile[:].bitcast(BF16) if hasattr(ptile[:], "bitcast") else ptile


def gla_consts():
    tri = np.triu(np.ones((128, 128), np.float32))
    ident = np.eye(128, dtype=np.float32)
    return tri, ident


def gla_inputs(xT_core, g, w_in, w_a2, b_a, g_norm, w_out, state):
    tri, ident = gla_consts()
    st = np.ascontiguousarray(np.asarray(state, np.float32).reshape(4, 2, 128, 512).transpose(2, 0, 1, 3).reshape(128, 8, 512))
    gnb = np.ascontiguousarray(np.broadcast_to(np.tile(np.asarray(g_norm, np.float32), 4)[None, :], (128, 2048)))
    wa2b = np.ascontiguousarray(np.concatenate([np.asarray(w_a2, np.float32), np.asarray(b_a, np.float32)[None, :]], axis=0))
    return {"xT": np.ascontiguousarray(xT_core, dtype=np.float32), "g": col16(g), "win": np.ascontiguousarray(w_in, dtype=np.float32),
            "wa2b": wa2b, "gnb": gnb, "wout": np.ascontiguousarray(w_out, dtype=np.float32), "st_in": st, "tri": tri, "ident": ident}


def gla_state_from_out(st_out):
    return np.ascontiguousarray(st_out.reshape(128, 4, 2, 512).transpose(1, 2, 0, 3).reshape(4, 256, 512))


def build_diff1_phase(do_qk=True, do_v=True, do_norm=True, ph=None):
    ph = ph or Phase()
    P = ph.P
    xin = ph.din("xT", [D, TC])
    g_in = ph.din("g", [128, DC])
    win = ph.din("win", [D, 3 * D])
    qkg_in = ph.din("qkg", [128, 16])
    qko = ph.dout("qkT", [2 * D, TC])
    vo = ph.dout("v", [TC, D])
    NT = 512
    xs = [ph.sb("xs%d" % i, [128, 512], F32) for i in range(3)]
    hT = ph.sb("hTs", [128, DC, NT], BF16)
    Wb = [ph.sb("Wb%d" % i, [128, DC, 512], BF16) for i in range(2)]
    qraw = [ph.sb("qraw%d" % i, [128, 512], F32) for i in range(2)]
    odt = BF16 if ph.bind.get("k_ds") is not None else F32
    qn = [ph.sb("qn%d" % i, [128, 512], odt) for i in range(3)]
    vs = [ph.sb("vs%d" % i, [128, 512], odt) for i in range(3)]
    sq = [ph.sb("sq%d" % i, [128, 512], BF16) for i in range(2)]
    rstd = ph.sb("rstd", [128, 512], F32)
    rs2 = [ph.sb("rs2%d" % i, [128, 512], F32) for i in range(2)]
    gcol = ph.sb("gcol", [128, DC], F32)
    qkg = ph.sb("qkgs", [128, 16], F32)
    ones = ph.sb("ones", [128, 128], BF16)
    pb = [ph.ps("pb%d" % i, [128, 512]) for i in range(8)]
    pbk = ["pb%d" % i for i in range(8)]
    P.dma("sp", lambda e: e.dma_start(out=gcol[:], in_=g_in[:, :]), writes=["gcol"])
    P.dma("sp", lambda e: e.dma_start(out=qkg[:], in_=qkg_in[:, :]), writes=["qg", "kg"])
    P.dve(lambda e: e.memset(ones[:], 1.0), writes=["ones"])
    win_v = win.rearrange("(c p) n -> p c n", p=128)
    xin_v = xin.rearrange("(c p) t -> p c t", p=128)
    k_ds = ph.bind.get("k_ds")
    v_ds = ph.bind.get("v_ds")
    if k_ds is not None:
        q_v = ph.bind["q_d"].rearrange("(c p) t -> p c t", p=128)
        k_vs = [kd_.rearrange("(c p) t -> p c t", p=128) for kd_ in k_ds]
        v_vs = [vd_.rearrange("(j p) n -> p j n", p=128) for vd_ in v_ds]
    else:
        qko_v = qko.rearrange("(c p) t -> p c t", p=128)
    vo_v = vo.rearrange("(j p) n -> p j n", p=128)
    xctr = [0]
    wctr = [0]
    pctr = [0]
    nctr = [0]

    def load_w(col0):
        i = wctr[0] % 2
        wctr[0] += 1
        P.dma("pool", lambda e, i=i, col0=col0: e.dma_start(out=Wb[i][:], in_=win_v[:, :, col0:col0 + 512]), writes=["Wb%d" % i])
        return Wb[i], "Wb%d" % i

    def next_pb():
        i = 2 + pctr[0] % 4
        pctr[0] += 1
        return pb[i], pbk[i]

    for tt in range(TC // NT):
        tsl = slice(tt * NT, (tt + 1) * NT)
        p_ss, p_ssk = pb[0], pbk[0]
        for c in range(DC):
            i = xctr[0] % 3
            xctr[0] += 1
            P.dma("sp", lambda e, i=i, c=c, tsl=tsl: e.dma_start(out=xs[i][:], in_=xin_v[:, c, tsl]), writes=["xs%d" % i])
            q = sq[c % 2]
            qk = "sq%d" % (c % 2)
            P.act(lambda e, q=q, i=i: e.activation(out=q[:], in_=xs[i][:], func=AF.Square), reads=["xs%d" % i], writes=[qk])
            P.pe(lambda e, q=q, c=c: e.matmul(p_ss[:], ones[:], q[:], start=(c == 0), stop=(c == DC - 1)),
                 reads=[qk, "ones"], writes=[p_ssk])
        P.dve(lambda e: e.tensor_scalar(out=rstd[:], in0=p_ss[:], scalar1=1.0 / D, scalar2=EPS, op0=ALU.mult, op1=ALU.add),
              reads=[p_ssk], writes=["rstd"])
        P.act(lambda e: e.activation(out=rstd[:], in_=rstd[:], func=AF.Sqrt), reads=["rstd"], writes=["rstd"])
        P.dve(lambda e: e.reciprocal(out=rstd[:], in_=rstd[:]), reads=["rstd"], writes=["rstd"])
        for c in range(DC):
            i = xctr[0] % 3
            xctr[0] += 1
            P.dma("sp", lambda e, i=i, c=c, tsl=tsl: e.dma_start(out=xs[i][:], in_=xin_v[:, c, tsl]), writes=["xs%d" % i])
            P.dve(lambda e, i=i, c=c: e.scalar_tensor_tensor(out=hT[:, c, :], in0=xs[i][:], scalar=gcol[:, c:c + 1],
                                                             in1=rstd[:], op0=ALU.mult, op1=ALU.mult),
                  reads=["xs%d" % i, "rstd", "gcol"], writes=[("h", c)])
        for which in (range(2) if do_qk else ()):
            gk = "qg" if which == 0 else "kg"
            for blk in range(4):
                Wq, Wqk = load_w(which * D + blk * 512)
                for dl in range(4):
                    hd = blk * 4 + dl
                    pq, pqk = next_pb()
                    for c in range(DC):
                        P.pe(lambda e, c=c, dl=dl, pq=pq, Wq=Wq: e.matmul(pq[:], Wq[:, c, dl * 128:(dl + 1) * 128], hT[:, c, :],
                                                                          start=(c == 0), stop=(c == DC - 1)),
                             reads=[Wqk, ("h", c)], writes=[pqk])
                    if do_norm:
                        qr = qraw[hd % 2]
                        qrk = "qraw%d" % (hd % 2)
                        sqb = sq[hd % 2]
                        sqk = "sq%d" % (hd % 2)
                        P.act(lambda e, pq=pq, sqb=sqb: e.activation(out=sqb[:], in_=pq[:], func=AF.Square), reads=[pqk], writes=[sqk])
                        P.act(lambda e, pq=pq, qr=qr: e.copy(out=qr[:], in_=pq[:]), reads=[pqk], writes=[qrk])
                        p2, p2k = pb[6 + hd % 2], pbk[6 + hd % 2]
                        P.pe(lambda e, p2=p2, sqb=sqb: e.matmul(p2[:], ones[:], sqb[:], start=True, stop=True),
                             reads=[sqk, "ones"], writes=[p2k])
                        r2 = rs2[hd % 2]
                        r2k = "rs2%d" % (hd % 2)
                        P.dve(lambda e, p2=p2, r2=r2: e.tensor_scalar(out=r2[:], in0=p2[:], scalar1=1.0 / 128.0, scalar2=EPS,
                                                                      op0=ALU.mult, op1=ALU.add), reads=[p2k], writes=[r2k])
                        P.act(lambda e, r2=r2: e.activation(out=r2[:], in_=r2[:], func=AF.Sqrt), reads=[r2k], writes=[r2k])
                        P.dve(lambda e, r2=r2: e.reciprocal(out=r2[:], in_=r2[:]), reads=[r2k], writes=[r2k])
                        ni = nctr[0] % 3
                        nctr[0] += 1
                        P.dve(lambda e, ni=ni, qr=qr, r2=r2, which=which: e.scalar_tensor_tensor(out=qn[ni][:], in0=qr[:], scalar=qkg[:, which:which + 1],
                                                                                              in1=r2[:], op0=ALU.mult, op1=ALU.mult),
                              reads=[qrk, r2k, gk], writes=["qn%d" % ni])
                    else:
                        ni = nctr[0] % 3
                        nctr[0] += 1
                        P.act(lambda e, pq=pq, ni=ni: e.copy(out=qn[ni][:], in_=pq[:]), reads=[pqk], writes=["qn%d" % ni])
                    if k_ds is not None:
                        dst_ap = q_v[:, hd, tsl] if which == 0 else k_vs[hd // 2][:, hd % 2, tsl]
                    else:
                        dst_ap = qko_v[:, which * 16 + hd, tsl]
                    P.dma("sp", lambda e, ni=ni, dst_ap=dst_ap: e.dma_start(out=dst_ap, in_=qn[ni][:]),
                          reads=["qn%d" % ni], writes=[("qo", which, tt, hd)])
                    ph.out_keys.append(("qo", which, tt, hd))
        for vb in (range(4) if do_v else ()):
            Wv, Wvk = load_w(2 * D + vb * 512)
            for j in range(4):
                jsl = slice(j * 128, (j + 1) * 128)
                pv, pvk = next_pb()
                for c in range(DC):
                    P.pe(lambda e, c=c, jsl=jsl, pv=pv, Wv=Wv: e.matmul(pv[:], hT[:, c, jsl], Wv[:, c, :],
                                                                        start=(c == 0), stop=(c == DC - 1)),
                         reads=[Wvk, ("h", c)], writes=[pvk])
                vi = nctr[0] % 3
                nctr[0] += 1
                P.act(lambda e, pv=pv, vi=vi: e.copy(out=vs[vi][:], in_=pv[:]), reads=[pvk], writes=["vs%d" % vi])
                if v_ds is not None:
                    for hh in range(2):
                        P.dma("sp", lambda e, vi=vi, tt=tt, j=j, vb=vb, hh=hh: e.dma_start(
                            out=v_vs[2 * vb + hh][:, tt * 4 + j, :], in_=vs[vi][:, hh * 256:(hh + 1) * 256]),
                            reads=["vs%d" % vi], writes=[("vo", tt, j, vb, hh)])
                else:
                    P.dma("sp", lambda e, vi=vi, tt=tt, j=j, vb=vb: e.dma_start(out=vo_v[:, tt * 4 + j, vb * 512:(vb + 1) * 512], in_=vs[vi][:]),
                          reads=["vs%d" % vi], writes=[("vo", tt, j, vb)])
                    ph.out_keys.append(("vo", tt, j, vb))
    return ph.finish()


LAM_INIT2 = 0.8 - 0.6 * math.exp(-0.3 * 2)
NEG = -30000.0


def build_diff2_phase(ph=None):
    ph = ph or Phase()
    P = ph.P
    fused = ph.master is not None
    qin = ph.din("qT", [D, TC])
    kin = None if fused else ph.din("kT", [D, 2 * TC])
    vin = None if fused else ph.din("va", [2 * TC, 8, 257])
    xin = ph.din("xT", [D, TC])
    wout = ph.din("wout", [D, D])
    b0_in = ph.din("B0", [128, 16, 128])
    b1_in = ph.din("B1", [128, 16, 128])
    b1p_in = ph.din("B1p", [128, 16, 128])
    cf_in = ph.din("cfar", [128, 16])
    cp_in = ph.din("cpre", [128, 16])
    lp_in = ph.din("lpb", [128, 4, 128])
    sg_in = ph.din("sgb", [128, 256])
    id_in = ph.din("ident", [128, 128])
    xout = ph.dout("xTo", [D, TC])
    SCALE = 128 ** -0.5
    Kt = ph.sb("Kt", [128, 2, 2 * TC], BF16)
    Va = ph.sb("Va", [128, 32, 257], BF16)
    Qt = ph.sb("Qt", [128, 2, TC], BF16)
    ao = ph.sb("ao", [128, 16, 2048], BF16)
    M0 = ph.sb("M0", [128, 16, 128], BF16)
    M1 = ph.sb("M1", [128, 16, 128], BF16)
    M1p = ph.sb("M1p", [128, 16, 128], BF16)
    btmp = ph.sb("btmp", [128, 16, 128], F32)
    cfar = ph.sb("cfars", [128, 16], F32)
    cpre = ph.sb("cpres", [128, 16], F32)
    negc = ph.sb("negc", [128, 16], F32)
    negcp = ph.sb("negcp", [128, 16], F32)
    lpb = ph.sb("lpbs", [128, 4, 128], F32)
    lt1 = ph.sb("lt1", [128, 128], F32)
    lsum = ph.sb("lsum", [128, 2], F32)
    neglam = ph.sb("neglam", [128, 1], F32)
    sgs = ph.sb("sgs", [128, 256], F32)
    ident = ph.sb("idents", [128, 128], BF16)
    PT = [ph.sb("PT%d" % i, [128, 2, 256], BF16) for i in range(3)]
    rc = ph.sb("rc", [128, 4], F32)
    uu = ph.sb("uu", [128, 256], F32)
    att = ph.sb("att", [128, 256], F32)
    asq = ph.sb("asq", [128, 256], F32)
    ssn = ph.sb("ssn", [128, 1], F32)
    ssn16 = ph.sb("ssn16", [128, 16], F32)
    aoT = ph.sb("aoT", [128, DC, 512], BF16)
    Wb = [ph.sb("Wb%d" % i, [128, DC, 512], BF16) for i in range(2)]
    xs = [ph.sb("xs%d" % i, [128, 512], F32) for i in range(3)]
    pb = [ph.ps("pb%d" % i, [128, 512]) for i in range(8)]
    pbk = ["pb%d" % i for i in range(8)]

    for (dst, src, k) in ((cfar, cf_in, "cfar"), (cpre, cp_in, "cpre"), (sgs, sg_in, "sgs")):
        P.dma("sp", lambda e, dst=dst, src=src: e.dma_start(out=dst[:], in_=src[:, :]), writes=[k])
    P.dma("sp", lambda e: e.dma_start(out=lpb[:], in_=lp_in[:, :, :]), writes=["lpb"])
    P.dma("pool", lambda e: e.dma_start(out=ident[:], in_=id_in[:, :]), writes=["ident"])
    P.dve(lambda e: e.tensor_scalar(out=negc[:], in0=cfar[:], scalar1=-1.0, scalar2=None, op0=ALU.mult), reads=["cfar"], writes=["negc"])
    P.dve(lambda e: e.tensor_scalar(out=negcp[:], in0=cpre[:], scalar1=-1.0, scalar2=None, op0=ALU.mult), reads=["cpre"], writes=["negcp"])
    for (Mt, src, nb, k) in ((M0, b0_in, negc, "M0"), (M1, b1_in, negc, "M1"), (M1p, b1p_in, negc, "M1p")):
        P.dma("sp", lambda e, src=src: e.dma_start(out=btmp[:], in_=src[:, :, :]), writes=["btmp"])
        for h in range(16):
            P.act(lambda e, Mt=Mt, nb=nb, h=h: e.activation(out=Mt[:, h, :], in_=btmp[:, h, :], func=AF.Exp, bias=nb[:, h:h + 1]),
                  reads=["btmp", "negc", "negcp"], writes=[k])
    for pi in range(2):
        P.dve(lambda e, pi=pi: e.tensor_tensor(out=lt1[:], in0=lpb[:, 2 * pi, :], in1=lpb[:, 2 * pi + 1, :], op=ALU.mult),
              reads=["lpb"], writes=["lt1"])
        P.dve(lambda e, pi=pi: e.reduce_sum(out=lsum[:, pi:pi + 1], in_=lt1[:], axis=AX.X), reads=["lt1"], writes=["lsum"])
    P.act(lambda e: e.activation(out=lsum[:], in_=lsum[:], func=AF.Exp), reads=["lsum"], writes=["lsum"])
    P.dve(lambda e: e.tensor_tensor(out=neglam[:], in0=lsum[:, 1:2], in1=lsum[:, 0:1], op=ALU.subtract), reads=["lsum"], writes=["neglam"])
    P.dve(lambda e: e.tensor_scalar(out=neglam[:], in0=neglam[:], scalar1=-LAM_INIT2, scalar2=None, op0=ALU.add),
          reads=["neglam"], writes=["neglam"])
    P.dve(lambda e: e.tensor_scalar(out=sgs[:], in0=sgs[:], scalar1=1.0 - LAM_INIT2, scalar2=None, op0=ALU.mult),
          reads=["sgs"], writes=["sgs"])

    qin_v = qin.rearrange("(c p) t -> p c t", p=128)
    if fused:
        kpre_vs = [a_.rearrange("(c p) t -> p c t", p=128) for a_ in ph.bind["kpres"]]
        kown_vs = [a_.rearrange("(c p) t -> p c t", p=128) for a_ in ph.bind["kowns"]]
        vpre_vs = [a_.rearrange("(kb p) n -> p kb n", p=128) for a_ in ph.bind["vpres"]]
        vown_vs = [a_.rearrange("(kb p) n -> p kb n", p=128) for a_ in ph.bind["vowns"]]
        P.dve(lambda e: e.memset(Va[:, :, 256:257], 1.0), writes=["Va1"])
        for ci in range(8):
            P.cc(lambda e, ci=ci: e.collective_compute("AllGather", ALU.bypass, replica_groups=PAIRS, ins=[ph.bind["kowns"][ci][:, :]],
                                                       outs=[ph.bind["k_alls"][ci][:, :]]), reads=[], writes=[("kall", ci)])
            P.cc(lambda e, ci=ci: e.collective_compute("AllGather", ALU.bypass, replica_groups=PAIRS, ins=[ph.bind["vowns"][ci][:, :]],
                                                       outs=[ph.bind["v_alls"][ci][:, :]]), reads=[], writes=[("vall", ci)])
    else:
        kin_v = kin.rearrange("(c p) t -> p c t", p=128)
        vin_v = vin.rearrange("(kb p) h n -> p kb h n", p=128)
    xin_v = xin.rearrange("(c p) t -> p c t", p=128)
    xout_v = xout.rearrange("(c p) t -> p c t", p=128)
    wout_v = wout.rearrange("(c p) n -> p c n", p=128)
    LOOK = 2
    NPT = 4
    PTs = [ph.sb("PTp%d" % i, [128, 2, 256], BF16) for i in range(NPT)]
    Osb = [ph.sb("Osb%d" % i, [128, 257], F32) for i in range(8)]
    zero_b = ph.sb("zero_b", [128, 1], F32)
    P.dve(lambda e: e.memset(zero_b[:], 0.0), writes=["zero_b"])
    gstep = [0]

    def stage_a(hp, qt, kb, sidx):
        kb_rel = kb - (16 + 2 * qt)
        qlo = 128 if kb_rel == 1 else 0
        ps, psk = pb[4 + sidx % 3], pbk[4 + sidx % 3]
        pt, ptk = PTs[sidx % NPT], "PTp%d" % (sidx % NPT)
        for i in range(2):
            P.pe(lambda e, ps=ps, i=i, kb=kb, qt=qt, qlo=qlo: e.matmul(
                ps[:, i * 256 + qlo:(i + 1) * 256], Kt[:, i, kb * 128:(kb + 1) * 128],
                Qt[:, i, qt * 256 + qlo:(qt + 1) * 256], start=True, stop=True),
                reads=["KtP" if kb < 16 else "KtO", "Qt"], writes=[psk])
        bias_ap = cpre[:, 0:1] if kb < 16 else zero_b[:, 0:1]
        P.act(lambda e, ps=ps, pt=pt, qlo=qlo, bias_ap=bias_ap: e.activation(
            out=pt[:, :, qlo:256], in_=ps[:].rearrange("p (i q) -> p i q", i=2)[:, :, qlo:256], func=AF.Exp,
            bias=bias_ap, scale=SCALE),
            reads=[psk, "cpre", "zero_b"], writes=[ptk])
        fix = []
        if kb_rel == -1:
            fix.append((0, M1p if kb == 15 else M1, "M1p" if kb == 15 else "M1"))
        elif kb_rel == 0:
            fix.append((0, M0, "M0"))
            fix.append((1, M1, "M1"))
        elif kb_rel == 1:
            fix.append((1, M0, "M0"))
        for (qb, Mt, mk) in fix:
            P.dve(lambda e, pt=pt, qb=qb, Mt=Mt, hp=hp: e.tensor_tensor(
                out=pt[:, :, qb * 128:(qb + 1) * 128], in0=pt[:, :, qb * 128:(qb + 1) * 128],
                in1=Mt[:, 2 * hp:2 * hp + 2, :], op=ALU.mult),
                reads=[ptk, mk], writes=[ptk])

    def stage_b(hp, qt, kb, sidx):
        pt, ptk = PTs[sidx % NPT], "PTp%d" % (sidx % NPT)
        for qb in range(2):
            last = 16 + 2 * qt + qb
            if kb > last:
                continue
            for i in range(2):
                acc = pb[qb * 2 + i]
                P.pe(lambda e, acc=acc, pt=pt, i=i, qb=qb, kb=kb, last=last: e.matmul(
                    acc[:, 0:257], pt[:, i, qb * 128:(qb + 1) * 128], Va[:, kb, :],
                    start=(kb == 16), stop=(kb == 15)),
                    reads=[ptk, "VaP" if kb < 16 else "VaO"], writes=[pbk[qb * 2 + i]])
        if kb == 15:
            finalize(hp, qt)

    fctr = [0]

    def finalize(hp, qt):
        fs = (fctr[0] % 2) * 4
        fctr[0] += 1
        for a_i in range(4):
            P.dve(lambda e, a_i=a_i, fs=fs: e.tensor_scalar(out=Osb[fs + a_i][:], in0=pb[a_i][:, 0:257], scalar1=1.0, scalar2=None,
                                                            op0=ALU.mult),
                  reads=[pbk[a_i]], writes=["Osb%d" % (fs + a_i)])
        for qb in range(2):
            o1, o1k = Osb[fs + qb * 2], "Osb%d" % (fs + qb * 2)
            o2, o2k = Osb[fs + qb * 2 + 1], "Osb%d" % (fs + qb * 2 + 1)
            P.dve(lambda e, o1=o1: e.reciprocal(out=rc[:, 0:1], in_=o1[:, 256:257]), reads=[o1k], writes=["rc"])
            P.dve(lambda e, o2=o2: e.reciprocal(out=rc[:, 1:2], in_=o2[:, 256:257]), reads=[o2k], writes=["rc"])
            P.dve(lambda e: e.tensor_tensor(out=rc[:, 1:2], in0=rc[:, 1:2], in1=neglam[:], op=ALU.mult),
                  reads=["rc", "neglam"], writes=["rc"])
            P.dve(lambda e, o2=o2: e.tensor_scalar(out=uu[:], in0=o2[:, 0:256], scalar1=rc[:, 1:2], scalar2=None, op0=ALU.mult),
                  reads=[o2k, "rc"], writes=["uu"])
            P.dve(lambda e, o1=o1: e.scalar_tensor_tensor(out=att[:], in0=o1[:, 0:256], scalar=rc[:, 0:1], in1=uu[:],
                                                          op0=ALU.mult, op1=ALU.add),
                  reads=[o1k, "rc", "uu"], writes=["att"])
            P.dve(lambda e: e.tensor_tensor(out=asq[:], in0=att[:], in1=att[:], op=ALU.mult), reads=["att"], writes=["asq"])
            qbg = 2 * qt + qb
            P.dve(lambda e, qbg=qbg: e.reduce_sum(out=ssn16[:, qbg:qbg + 1], in_=asq[:], axis=AX.X), reads=["asq"], writes=["ssn16"])
            P.dve(lambda e, qbg=qbg, hp=hp: e.tensor_scalar(out=ao[:, qbg, hp * 256:(hp + 1) * 256], in0=att[:], scalar1=1.0,
                                                            scalar2=None, op0=ALU.mult),
                  reads=["att"], writes=[("ao", qbg)])

    def post_hp(hp):
        P.dve(lambda e: e.tensor_scalar(out=ssn16[:], in0=ssn16[:], scalar1=1.0 / 256.0, scalar2=EPS, op0=ALU.mult, op1=ALU.add),
              reads=["ssn16"], writes=["ssn16"])
        P.act(lambda e: e.activation(out=ssn16[:], in_=ssn16[:], func=AF.Sqrt), reads=["ssn16"], writes=["ssn16"])
        P.dve(lambda e: e.reciprocal(out=ssn16[:], in_=ssn16[:]), reads=["ssn16"], writes=["ssn16"])
        for qbg in range(16):
            P.dve(lambda e, qbg=qbg, hp=hp: e.scalar_tensor_tensor(out=ao[:, qbg, hp * 256:(hp + 1) * 256],
                                                                   in0=ao[:, qbg, hp * 256:(hp + 1) * 256],
                                                                   scalar=ssn16[:, qbg:qbg + 1], in1=sgs[:], op0=ALU.mult, op1=ALU.mult),
                  reads=[("ao", qbg), "ssn16", "sgs"], writes=[("ao", qbg)])

    for hp in range(8):
        if fused:
            P.dma("sp", lambda e, hp=hp: e.dma_start(out=Kt[:, :, TC:2 * TC], in_=kown_vs[hp][:, :, :]), writes=["KtO"])
            P.dma("sp", lambda e, hp=hp: e.dma_start(out=Va[:, 16:32, 0:256], in_=vown_vs[hp][:, :, :]), reads=["Va1"], writes=["VaO"])
            P.dma("sp", lambda e, hp=hp: e.dma_start(out=Qt[:], in_=qin_v[:, 2 * hp:2 * hp + 2, :]), writes=["Qt"])
            P.dma("sp", lambda e, hp=hp: e.dma_start(out=Kt[:, :, 0:TC], in_=kpre_vs[hp][:, :, :]), reads=[("kall", hp)], writes=["KtP"])
            P.dma("sp", lambda e, hp=hp: e.dma_start(out=Va[:, 0:16, 0:256], in_=vpre_vs[hp][:, :, :]), reads=["Va1", ("vall", hp)],
                  writes=["VaP"])
        else:
            P.dma("pool", lambda e, hp=hp: e.dma_start(out=Kt[:, :, 0:TC], in_=kin_v[:, 2 * hp:2 * hp + 2, 0:TC]), writes=["KtP"])
            P.dma("pool", lambda e, hp=hp: e.dma_start(out=Kt[:, :, TC:2 * TC], in_=kin_v[:, 2 * hp:2 * hp + 2, TC:2 * TC]), writes=["KtO"])
            P.dma("pool", lambda e, hp=hp: e.dma_start(out=Va[:, 0:16, :], in_=vin_v[:, 0:16, hp, :]), writes=["VaP"])
            P.dma("pool", lambda e, hp=hp: e.dma_start(out=Va[:, 16:32, :], in_=vin_v[:, 16:32, hp, :]), writes=["VaO"])
            P.dma("pool", lambda e, hp=hp: e.dma_start(out=Qt[:], in_=qin_v[:, 2 * hp:2 * hp + 2, :]), writes=["Qt"])
        steps = [(qt, kb) for qt in range(8) for kb in (list(range(16, 16 + 2 * qt + 2)) + list(range(16)))]
        base = gstep[0]
        for n in range(len(steps) + LOOK):
            if n < len(steps):
                stage_a(hp, steps[n][0], steps[n][1], base + n)
            if n >= LOOK:
                stage_b(hp, steps[n - LOOK][0], steps[n - LOOK][1], base + n - LOOK)
        gstep[0] += len(steps)
        post_hp(hp)
    xctr = [0]
    wctr = [0]
    pctr = [0]
    for tt in range(4):
        tsl = slice(tt * 512, (tt + 1) * 512)
        for j in range(4):
            qbg = tt * 4 + j
            for fb in range(4):
                ptr, ptrk = pb[6], pbk[6]
                for fi in range(4):
                    fc = fb * 4 + fi
                    P.pe(lambda e, ptr=ptr, fi=fi, fc=fc, qbg=qbg: e.transpose(pbf(ptr)[:, fi * 128:(fi + 1) * 128],
                                                                              ao[:, qbg, fc * 128:(fc + 1) * 128], ident[:]),
                         reads=[("ao", qbg), "ident"], writes=[ptrk])
                P.act(lambda e, ptr=ptr, fb=fb, j=j: e.copy(out=aoT[:, fb * 4:(fb + 1) * 4, j * 128:(j + 1) * 128],
                                                            in_=pbf(ptr)[:, 0:512].rearrange("p (f t) -> p f t", f=4)),
                      reads=[ptrk], writes=[("aoT", fb)])
        for ob in range(4):
            wi = wctr[0] % 2
            wctr[0] += 1
            P.dma("pool", lambda e, wi=wi, ob=ob: e.dma_start(out=Wb[wi][:], in_=wout_v[:, :, ob * 512:(ob + 1) * 512]),
                  writes=["Wb%d" % wi])
            for dd in range(4):
                d = ob * 4 + dd
                pi = pctr[0] % 2
                pctr[0] += 1
                py_, pyk = pb[pi], pbk[pi]
                for fc in range(DC):
                    P.pe(lambda e, fc=fc, dd=dd, py_=py_, wi=wi: e.matmul(py_[:], Wb[wi][:, fc, dd * 128:(dd + 1) * 128], aoT[:, fc, :],
                                                                          start=(fc == 0), stop=(fc == DC - 1)),
                         reads=["Wb%d" % wi, ("aoT", fc // 4)], writes=[pyk])
                xi = xctr[0] % 3
                xctr[0] += 1
                P.dma("sp", lambda e, xi=xi, d=d, tsl=tsl: e.dma_start(out=xs[xi][:], in_=xin_v[:, d, tsl]), writes=["xs%d" % xi])
                P.dve(lambda e, xi=xi, py_=py_: e.tensor_tensor(out=xs[xi][:], in0=xs[xi][:], in1=py_[:], op=ALU.add),
                      reads=[pyk, "xs%d" % xi], writes=["xs%d" % xi])
                P.dma("sp", lambda e, xi=xi, d=d, tsl=tsl: e.dma_start(out=xout_v[:, d, tsl], in_=xs[xi][:]),
                      reads=["xs%d" % xi], writes=[("xo", tt, d)])
                ph.out_keys.append(("xo", tt, d))
    return ph.finish()


def rel_bucket_np(rel):
    n = np.maximum(rel, 0)
    nf = np.maximum(n, 1).astype(np.float32)
    large = 16 + (np.log(nf / np.float32(16)) / np.float32(math.log(128 / 16)) * np.float32(16)).astype(np.int32)
    large = np.minimum(large, 31)
    return np.where(n < 16, n, large)


def diff_bias_tiles(rel_bias, first_half):
    tab = np.concatenate([np.asarray(rel_bias, np.float32), np.full((1, 16), NEG, np.float32),
                          np.full((1, 16), 2 * NEG, np.float32)], axis=0)
    k = np.arange(128)[:, None]
    q = np.arange(128)[None, :]
    rel0 = q - k
    idx0 = np.where(rel0 >= 0, rel_bucket_np(rel0), 32)
    idx1 = rel_bucket_np(128 + q - k)
    B0 = np.ascontiguousarray(tab[idx0].transpose(0, 2, 1))
    B1 = np.ascontiguousarray(tab[idx1].transpose(0, 2, 1))
    if first_half:
        B1p = np.full((128, 16, 128), 2 * NEG, np.float32)
        cpre = np.full((128, 16), NEG, np.float32)
    else:
        B1p = B1.copy()
        cpre = np.zeros((128, 16), np.float32)
    cfar = np.ascontiguousarray(np.broadcast_to(tab[31][None, :], (128, 16)))
    return B0, B1, B1p, cfar, cpre


def diff2_inputs(qT, kT_own, v_own, kT_prev, v_prev, xT, w_out, rel_bias, lam_params, sub_gain, first_half):
    kT_all = np.zeros((D, 2 * TC), np.float32)
    va = np.zeros((2 * TC, 8, 257), np.float32)
    kT_all[:, TC:] = kT_own
    va[TC:, :, :256] = v_own.reshape(TC, 8, 256)
    va[TC:, :, 256] = 1.0
    if not first_half:
        kT_all[:, :TC] = kT_prev
        va[:TC, :, :256] = v_prev.reshape(TC, 8, 256)
        va[:TC, :, 256] = 1.0
    B0, B1, B1p, cfar, cpre = diff_bias_tiles(rel_bias, first_half)
    lpb = np.ascontiguousarray(np.broadcast_to(np.asarray(lam_params, np.float32)[None], (128, 4, 128)))
    sgb = np.ascontiguousarray(np.broadcast_to(np.asarray(sub_gain, np.float32)[None, :], (128, 256)))
    return {"qT": np.ascontiguousarray(qT), "kT": kT_all, "va": va, "xT": np.ascontiguousarray(xT),
            "wout": np.ascontiguousarray(w_out, dtype=np.float32), "B0": B0, "B1": B1, "B1p": B1p, "cfar": cfar, "cpre": cpre,
            "lpb": lpb, "sgb": sgb, "ident": np.eye(128, dtype=np.float32)}


def diff1_inputs(xT, g, w_in, qg, kg):
    qkg = np.zeros((128, 16), np.float32)
    qkg[:, 0] = np.asarray(qg, np.float32)
    qkg[:, 1] = np.asarray(kg, np.float32)
    return {"xT": np.ascontiguousarray(xT), "g": col16(g), "win": np.ascontiguousarray(w_in, dtype=np.float32), "qkg": qkg}


PAIRS = [[0, 1], [2, 3], [4, 5], [6, 7]]
DEPTH = 4


def build_fused(plan=("gla0", "ffn0", "pool", "ffn1", "diff", "ffn2", "gla1", "ffn3")):
    mp = Phase()
    m = mp
    nc, P = mp.nc, mp.P
    mp.btile = mp.sb("btile", [128, 1], F32)
    cache = {}

    def lz(name, shape):
        if name not in cache:
            cache[name] = mp.din(name, shape)
        return cache[name]

    def lzi(name, shape):
        if name not in cache:
            cache[name] = mp.dint(name, shape)
        return cache[name]

    xT_in = mp.din("xT", [D, TC])
    xT_out = mp.dout("xTo", [D, TC])

    def ngf(l, i):
        return lz("ng_%d_%d" % (l, i), [128, DC])

    def barrier():
        P.barrier(lambda e, bt=mp.btile: e.memset(bt[:], 0.0))

    def gather(src, dst, tag):
        P.cc(lambda e: e.collective_compute("AllGather", ALU.bypass, replica_groups=PAIRS, ins=[src], outs=[dst]),
             reads=[], writes=[("cc", tag)])
        barrier()

    def lzb(name, shape):
        if name not in cache:
            cache[name] = mp.dint(name, shape, BF16)
        return cache[name]

    def gla_layer(sl, layer, xin, xout):
        st_src = lzi("st_src", [1024, 512])
        st_all = lzi("st_all", [2048, 512])
        dr = dict(q=lzb("g_q", [1024, TC]), k=lzb("g_k", [1024, TC]), kdec=lzb("g_kd", [TC, 1024]), v=lzb("g_v", [TC, 2048]),
                  sr=lzb("g_sr", [TC, 2048]), elast=lzi("g_el", [128, 128]))
        common = dict(xT=xin, g=ngf(layer, 0), win=lz("gla%d_win" % sl, [D, GLA_NCOL]), wa2b=lz("gla%d_wa2b" % sl, [17, GLA_DKT]),
                      gnb=lz("gla%d_gnb" % sl, [128, 2048]), wout=lz("gla%d_wout" % sl, [D, D]), tri=lz("tri", [128, 128]),
                      ident=lz("ident", [128, 128]), isb=lz("isb", [128, 16]), dr=dr)
        b1 = dict(common)
        b1.update(st_in=None, st_out=None)
        build_gla_phase(Phase(mp, "g%dp_" % layer, b1), mode="proj")
        build_gla_scan0(Phase(mp, "g%ds_" % layer, dict(dr=dr, st_out=st_src.rearrange("(p c) v -> p c v", c=8))))
        gather(st_src[:, :], st_all[:, :], ("st", layer))
        b2 = dict(common)
        b2.update(st_in=st_all[0:1024, :].rearrange("(p c) v -> p c v", c=8), st_out=None, xTo=xout)
        build_gla_phase(Phase(mp, "g%df_" % layer, b2), mode="scanf")

    def ffn_layer(layer, xin, xout):
        build_ffn_phase(Phase(mp, "f%d_" % layer, dict(xT=xin, xTo=xout, g=ngf(layer, 1), wgu=lz("ffn%d_wgu" % layer, [D, 2 * FH]),
                                                      wdn=lz("ffn%d_wdn" % layer, [FH, D]))))

    def pool_layer(layer, xin, xout):
        halo_src = lzi("halo_src", [D, 16])
        halo_all = lzi("halo_all", [2 * D, 16])
        P.dma("sp", lambda e: e.dma_start(out=halo_src[:, :], in_=xin[:, TC - 16:TC]), writes=["halo_src"])
        barrier()
        gather(halo_src[:, :], halo_all[:, :], "halo")
        build_pool_phase(Phase(mp, "p1_", dict(xT=xin, halo=halo_all[0:D, :], isb=lz("isb", [128, 16]), g=ngf(layer, 0),
                                               wp=lz("pool_wp", [4, 512, 512]), psc=lz("pool_psc", [128, DC]),
                                               invc=lz("pool_invc", [128, 4, 16]), xTo=xout)))

    def diff_layer(layer, xin, xout):
        q_d = lzb("q_d", [D, TC])
        k_ds = [lzb("k_d%d" % i, [256, TC]) for i in range(8)]
        v_ds = [lzb("v_d%d" % i, [TC, 256]) for i in range(8)]
        k_alls = [lzb("k_all%d" % i, [512, TC]) for i in range(8)]
        v_alls = [lzb("v_all%d" % i, [2 * TC, 256]) for i in range(8)]
        build_diff1_phase(ph=Phase(mp, "d1_", dict(xT=xin, g=ngf(layer, 0), win=lz("diff_win", [D, 3 * D]),
                                                   qkg=lz("diff_qkg", [128, 16]), qkT=q_d, q_d=q_d, k_ds=k_ds, v_ds=v_ds, v=v_ds[0])))
        build_diff2_phase(Phase(mp, "d2_", dict(qT=q_d, kpres=[a_[0:256, :] for a_ in k_alls], kowns=k_ds,
                                                vpres=[a_[0:TC, :] for a_ in v_alls], vowns=v_ds, xT=xin,
                                                k_alls=k_alls, v_alls=v_alls,
                                                wout=lz("diff_wout", [D, D]), B0=lz("diff_B0", [128, 16, 128]),
                                                B1=lz("diff_B1", [128, 16, 128]), B1p=lz("diff_B1p", [128, 16, 128]),
                                                cfar=lz("diff_cfar", [128, 16]), cpre=lz("diff_cpre", [128, 16]),
                                                lpb=lz("diff_lpb", [128, 4, 128]), sgb=lz("diff_sgb", [128, 256]),
                                                ident=lz("ident", [128, 128]), xTo=xout)))

    bufs = [lzi("xa", [D, TC]), lzi("xb", [D, TC])]
    cur = xT_in
    for si, step in enumerate(plan):
        dst = xT_out if si == len(plan) - 1 else bufs[si % 2]
        layer = int(step[-1]) if step[:3] == "ffn" else {"gla0": 0, "pool": 1, "diff": 2, "gla1": 3}[step]
        if step[:3] == "ffn":
            ffn_layer(layer, cur, dst)
        elif step[:3] == "gla":
            gla_layer(int(step[3]), layer, cur, dst)
        elif step == "pool":
            pool_layer(layer, cur, dst)
        else:
            diff_layer(layer, cur, dst)
        cur = dst
    mp.input_names = [k for k in cache if not k.startswith("g_") and not k in ("xa", "xb", "st_src", "st_all", "halo_src", "halo_all", "q_d", "k_d", "v_d", "k_all", "v_all")]
    return mp.finish()


_FUSED = []


def kernel(x, norm_g, gla_w_in, gla_w_a2, gla_b_a, gla_g_norm, gla_w_out, pool_w, pool_scale, diff_w_in,
           diff_q_gain, diff_k_gain, diff_lambda, diff_sub_gain, diff_w_out, rel_bias, ffn_w_gu, ffn_w_down):
    x = np.asarray(x, np.float32)
    B, S, _ = x.shape
    if not _FUSED:
        _FUSED.append(build_fused())
    nc = _FUSED[0]
    f32c = lambda a: np.ascontiguousarray(np.asarray(a, np.float32))
    tri, ident = gla_consts()
    shared = {"tri": tri, "ident": ident}
    for l in range(DEPTH):
        for i in range(2):
            shared["ng_%d_%d" % (l, i)] = col16(norm_g[l, i])
        shared["ffn%d_wgu" % l] = f32c(ffn_w_gu[l])
        shared["ffn%d_wdn" % l] = f32c(ffn_w_down[l])
    for sl in range(2):
        shared["gla%d_win" % sl] = f32c(gla_w_in[sl])
        shared["gla%d_wa2b" % sl] = f32c(np.concatenate([np.asarray(gla_w_a2[sl], np.float32),
                                                         np.asarray(gla_b_a[sl], np.float32)[None, :]], axis=0))
        shared["gla%d_gnb" % sl] = f32c(np.broadcast_to(np.tile(np.asarray(gla_g_norm[sl], np.float32), 4)[None, :], (128, 2048)))
        shared["gla%d_wout" % sl] = f32c(gla_w_out[sl])
    shared["pool_wp"] = f32c(pool_w[0])
    shared["pool_psc"] = col16(pool_scale[0])
    shared["diff_win"] = f32c(diff_w_in[0])
    qkg = np.zeros((128, 16), np.float32)
    qkg[:, 0] = np.asarray(diff_q_gain[0], np.float32)
    qkg[:, 1] = np.asarray(diff_k_gain[0], np.float32)
    shared["diff_qkg"] = qkg
    shared["diff_wout"] = f32c(diff_w_out[0])
    shared["diff_lpb"] = f32c(np.broadcast_to(np.asarray(diff_lambda[0], np.float32)[None], (128, 4, 128)))
    shared["diff_sgb"] = f32c(np.broadcast_to(np.asarray(diff_sub_gain[0], np.float32)[None, :], (128, 256)))
    per_half = []
    for half in range(2):
        B0, B1, B1p, cfar, cpre = diff_bias_tiles(rel_bias, half == 0)
        invc = np.zeros((128, 4, 16), np.float32)
        for gi, w in enumerate(POOL_W):
            for t in range(16):
                invc[:, gi, t] = 1.0 / (min(t + 1, w) if half == 0 else w)
        per_half.append({"diff_B0": B0, "diff_B1": B1, "diff_B1p": B1p, "diff_cfar": cfar, "diff_cpre": cpre,
                         "pool_invc": invc, "isb": np.full((128, 16), float(half), np.float32)})
    in_maps = []
    for c in range(NCORES):
        im = dict(shared)
        im.update(per_half[c % 2])
        im["xT"] = np.ascontiguousarray(x[c // 2, (c % 2) * TC:(c % 2 + 1) * TC].T)
        in_maps.append(im)
    res = run_bass_kernel_spmd(nc, in_maps, core_ids=list(range(NCORES))).results
    out = np.empty((B, S, D), np.float32)
    for c in range(NCORES):
        out[c // 2, (c % 2) * TC:(c % 2 + 1) * TC] = res[c]["xTo"].T
    return out
```

```python
from contextlib import ExitStack
import math
import numpy as np
import concourse.bass as bass
import concourse.mybir as mybir
from concourse.bass_utils import run_bass_kernel_spmd

F32 = mybir.dt.float32
BF16 = mybir.dt.bfloat16
AF = mybir.ActivationFunctionType
ALU = mybir.AluOpType
AX = mybir.AxisListType

D = 2048
DC = D // 128
TC = 2048
FH = 5632
HC = FH // 128
EPS = 1e-6
NCORES = 8

ENGINES = ("pe", "act", "dve", "pool", "sp")


class Op:
    __slots__ = ("eng", "fn", "reads", "writes", "dma", "waits", "sig", "idx", "cc", "barrier")

    def __init__(self, eng, fn, reads, writes, dma):
        self.cc = False
        self.barrier = False
        self.eng = eng
        self.fn = fn
        self.reads = reads
        self.writes = writes
        self.dma = dma
        self.waits = []
        self.sig = None
        self.idx = -1


class Prog:
    NDMA = 32

    def __init__(self, nc):
        self.nc = nc
        self.ops = []

    LOOPVARS = frozenset("tt tsl j jsl h hs dc di dl blk c fc fi fb d dd ob vb rb i ei s sl mi grp b g w at atk kd kdk pq pk_ pv py_ ptr pkv cur oth sh a Wq Wk Wv Wr Wo wb db dp hf step q qk qb kb ki qi hp".split())

    def op(self, eng, fn, reads=(), writes=(), dma=False):
        bad = self.LOOPVARS.intersection(fn.__code__.co_freevars)
        if bad:
            raise RuntimeError("late-bound loop variable(s) %s in lambda at line %d" % (sorted(bad), fn.__code__.co_firstlineno))
        o = Op(eng, fn, tuple(reads), tuple(writes), dma)
        o.idx = len(self.ops)
        self.ops.append(o)
        return o

    def pe(self, fn, reads=(), writes=()):
        return self.op("pe", fn, reads, writes)

    def act(self, fn, reads=(), writes=()):
        return self.op("act", fn, reads, writes)

    def dve(self, fn, reads=(), writes=()):
        return self.op("dve", fn, reads, writes)

    def pool(self, fn, reads=(), writes=()):
        return self.op("pool", fn, reads, writes)

    def dma(self, eng, fn, reads=(), writes=()):
        return self.op(eng, fn, reads, writes, dma=True)

    def cc(self, fn, reads=(), writes=()):
        o = self.op("pool", fn, reads, writes, dma=True)
        o.cc = True
        return o

    def barrier(self, fn):
        o = self.op("dve", fn, (), ())
        o.barrier = True
        return o

    def finalize(self, final_wait_keys=()):
        ops = self.ops
        deps = [set() for _ in ops]
        last_writer = {}
        readers = {}
        since = []
        last_barrier = None
        for o in ops:
            if o.barrier:
                deps[o.idx] |= set(since)
                if last_barrier is not None:
                    deps[o.idx].add(last_barrier)
                since = []
                last_barrier = o.idx
                last_writer_final = dict(last_writer)
                last_writer = {}
                readers = {}
                continue
            since.append(o.idx)
            if last_barrier is not None:
                deps[o.idx].add(last_barrier)
            for k in o.reads:
                w = last_writer.get(k)
                if w is not None:
                    deps[o.idx].add(w.idx)
            for k in o.writes:
                w = last_writer.get(k)
                if w is not None:
                    deps[o.idx].add(w.idx)
                for r in readers.get(k, ()):
                    if r.idx != o.idx:
                        deps[o.idx].add(r.idx)
            for k in o.reads:
                readers.setdefault(k, []).append(o)
            for k in o.writes:
                last_writer[k] = o
                readers[k] = []
        final_ops = [last_writer[k].idx for k in final_wait_keys if k in last_writer]
        if last_barrier is not None:
            final_ops.append(last_barrier)
        needed = [set() for _ in ops]
        for o in ops:
            best = {}
            for d in deps[o.idx]:
                p = ops[d]
                if p.dma:
                    needed[o.idx].add(d)
                    continue
                if p.eng == "pe" and o.eng == "pe" and not o.dma:
                    continue
                if best.get(p.eng, -1) < d:
                    best[p.eng] = d
            needed[o.idx] |= set(best.values())
        signaled = set(final_ops)
        for o in ops:
            signaled |= needed[o.idx]
        eng_count = {e: 0 for e in ENGINES}
        NDMA = self.NDMA
        dma_count = [0] * NDMA
        dma_last = [None] * NDMA
        rr = 0
        rr_sw = 0
        cc_count = 0
        cc_last = None
        for o in ops:
            if o.dma and not o.cc:
                signaled.add(o.idx)
            if o.cc:
                signaled.add(o.idx)
                if cc_last is not None:
                    needed[o.idx].add(cc_last)
                cc_count += 1
                cc_last = o.idx
                o.sig = (("cc", 0), None, cc_count)
                continue
            if o.idx not in signaled:
                continue
            if o.dma:
                half = NDMA // 2
                if o.eng == "pool":
                    s = half + rr_sw % half
                    rr_sw += 1
                else:
                    s = rr % half
                    rr += 1
                if dma_last[s] is not None:
                    needed[o.idx].add(dma_last[s])
                dma_count[s] += 1
                dma_last[s] = o.idx
                o.sig = (("dma", s), 16, dma_count[s] * 16)
            else:
                eng_count[o.eng] += 1
                o.sig = (("eng", o.eng), 1, eng_count[o.eng])
        seen = {e: {} for e in ENGINES}
        for o in ops:
            ws = {}
            for d in needed[o.idx]:
                semkey, _, val = ops[d].sig
                if ws.get(semkey, 0) < val:
                    ws[semkey] = val
            for semkey, val in ws.items():
                if seen[o.eng].get(semkey, 0) >= val:
                    continue
                seen[o.eng][semkey] = val
                o.waits.append((semkey, val))
        fw = {}
        for d in final_ops:
            semkey, _, val = ops[d].sig
            fw[semkey] = max(fw.get(semkey, 0), val)
        self.final_waits = fw
        self.eng_count = eng_count

    def emit(self, block, sems):
        per_eng = {e: [] for e in ENGINES}
        for o in self.ops:
            per_eng[o.eng].append(o)
        final_waits = self.final_waits

        def run(engobj, lst, is_last):
            for o in lst:
                for semkey, val in o.waits:
                    engobj.wait_ge(sems[semkey], val)
                ins = o.fn(engobj)
                if o.sig is not None:
                    if o.sig[1] is None:
                        ins.then_inc(sems[o.sig[0]])
                    else:
                        ins.then_inc(sems[o.sig[0]], o.sig[1])
            if is_last:
                for semkey, val in final_waits.items():
                    engobj.wait_ge(sems[semkey], val)

        @block.tensor
        def _(e):
            run(e, per_eng["pe"], False)

        @block.scalar
        def _(e):
            run(e, per_eng["act"], False)

        @block.vector
        def _(e):
            run(e, per_eng["dve"], False)

        @block.gpsimd
        def _(e):
            run(e, per_eng["pool"], False)

        @block.sync
        def _(e):
            run(e, per_eng["sp"], True)


class Phase:
    def __init__(self, master=None, prefix="", bind=None):
        self.master = master
        self.prefix = prefix
        self.bind = bind or {}
        if master is None:
            self.nc = bass.Bass("TRN2", target_bir_lowering=False)
            self.P = Prog(self.nc)
            self.out_keys = []
        else:
            self.nc = master.nc
            self.P = master.P
            self.out_keys = master.out_keys
        self.es = ExitStack()

    def din(self, name, shape, dtype=F32):
        if name in self.bind:
            return self.bind[name]
        return self.nc.dram_tensor(self.prefix + name, list(shape), dtype, kind="ExternalInput").ap()

    def dout(self, name, shape, dtype=F32):
        if name in self.bind:
            return self.bind[name]
        return self.nc.dram_tensor(self.prefix + name, list(shape), dtype, kind="ExternalOutput").ap()

    def dint(self, name, shape, dtype=F32):
        return self.nc.dram_tensor(self.prefix + name, list(shape), dtype).ap()

    def sb(self, name, shape, dtype=F32):
        return self.es.enter_context(self.nc.sbuf_tensor(self.prefix + name, list(shape), dtype))

    def ps(self, name, shape, dtype=F32):
        return self.es.enter_context(self.nc.psum_tensor(self.prefix + name, list(shape), dtype))

    def finish(self):
        if self.master is not None:
            self.es.close()
            bt = self.master.btile
            self.P.barrier(lambda e: e.memset(bt[:], 0.0))
            return None
        P = self.P
        P.finalize(final_wait_keys=self.out_keys)
        sems = {}
        for e in ENGINES:
            sems[("eng", e)] = self.es.enter_context(self.nc.semaphore("s_" + e))
        for i in range(P.NDMA):
            sems[("dma", i)] = self.es.enter_context(self.nc.semaphore("d%d" % i))
        sems[("cc", 0)] = self.es.enter_context(self.nc.semaphore("s_cc"))
        block = self.es.enter_context(self.nc.Block())
        P.emit(block, sems)
        self.es.close()
        return self.nc


def emit_rmsnorm(ph, xT, hT, gcol, ones_bf, sq, pss, rstd, ntok, xkey, hkey, tag):
    P = ph.P
    nsub = ntok // 512
    for s in range(nsub):
        sl = slice(s * 512, (s + 1) * 512)
        pb = pss[s % 2]
        pk = "pss%d" % (s % 2)
        for c in range(DC):
            q = sq[c % 2]
            qk = "sq%d" % (c % 2)
            P.act(lambda e, q=q, c=c, sl=sl: e.activation(out=q[:], in_=xT[:, c, sl], func=AF.Square),
                  reads=[(xkey, c)], writes=[qk])
            P.pe(lambda e, q=q, c=c, pb=pb: e.matmul(pb[:], ones_bf[:], q[:], start=(c == 0), stop=(c == DC - 1)),
                 reads=[qk, "ones"], writes=[pk])
        rk = ("rstd", tag, s)
        P.dve(lambda e, pb=pb, sl=sl: e.tensor_scalar(out=rstd[:, sl], in0=pb[:], scalar1=1.0 / D, scalar2=EPS,
                                                      op0=ALU.mult, op1=ALU.add),
              reads=[pk], writes=[rk])
        P.act(lambda e, sl=sl: e.activation(out=rstd[:, sl], in_=rstd[:, sl], func=AF.Sqrt),
              reads=[rk], writes=[rk])
        P.dve(lambda e, sl=sl: e.reciprocal(out=rstd[:, sl], in_=rstd[:, sl]),
              reads=[rk], writes=[rk])
        for c in range(DC):
            P.dve(lambda e, c=c, sl=sl: e.scalar_tensor_tensor(out=hT[:, c, sl], in0=xT[:, c, sl],
                                                               scalar=gcol[:, c:c + 1], in1=rstd[:, sl],
                                                               op0=ALU.mult, op1=ALU.mult),
                  reads=[(xkey, c), rk, "gcol"], writes=[(hkey, c, s)])


def build_ffn_phase(ph=None):
    ph = ph or Phase()
    P = ph.P
    nc = ph.nc
    xin = ph.din("xT", [D, TC])
    g_in = ph.din("g", [128, DC])
    wgu = ph.din("wgu", [D, 2 * FH])
    wdn = ph.din("wdn", [FH, D])
    xout = ph.dout("xTo", [D, TC])
    NT = 1024
    NS = NT // 512
    HG = 11
    NG = HC // HG
    xT = ph.sb("xTs", [128, DC, NT], F32)
    hT = ph.sb("hTs", [128, DC, NT], BF16)
    aT = ph.sb("aTs", [128, HG, NT], BF16)
    wg = [ph.sb("wg%d" % i, [128, DC, 256], BF16) for i in range(3)]
    wd = [ph.sb("wd%d" % i, [128, HG, 256], BF16) for i in range(2)]
    sq = [ph.sb("sq%d" % i, [128, 512], BF16) for i in range(2)]
    sg = [ph.sb("sg%d" % i, [128, 512], F32) for i in range(2)]
    rstd = ph.sb("rstd", [128, NT], F32)
    gcol = ph.sb("gcol", [128, DC], F32)
    ones = ph.sb("ones", [128, 128], BF16)
    pss = [ph.ps("pss%d" % i, [128, 512]) for i in range(2)]
    pg = [ph.ps("pg%d" % i, [128, 512]) for i in range(2)]
    pu = [ph.ps("pu%d" % i, [128, 512]) for i in range(2)]
    py = [ph.ps("py%d" % i, [128, 512]) for i in range(2)]

    P.dma("sp", lambda e: e.dma_start(out=gcol[:], in_=g_in[:, :]), writes=["gcol"])
    P.dve(lambda e: e.memset(ones[:], 1.0), writes=["ones"])
    xin_v = xin.rearrange("(c p) t -> p c t", p=128)
    xout_v = xout.rearrange("(c p) t -> p c t", p=128)
    wgu_v = wgu.rearrange("(c p) n -> p c n", p=128)
    wdn_v = wdn.rearrange("(m p) n -> p m n", p=128)
    wgi = 0
    wdi = 0
    cnt = 0
    for tt in range(TC // NT):
        tsl = slice(tt * NT, (tt + 1) * NT)
        for c in range(DC):
            P.dma("sp",
                  lambda e, c=c, tsl=tsl: e.dma_start(out=xT[:, c, :], in_=xin_v[:, c, tsl]),
                  writes=[("x", c)])
        emit_rmsnorm(ph, xT, hT, gcol, ones, sq, pss, rstd, NT, "x", "h", tt)
        hkeys = [("h", c, s) for c in range(DC) for s in range(NS)]
        for grp in range(NG):
            for mi in range(HG):
                m = grp * HG + mi
                wb = wg[wgi % 3]
                wk = "wg%d" % (wgi % 3)
                wgi += 1
                P.dma("pool", lambda e, wb=wb, m=m: e.dma_start(out=wb[:, :, 0:128], in_=wgu_v[:, :, m * 128:(m + 1) * 128]),
                      writes=[wk])
                P.dma("pool", lambda e, wb=wb, m=m: e.dma_start(out=wb[:, :, 128:256],
                                                                 in_=wgu_v[:, :, FH + m * 128:FH + (m + 1) * 128]),
                      writes=[wk])
                for s in range(NS):
                    sl = slice(s * 512, (s + 1) * 512)
                    b = cnt % 2
                    cnt += 1
                    for c in range(DC):
                        P.pe(lambda e, wb=wb, c=c, sl=sl, b=b: e.matmul(pg[b][:], wb[:, c, 0:128], hT[:, c, sl],
                                                                        start=(c == 0), stop=(c == DC - 1)),
                             reads=[wk, ("h", c, s)], writes=["pg%d" % b])
                    for c in range(DC):
                        P.pe(lambda e, wb=wb, c=c, sl=sl, b=b: e.matmul(pu[b][:], wb[:, c, 128:256], hT[:, c, sl],
                                                                        start=(c == 0), stop=(c == DC - 1)),
                             reads=[wk, ("h", c, s)], writes=["pu%d" % b])
                    P.act(lambda e, b=b: e.activation(out=sg[b][:], in_=pg[b][:], func=AF.Silu),
                          reads=["pg%d" % b], writes=["sg%d" % b])
                    P.dve(lambda e, b=b, mi=mi, sl=sl: e.tensor_tensor(out=aT[:, mi, sl], in0=sg[b][:], in1=pu[b][:],
                                                                       op=ALU.mult),
                          reads=["sg%d" % b, "pu%d" % b], writes=[("a", mi, s)])
            for dp in range(DC // 2):
                db = wd[wdi % 2]
                dk = "wd%d" % (wdi % 2)
                wdi += 1
                P.dma("pool", lambda e, db=db, dp=dp, grp=grp: e.dma_start(
                    out=db[:], in_=wdn_v[:, grp * HG:(grp + 1) * HG, dp * 256:(dp + 1) * 256]), writes=[dk])
                for dd in range(2):
                    d = dp * 2 + dd
                    for s in range(NS):
                        sl = slice(s * 512, (s + 1) * 512)
                        b = cnt % 2
                        cnt += 1
                        for mi in range(HG):
                            P.pe(lambda e, db=db, mi=mi, dd=dd, sl=sl, b=b: e.matmul(
                                py[b][:], db[:, mi, dd * 128:(dd + 1) * 128], aT[:, mi, sl],
                                start=(mi == 0), stop=(mi == HG - 1)),
                                reads=[dk, ("a", mi, s)], writes=["py%d" % b])
                        P.dve(lambda e, d=d, sl=sl, b=b: e.tensor_tensor(out=xT[:, d, sl], in0=xT[:, d, sl], in1=py[b][:],
                                                                         op=ALU.add),
                              reads=["py%d" % b, ("x", d)], writes=[("x", d)])
        for c in range(DC):
            P.dma("sp",
                  lambda e, c=c, tsl=tsl: e.dma_start(out=xout_v[:, c, tsl], in_=xT[:, c, :]),
                  reads=[("x", c)], writes=[("xo", tt, c)])
            ph.out_keys.append(("xo", tt, c))
    return ph.finish()


def emit_rmsnorm_cols(ph, xT, xoff, hT, hoff, ncols, gcol, ones_bf, sq, pb, pk, rstd, xkeys, hkeys, tag):
    P = ph.P
    xsl_ = slice(xoff, xoff + ncols)
    hsl_ = slice(hoff, hoff + ncols)
    for c in range(DC):
        q = sq[c % 2]
        qk = "sq%d" % (c % 2)
        P.act(lambda e, q=q, c=c: e.activation(out=q[:, 0:ncols], in_=xT[:, c, xsl_], func=AF.Square),
              reads=[xkeys(c)], writes=[qk])
        P.pe(lambda e, q=q, c=c: e.matmul(pb[:, 0:ncols], ones_bf[:], q[:, 0:ncols], start=(c == 0), stop=(c == DC - 1)),
             reads=[qk, "ones"], writes=[pk])
    rk = ("rstd", tag)
    P.dve(lambda e: e.tensor_scalar(out=rstd[:, 0:ncols], in0=pb[:, 0:ncols], scalar1=1.0 / D, scalar2=EPS,
                                    op0=ALU.mult, op1=ALU.add), reads=[pk], writes=[rk])
    P.act(lambda e: e.activation(out=rstd[:, 0:ncols], in_=rstd[:, 0:ncols], func=AF.Sqrt), reads=[rk], writes=[rk])
    P.dve(lambda e: e.reciprocal(out=rstd[:, 0:ncols], in_=rstd[:, 0:ncols]), reads=[rk], writes=[rk])
    for c in range(DC):
        P.dve(lambda e, c=c: e.scalar_tensor_tensor(out=hT[:, c, hsl_], in0=xT[:, c, xsl_], scalar=gcol[:, c:c + 1],
                                                    in1=rstd[:, 0:ncols], op0=ALU.mult, op1=ALU.mult),
              reads=[xkeys(c), rk, "gcol"], writes=[hkeys(c)])


POOL_W = (2, 4, 8, 16)


def build_pool_phase(ph=None):
    ph = ph or Phase()
    P = ph.P
    fused = ph.master is not None
    xin = None if fused else ph.din("xTe", [D, 16 + TC])
    g_in = ph.din("g", [128, DC])
    wp_in = ph.din("wp", [4, 512, 512])
    sc_in = ph.din("psc", [128, DC])
    ic_in = ph.din("invc", [128, 4, 16])
    xout = ph.dout("xTo", [D, TC])
    NT = 512
    xT = ph.sb("xTs", [128, DC, NT], F32)
    xh = ph.sb("xh", [128, DC, 16], F32)
    hx = ph.sb("hx", [128, DC, 16 + NT], F32)
    sA = [ph.sb("sA%d" % i, [128, 16 + NT], F32) for i in range(2)]
    sB = [ph.sb("sB%d" % i, [128, 16 + NT], F32) for i in range(2)]
    yT = ph.sb("yT", [128, DC, NT], BF16)
    wp = ph.sb("wps", [128, 4, 4, 512], BF16)
    sq = [ph.sb("sq%d" % i, [128, 512], BF16) for i in range(2)]
    rstd = ph.sb("rstd", [128, 512], F32)
    gcol = ph.sb("gcol", [128, DC], F32)
    psc = ph.sb("pscs", [128, DC], F32)
    invc = ph.sb("invcs", [128, 4, 16], F32)
    ones = ph.sb("ones", [128, 128], BF16)
    pss = ph.ps("pss", [128, 512])
    pz = [ph.ps("pz%d" % i, [128, 512]) for i in range(2)]

    P.dma("sp", lambda e: e.dma_start(out=gcol[:], in_=g_in[:, :]), writes=["gcol"])
    P.dma("sp", lambda e: e.dma_start(out=psc[:], in_=sc_in[:, :]), writes=["psc"])
    P.dma("sp", lambda e: e.dma_start(out=invc[:], in_=ic_in[:, :, :]), writes=["invc"])
    P.dve(lambda e: e.memset(ones[:], 1.0), writes=["ones"])
    wp_v = wp_in.rearrange("g (ci p) n -> p g ci n", p=128)
    for g in range(4):
        P.dma("pool", lambda e, g=g: e.dma_start(out=wp[:, g, :, :], in_=wp_v[:, g, :, :]), writes=["wp"])
    if fused:
        xmain_v = ph.bind["xT"].rearrange("(c p) t -> p c t", p=128)
        halo_v = ph.bind["halo"].rearrange("(c p) t -> p c t", p=128)
        isb = ph.sb("isb", [128, 16], F32)
        P.dma("sp", lambda e: e.dma_start(out=isb[:], in_=ph.bind["isb"][:, :]), writes=["isb"])
        P.dma("sp", lambda e: e.dma_start(out=xh[:], in_=halo_v[:, :, :]), writes=["xh"])
        P.dve(lambda e: e.tensor_scalar(out=xh[:], in0=xh[:], scalar1=isb[:, 0:1], scalar2=None, op0=ALU.mult),
              reads=["xh", "isb"], writes=["xh"])
        OFF = 0
    else:
        xin_v = xin.rearrange("(c p) t -> p c t", p=128)
        xmain_v = xin_v
        OFF = 16
        P.dma("sp", lambda e: e.dma_start(out=xh[:], in_=xin_v[:, :, 0:16]), writes=["xh"])
    xout_v = xout.rearrange("(c p) t -> p c t", p=128)
    emit_rmsnorm_cols(ph, xh, 0, hx, 0, 16, gcol, ones, sq, pss, "pss", rstd,
                      lambda c: "xh", lambda c: ("hx", c), "halo")
    cnt = 0
    for tt in range(TC // NT):
        for c in range(DC):
            P.dma("sp", lambda e, c=c, tt=tt: e.dma_start(out=xT[:, c, :], in_=xmain_v[:, c, OFF + tt * NT:OFF + (tt + 1) * NT]),
                  writes=[("x", c)])
        emit_rmsnorm_cols(ph, xT, 0, hx, 16, NT, gcol, ones, sq, pss, "pss", rstd,
                          lambda c: ("x", c), lambda c: ("hx", c), ("t", tt))
        W = 16 + NT
        for c in range(DC):
            g = c // 4
            eng = P.dve if c % 2 == 0 else P.pool
            a = sA[c % 2]
            b = sB[c % 2]
            ak = "sA%d" % (c % 2)
            bk = "sB%d" % (c % 2)
            eng(lambda e, a=a, c=c: e.tensor_tensor(out=a[:, 1:W], in0=hx[:, c, 1:W], in1=hx[:, c, 0:W - 1], op=ALU.add),
                reads=[("hx", c)], writes=[ak])
            cur, curk, oth, othk = a, ak, b, bk
            sh = 2
            for step in range(g):
                eng(lambda e, cur=cur, oth=oth, sh=sh: e.tensor_tensor(out=oth[:, 1 + sh:W], in0=cur[:, 1 + sh:W],
                                                                      in1=cur[:, 1:W - sh], op=ALU.add),
                    reads=[curk], writes=[othk])
                cur, curk, oth, othk = oth, othk, cur, curk
                sh *= 2
            w = POOL_W[g]
            P.dve(lambda e, cur=cur, c=c, w=w: e.scalar_tensor_tensor(out=yT[:, c, :], in0=cur[:, 16:W], scalar=1.0 / w,
                                                                    in1=hx[:, c, 16:W], op0=ALU.mult, op1=ALU.subtract),
                reads=[curk, ("hx", c)], writes=[("y", c)])
            if tt == 0:
                eng(lambda e, cur=cur, g=g: e.tensor_tensor(out=cur[:, 16:32], in0=cur[:, 16:32], in1=invc[:, g, :], op=ALU.mult),
                    reads=[curk, "invc", ("y", c)], writes=[curk])
                eng(lambda e, cur=cur, c=c: e.tensor_tensor(out=yT[:, c, 0:16], in0=cur[:, 16:32], in1=hx[:, c, 16:32],
                                                            op=ALU.subtract),
                    reads=[curk, ("hx", c)], writes=[("y", c)])
            if tt + 1 < TC // NT:
                eng(lambda e, c=c: e.tensor_copy(out=hx[:, c, 0:16], in_=hx[:, c, NT:NT + 16]),
                    reads=[("y", c), curk, ak, bk], writes=[("hx", c)])
        for d in range(DC):
            g = d // 4
            b = cnt % 2
            cnt += 1
            for ci in range(4):
                P.pe(lambda e, g=g, ci=ci, d=d, b=b: e.matmul(pz[b][:], wp[:, g, ci, (d % 4) * 128:(d % 4 + 1) * 128],
                                                             yT[:, 4 * g + ci, :], start=(ci == 0), stop=(ci == 3)),
                     reads=["wp", ("y", 4 * g + ci)], writes=["pz%d" % b])
            P.dve(lambda e, d=d, b=b: e.scalar_tensor_tensor(out=xT[:, d, :], in0=pz[b][:], scalar=psc[:, d:d + 1],
                                                            in1=xT[:, d, :], op0=ALU.mult, op1=ALU.add),
                  reads=["pz%d" % b, "psc", ("x", d)], writes=[("x", d)])
        for c in range(DC):
            P.dma("sp", lambda e, c=c, tt=tt: e.dma_start(out=xout_v[:, c, tt * NT:(tt + 1) * NT], in_=xT[:, c, :]),
                  reads=[("x", c)], writes=[("xo", tt, c)])
            ph.out_keys.append(("xo", tt, c))
    return ph.finish()


def col16(v):
    return np.ascontiguousarray(np.asarray(v, np.float32).reshape(DC, 128).T)


def pool_inputs(x_seq, half, g, wp, psc):
    xe = np.zeros((D, 16 + TC), np.float32)
    t0 = half * TC
    xe[:, 16:] = x_seq[t0:t0 + TC].T
    if half == 1:
        xe[:, :16] = x_seq[t0 - 16:t0].T
    invc = np.zeros((128, 4, 16), np.float32)
    for gi, w in enumerate(POOL_W):
        for t in range(16):
            cnt = min(t + 1, w) if half == 0 else w
            invc[:, gi, t] = 1.0 / cnt
    return {"xTe": xe, "g": col16(g), "wp": np.ascontiguousarray(wp, dtype=np.float32), "psc": col16(psc), "invc": invc}


GLA_DKT = 1024
GLA_NCOL = 6160


def build_gla_phase(ph=None, state_only=False, mode="full"):
    ph = ph or Phase()
    P = ph.P
    fused = ph.master is not None
    xin = ph.din("xT", [D, TC])
    g_in = ph.din("g", [128, DC])
    win = ph.din("win", [D, GLA_NCOL])
    wa2b_in = ph.din("wa2b", [17, GLA_DKT])
    gn_in = ph.din("gnb", [128, 2048])
    wout = ph.din("wout", [D, D])
    st_in = ph.bind.get("st_in") if fused else ph.din("st_in", [128, 8, 512])
    tri_in = ph.din("tri", [128, 128])
    id_in = ph.din("ident", [128, 128])
    xout = None if (state_only or mode == "proj") else ph.dout("xTo", [D, TC])
    st_out = ph.bind.get("st_out") if fused else ph.dout("st_out", [128, 8, 512])
    NT = 512
    NJ = 4
    xs = [ph.sb("xs%d" % i, [128, 512], F32) for i in range(3)]
    hT = ph.sb("hTs", [128, DC, NT], BF16)
    Wb = [ph.sb("Wb%d" % i, [128, DC, 512], BF16) for i in range(2)]
    Wa = ph.sb("Wa", [128, DC, 16], BF16)
    qT = ph.sb("qT", [128, 8, NT], BF16)
    kT = ph.sb("kT", [128, 8, NT], BF16)
    kdec = ph.sb("kdec", [128, NJ, 1024], BF16)
    kd_s = [ph.sb("kds%d" % i, [128, 512], BF16) for i in range(2)]
    vt = ph.sb("vt", [128, NJ, 2048], BF16)
    sr = ph.sb("sr", [128, NJ, 2048], BF16)
    gated2 = [ph.sb("gated%d" % i, [128, 2048], BF16) for i in range(2)]
    gT = ph.sb("gT", [128, DC, NT], BF16)
    S = ph.sb("S", [128, 8, 512], F32)
    Sb = ph.sb("Sb", [128, 8, 512], BF16)
    lt = ph.sb("lt", [128, NJ, 1024], F32)
    e1 = ph.sb("e1", [128, 1024], F32)
    Eq = [ph.sb("Eq%d" % i, [128, 512], F32) for i in range(1)]
    Ek = [ph.sb("Ek%d" % i, [128, 512], F32) for i in range(1)]
    Elast = ph.sb("Elast", [128, 8, TC // 128], F32)
    dr = ph.bind.get("dr")
    alr1 = ph.sb("alr1", [32, NT], F32)
    wa2b = ph.sb("wa2bs", [32, GLA_DKT], F32)
    gnb = ph.sb("gnbs", [128, 2048], BF16)
    tri = ph.sb("tris", [128, 128], F32)
    ident = ph.sb("idents", [128, 128], BF16)
    AT4 = [ph.sb("AT4%d" % i, [128, 128], BF16) for i in range(4)]
    osq = ph.sb("osq", [128, 2048], BF16)
    ssq = ph.sb("ssq", [128, 4], F32)
    sq = [ph.sb("sq%d" % i, [128, 512], BF16) for i in range(2)]
    rstd = ph.sb("rstd", [128, 512], F32)
    gcol = ph.sb("gcol", [128, DC], F32)
    ones = ph.sb("ones", [128, 128], BF16)
    pb = [ph.ps("pb%d" % i, [128, 512]) for i in range(8)]
    pbk = ["pb%d" % i for i in range(8)]

    P.dma("sp", lambda e: e.dma_start(out=gcol[:], in_=g_in[:, :]), writes=["gcol"])
    P.dma("sp", lambda e: e.dma_start(out=tri[:], in_=tri_in[:, :]), writes=["tri"])
    P.dma("sp", lambda e: e.dma_start(out=wa2b[0:17, :], in_=wa2b_in[:, :]), writes=["wa2b"])
    if st_in is None:
        P.dve(lambda e: e.memset(S[:], 0.0), writes=[("S", dc) for dc in range(8)])
    else:
        P.dma("sp", lambda e: e.dma_start(out=S[:], in_=st_in[:, :, :]), writes=[("S", dc) for dc in range(8)])
        if fused:
            isb = ph.sb("isb", [128, 16], F32)
            P.dma("sp", lambda e: e.dma_start(out=isb[:], in_=ph.bind["isb"][:, :]), writes=["isb"])
            P.dve(lambda e: e.tensor_scalar(out=S[:], in0=S[:], scalar1=isb[:, 0:1], scalar2=None, op0=ALU.mult),
                  reads=[("S", dc) for dc in range(8)] + ["isb"], writes=[("S", dc) for dc in range(8)])
    P.dma("pool", lambda e: e.dma_start(out=ident[:], in_=id_in[:, :]), writes=["ident"])
    P.dma("pool", lambda e: e.dma_start(out=gnb[:], in_=gn_in[:, :]), writes=["gnb"])
    P.dve(lambda e: e.memset(ones[:], 1.0), writes=["ones"])
    P.dve(lambda e: e.memset(alr1[:], 1.0), writes=["alr1"])
    P.act(lambda e: e.copy(out=Sb[:], in_=S[:]), reads=[("S", dc) for dc in range(8)], writes=[("Sb", dc) for dc in range(8)])
    win_v = win.rearrange("(c p) n -> p c n", p=128)
    wout_v = wout.rearrange("(c p) n -> p c n", p=128)
    xin_v = xin.rearrange("(c p) t -> p c t", p=128)
    xout_v = None if xout is None else xout.rearrange("(c p) t -> p c t", p=128)
    P.dma("pool", lambda e: e.dma_start(out=Wa[:], in_=win_v[:, :, 6144:6160]), writes=["Wa"])

    if mode == "scanf":
        P.dma("sp", lambda e: e.dma_start(out=Elast[:], in_=dr["elast"].rearrange("p (dc jj) -> p dc jj", dc=8)),
              writes=[("Elast", dc) for dc in range(8)])
    wctr = [0]

    def load_w(src_v, col0):
        i = wctr[0] % 2
        wctr[0] += 1
        P.dma("pool", lambda e, i=i: e.dma_start(out=Wb[i][:], in_=src_v[:, :, col0:col0 + 512]), writes=["Wb%d" % i])
        return Wb[i], "Wb%d" % i

    pctr = [0]

    def next_pb(lo=4, n=4):
        i = lo + pctr[0] % n
        pctr[0] += 1
        return pb[i], pbk[i]

    xctr = [0]
    for tt in range(TC // NT):
        tsl = slice(tt * NT, (tt + 1) * NT)
        if mode == "scanf":
            P.dma("sp", lambda e, tsl=tsl: e.dma_start(out=qT[:], in_=dr["q"].rearrange("(dc p) t -> p dc t", p=128)[:, :, tsl]),
                  writes=[("qT", dc) for dc in range(8)])
            P.dma("sp", lambda e, tsl=tsl: e.dma_start(out=kT[:], in_=dr["k"].rearrange("(dc p) t -> p dc t", p=128)[:, :, tsl]),
                  writes=[("kT", dc) for dc in range(8)])
            P.dma("sp", lambda e, tt=tt: e.dma_start(out=kdec[:], in_=dr["kdec"].rearrange("(jj p) d -> p jj d", p=128)[:, tt * NJ:(tt + 1) * NJ, :]),
                  writes=[("kdec", dc) for dc in range(8)])
            P.dma("sp", lambda e, tt=tt: e.dma_start(out=vt[:], in_=dr["v"].rearrange("(jj p) n -> p jj n", p=128)[:, tt * NJ:(tt + 1) * NJ, :]),
                  writes=[("vt", j, h) for j in range(NJ) for h in range(4)])
            P.dma("sp", lambda e, tt=tt: e.dma_start(out=sr[:], in_=dr["sr"].rearrange("(jj p) n -> p jj n", p=128)[:, tt * NJ:(tt + 1) * NJ, :]),
                  writes=[("sr", j) for j in range(NJ)])
        else:
            p_ss, p_ssk = pb[0], pbk[0]
            for c in range(DC):
                i = xctr[0] % 3
                xctr[0] += 1
                P.dma("sp", lambda e, i=i, c=c, tsl=tsl: e.dma_start(out=xs[i][:], in_=xin_v[:, c, tsl]), writes=["xs%d" % i])
                q = sq[c % 2]
                qk = "sq%d" % (c % 2)
                P.act(lambda e, q=q, i=i: e.activation(out=q[:], in_=xs[i][:], func=AF.Square), reads=["xs%d" % i], writes=[qk])
                P.pe(lambda e, q=q, c=c: e.matmul(p_ss[:], ones[:], q[:], start=(c == 0), stop=(c == DC - 1)),
                     reads=[qk, "ones"], writes=[p_ssk])
            P.dve(lambda e: e.tensor_scalar(out=rstd[:], in0=p_ss[:], scalar1=1.0 / D, scalar2=EPS, op0=ALU.mult, op1=ALU.add),
                  reads=[p_ssk], writes=["rstd"])
            P.act(lambda e: e.activation(out=rstd[:], in_=rstd[:], func=AF.Sqrt), reads=["rstd"], writes=["rstd"])
            P.dve(lambda e: e.reciprocal(out=rstd[:], in_=rstd[:]), reads=["rstd"], writes=["rstd"])
            for c in range(DC):
                i = xctr[0] % 3
                xctr[0] += 1
                P.dma("sp", lambda e, i=i, c=c, tsl=tsl: e.dma_start(out=xs[i][:], in_=xin_v[:, c, tsl]), writes=["xs%d" % i])
                P.dve(lambda e, i=i, c=c: e.scalar_tensor_tensor(out=hT[:, c, :], in0=xs[i][:], scalar=gcol[:, c:c + 1],
                                                                 in1=rstd[:], op0=ALU.mult, op1=ALU.mult),
                      reads=["xs%d" % i, "rstd", "gcol"], writes=[("h", c)])
            hk = [("h", c) for c in range(DC)]
            pa, pak = pb[1], pbk[1]
            for c in range(DC):
                P.pe(lambda e, c=c: e.matmul(pa[0:16, :], Wa[:, c, :], hT[:, c, :], start=(c == 0), stop=(c == DC - 1)),
                     reads=["Wa", ("h", c)], writes=[pak])
            P.act(lambda e: e.copy(out=alr1[0:16, :], in_=pa[0:16, :]), reads=[pak], writes=["alr1"])
            for j in range(NJ):
                jsl = slice(j * 128, (j + 1) * 128)
                for hf in range(2):
                    P.pe(lambda e, jsl=jsl, hf=hf: e.matmul(pb[2 + hf][:], alr1[0:17, jsl], wa2b[0:17, hf * 512:(hf + 1) * 512],
                                                            start=True, stop=True),
                         reads=["alr1", "wa2b"], writes=[pbk[2 + hf]])
                    P.act(lambda e, hf=hf: e.activation(out=e1[:, hf * 512:(hf + 1) * 512], in_=pb[2 + hf][:], func=AF.Exp, scale=-1.0),
                          reads=[pbk[2 + hf]], writes=[("e1", hf)])
                    P.act(lambda e, hf=hf, j=j: e.activation(out=lt[:, j, hf * 512:(hf + 1) * 512], in_=e1[:, hf * 512:(hf + 1) * 512],
                                                             func=AF.Ln, bias=1.0),
                          reads=[("e1", hf)], writes=[("lt", j)])
            pend_tr = []
            for blk in range(2):
                if not state_only:
                    Wq, Wqk = load_w(win_v, blk * 512)
                Wk, Wkk = load_w(win_v, 1024 + blk * 512)
                for dl in range(4):
                    dc = blk * 4 + dl
                    pbt, pbtk = pb[0], pbk[0]
                    for j in range(NJ):
                        jsl = slice(j * 128, (j + 1) * 128)
                        P.pe(lambda e, j=j, jsl=jsl, dc=dc: e.matmul(pbt[:, jsl], lt[:, j, dc * 128:(dc + 1) * 128], tri[:],
                                                                     start=True, stop=True),
                             reads=[("lt", j), "tri"], writes=[pbtk])
                    ei = 0
                    P.act(lambda e, ei=ei: e.activation(out=Eq[ei][:], in_=pbt[:], func=AF.Exp, scale=-1.0 / 16.0),
                          reads=[pbtk], writes=["Eq%d" % ei])
                    P.act(lambda e, ei=ei: e.activation(out=Ek[ei][:], in_=pbt[:], func=AF.Exp, scale=1.0 / 16.0),
                          reads=[pbtk], writes=["Ek%d" % ei])
                    P.dve(lambda e, ei=ei, dc=dc, tt=tt: e.tensor_copy(out=Elast[:, dc, tt * NJ:(tt + 1) * NJ], in_=Eq[ei][:, 127::128]),
                          reads=["Eq%d" % ei], writes=[("Elast", dc)])
                    if not state_only:
                        pq, pqk = next_pb()
                        for c in range(DC):
                            P.pe(lambda e, c=c, dl=dl, pq=pq, Wq=Wq: e.matmul(pq[:], Wq[:, c, dl * 128:(dl + 1) * 128], hT[:, c, :],
                                                                       start=(c == 0), stop=(c == DC - 1)),
                                 reads=[Wqk, ("h", c)], writes=[pqk])
                        P.dve(lambda e, pq=pq, ei=ei, dc=dc: e.scalar_tensor_tensor(out=qT[:, dc, :], in0=pq[:], scalar=1.0 / 16.0,
                                                                                    in1=Eq[ei][:], op0=ALU.mult, op1=ALU.mult),
                              reads=[pqk, "Eq%d" % ei], writes=[("qT", dc)])
                    pk_, pkk = next_pb()
                    for c in range(DC):
                        P.pe(lambda e, c=c, dl=dl, pk_=pk_, Wk=Wk: e.matmul(pk_[:], Wk[:, c, dl * 128:(dl + 1) * 128], hT[:, c, :],
                                                                     start=(c == 0), stop=(c == DC - 1)),
                             reads=[Wkk, ("h", c)], writes=[pkk])
                    P.dve(lambda e, pk_=pk_, ei=ei, dc=dc: e.tensor_tensor(out=kT[:, dc, :], in0=pk_[:], in1=Ek[ei][:], op=ALU.mult),
                          reads=[pkk, "Ek%d" % ei], writes=[("kT", dc)])
                    kd = kd_s[dc % 2]
                    kdk = "kds%d" % (dc % 2)
                    for j in range(NJ):
                        jsl = slice(j * 128, (j + 1) * 128)
                        P.dve(lambda e, kd=kd, dc=dc, j=j, jsl=jsl, tt=tt: e.tensor_scalar(out=kd[:, jsl], in0=kT[:, dc, jsl],
                                                                                    scalar1=Elast[:, dc, tt * NJ + j:tt * NJ + j + 1], scalar2=None,
                                                                                    op0=ALU.mult),
                              reads=[("kT", dc), ("Elast", dc)], writes=[kdk])
                    def emit_tr(kd=kd, kdk=kdk, dc=dc):
                        ptr, ptrk = next_pb()
                        for j2 in range(NJ):
                            jsl2 = slice(j2 * 128, (j2 + 1) * 128)
                            P.pe(lambda e, kd=kd, jsl2=jsl2, ptr=ptr: e.transpose(pbf(ptr)[:, jsl2], kd[:, jsl2], ident[:]),
                                 reads=[kdk, "ident"], writes=[ptrk])
                        P.act(lambda e, ptr=ptr, dc=dc: e.copy(out=kdec[:, :, dc * 128:(dc + 1) * 128],
                                                               in_=pbf(ptr)[:, 0:512].rearrange("p (j d) -> p j d", j=NJ)),
                              reads=[ptrk], writes=[("kdec", dc)])
                    pend_tr.append(emit_tr)
                    if len(pend_tr) > 1:
                        pend_tr.pop(0)()
            while pend_tr:
                pend_tr.pop(0)()
            for vb in range(4):
                Wv, Wvk = load_w(win_v, 2048 + vb * 512)
                for j in range(NJ):
                    jsl = slice(j * 128, (j + 1) * 128)
                    pv, pvk = next_pb()
                    for c in range(DC):
                        P.pe(lambda e, c=c, jsl=jsl, pv=pv, Wv=Wv: e.matmul(pv[:], hT[:, c, jsl], Wv[:, c, :],
                                                                            start=(c == 0), stop=(c == DC - 1)),
                             reads=[Wvk, ("h", c)], writes=[pvk])
                    P.act(lambda e, pv=pv, j=j, vb=vb: e.copy(out=vt[:, j, vb * 512:(vb + 1) * 512], in_=pv[:]),
                          reads=[pvk], writes=[("vt", j, vb)])
            for rb in (range(4) if not state_only else ()):
                Wr, Wrk = load_w(win_v, 4096 + rb * 512)
                for j in range(NJ):
                    jsl = slice(j * 128, (j + 1) * 128)
                    pv, pvk = next_pb()
                    for c in range(DC):
                        P.pe(lambda e, c=c, jsl=jsl, pv=pv, Wr=Wr: e.matmul(pv[:], hT[:, c, jsl], Wr[:, c, :],
                                                                            start=(c == 0), stop=(c == DC - 1)),
                             reads=[Wrk, ("h", c)], writes=[pvk])
                    P.act(lambda e, pv=pv, j=j, rb=rb: e.activation(out=sr[:, j, rb * 512:(rb + 1) * 512], in_=pv[:], func=AF.Silu),
                          reads=[pvk], writes=[("sr", j)])
        if mode == "proj":
            for j in range(NJ):
                P.pool(lambda e, j=j: e.tensor_tensor(out=sr[:, j, :], in0=sr[:, j, :], in1=gnb[:], op=ALU.mult),
                       reads=[("sr", j), "gnb"], writes=[("sr", j)])
            P.dma("sp", lambda e, tsl=tsl: e.dma_start(out=dr["q"].rearrange("(dc p) t -> p dc t", p=128)[:, :, tsl], in_=qT[:]),
                  reads=[("qT", dc) for dc in range(8)], writes=[("dq", tt)])
            P.dma("sp", lambda e, tsl=tsl: e.dma_start(out=dr["k"].rearrange("(dc p) t -> p dc t", p=128)[:, :, tsl], in_=kT[:]),
                  reads=[("kT", dc) for dc in range(8)], writes=[("dk", tt)])
            P.dma("sp", lambda e, tt=tt: e.dma_start(out=dr["kdec"].rearrange("(jj p) d -> p jj d", p=128)[:, tt * NJ:(tt + 1) * NJ, :], in_=kdec[:]),
                  reads=[("kdec", dc) for dc in range(8)], writes=[("dkd", tt)])
            P.dma("sp", lambda e, tt=tt: e.dma_start(out=dr["v"].rearrange("(jj p) n -> p jj n", p=128)[:, tt * NJ:(tt + 1) * NJ, :], in_=vt[:]),
                  reads=[("vt", j, h) for j in range(NJ) for h in range(4)], writes=[("dv", tt)])
            P.dma("sp", lambda e, tt=tt: e.dma_start(out=dr["sr"].rearrange("(jj p) n -> p jj n", p=128)[:, tt * NJ:(tt + 1) * NJ, :], in_=sr[:]),
                  reads=[("sr", j) for j in range(NJ)], writes=[("dsr", tt)])
            continue
        pend_g = []
        for j in range(NJ):
            jsl = slice(j * 128, (j + 1) * 128)
            gt_ = gated2[j % 2]
            gk_ = "gated%d" % (j % 2)
            if not state_only:
                if mode != "scanf":
                    P.pool(lambda e, j=j: e.tensor_tensor(out=sr[:, j, :], in0=sr[:, j, :], in1=gnb[:], op=ALU.mult),
                           reads=[("sr", j), "gnb"], writes=[("sr", j)])
                pst, pstk = pb[4], pbk[4]
                for h in range(4):
                    hs = slice(h * 128, (h + 1) * 128)
                    for di in range(2):
                        dc = 2 * h + di
                        P.pe(lambda e, dc=dc, di=di, hs=hs, jsl=jsl: e.matmul(pst[:, hs], kT[:, dc, jsl], qT[:, dc, jsl],
                                                                              start=(di == 0), stop=(di == 1)),
                             reads=[("kT", dc), ("qT", dc)], writes=[pstk])
                for h in range(4):
                    hs = slice(h * 128, (h + 1) * 128)
                    P.dve(lambda e, h=h, hs=hs: e.tensor_tensor(out=AT4[h][:], in0=pst[:, hs], in1=tri[:], op=ALU.mult),
                          reads=[pstk, "tri"], writes=["AT4%d" % h])
            for h in range(4):
                vkeys = [("vt", j, h)]
                for di in range(2):
                    dc = 2 * h + di
                    pkv, pkvk = pb[5 + di], pbk[5 + di]
                    P.pe(lambda e, dc=dc, j=j, h=h, pkv=pkv: e.matmul(pkv[:], kdec[:, j, dc * 128:(dc + 1) * 128],
                                                                      vt[:, j, h * 512:(h + 1) * 512], start=True, stop=True),
                         reads=[("kdec", dc)] + vkeys, writes=[pkvk])
                    P.dve(lambda e, dc=dc, j=j, pkv=pkv, tt=tt: e.scalar_tensor_tensor(out=S[:, dc, :], in0=S[:, dc, :],
                                                                                scalar=Elast[:, dc, tt * NJ + j:tt * NJ + j + 1], in1=pkv[:],
                                                                                op0=ALU.mult, op1=ALU.add),
                          reads=[pkvk, ("Elast", dc), ("S", dc)], writes=[("S", dc)])
                if not state_only:
                    for di in range(2):
                        dc = 2 * h + di
                        P.pe(lambda e, dc=dc, di=di, h=h, jsl=jsl: e.matmul(pb[h][:], qT[:, dc, jsl], Sb[:, dc, :],
                                                                            start=(di == 0), stop=False),
                             reads=[("qT", dc), ("Sb", dc)], writes=[pbk[h]])
                    P.pe(lambda e, h=h, j=j: e.matmul(pb[h][:], AT4[h][:], vt[:, j, h * 512:(h + 1) * 512], start=False, stop=True),
                         reads=["AT4%d" % h] + vkeys, writes=[pbk[h]])
                    for di in range(2):
                        dc = 2 * h + di
                        P.act(lambda e, dc=dc: e.copy(out=Sb[:, dc, :], in_=S[:, dc, :]), reads=[("S", dc)], writes=[("Sb", dc)])
            if state_only:
                continue
            for h in range(4):
                P.act(lambda e, h=h: e.activation(out=osq[:, h * 512:(h + 1) * 512], in_=pb[h][:], func=AF.Square),
                      reads=[pbk[h]], writes=[("osq", h)])
            P.dve(lambda e: e.reduce_sum(out=ssq[:], in_=osq[:].rearrange("p (h v) -> p h v", h=4), axis=AX.X),
                  reads=[("osq", h) for h in range(4)], writes=["ssq"])
            P.dve(lambda e: e.tensor_scalar(out=ssq[:], in0=ssq[:], scalar1=1.0 / 512.0, scalar2=EPS, op0=ALU.mult, op1=ALU.add),
                  reads=["ssq"], writes=["ssq"])
            P.act(lambda e: e.activation(out=ssq[:], in_=ssq[:], func=AF.Sqrt), reads=["ssq"], writes=["ssq"])
            P.dve(lambda e: e.reciprocal(out=ssq[:], in_=ssq[:]), reads=["ssq"], writes=["ssq"])
            for h in range(4):
                P.dve(lambda e, h=h, j=j, gt_=gt_: e.scalar_tensor_tensor(out=gt_[:, h * 512:(h + 1) * 512], in0=pb[h][:],
                                                                         scalar=ssq[:, h:h + 1], in1=sr[:, j, h * 512:(h + 1) * 512],
                                                                         op0=ALU.mult, op1=ALU.mult),
                      reads=[pbk[h], "ssq", ("sr", j)], writes=[(gk_, h)])

            def emit_gtr(gt_=gt_, gk_=gk_, j=j, jsl=jsl):
                for fb in range(4):
                    ptr, ptrk = pb[7], pbk[7]
                    for fi in range(4):
                        fc = fb * 4 + fi
                        P.pe(lambda e, fc=fc, fi=fi, ptr=ptr, gt_=gt_: e.transpose(pbf(ptr)[:, fi * 128:(fi + 1) * 128],
                                                                                  gt_[:, fc * 128:(fc + 1) * 128], ident[:]),
                             reads=[(gk_, fb), "ident"], writes=[ptrk])
                    P.act(lambda e, fb=fb, jsl=jsl, ptr=ptr: e.copy(out=gT[:, fb * 4:(fb + 1) * 4, jsl],
                                                                   in_=pbf(ptr)[:, 0:512].rearrange("p (f t) -> p f t", f=4)),
                          reads=[ptrk], writes=[("gT", fb, j)])
            pend_g.append(emit_gtr)
            if len(pend_g) > 1:
                pend_g.pop(0)()
        while pend_g:
            pend_g.pop(0)()
        for ob in (range(4) if not state_only else ()):
            Wo, Wok = load_w(wout_v, ob * 512)
            for dd in range(4):
                d = ob * 4 + dd
                py_, pyk = next_pb()
                for fc in range(DC):
                    P.pe(lambda e, fc=fc, dd=dd, py_=py_, Wo=Wo: e.matmul(py_[:], Wo[:, fc, dd * 128:(dd + 1) * 128], gT[:, fc, :],
                                                                          start=(fc == 0), stop=(fc == DC - 1)),
                         reads=[Wok] + [("gT", fc // 4, j) for j in range(NJ)], writes=[pyk])
                i = xctr[0] % 3
                xctr[0] += 1
                P.dma("sp", lambda e, i=i, d=d, tsl=tsl: e.dma_start(out=xs[i][:], in_=xin_v[:, d, tsl]), writes=["xs%d" % i])
                P.dve(lambda e, i=i, py_=py_: e.tensor_tensor(out=xs[i][:], in0=xs[i][:], in1=py_[:], op=ALU.add),
                      reads=[pyk, "xs%d" % i], writes=["xs%d" % i])
                P.dma("sp", lambda e, i=i, d=d, tsl=tsl: e.dma_start(out=xout_v[:, d, tsl], in_=xs[i][:]),
                      reads=["xs%d" % i], writes=[("xo", tt, d)])
                ph.out_keys.append(("xo", tt, d))
    if mode == "proj":
        P.dma("sp", lambda e: e.dma_start(out=dr["elast"].rearrange("p (dc jj) -> p dc jj", dc=8), in_=Elast[:]),
              reads=[("Elast", dc) for dc in range(8)], writes=["del"])
    if st_out is not None:
        P.dma("sp", lambda e: e.dma_start(out=st_out[:, :, :], in_=S[:]), reads=[("S", dc) for dc in range(8)],
              writes=["st_out"])
        ph.out_keys.append("st_out")
    return ph.finish()


def build_gla_scan0(ph):
    P = ph.P
    dr = ph.bind["dr"]
    st_out = ph.bind["st_out"]
    NCH = TC // 128
    kd_all = ph.sb("kd_all", [128, NCH, 1024], BF16)
    v_all = ph.sb("v_all", [128, NCH, 2048], BF16)
    El = ph.sb("El", [128, 8, NCH], F32)
    S = ph.sb("S", [128, 8, 512], F32)
    pb = [ph.ps("pb%d" % i, [128, 512]) for i in range(8)]
    kd_v = dr["kdec"].rearrange("(jj p) d -> p jj d", p=128)
    v_v = dr["v"].rearrange("(jj p) n -> p jj n", p=128)
    for q4 in range(4):
        P.dma("sp", lambda e, q4=q4: e.dma_start(out=kd_all[:, q4 * 4:(q4 + 1) * 4, :], in_=kd_v[:, q4 * 4:(q4 + 1) * 4, :]),
              writes=[("kd", q4)])
        P.dma("sp", lambda e, q4=q4: e.dma_start(out=v_all[:, q4 * 4:(q4 + 1) * 4, :], in_=v_v[:, q4 * 4:(q4 + 1) * 4, :]),
              writes=[("v", q4)])
    P.dma("sp", lambda e: e.dma_start(out=El[:], in_=dr["elast"].rearrange("p (dc jj) -> p dc jj", dc=8)), writes=["El"])
    P.dve(lambda e: e.memset(S[:], 0.0), writes=[("S", dc) for dc in range(8)])
    cnt = 0
    for jg in range(NCH):
        for h in range(4):
            for di in range(2):
                dc = 2 * h + di
                pkv, pkvk = pb[cnt % 8], "pb%d" % (cnt % 8)
                cnt += 1
                P.pe(lambda e, dc=dc, jg=jg, h=h, pkv=pkv: e.matmul(pkv[:], kd_all[:, jg, dc * 128:(dc + 1) * 128],
                                                                  v_all[:, jg, h * 512:(h + 1) * 512], start=True, stop=True),
                     reads=[("kd", jg // 4), ("v", jg // 4)], writes=[pkvk])
                P.dve(lambda e, dc=dc, jg=jg, pkv=pkv: e.scalar_tensor_tensor(out=S[:, dc, :], in0=S[:, dc, :],
                                                                             scalar=El[:, dc, jg:jg + 1], in1=pkv[:],
                                                                             op0=ALU.mult, op1=ALU.add),
                      reads=[pkvk, "El", ("S", dc)], writes=[("S", dc)])
    P.dma("sp", lambda e: e.dma_start(out=st_out[:, :, :], in_=S[:]), reads=[("S", dc) for dc in range(8)], writes=["st_out"])
    return ph.finish()


def pbf(ptile):
    return ptile[:].bitcast(BF16) if hasattr(ptile[:], "bitcast") else ptile


def gla_consts():
    tri = np.triu(np.ones((128, 128), np.float32))
    ident = np.eye(128, dtype=np.float32)
    return tri, ident


def gla_inputs(xT_core, g, w_in, w_a2, b_a, g_norm, w_out, state):
    tri, ident = gla_consts()
    st = np.ascontiguousarray(np.asarray(state, np.float32).reshape(4, 2, 128, 512).transpose(2, 0, 1, 3).reshape(128, 8, 512))
    gnb = np.ascontiguousarray(np.broadcast_to(np.tile(np.asarray(g_norm, np.float32), 4)[None, :], (128, 2048)))
    wa2b = np.ascontiguousarray(np.concatenate([np.asarray(w_a2, np.float32), np.asarray(b_a, np.float32)[None, :]], axis=0))
    return {"xT": np.ascontiguousarray(xT_core, dtype=np.float32), "g": col16(g), "win": np.ascontiguousarray(w_in, dtype=np.float32),
            "wa2b": wa2b, "gnb": gnb, "wout": np.ascontiguousarray(w_out, dtype=np.float32), "st_in": st, "tri": tri, "ident": ident}


def gla_state_from_out(st_out):
    return np.ascontiguousarray(st_out.reshape(128, 4, 2, 512).transpose(1, 2, 0, 3).reshape(4, 256, 512))


def build_diff1_phase(do_qk=True, do_v=True, do_norm=True, ph=None):
    ph = ph or Phase()
    P = ph.P
    xin = ph.din("xT", [D, TC])
    g_in = ph.din("g", [128, DC])
    win = ph.din("win", [D, 3 * D])
    qkg_in = ph.din("qkg", [128, 16])
    qko = ph.dout("qkT", [2 * D, TC])
    vo = ph.dout("v", [TC, D])
    NT = 512
    xs = [ph.sb("xs%d" % i, [128, 512], F32) for i in range(3)]
    hT = ph.sb("hTs", [128, DC, NT], BF16)
    Wb = [ph.sb("Wb%d" % i, [128, DC, 512], BF16) for i in range(2)]
    qraw = [ph.sb("qraw%d" % i, [128, 512], F32) for i in range(2)]
    odt = BF16 if ph.bind.get("k_ds") is not None else F32
    qn = [ph.sb("qn%d" % i, [128, 512], odt) for i in range(3)]
    vs = [ph.sb("vs%d" % i, [128, 512], odt) for i in range(3)]
    sq = [ph.sb("sq%d" % i, [128, 512], BF16) for i in range(2)]
    rstd = ph.sb("rstd", [128, 512], F32)
    rs2 = [ph.sb("rs2%d" % i, [128, 512], F32) for i in range(2)]
    gcol = ph.sb("gcol", [128, DC], F32)
    qkg = ph.sb("qkgs", [128, 16], F32)
    ones = ph.sb("ones", [128, 128], BF16)
    pb = [ph.ps("pb%d" % i, [128, 512]) for i in range(8)]
    pbk = ["pb%d" % i for i in range(8)]
    P.dma("sp", lambda e: e.dma_start(out=gcol[:], in_=g_in[:, :]), writes=["gcol"])
    P.dma("sp", lambda e: e.dma_start(out=qkg[:], in_=qkg_in[:, :]), writes=["qg", "kg"])
    P.dve(lambda e: e.memset(ones[:], 1.0), writes=["ones"])
    win_v = win.rearrange("(c p) n -> p c n", p=128)
    xin_v = xin.rearrange("(c p) t -> p c t", p=128)
    k_ds = ph.bind.get("k_ds")
    v_ds = ph.bind.get("v_ds")
    if k_ds is not None:
        q_v = ph.bind["q_d"].rearrange("(c p) t -> p c t", p=128)
        k_vs = [kd_.rearrange("(c p) t -> p c t", p=128) for kd_ in k_ds]
        v_vs = [vd_.rearrange("(j p) n -> p j n", p=128) for vd_ in v_ds]
    else:
        qko_v = qko.rearrange("(c p) t -> p c t", p=128)
    vo_v = vo.rearrange("(j p) n -> p j n", p=128)
    xctr = [0]
    wctr = [0]
    pctr = [0]
    nctr = [0]

    def load_w(col0):
        i = wctr[0] % 2
        wctr[0] += 1
        P.dma("pool", lambda e, i=i, col0=col0: e.dma_start(out=Wb[i][:], in_=win_v[:, :, col0:col0 + 512]), writes=["Wb%d" % i])
        return Wb[i], "Wb%d" % i

    def next_pb():
        i = 2 + pctr[0] % 4
        pctr[0] += 1
        return pb[i], pbk[i]

    for tt in range(TC // NT):
        tsl = slice(tt * NT, (tt + 1) * NT)
        p_ss, p_ssk = pb[0], pbk[0]
        for c in range(DC):
            i = xctr[0] % 3
            xctr[0] += 1
            P.dma("sp", lambda e, i=i, c=c, tsl=tsl: e.dma_start(out=xs[i][:], in_=xin_v[:, c, tsl]), writes=["xs%d" % i])
            q = sq[c % 2]
            qk = "sq%d" % (c % 2)
            P.act(lambda e, q=q, i=i: e.activation(out=q[:], in_=xs[i][:], func=AF.Square), reads=["xs%d" % i], writes=[qk])
            P.pe(lambda e, q=q, c=c: e.matmul(p_ss[:], ones[:], q[:], start=(c == 0), stop=(c == DC - 1)),
                 reads=[qk, "ones"], writes=[p_ssk])
        P.dve(lambda e: e.tensor_scalar(out=rstd[:], in0=p_ss[:], scalar1=1.0 / D, scalar2=EPS, op0=ALU.mult, op1=ALU.add),
              reads=[p_ssk], writes=["rstd"])
        P.act(lambda e: e.activation(out=rstd[:], in_=rstd[:], func=AF.Sqrt), reads=["rstd"], writes=["rstd"])
        P.dve(lambda e: e.reciprocal(out=rstd[:], in_=rstd[:]), reads=["rstd"], writes=["rstd"])
        for c in range(DC):
            i = xctr[0] % 3
            xctr[0] += 1
            P.dma("sp", lambda e, i=i, c=c, tsl=tsl: e.dma_start(out=xs[i][:], in_=xin_v[:, c, tsl]), writes=["xs%d" % i])
            P.dve(lambda e, i=i, c=c: e.scalar_tensor_tensor(out=hT[:, c, :], in0=xs[i][:], scalar=gcol[:, c:c + 1],
                                                             in1=rstd[:], op0=ALU.mult, op1=ALU.mult),
                  reads=["xs%d" % i, "rstd", "gcol"], writes=[("h", c)])
        pend_h = []
        for which in (range(2) if do_qk else ()):
            gk = "qg" if which == 0 else "kg"
            for blk in range(4):
                Wq, Wqk = load_w(which * D + blk * 512)
                for dl in range(4):
                    hd = blk * 4 + dl
                    pq, pqk = next_pb()
                    for c in range(DC):
                        P.pe(lambda e, c=c, dl=dl, pq=pq, Wq=Wq: e.matmul(pq[:], Wq[:, c, dl * 128:(dl + 1) * 128], hT[:, c, :],
                                                                          start=(c == 0), stop=(c == DC - 1)),
                             reads=[Wqk, ("h", c)], writes=[pqk])
                    if pend_h:
                        pend_h.pop(0)()
                    if do_norm:
                        qr = qraw[hd % 2]
                        qrk = "qraw%d" % (hd % 2)
                        sqb = sq[hd % 2]
                        sqk = "sq%d" % (hd % 2)
                        P.act(lambda e, pq=pq, sqb=sqb: e.activation(out=sqb[:], in_=pq[:], func=AF.Square), reads=[pqk], writes=[sqk])
                        P.act(lambda e, pq=pq, qr=qr: e.copy(out=qr[:], in_=pq[:]), reads=[pqk], writes=[qrk])
                        def tail(hd=hd, which=which, tsl=tsl, tt=tt, qr=qr, qrk=qrk, sqb=sqb, sqk=sqk, gk=gk):
                            p2, p2k = pb[6 + hd % 2], pbk[6 + hd % 2]
                            P.pe(lambda e, p2=p2, sqb=sqb: e.matmul(p2[:], ones[:], sqb[:], start=True, stop=True),
                                 reads=[sqk, "ones"], writes=[p2k])
                            r2 = rs2[hd % 2]
                            r2k = "rs2%d" % (hd % 2)
                            P.dve(lambda e, p2=p2, r2=r2: e.tensor_scalar(out=r2[:], in0=p2[:], scalar1=1.0 / 128.0, scalar2=EPS,
                                                                          op0=ALU.mult, op1=ALU.add), reads=[p2k], writes=[r2k])
                            P.act(lambda e, r2=r2: e.activation(out=r2[:], in_=r2[:], func=AF.Sqrt), reads=[r2k], writes=[r2k])
                            P.dve(lambda e, r2=r2: e.reciprocal(out=r2[:], in_=r2[:]), reads=[r2k], writes=[r2k])
                            ni = nctr[0] % 3
                            nctr[0] += 1
                            P.dve(lambda e, ni=ni, qr=qr, r2=r2, which=which: e.scalar_tensor_tensor(out=qn[ni][:], in0=qr[:], scalar=qkg[:, which:which + 1],
                                                                                                  in1=r2[:], op0=ALU.mult, op1=ALU.mult),
                                  reads=[qrk, r2k, gk], writes=["qn%d" % ni])
                            if k_ds is not None:
                                dst_ap = q_v[:, hd, tsl] if which == 0 else k_vs[hd // 2][:, hd % 2, tsl]
                            else:
                                dst_ap = qko_v[:, which * 16 + hd, tsl]
                            P.dma("sp", lambda e, ni=ni, dst_ap=dst_ap: e.dma_start(out=dst_ap, in_=qn[ni][:]),
                                  reads=["qn%d" % ni], writes=[("qo", which, tt, hd)])
                            ph.out_keys.append(("qo", which, tt, hd))
                        pend_h.append(tail)
                    else:
                        ni = nctr[0] % 3
                        nctr[0] += 1
                        P.act(lambda e, pq=pq, ni=ni: e.copy(out=qn[ni][:], in_=pq[:]), reads=[pqk], writes=["qn%d" % ni])
                    if not do_norm:
                        if k_ds is not None:
                            dst_ap = q_v[:, hd, tsl] if which == 0 else k_vs[hd // 2][:, hd % 2, tsl]
                        else:
                            dst_ap = qko_v[:, which * 16 + hd, tsl]
                        P.dma("sp", lambda e, ni=ni, dst_ap=dst_ap: e.dma_start(out=dst_ap, in_=qn[ni][:]),
                              reads=["qn%d" % ni], writes=[("qo", which, tt, hd)])
                        ph.out_keys.append(("qo", which, tt, hd))
        while pend_h:
            pend_h.pop(0)()
        for vb in (range(4) if do_v else ()):
            Wv, Wvk = load_w(2 * D + vb * 512)
            for j in range(4):
                jsl = slice(j * 128, (j + 1) * 128)
                pv, pvk = next_pb()
                for c in range(DC):
                    P.pe(lambda e, c=c, jsl=jsl, pv=pv, Wv=Wv: e.matmul(pv[:], hT[:, c, jsl], Wv[:, c, :],
                                                                        start=(c == 0), stop=(c == DC - 1)),
                         reads=[Wvk, ("h", c)], writes=[pvk])
                vi = nctr[0] % 3
                nctr[0] += 1
                P.act(lambda e, pv=pv, vi=vi: e.copy(out=vs[vi][:], in_=pv[:]), reads=[pvk], writes=["vs%d" % vi])
                if v_ds is not None:
                    for hh in range(2):
                        P.dma("sp", lambda e, vi=vi, tt=tt, j=j, vb=vb, hh=hh: e.dma_start(
                            out=v_vs[2 * vb + hh][:, tt * 4 + j, :], in_=vs[vi][:, hh * 256:(hh + 1) * 256]),
                            reads=["vs%d" % vi], writes=[("vo", tt, j, vb, hh)])
                else:
                    P.dma("sp", lambda e, vi=vi, tt=tt, j=j, vb=vb: e.dma_start(out=vo_v[:, tt * 4 + j, vb * 512:(vb + 1) * 512], in_=vs[vi][:]),
                          reads=["vs%d" % vi], writes=[("vo", tt, j, vb)])
                    ph.out_keys.append(("vo", tt, j, vb))
    return ph.finish()


LAM_INIT2 = 0.8 - 0.6 * math.exp(-0.3 * 2)
NEG = -30000.0


def build_diff2_phase(ph=None):
    ph = ph or Phase()
    P = ph.P
    fused = ph.master is not None
    qin = ph.din("qT", [D, TC])
    kin = None if fused else ph.din("kT", [D, 2 * TC])
    vin = None if fused else ph.din("va", [2 * TC, 8, 257])
    xin = ph.din("xT", [D, TC])
    wout = ph.din("wout", [D, D])
    b0_in = ph.din("B0", [128, 16, 128])
    b1_in = ph.din("B1", [128, 16, 128])
    b1p_in = ph.din("B1p", [128, 16, 128])
    cf_in = ph.din("cfar", [128, 16])
    cp_in = ph.din("cpre", [128, 16])
    lp_in = ph.din("lpb", [128, 4, 128])
    sg_in = ph.din("sgb", [128, 256])
    id_in = ph.din("ident", [128, 128])
    xout = ph.dout("xTo", [D, TC])
    SCALE = 128 ** -0.5
    Kt = ph.sb("Kt", [128, 2, 2 * TC], BF16)
    Va = ph.sb("Va", [128, 32, 257], BF16)
    Qt = ph.sb("Qt", [128, 2, TC], BF16)
    ao = ph.sb("ao", [128, 16, 2048], BF16)
    M0 = ph.sb("M0", [128, 16, 128], BF16)
    M1 = ph.sb("M1", [128, 16, 128], BF16)
    M1p = ph.sb("M1p", [128, 16, 128], BF16)
    btmp = ph.sb("btmp", [128, 16, 128], F32)
    cfar = ph.sb("cfars", [128, 16], F32)
    cpre = ph.sb("cpres", [128, 16], F32)
    negc = ph.sb("negc", [128, 16], F32)
    negcp = ph.sb("negcp", [128, 16], F32)
    lpb = ph.sb("lpbs", [128, 4, 128], F32)
    lt1 = ph.sb("lt1", [128, 128], F32)
    lsum = ph.sb("lsum", [128, 2], F32)
    neglam = ph.sb("neglam", [128, 1], F32)
    sgs = ph.sb("sgs", [128, 256], F32)
    ident = ph.sb("idents", [128, 128], BF16)
    PT = [ph.sb("PT%d" % i, [128, 2, 256], BF16) for i in range(3)]
    rc = ph.sb("rc", [128, 4], F32)
    uu = ph.sb("uu", [128, 256], F32)
    att = ph.sb("att", [128, 256], F32)
    asq = ph.sb("asq", [128, 256], F32)
    ssn = ph.sb("ssn", [128, 1], F32)
    ssn16 = ph.sb("ssn16", [128, 16], F32)
    aoT = ph.sb("aoT", [128, DC, 512], BF16)
    Wb = [ph.sb("Wb%d" % i, [128, DC, 512], BF16) for i in range(2)]
    xs = [ph.sb("xs%d" % i, [128, 512], F32) for i in range(3)]
    pb = [ph.ps("pb%d" % i, [128, 512]) for i in range(8)]
    pbk = ["pb%d" % i for i in range(8)]

    for (dst, src, k) in ((cfar, cf_in, "cfar"), (cpre, cp_in, "cpre"), (sgs, sg_in, "sgs")):
        P.dma("sp", lambda e, dst=dst, src=src: e.dma_start(out=dst[:], in_=src[:, :]), writes=[k])
    P.dma("sp", lambda e: e.dma_start(out=lpb[:], in_=lp_in[:, :, :]), writes=["lpb"])
    P.dma("pool", lambda e: e.dma_start(out=ident[:], in_=id_in[:, :]), writes=["ident"])
    P.dve(lambda e: e.tensor_scalar(out=negc[:], in0=cfar[:], scalar1=-1.0, scalar2=None, op0=ALU.mult), reads=["cfar"], writes=["negc"])
    P.dve(lambda e: e.tensor_scalar(out=negcp[:], in0=cpre[:], scalar1=-1.0, scalar2=None, op0=ALU.mult), reads=["cpre"], writes=["negcp"])
    for (Mt, src, nb, k) in ((M0, b0_in, negc, "M0"), (M1, b1_in, negc, "M1"), (M1p, b1p_in, negc, "M1p")):
        P.dma("sp", lambda e, src=src: e.dma_start(out=btmp[:], in_=src[:, :, :]), writes=["btmp"])
        for h in range(16):
            P.act(lambda e, Mt=Mt, nb=nb, h=h: e.activation(out=Mt[:, h, :], in_=btmp[:, h, :], func=AF.Exp, bias=nb[:, h:h + 1]),
                  reads=["btmp", "negc", "negcp"], writes=[k])
    for pi in range(2):
        P.dve(lambda e, pi=pi: e.tensor_tensor(out=lt1[:], in0=lpb[:, 2 * pi, :], in1=lpb[:, 2 * pi + 1, :], op=ALU.mult),
              reads=["lpb"], writes=["lt1"])
        P.dve(lambda e, pi=pi: e.reduce_sum(out=lsum[:, pi:pi + 1], in_=lt1[:], axis=AX.X), reads=["lt1"], writes=["lsum"])
    P.act(lambda e: e.activation(out=lsum[:], in_=lsum[:], func=AF.Exp), reads=["lsum"], writes=["lsum"])
    P.dve(lambda e: e.tensor_tensor(out=neglam[:], in0=lsum[:, 1:2], in1=lsum[:, 0:1], op=ALU.subtract), reads=["lsum"], writes=["neglam"])
    P.dve(lambda e: e.tensor_scalar(out=neglam[:], in0=neglam[:], scalar1=-LAM_INIT2, scalar2=None, op0=ALU.add),
          reads=["neglam"], writes=["neglam"])
    P.dve(lambda e: e.tensor_scalar(out=sgs[:], in0=sgs[:], scalar1=1.0 - LAM_INIT2, scalar2=None, op0=ALU.mult),
          reads=["sgs"], writes=["sgs"])

    qin_v = qin.rearrange("(c p) t -> p c t", p=128)
    if fused:
        kpre_vs = [a_.rearrange("(c p) t -> p c t", p=128) for a_ in ph.bind["kpres"]]
        kown_vs = [a_.rearrange("(c p) t -> p c t", p=128) for a_ in ph.bind["kowns"]]
        vpre_vs = [a_.rearrange("(kb p) n -> p kb n", p=128) for a_ in ph.bind["vpres"]]
        vown_vs = [a_.rearrange("(kb p) n -> p kb n", p=128) for a_ in ph.bind["vowns"]]
        P.dve(lambda e: e.memset(Va[:, :, 256:257], 1.0), writes=["Va1"])
        for ci in range(8):
            P.cc(lambda e, ci=ci: e.collective_compute("AllGather", ALU.bypass, replica_groups=PAIRS, ins=[ph.bind["kowns"][ci][:, :]],
                                                       outs=[ph.bind["k_alls"][ci][:, :]]), reads=[], writes=[("kall", ci)])
            P.cc(lambda e, ci=ci: e.collective_compute("AllGather", ALU.bypass, replica_groups=PAIRS, ins=[ph.bind["vowns"][ci][:, :]],
                                                       outs=[ph.bind["v_alls"][ci][:, :]]), reads=[], writes=[("vall", ci)])
    else:
        kin_v = kin.rearrange("(c p) t -> p c t", p=128)
        vin_v = vin.rearrange("(kb p) h n -> p kb h n", p=128)
    xin_v = xin.rearrange("(c p) t -> p c t", p=128)
    xout_v = xout.rearrange("(c p) t -> p c t", p=128)
    wout_v = wout.rearrange("(c p) n -> p c n", p=128)
    LOOK = 2
    NPT = 4
    PTs = [ph.sb("PTp%d" % i, [128, 2, 256], BF16) for i in range(NPT)]
    Osb = [ph.sb("Osb%d" % i, [128, 257], F32) for i in range(8)]
    zero_b = ph.sb("zero_b", [128, 1], F32)
    P.dve(lambda e: e.memset(zero_b[:], 0.0), writes=["zero_b"])
    gstep = [0]

    def stage_a(hp, qt, kb, sidx):
        kb_rel = kb - (16 + 2 * qt)
        qlo = 128 if kb_rel == 1 else 0
        ps, psk = pb[4 + sidx % 3], pbk[4 + sidx % 3]
        pt, ptk = PTs[sidx % NPT], "PTp%d" % (sidx % NPT)
        for i in range(2):
            P.pe(lambda e, ps=ps, i=i, kb=kb, qt=qt, qlo=qlo: e.matmul(
                ps[:, i * 256 + qlo:(i + 1) * 256], Kt[:, i, kb * 128:(kb + 1) * 128],
                Qt[:, i, qt * 256 + qlo:(qt + 1) * 256], start=True, stop=True),
                reads=["KtP" if kb < 16 else "KtO", "Qt"], writes=[psk])
        bias_ap = cpre[:, 0:1] if kb < 16 else zero_b[:, 0:1]
        P.act(lambda e, ps=ps, pt=pt, qlo=qlo, bias_ap=bias_ap: e.activation(
            out=pt[:, :, qlo:256], in_=ps[:].rearrange("p (i q) -> p i q", i=2)[:, :, qlo:256], func=AF.Exp,
            bias=bias_ap, scale=SCALE),
            reads=[psk, "cpre", "zero_b"], writes=[ptk])
        fix = []
        if kb_rel == -1:
            fix.append((0, M1p if kb == 15 else M1, "M1p" if kb == 15 else "M1"))
        elif kb_rel == 0:
            fix.append((0, M0, "M0"))
            fix.append((1, M1, "M1"))
        elif kb_rel == 1:
            fix.append((1, M0, "M0"))
        for (qb, Mt, mk) in fix:
            P.dve(lambda e, pt=pt, qb=qb, Mt=Mt, hp=hp: e.tensor_tensor(
                out=pt[:, :, qb * 128:(qb + 1) * 128], in0=pt[:, :, qb * 128:(qb + 1) * 128],
                in1=Mt[:, 2 * hp:2 * hp + 2, :], op=ALU.mult),
                reads=[ptk, mk], writes=[ptk])

    def stage_b(hp, qt, kb, sidx):
        pt, ptk = PTs[sidx % NPT], "PTp%d" % (sidx % NPT)
        for qb in range(2):
            last = 16 + 2 * qt + qb
            if kb > last:
                continue
            for i in range(2):
                acc = pb[qb * 2 + i]
                P.pe(lambda e, acc=acc, pt=pt, i=i, qb=qb, kb=kb, last=last: e.matmul(
                    acc[:, 0:257], pt[:, i, qb * 128:(qb + 1) * 128], Va[:, kb, :],
                    start=(kb == 16), stop=(kb == 15)),
                    reads=[ptk, "VaP" if kb < 16 else "VaO"], writes=[pbk[qb * 2 + i]])
        if kb == 15:
            finalize(hp, qt)

    fctr = [0]

    def finalize(hp, qt):
        fs = (fctr[0] % 2) * 4
        fctr[0] += 1
        for a_i in range(4):
            P.dve(lambda e, a_i=a_i, fs=fs: e.tensor_scalar(out=Osb[fs + a_i][:], in0=pb[a_i][:, 0:257], scalar1=1.0, scalar2=None,
                                                            op0=ALU.mult),
                  reads=[pbk[a_i]], writes=["Osb%d" % (fs + a_i)])
        for qb in range(2):
            o1, o1k = Osb[fs + qb * 2], "Osb%d" % (fs + qb * 2)
            o2, o2k = Osb[fs + qb * 2 + 1], "Osb%d" % (fs + qb * 2 + 1)
            P.dve(lambda e, o1=o1: e.reciprocal(out=rc[:, 0:1], in_=o1[:, 256:257]), reads=[o1k], writes=["rc"])
            P.dve(lambda e, o2=o2: e.reciprocal(out=rc[:, 1:2], in_=o2[:, 256:257]), reads=[o2k], writes=["rc"])
            P.dve(lambda e: e.tensor_tensor(out=rc[:, 1:2], in0=rc[:, 1:2], in1=neglam[:], op=ALU.mult),
                  reads=["rc", "neglam"], writes=["rc"])
            P.dve(lambda e, o2=o2: e.tensor_scalar(out=uu[:], in0=o2[:, 0:256], scalar1=rc[:, 1:2], scalar2=None, op0=ALU.mult),
                  reads=[o2k, "rc"], writes=["uu"])
            P.dve(lambda e, o1=o1: e.scalar_tensor_tensor(out=att[:], in0=o1[:, 0:256], scalar=rc[:, 0:1], in1=uu[:],
                                                          op0=ALU.mult, op1=ALU.add),
                  reads=[o1k, "rc", "uu"], writes=["att"])
            P.dve(lambda e: e.tensor_tensor(out=asq[:], in0=att[:], in1=att[:], op=ALU.mult), reads=["att"], writes=["asq"])
            qbg = 2 * qt + qb
            P.dve(lambda e, qbg=qbg: e.reduce_sum(out=ssn16[:, qbg:qbg + 1], in_=asq[:], axis=AX.X), reads=["asq"], writes=["ssn16"])
            P.dve(lambda e, qbg=qbg, hp=hp: e.tensor_scalar(out=ao[:, qbg, hp * 256:(hp + 1) * 256], in0=att[:], scalar1=1.0,
                                                            scalar2=None, op0=ALU.mult),
                  reads=["att"], writes=[("ao", qbg)])

    def post_hp(hp):
        P.dve(lambda e: e.tensor_scalar(out=ssn16[:], in0=ssn16[:], scalar1=1.0 / 256.0, scalar2=EPS, op0=ALU.mult, op1=ALU.add),
              reads=["ssn16"], writes=["ssn16"])
        P.act(lambda e: e.activation(out=ssn16[:], in_=ssn16[:], func=AF.Sqrt), reads=["ssn16"], writes=["ssn16"])
        P.dve(lambda e: e.reciprocal(out=ssn16[:], in_=ssn16[:]), reads=["ssn16"], writes=["ssn16"])
        for qbg in range(16):
            P.dve(lambda e, qbg=qbg, hp=hp: e.scalar_tensor_tensor(out=ao[:, qbg, hp * 256:(hp + 1) * 256],
                                                                   in0=ao[:, qbg, hp * 256:(hp + 1) * 256],
                                                                   scalar=ssn16[:, qbg:qbg + 1], in1=sgs[:], op0=ALU.mult, op1=ALU.mult),
                  reads=[("ao", qbg), "ssn16", "sgs"], writes=[("ao", qbg)])

    for hp in range(8):
        if fused:
            P.dma("sp", lambda e, hp=hp: e.dma_start(out=Kt[:, :, TC:2 * TC], in_=kown_vs[hp][:, :, :]), writes=["KtO"])
            P.dma("sp", lambda e, hp=hp: e.dma_start(out=Va[:, 16:32, 0:256], in_=vown_vs[hp][:, :, :]), reads=["Va1"], writes=["VaO"])
            P.dma("sp", lambda e, hp=hp: e.dma_start(out=Qt[:], in_=qin_v[:, 2 * hp:2 * hp + 2, :]), writes=["Qt"])
            P.dma("sp", lambda e, hp=hp: e.dma_start(out=Kt[:, :, 0:TC], in_=kpre_vs[hp][:, :, :]), reads=[("kall", hp)], writes=["KtP"])
            P.dma("sp", lambda e, hp=hp: e.dma_start(out=Va[:, 0:16, 0:256], in_=vpre_vs[hp][:, :, :]), reads=["Va1", ("vall", hp)],
                  writes=["VaP"])
        else:
            P.dma("pool", lambda e, hp=hp: e.dma_start(out=Kt[:, :, 0:TC], in_=kin_v[:, 2 * hp:2 * hp + 2, 0:TC]), writes=["KtP"])
            P.dma("pool", lambda e, hp=hp: e.dma_start(out=Kt[:, :, TC:2 * TC], in_=kin_v[:, 2 * hp:2 * hp + 2, TC:2 * TC]), writes=["KtO"])
            P.dma("pool", lambda e, hp=hp: e.dma_start(out=Va[:, 0:16, :], in_=vin_v[:, 0:16, hp, :]), writes=["VaP"])
            P.dma("pool", lambda e, hp=hp: e.dma_start(out=Va[:, 16:32, :], in_=vin_v[:, 16:32, hp, :]), writes=["VaO"])
            P.dma("pool", lambda e, hp=hp: e.dma_start(out=Qt[:], in_=qin_v[:, 2 * hp:2 * hp + 2, :]), writes=["Qt"])
        steps = [(qt, kb) for qt in range(8) for kb in (list(range(16, 16 + 2 * qt + 2)) + list(range(16)))]
        base = gstep[0]
        for n in range(len(steps) + LOOK):
            if n < len(steps):
                stage_a(hp, steps[n][0], steps[n][1], base + n)
            if n >= LOOK:
                stage_b(hp, steps[n - LOOK][0], steps[n - LOOK][1], base + n - LOOK)
        gstep[0] += len(steps)
        post_hp(hp)
    xctr = [0]
    wctr = [0]
    pctr = [0]
    for tt in range(4):
        tsl = slice(tt * 512, (tt + 1) * 512)
        for j in range(4):
            qbg = tt * 4 + j
            for fb in range(4):
                ptr, ptrk = pb[6], pbk[6]
                for fi in range(4):
                    fc = fb * 4 + fi
                    P.pe(lambda e, ptr=ptr, fi=fi, fc=fc, qbg=qbg: e.transpose(pbf(ptr)[:, fi * 128:(fi + 1) * 128],
                                                                              ao[:, qbg, fc * 128:(fc + 1) * 128], ident[:]),
                         reads=[("ao", qbg), "ident"], writes=[ptrk])
                P.act(lambda e, ptr=ptr, fb=fb, j=j: e.copy(out=aoT[:, fb * 4:(fb + 1) * 4, j * 128:(j + 1) * 128],
                                                            in_=pbf(ptr)[:, 0:512].rearrange("p (f t) -> p f t", f=4)),
                      reads=[ptrk], writes=[("aoT", fb)])
        for ob in range(4):
            wi = wctr[0] % 2
            wctr[0] += 1
            P.dma("pool", lambda e, wi=wi, ob=ob: e.dma_start(out=Wb[wi][:], in_=wout_v[:, :, ob * 512:(ob + 1) * 512]),
                  writes=["Wb%d" % wi])
            for dd in range(4):
                d = ob * 4 + dd
                pi = pctr[0] % 2
                pctr[0] += 1
                py_, pyk = pb[pi], pbk[pi]
                for fc in range(DC):
                    P.pe(lambda e, fc=fc, dd=dd, py_=py_, wi=wi: e.matmul(py_[:], Wb[wi][:, fc, dd * 128:(dd + 1) * 128], aoT[:, fc, :],
                                                                          start=(fc == 0), stop=(fc == DC - 1)),
                         reads=["Wb%d" % wi, ("aoT", fc // 4)], writes=[pyk])
                xi = xctr[0] % 3
                xctr[0] += 1
                P.dma("sp", lambda e, xi=xi, d=d, tsl=tsl: e.dma_start(out=xs[xi][:], in_=xin_v[:, d, tsl]), writes=["xs%d" % xi])
                P.dve(lambda e, xi=xi, py_=py_: e.tensor_tensor(out=xs[xi][:], in0=xs[xi][:], in1=py_[:], op=ALU.add),
                      reads=[pyk, "xs%d" % xi], writes=["xs%d" % xi])
                P.dma("sp", lambda e, xi=xi, d=d, tsl=tsl: e.dma_start(out=xout_v[:, d, tsl], in_=xs[xi][:]),
                      reads=["xs%d" % xi], writes=[("xo", tt, d)])
                ph.out_keys.append(("xo", tt, d))
    return ph.finish()


def rel_bucket_np(rel):
    n = np.maximum(rel, 0)
    nf = np.maximum(n, 1).astype(np.float32)
    large = 16 + (np.log(nf / np.float32(16)) / np.float32(math.log(128 / 16)) * np.float32(16)).astype(np.int32)
    large = np.minimum(large, 31)
    return np.where(n < 16, n, large)


def diff_bias_tiles(rel_bias, first_half):
    tab = np.concatenate([np.asarray(rel_bias, np.float32), np.full((1, 16), NEG, np.float32),
                          np.full((1, 16), 2 * NEG, np.float32)], axis=0)
    k = np.arange(128)[:, None]
    q = np.arange(128)[None, :]
    rel0 = q - k
    idx0 = np.where(rel0 >= 0, rel_bucket_np(rel0), 32)
    idx1 = rel_bucket_np(128 + q - k)
    B0 = np.ascontiguousarray(tab[idx0].transpose(0, 2, 1))
    B1 = np.ascontiguousarray(tab[idx1].transpose(0, 2, 1))
    if first_half:
        B1p = np.full((128, 16, 128), 2 * NEG, np.float32)
        cpre = np.full((128, 16), NEG, np.float32)
    else:
        B1p = B1.copy()
        cpre = np.zeros((128, 16), np.float32)
    cfar = np.ascontiguousarray(np.broadcast_to(tab[31][None, :], (128, 16)))
    return B0, B1, B1p, cfar, cpre


def diff2_inputs(qT, kT_own, v_own, kT_prev, v_prev, xT, w_out, rel_bias, lam_params, sub_gain, first_half):
    kT_all = np.zeros((D, 2 * TC), np.float32)
    va = np.zeros((2 * TC, 8, 257), np.float32)
    kT_all[:, TC:] = kT_own
    va[TC:, :, :256] = v_own.reshape(TC, 8, 256)
    va[TC:, :, 256] = 1.0
    if not first_half:
        kT_all[:, :TC] = kT_prev
        va[:TC, :, :256] = v_prev.reshape(TC, 8, 256)
        va[:TC, :, 256] = 1.0
    B0, B1, B1p, cfar, cpre = diff_bias_tiles(rel_bias, first_half)
    lpb = np.ascontiguousarray(np.broadcast_to(np.asarray(lam_params, np.float32)[None], (128, 4, 128)))
    sgb = np.ascontiguousarray(np.broadcast_to(np.asarray(sub_gain, np.float32)[None, :], (128, 256)))
    return {"qT": np.ascontiguousarray(qT), "kT": kT_all, "va": va, "xT": np.ascontiguousarray(xT),
            "wout": np.ascontiguousarray(w_out, dtype=np.float32), "B0": B0, "B1": B1, "B1p": B1p, "cfar": cfar, "cpre": cpre,
            "lpb": lpb, "sgb": sgb, "ident": np.eye(128, dtype=np.float32)}


def diff1_inputs(xT, g, w_in, qg, kg):
    qkg = np.zeros((128, 16), np.float32)
    qkg[:, 0] = np.asarray(qg, np.float32)
    qkg[:, 1] = np.asarray(kg, np.float32)
    return {"xT": np.ascontiguousarray(xT), "g": col16(g), "win": np.ascontiguousarray(w_in, dtype=np.float32), "qkg": qkg}


PAIRS = [[0, 1], [2, 3], [4, 5], [6, 7]]
DEPTH = 4


def build_fused(plan=("gla0", "ffn0", "pool", "ffn1", "diff", "ffn2", "gla1", "ffn3")):
    mp = Phase()
    m = mp
    nc, P = mp.nc, mp.P
    mp.btile = mp.sb("btile", [128, 1], F32)
    cache = {}

    def lz(name, shape):
        if name not in cache:
            cache[name] = mp.din(name, shape)
        return cache[name]

    def lzi(name, shape):
        if name not in cache:
            cache[name] = mp.dint(name, shape)
        return cache[name]

    xT_in = mp.din("xT", [D, TC])
    xT_out = mp.dout("xTo", [D, TC])

    def ngf(l, i):
        return lz("ng_%d_%d" % (l, i), [128, DC])

    def barrier():
        P.barrier(lambda e, bt=mp.btile: e.memset(bt[:], 0.0))

    def gather(src, dst, tag):
        P.cc(lambda e: e.collective_compute("AllGather", ALU.bypass, replica_groups=PAIRS, ins=[src], outs=[dst]),
             reads=[], writes=[("cc", tag)])
        barrier()

    def lzb(name, shape):
        if name not in cache:
            cache[name] = mp.dint(name, shape, BF16)
        return cache[name]

    def gla_layer(sl, layer, xin, xout):
        st_src = lzi("st_src", [1024, 512])
        st_all = lzi("st_all", [2048, 512])
        dr = dict(q=lzb("g_q", [1024, TC]), k=lzb("g_k", [1024, TC]), kdec=lzb("g_kd", [TC, 1024]), v=lzb("g_v", [TC, 2048]),
                  sr=lzb("g_sr", [TC, 2048]), elast=lzi("g_el", [128, 128]))
        common = dict(xT=xin, g=ngf(layer, 0), win=lz("gla%d_win" % sl, [D, GLA_NCOL]), wa2b=lz("gla%d_wa2b" % sl, [17, GLA_DKT]),
                      gnb=lz("gla%d_gnb" % sl, [128, 2048]), wout=lz("gla%d_wout" % sl, [D, D]), tri=lz("tri", [128, 128]),
                      ident=lz("ident", [128, 128]), isb=lz("isb", [128, 16]), dr=dr)
        b1 = dict(common)
        b1.update(st_in=None, st_out=None)
        build_gla_phase(Phase(mp, "g%dp_" % layer, b1), mode="proj")
        build_gla_scan0(Phase(mp, "g%ds_" % layer, dict(dr=dr, st_out=st_src.rearrange("(p c) v -> p c v", c=8))))
        gather(st_src[:, :], st_all[:, :], ("st", layer))
        b2 = dict(common)
        b2.update(st_in=st_all[0:1024, :].rearrange("(p c) v -> p c v", c=8), st_out=None, xTo=xout)
        build_gla_phase(Phase(mp, "g%df_" % layer, b2), mode="scanf")

    def ffn_layer(layer, xin, xout):
        build_ffn_phase(Phase(mp, "f%d_" % layer, dict(xT=xin, xTo=xout, g=ngf(layer, 1), wgu=lz("ffn%d_wgu" % layer, [D, 2 * FH]),
                                                      wdn=lz("ffn%d_wdn" % layer, [FH, D]))))

    def pool_layer(layer, xin, xout):
        halo_src = lzi("halo_src", [D, 16])
        halo_all = lzi("halo_all", [2 * D, 16])
        P.dma("sp", lambda e: e.dma_start(out=halo_src[:, :], in_=xin[:, TC - 16:TC]), writes=["halo_src"])
        barrier()
        gather(halo_src[:, :], halo_all[:, :], "halo")
        build_pool_phase(Phase(mp, "p1_", dict(xT=xin, halo=halo_all[0:D, :], isb=lz("isb", [128, 16]), g=ngf(layer, 0),
                                               wp=lz("pool_wp", [4, 512, 512]), psc=lz("pool_psc", [128, DC]),
                                               invc=lz("pool_invc", [128, 4, 16]), xTo=xout)))

    def diff_layer(layer, xin, xout):
        q_d = lzb("q_d", [D, TC])
        k_ds = [lzb("k_d%d" % i, [256, TC]) for i in range(8)]
        v_ds = [lzb("v_d%d" % i, [TC, 256]) for i in range(8)]
        k_alls = [lzb("k_all%d" % i, [512, TC]) for i in range(8)]
        v_alls = [lzb("v_all%d" % i, [2 * TC, 256]) for i in range(8)]
        build_diff1_phase(ph=Phase(mp, "d1_", dict(xT=xin, g=ngf(layer, 0), win=lz("diff_win", [D, 3 * D]),
                                                   qkg=lz("diff_qkg", [128, 16]), qkT=q_d, q_d=q_d, k_ds=k_ds, v_ds=v_ds, v=v_ds[0])))
        build_diff2_phase(Phase(mp, "d2_", dict(qT=q_d, kpres=[a_[0:256, :] for a_ in k_alls], kowns=k_ds,
                                                vpres=[a_[0:TC, :] for a_ in v_alls], vowns=v_ds, xT=xin,
                                                k_alls=k_alls, v_alls=v_alls,
                                                wout=lz("diff_wout", [D, D]), B0=lz("diff_B0", [128, 16, 128]),
                                                B1=lz("diff_B1", [128, 16, 128]), B1p=lz("diff_B1p", [128, 16, 128]),
                                                cfar=lz("diff_cfar", [128, 16]), cpre=lz("diff_cpre", [128, 16]),
                                                lpb=lz("diff_lpb", [128, 4, 128]), sgb=lz("diff_sgb", [128, 256]),
                                                ident=lz("ident", [128, 128]), xTo=xout)))

    bufs = [lzi("xa", [D, TC]), lzi("xb", [D, TC])]
    cur = xT_in
    for si, step in enumerate(plan):
        dst = xT_out if si == len(plan) - 1 else bufs[si % 2]
        layer = int(step[-1]) if step[:3] == "ffn" else {"gla0": 0, "pool": 1, "diff": 2, "gla1": 3}[step]
        if step[:3] == "ffn":
            ffn_layer(layer, cur, dst)
        elif step[:3] == "gla":
            gla_layer(int(step[3]), layer, cur, dst)
        elif step == "pool":
            pool_layer(layer, cur, dst)
        else:
            diff_layer(layer, cur, dst)
        cur = dst
    mp.input_names = [k for k in cache if not k.startswith("g_") and not k in ("xa", "xb", "st_src", "st_all", "halo_src", "halo_all", "q_d", "k_d", "v_d", "k_all", "v_all")]
    return mp.finish()


_FUSED = []


def kernel(x, norm_g, gla_w_in, gla_w_a2, gla_b_a, gla_g_norm, gla_w_out, pool_w, pool_scale, diff_w_in,
           diff_q_gain, diff_k_gain, diff_lambda, diff_sub_gain, diff_w_out, rel_bias, ffn_w_gu, ffn_w_down):
    x = np.asarray(x, np.float32)
    B, S, _ = x.shape
    if not _FUSED:
        _FUSED.append(build_fused())
    nc = _FUSED[0]
    f32c = lambda a: np.ascontiguousarray(np.asarray(a, np.float32))
    tri, ident = gla_consts()
    shared = {"tri": tri, "ident": ident}
    for l in range(DEPTH):
        for i in range(2):
            shared["ng_%d_%d" % (l, i)] = col16(norm_g[l, i])
        shared["ffn%d_wgu" % l] = f32c(ffn_w_gu[l])
        shared["ffn%d_wdn" % l] = f32c(ffn_w_down[l])
    for sl in range(2):
        shared["gla%d_win" % sl] = f32c(gla_w_in[sl])
        shared["gla%d_wa2b" % sl] = f32c(np.concatenate([np.asarray(gla_w_a2[sl], np.float32),
                                                         np.asarray(gla_b_a[sl], np.float32)[None, :]], axis=0))
        shared["gla%d_gnb" % sl] = f32c(np.broadcast_to(np.tile(np.asarray(gla_g_norm[sl], np.float32), 4)[None, :], (128, 2048)))
        shared["gla%d_wout" % sl] = f32c(gla_w_out[sl])
    shared["pool_wp"] = f32c(pool_w[0])
    shared["pool_psc"] = col16(pool_scale[0])
    shared["diff_win"] = f32c(diff_w_in[0])
    qkg = np.zeros((128, 16), np.float32)
    qkg[:, 0] = np.asarray(diff_q_gain[0], np.float32)
    qkg[:, 1] = np.asarray(diff_k_gain[0], np.float32)
    shared["diff_qkg"] = qkg
    shared["diff_wout"] = f32c(diff_w_out[0])
    shared["diff_lpb"] = f32c(np.broadcast_to(np.asarray(diff_lambda[0], np.float32)[None], (128, 4, 128)))
    shared["diff_sgb"] = f32c(np.broadcast_to(np.asarray(diff_sub_gain[0], np.float32)[None, :], (128, 256)))
    per_half = []
    for half in range(2):
        B0, B1, B1p, cfar, cpre = diff_bias_tiles(rel_bias, half == 0)
        invc = np.zeros((128, 4, 16), np.float32)
        for gi, w in enumerate(POOL_W):
            for t in range(16):
                invc[:, gi, t] = 1.0 / (min(t + 1, w) if half == 0 else w)
        per_half.append({"diff_B0": B0, "diff_B1": B1, "diff_B1p": B1p, "diff_cfar": cfar, "diff_cpre": cpre,
                         "pool_invc": invc, "isb": np.full((128, 16), float(half), np.float32)})
    in_maps = []
    for c in range(NCORES):
        im = dict(shared)
        im.update(per_half[c % 2])
        im["xT"] = np.ascontiguousarray(x[c // 2, (c % 2) * TC:(c % 2 + 1) * TC].T)
        in_maps.append(im)
    res = run_bass_kernel_spmd(nc, in_maps, core_ids=list(range(NCORES))).results
    out = np.empty((B, S, D), np.float32)
    for c in range(NCORES):
        out[c // 2, (c % 2) * TC:(c % 2 + 1) * TC] = res[c]["xTo"].T
    return out
```
